# Optimizing a Trainium2 kernel written in Bass

```python
import jax
import jax.numpy as jnp
from jax import lax
import numpy as np

D_MODEL = 1024
BATCH = 16
SEQ = 2048
DEPTH = 4

CHUNK = 64
MEM_TOKENS = 256
N_EVEN = (DEPTH + 1) // 2
N_ODD = DEPTH // 2
EPS = 1e-6

A_HEADS = 4
A_HEAD_DIM = D_MODEL // 8
A_WIDTH = A_HEADS * A_HEAD_DIM
A_SUB = 8
A_NSUB = CHUNK // A_SUB

B_HEADS = 8
B_HEAD_DIM = D_MODEL // 16
B_WIDTH = B_HEADS * B_HEAD_DIM
B_PREV_CHUNKS = 8
B_BAND = B_PREV_CHUNKS + 1
REL_CLIP = 2 * CHUNK

AB_IN = 4 * A_WIDTH + 3 * B_WIDTH
AB_OUT = A_WIDTH + B_WIDTH
AB_SPLITS = (A_WIDTH, 2 * A_WIDTH, 3 * A_WIDTH, 4 * A_WIDTH,
             4 * A_WIDTH + B_WIDTH, 4 * A_WIDTH + 2 * B_WIDTH)

C_HEADS = 16
C_HEAD_DIM = D_MODEL // C_HEADS
C_WIDTH = C_HEADS * C_HEAD_DIM
C_IN = 3 * C_WIDTH + C_HEADS
Q_BLOCK = 128
FOX_BIAS_CENTER = 3.0

X_HEADS = 4
X_HEAD_DIM = D_MODEL // X_HEADS
X_WIDTH = X_HEADS * X_HEAD_DIM

D_FF = 4 * D_MODEL

kernel_name = 'hybrid_hgrn2_chunkattn_fox_encoder'


def rms_norm(x, gain):
    xf = x.astype(jnp.float32)
    y = xf * lax.rsqrt(jnp.mean(xf * xf, axis=-1, keepdims=True) + EPS)
    return (y * gain.astype(jnp.float32)).astype(x.dtype)


def split_heads(t, n_heads):
    return t.reshape(t.shape[0], t.shape[1], n_heads, -1)


def hgrn2_chunkwise(q, f_logit, i_val, lb):
    bsz, seq = q.shape[:2]
    n_chunks = seq // CHUNK
    z = f_logit.astype(jnp.float32)
    lb = lb.astype(jnp.float32)
    log_f = jnp.log(lb + (1.0 - lb) * jax.nn.sigmoid(z))
    k = (1.0 - lb) * jax.nn.sigmoid(-z)
    q = jax.nn.silu(q.astype(jnp.float32)) * A_HEAD_DIM ** -0.5
    v = i_val.astype(jnp.float32)

    def blocks(t):
        t = t.reshape(bsz, n_chunks, A_NSUB, A_SUB, A_HEADS, t.shape[-1])
        return t.transpose(0, 4, 1, 2, 3, 5)

    q, k, v, log_f = blocks(q), blocks(k), blocks(v), blocks(log_f)
    shp = log_f.shape
    b = jnp.cumsum(log_f.reshape(shp[:3] + (CHUNK, A_HEAD_DIM)), axis=3).reshape(shp)

    b_last = b[:, :, :, -1, -1]
    k_to_end = k * jnp.exp(b_last[:, :, :, None, None] - b)
    chunk_upd = jnp.einsum('bhnisk,bhnisv->bhnkv', k_to_end, v)

    def step(state, inp):
        decay, upd = inp
        return decay[..., None] * state + upd, state

    state0 = jnp.zeros((bsz, A_HEADS, A_HEAD_DIM, A_HEAD_DIM), jnp.float32)
    _, s_start = lax.scan(step, state0,
                          (jnp.moveaxis(jnp.exp(b_last), 2, 0), jnp.moveaxis(chunk_upd, 2, 0)))
    s_start = jnp.moveaxis(s_start, 0, 2)
    o = jnp.einsum('bhnitk,bhnkv->bhnitv', q * jnp.exp(b), s_start)

    r = b[:, :, :, :, -1]
    sub = jnp.arange(A_NSUB)
    earlier = (sub[:, None] > sub[None, :])[:, None, :, None]
    expo = b[:, :, :, :, :, None, :] - r[:, :, :, None, None, :, :]
    q_ref = q[:, :, :, :, :, None, :] * jnp.exp(jnp.where(earlier, expo, -jnp.inf))
    k_ref = k * jnp.exp(r[:, :, :, :, None, :] - b)
    a_off = jnp.einsum('bhnitjk,bhnjsk->bhnitjs', q_ref, k_ref)
    o = o + jnp.einsum('bhnitjs,bhnjsv->bhnitv', a_off, v)

    tri = jnp.tril(jnp.ones((A_SUB, A_SUB), dtype=bool))[:, :, None]
    pair = b[:, :, :, :, :, None, :] - b[:, :, :, :, None, :, :]
    decay = jnp.exp(jnp.where(tri, pair, -jnp.inf))
    a_diag = jnp.einsum('bhnitk,bhnitsk,bhnisk->bhnits', q, decay, k)
    o = o + jnp.einsum('bhnits,bhnisv->bhnitv', a_diag, v)
    return o.transpose(0, 2, 3, 4, 1, 5).reshape(bsz, seq, A_WIDTH)


def chunk_band_attention(q, k, v, rel_table):
    bsz, seq, n_heads, dh = q.shape
    n_chunks = seq // CHUNK
    qc = q.reshape(bsz, n_chunks, CHUNK, n_heads, dh)
    pad = ((0, 0), (B_PREV_CHUNKS, 0), (0, 0), (0, 0), (0, 0))
    kc = jnp.pad(k.reshape(bsz, n_chunks, CHUNK, n_heads, dh), pad)
    vc = jnp.pad(v.reshape(bsz, n_chunks, CHUNK, n_heads, dh), pad)
    band_idx = jnp.arange(n_chunks)[:, None] + jnp.arange(B_BAND)[None, :]
    kb = kc[:, band_idx].reshape(bsz, n_chunks, B_BAND * CHUNK, n_heads, dh)
    vb = vc[:, band_idx].reshape(bsz, n_chunks, B_BAND * CHUNK, n_heads, dh)
    logits = jnp.einsum('bnqhd,bnkhd->bhnqk', qc, kb).astype(jnp.float32) * dh ** -0.5
    offs = jnp.arange(B_BAND * CHUNK)
    rel = B_PREV_CHUNKS * CHUNK + jnp.arange(CHUNK)[:, None] - offs[None, :]
    rel_idx = jnp.clip(rel, -REL_CLIP, REL_CLIP) + REL_CLIP
    bias = rel_table.astype(jnp.float32)[:, rel_idx]
    valid = (jnp.arange(n_chunks)[:, None] - B_PREV_CHUNKS + offs[None, :] // CHUNK) >= 0
    logits = jnp.where(valid[None, None, :, None, :], logits + bias[None, :, None], -jnp.inf)
    p = jax.nn.softmax(logits, axis=-1).astype(v.dtype)
    out = jnp.einsum('bhnqk,bnkhd->bnqhd', p, vb)
    return out.reshape(bsz, seq, n_heads * dh)


def forgetting_attention(q, k, v, f_logit):
    bsz, seq, n_heads, dh = q.shape
    n_blocks = seq // Q_BLOCK
    cum = jnp.cumsum(jax.nn.log_sigmoid(f_logit.astype(jnp.float32)), axis=1)
    cum_k = cum.transpose(0, 2, 1)
    q_blocks = q.reshape(bsz, n_blocks, Q_BLOCK, n_heads, dh).transpose(1, 0, 2, 3, 4)
    c_blocks = cum.reshape(bsz, n_blocks, Q_BLOCK, n_heads).transpose(1, 0, 3, 2)
    pos_q = jnp.arange(seq).reshape(n_blocks, Q_BLOCK)
    pos_k = jnp.arange(seq)

    def one_block(args):
        q_blk, c_blk, p_blk = args
        s = jnp.einsum('bqhd,bkhd->bhqk', q_blk, k).astype(jnp.float32) * dh ** -0.5
        s = s + c_blk[..., None] - cum_k[:, :, None, :]
        s = jnp.where((p_blk[:, None] >= pos_k[None, :])[None, None], s, -jnp.inf)
        p = jax.nn.softmax(s, axis=-1).astype(v.dtype)
        return jnp.einsum('bhqk,bkhd->bqhd', p, v)

    out = lax.map(one_block, (q_blocks, c_blocks, pos_q))
    return out.transpose(1, 0, 2, 3, 4).reshape(bsz, seq, n_heads * dh)


def hgrn2_chunkattn_mixer(xn, w_in, lb, a_gain, rel_table, w_out):
    h = xn @ w_in
    q_a, f_a, i_a, g_a, q_b, k_b, v_b = jnp.split(h, AB_SPLITS, axis=-1)
    o_a = hgrn2_chunkwise(split_heads(q_a, A_HEADS), split_heads(f_a, A_HEADS),
                          split_heads(i_a, A_HEADS), lb.reshape(A_HEADS, A_HEAD_DIM))
    o_a = rms_norm(split_heads(o_a, A_HEADS), a_gain.reshape(A_HEADS, A_HEAD_DIM)).reshape(o_a.shape)
    o_a = (o_a * jax.nn.silu(g_a.astype(jnp.float32))).astype(xn.dtype)
    o_b = chunk_band_attention(split_heads(q_b, B_HEADS), split_heads(k_b, B_HEADS),
                               split_heads(v_b, B_HEADS), rel_table)
    return jnp.concatenate([o_a, o_b], axis=-1) @ w_out


def forgetting_mixer(xn, w_in, f_bias, w_out):
    h = xn @ w_in
    q, k, v, f = jnp.split(h, (C_WIDTH, 2 * C_WIDTH, 3 * C_WIDTH), axis=-1)
    o = forgetting_attention(split_heads(q, C_HEADS), split_heads(k, C_HEADS),
                             split_heads(v, C_HEADS), f + f_bias)
    return o @ w_out


def memory_cross_attention(xn, mem_n, w_q, w_kv, w_o):
    q = split_heads(xn @ w_q, X_HEADS)
    k, v = jnp.split(mem_n @ w_kv, 2, axis=-1)
    k, v = split_heads(k, X_HEADS), split_heads(v, X_HEADS)
    s = jnp.einsum('bqhd,bkhd->bhqk', q, k).astype(jnp.float32) * X_HEAD_DIM ** -0.5
    p = jax.nn.softmax(s, axis=-1).astype(v.dtype)
    o = jnp.einsum('bhqk,bkhd->bqhd', p, v)
    return o.reshape(xn.shape[0], xn.shape[1], X_WIDTH) @ w_o


def squared_relu_mlp(xn, w_up, w_down):
    return jnp.square(jax.nn.relu(xn @ w_up)) @ w_down


def setup_inputs(seed: int = 0) -> dict:
    key = jax.random.key(seed)
    ks = jax.random.split(key, 20)
    f32 = jnp.float32

    def dense(k, shape):
        return jax.random.normal(k, shape, f32) * shape[-2] ** -0.5

    def gain(k, shape):
        return 1.0 + 0.05 * jax.random.normal(k, shape, f32)

    return {
        'x': jax.random.normal(ks[0], (BATCH, SEQ, D_MODEL), f32),
        'mem': jax.random.normal(ks[1], (BATCH, MEM_TOKENS, D_MODEL), f32),
        'norm_mix': gain(ks[2], (DEPTH, D_MODEL)),
        'norm_xattn': gain(ks[3], (DEPTH, D_MODEL)),
        'norm_mem': gain(ks[4], (DEPTH, D_MODEL)),
        'norm_mlp': gain(ks[5], (DEPTH, D_MODEL)),
        'norm_final': gain(ks[6], (D_MODEL,)),
        'w_in_ab': dense(ks[7], (N_EVEN, D_MODEL, AB_IN)),
        'a_lb_logits': 0.5 * jax.random.normal(ks[8], (N_EVEN, A_WIDTH), f32),
        'a_out_gain': gain(ks[9], (N_EVEN, A_WIDTH)),
        'b_rel_bias': 0.2 * jax.random.normal(ks[10], (N_EVEN, B_HEADS, 2 * REL_CLIP + 1), f32),
        'w_out_ab': dense(ks[11], (N_EVEN, AB_OUT, D_MODEL)),
        'w_in_c': dense(ks[12], (N_ODD, D_MODEL, C_IN)),
        'c_fgate_bias': FOX_BIAS_CENTER + 0.5 * jax.random.normal(ks[13], (N_ODD, C_HEADS), f32),
        'w_out_c': dense(ks[14], (N_ODD, C_WIDTH, D_MODEL)),
        'w_xq': dense(ks[15], (DEPTH, D_MODEL, X_WIDTH)),
        'w_xkv': dense(ks[16], (DEPTH, D_MODEL, 2 * X_WIDTH)),
        'w_xo': dense(ks[17], (DEPTH, X_WIDTH, D_MODEL)),
        'w_up': dense(ks[18], (DEPTH, D_MODEL, D_FF)),
        'w_down': dense(ks[19], (DEPTH, D_FF, D_MODEL)),
    }


def reference(x, mem, norm_mix, norm_xattn, norm_mem, norm_mlp, norm_final,
              w_in_ab, a_lb_logits, a_out_gain, b_rel_bias, w_out_ab,
              w_in_c, c_fgate_bias, w_out_c, w_xq, w_xkv, w_xo, w_up, w_down):
    lb_w = jax.nn.softmax(a_lb_logits.astype(jnp.float32), axis=0)
    lower_bounds = jnp.cumsum(lb_w, axis=0) - lb_w[0]
    h = x
    for layer in range(DEPTH):
        xn = rms_norm(h, norm_mix[layer])
        if layer % 2 == 0:
            e = layer // 2
            mix = hgrn2_chunkattn_mixer(xn, w_in_ab[e], lower_bounds[e], a_out_gain[e],
                                        b_rel_bias[e], w_out_ab[e])
        else:
            o = layer // 2
            mix = forgetting_mixer(xn, w_in_c[o], c_fgate_bias[o], w_out_c[o])
        h = h + mix
        h = h + memory_cross_attention(rms_norm(h, norm_xattn[layer]), rms_norm(mem, norm_mem[layer]),
                                       w_xq[layer], w_xkv[layer], w_xo[layer])
        h = h + squared_relu_mlp(rms_norm(h, norm_mlp[layer]), w_up[layer], w_down[layer])
    return rms_norm(h, norm_final)
```

```python
import numpy as np
from contextlib import ExitStack
import concourse.bass as bass
import concourse.mybir as mybir
from concourse.bass_utils import run_bass_kernel_spmd

F32 = mybir.dt.float32
BF16 = mybir.dt.bfloat16
ALU = mybir.AluOpType
AF = mybir.ActivationFunctionType

D = 1024
T = 2048
NCH = 8
NTB = 4
NT = 16
MEM = 256
EPS = 1e-6
EPOCH = 20000
NEG = -30000.0
import os
DBG = int(os.environ.get('KDBG', '0'))
KSTOP = int(os.environ.get('KSTOP', '0'))


class Op:
    pass


class Rec:
    ENGS = ('sp', 'pe', 'act', 'dve', 'pool')

    def __init__(s):
        s.ops = []
        s.lastw = {}
        s.rd_eng = {}
        s.rd_dma = {}
        s.eng_ops = {e: [] for e in s.ENGS}
        s.dma_cnt = {}
        s.fdeps = {}

    def fence(s, region):
        deps = set(s.fdeps.get(region, ()))
        for k in list(s.lastw):
            if k[0] == region:
                deps.add(s.lastw.pop(k))
        for k in list(s.rd_eng):
            if k[0] == region:
                deps.update(s.rd_eng.pop(k).values())
        for k in list(s.rd_dma):
            if k[0] == region:
                deps.update(s.rd_dma.pop(k))
        best = {}
        for di in deps:
            o = s.ops[di]
            kk = ('d', o.dsemkey) if o.dma else ('e', o.eng)
            if kk not in best or best[kk] < di:
                best[kk] = di
        s.fdeps[region] = set(best.values())

    def op(s, eng, fn, r=(), w=(), dma=None):
        o = Op()
        o.eng = eng
        o.fn = fn
        o.idx = len(s.ops)
        o.pos = len(s.eng_ops[eng])
        o.signal = False
        o.dma = dma is not None
        deps = set()
        raw = set()
        for k in r:
            d = s.lastw.get(k)
            if d is not None:
                deps.add(d)
                raw.add(d)
            elif k[0] in s.fdeps:
                deps.update(s.fdeps[k[0]])
                raw.update(s.fdeps[k[0]])
            if k[0] == 'ps':
                for e2, d2 in s.rd_eng.get(k, {}).items():
                    if e2 != eng:
                        deps.add(d2)
        for k in w:
            d = s.lastw.get(k)
            if d is not None:
                deps.add(d)
            elif k[0] in s.fdeps:
                deps.update(s.fdeps[k[0]])
                raw.update(s.fdeps[k[0]])
            deps.update(s.rd_eng.get(k, {}).values())
            deps.update(s.rd_dma.get(k, ()))
        for k in r:
            if o.dma:
                s.rd_dma.setdefault(k, []).append(o.idx)
            else:
                s.rd_eng.setdefault(k, {})[eng] = o.idx
        for k in w:
            s.lastw[k] = o.idx
            s.rd_eng[k] = {}
            s.rd_dma[k] = []
        o.deps = deps
        o.raw = raw
        if o.dma:
            semkey, n = dma
            s.dma_cnt[semkey] = s.dma_cnt.get(semkey, 0) + n
            o.dsemkey = semkey
            o.dtarget = 16 * s.dma_cnt[semkey]
        s.ops.append(o)
        s.eng_ops[eng].append(o)
        return o

    def plan(s):
        waited = {e: {f: -1 for f in s.ENGS} for e in s.ENGS}
        waited_dma = {e: {} for e in s.ENGS}
        for o in s.ops:
            o.w_eng = []
            o.w_dma = []
            best = {}
            bestd = {}
            for di in o.deps:
                Dd = s.ops[di]
                if Dd.dma:
                    if waited_dma[o.eng].get(Dd.dsemkey, 0) >= Dd.dtarget:
                        continue
                    if bestd.get(Dd.dsemkey, 0) < Dd.dtarget:
                        bestd[Dd.dsemkey] = Dd.dtarget
                else:
                    if Dd.eng == o.eng and (o.eng == 'pe' or di not in o.raw):
                        continue
                    if Dd.pos <= waited[o.eng][Dd.eng]:
                        continue
                    if Dd.eng not in best or best[Dd.eng].pos < Dd.pos:
                        best[Dd.eng] = Dd
            for f, Dd in best.items():
                Dd.signal = True
                waited[o.eng][f] = Dd.pos
                o.w_eng.append(Dd)
            for k, t in bestd.items():
                waited_dma[o.eng][k] = t
                o.w_dma.append((k, t))
        s.nepoch = {}
        for e in s.ENGS:
            c = 0
            for o in s.eng_ops[e]:
                if o.signal:
                    o.epoch = c // EPOCH
                    o.sigval = c % EPOCH + 1
                    c += 1
            s.nepoch[e] = (c + EPOCH - 1) // EPOCH


def build(nbatch=2, layers=(0, 1, 2, 3), final_norm=True):
    nc = bass.Bass("TRN2", target_bir_lowering=False)
    rec = Rec()

    def din(name, shape):
        return nc.dram_tensor(name, list(shape), F32, kind="ExternalInput").ap()

    xT_d = din("xT", [2, D, T])
    memT_d = din("memT", [2, D, MEM])
    gains_d = din("gains", [128, 136])
    again_d = din("again", [128, 8])
    lbl_d = din("lbl", [2, 512])
    fbias_d = din("fbias", [2, 16])
    relb_d = din("relb", [2, 8, 128, 640])
    consts_d = din("consts", [128, 516])
    selc_d = din("selc", [64, 2048])
    w_in_ab = din("w_in_ab", [2, D, 3584])
    w_out_ab = din("w_out_ab", [2, D, D])
    w_in_c = din("w_in_c", [2, D, 3088])
    w_out_c = din("w_out_c", [2, D, D])
    w_xq = din("w_xq", [4, D, D])
    w_xkv = din("w_xkv", [4, D, 2 * D])
    w_xo = din("w_xo", [4, D, D])
    w_up = din("w_up", [4, D, 4 * D])
    w_down = din("w_down", [4, 4 * D, D])
    outT_d = nc.dram_tensor("outT", [2, D, T], F32, kind="ExternalOutput").ap()

    es = ExitStack()

    def sb(name, shape, dt):
        return es.enter_context(nc.sbuf_tensor(name, list(shape), dt))

    hT = sb("hT", [128, NCH, T], F32)
    xn = sb("xn", [128, NCH, T], BF16)
    Wt = [sb("W0", [128, 8192], BF16), sb("W1", [128, 8192], BF16)]
    WK = sb("WK", [128, 8192], F32)
    gains = sb("gains_s", [128, 136], F32)
    again = sb("again_s", [128, 8], F32)
    consts = sb("consts_s", [128, 516], F32)
    tri_bf = sb("tri_bf", [128, 128], BF16)
    ones_bf = sb("ones_bf", [128, 128], BF16)
    ones_f = sb("ones_f", [128, 128], F32)
    SEL = sb("sel", [64, 16, 128], BF16)
    oml = sb("oml", [128, 512], F32)
    fbt = sb("fbt", [128, 16], F32)
    lf = sb("lf", [128, 16, 16], F32)
    cum = sb("cum", [128, 16, 16], F32)
    zf = cum
    rcar = sb("rcar", [128, 4, 16], F32)
    fbtab = sb("fbtab", [128, 4, 16, 16], F32)
    SQ = [sb(f"sq{i}", [128, 512], BF16) for i in range(2)]
    RS = [sb(f"rs{i}", [128, 512], F32) for i in range(2)]
    PT = [sb(f"pt{i}", [128, 512], BF16) for i in range(4)]
    TMP = [sb(f"tmp{i}", [128, 512], F32) for i in range(2)]
    REC = [sb(f"rec{i}", [128, 512], F32) for i in range(2)]
    BH = [WK[:, 5120:5760]]
    ER = [sb(f"er{i}", [128, 128], F32) for i in range(2)]
    E0 = [sb(f"e0{i}", [128, 128], F32) for i in range(2)]
    EP = [sb(f"ep{i}", [128, 128], F32) for i in range(2)]
    EM = [sb(f"em{i}", [128, 128], F32) for i in range(2)]
    MP = [sb(f"mp{i}", [128, 4], F32) for i in range(2)]
    MN = [sb(f"mn{i}", [128, 4], F32) for i in range(2)]
    QI = [sb(f"qi{i}", [128, 128], BF16) for i in range(2)]
    QH = [sb(f"qh{i}", [128, 128], BF16) for i in range(2)]
    KH = [sb(f"kh{i}", [128, 128], BF16) for i in range(2)]
    KTC = [sb(f"ktc{i}", [128, 4, 128], BF16) for i in range(2)]
    AM = [sb(f"am{i}", [128, 128], BF16) for i in range(2)]
    SFS = [sb(f"sf{i}", [128, 128], F32) for i in range(2)]
    EB1 = sb("eb1", [128, 640], BF16)
    SB = [sb(f"sbf{i}", [128, 128], BF16) for i in range(4)]

    PS = [es.enter_context(nc.psum_tensor(f"ps{i}", [128, 512], F32)) for i in range(8)]

    ident = consts[:, 0:128]
    tri_f = consts[:, 128:256]
    maskBD = consts[:, 256:384]
    triRev = consts[:, 384:512]
    cmask = consts[:, 512:516]

    cnt = {'mm': 0, 'acc': 0}

    def ps_mm():
        cnt['mm'] += 1
        return cnt['mm'] % 4

    def ps_acc():
        cnt['acc'] += 1
        return 4 + cnt['acc'] % 4

    rot = {}

    def rotate(name, n):
        rot[name] = rot.get(name, -1) + 1
        return rot[name] % n

    def mm(out, lhsT, rhs, start, stop, r, w):
        rec.op('pe', lambda e: e.matmul(out, lhsT, rhs, start=start, stop=stop), r, w)

    def act(out, in_, func, r, w, bias=0.0, scale=1.0):
        rec.op('act', lambda e: e.activation(out, in_, func, bias=bias, scale=scale), r, w)

    def dve_tt(out, a, b, op, r, w, eng='dve'):
        rec.op(eng, lambda e: e.tensor_tensor(out, a, b, op), r, w)

    def dve_ts(out, a, s1, s2, op0, op1, r, w, eng='dve'):
        rec.op(eng, lambda e: e.tensor_scalar(out, a, s1, s2, op0, op1), r, w)

    def dve_stt(out, a, sc, b, op0, op1, r, w):
        rec.op('dve', lambda e: e.scalar_tensor_tensor(out, a, sc, b, op0, op1), r, w)

    def dve_copy(out, a, r, w, eng='dve'):
        rec.op(eng, lambda e: e.tensor_copy(out, a), r, w)

    def dma(eng, pairs, r, w, semkey):
        pairs = list(pairs)
        rec.op(eng, lambda e: [e.dma_start(out=o, in_=i) for (o, i) in pairs], r, w, dma=(semkey, len(pairs)))

    def kps(b):
        return ('ps', b)

    dma('sp', [(gains[:], gains_d), (again[:], again_d), (consts[:], consts_d)], [], [('c', 'g')], 'c0')
    dma('pool', [(tri_bf[:], consts_d[:, 128:256])], [], [('c', 'tribf')], 'c2')
    dma('pool', [(SEL[:].rearrange("p h m -> p (h m)"), selc_d)], [], [('c', 'sel')], 'c5')
    rec.op('pool', lambda e: e.memset(ones_bf[:], 1.0), [], [('c', 'ones')])
    rec.op('pool', lambda e: e.memset(ones_f[:], 1.0), [], [('c', 'onesf')])
    CG = ('c', 'g')

    wstate = {'i': 0}

    def wslot():
        wstate['i'] += 1
        return wstate['i'] % 2

    def wview(slot, off, kc, n):
        return Wt[slot][:, off:off + kc * n].rearrange("p (kc n) -> p kc n", kc=kc)

    def wsrc_cols(wd, c0, n):
        return wd.rearrange("(kc p) n -> p kc n", p=128)[:, :, c0:c0 + n]

    def wsrc_rows(wd, r0, kc):
        return wd[r0:r0 + kc * 128, :].rearrange("(kc p) n -> p kc n", p=128)

    def load_w(slot, pairs):
        dma('pool', pairs, [], [('W', slot)], ('W', slot))

    def rstd_from_ps(psb, ncols, rs, rkeys, scale):
        act(rs[:, 0:ncols], PS[psb][:, 0:ncols], AF.Ln, rkeys + [kps(psb)], [('rs', id(rs))], bias=EPS, scale=scale)
        act(rs[:, 0:ncols], rs[:, 0:ncols], AF.Exp, [('rs', id(rs))], [('rs', id(rs))], scale=-0.5)

    def norm_h(gidx, out_fn, out_keys_fn, after_block=None):
        for tb in range(NTB):
            sl = slice(tb * 512, (tb + 1) * 512)
            pb = ps_acc()
            for c in range(NCH):
                q = SQ[rotate('sq', 2)]
                act(q[:], hT[:, c, sl], AF.Square, [('hT', c, tb)], [('sq', id(q))])
                mm(PS[pb][:], ones_bf[:], q[:], c == 0, c == NCH - 1, [('sq', id(q)), ('c', 'ones')], [kps(pb)])
            rs = RS[rotate('rs', 2)]
            rstd_from_ps(pb, 512, rs, [], 1.0 / D)
            for c in range(NCH):
                dve_stt(out_fn(c, tb), hT[:, c, sl], gains[:, gidx * 8 + c:gidx * 8 + c + 1], rs[:], ALU.mult, ALU.mult,
                        [('hT', c, tb), ('rs', id(rs)), CG], out_keys_fn(c, tb))
            if after_block is not None:
                after_block(tb)

    def norm_to_xn(gidx):
        norm_h(gidx, lambda c, tb: xn[:, c, tb * 512:(tb + 1) * 512], lambda c, tb: [('xn', tb, c)])

    def xn_keys(tb):
        return [('xn', tb, c) for c in range(NCH)]

    def projT(wv, c0, tb, slot):
        pb = ps_mm()
        for kc in range(NCH):
            mm(PS[pb][:], wv[:, kc, c0:c0 + 128], xn[:, kc, tb * 512:(tb + 1) * 512], kc == 0, kc == NCH - 1,
               [('W', slot), ('xn', tb, kc)], [kps(pb)])
        return pb

    def outproj_acc(wo, nkc, src_fn, src_keys_fn, slot):
        for tb in range(NTB):
            for oc in range(NCH):
                pb = ps_mm()
                for kc in range(nkc):
                    mm(PS[pb][:], wo[:, kc, oc * 128:(oc + 1) * 128], src_fn(kc, tb), kc == 0, kc == nkc - 1,
                       [('W', slot)] + src_keys_fn(kc, tb), [kps(pb)])
                sl = slice(tb * 512, (tb + 1) * 512)
                dve_tt(hT[:, oc, sl], hT[:, oc, sl], PS[pb][:], ALU.add, [('hT', oc, tb), kps(pb)], [('hT', oc, tb)])

    def wk_bf(off, n):
        return WK[:, off:off + n // 2].bitcast(BF16)

    def dve_recip(out, a, r, w):
        rec.op('dve', lambda e: e.reciprocal(out, a), r, w)

    def act_recip(out, a, r, w):
        act(out, a, AF.Ln, r, w)
        act(out, out, AF.Exp, w, w, scale=-1.0)

    def attn_finalize(ob, out_ap, out_keys):
        rc = REC[rotate('rec', 2)]
        act_recip(rc[0:64, :], PS[ob][64:128, :], [kps(ob)], [('rec', id(rc))])
        dve_tt(out_ap, PS[ob][0:64, :], rc[0:64, :], ALU.mult, [kps(ob), ('rec', id(rc))], out_keys)

    SKEW = 3

    def run_tiles(tiles):
        pend = []
        for t in tiles + [None] * SKEW:
            if t is not None:
                if t.get('first') is not None:
                    t['first']()
                pend.append((t, t['score']()))
            if pend and (len(pend) > SKEW or t is None):
                pt_, info = pend.pop(0)
                pt_['pv'](info)
                if pt_.get('fin') is not None:
                    pt_['fin']()
        assert not pend

    for b in range(nbatch):
        for c in range(NCH):
            dma('sp', [(hT[:, c, :], xT_d[b, c * 128:(c + 1) * 128, :])], [],
                [('hT', c, tb) for tb in range(NTB)], ('ld', c))

        for l in layers:
            rec.fence('WK')
            norm_to_xn(l)
            if l % 2 == 1:
                o_ = l // 2
                Qh = [wk_bf(0, 2048), wk_bf(1024, 2048)]
                Kh = [wk_bf(2048, 2048), wk_bf(3072, 2048)]
                OTv = wk_bf(4096, 2048)
                s0 = wslot()
                wf = wview(s0, 0, 8, 16)
                load_w(s0, [(wf, wsrc_cols(w_in_c[o_], 3072, 16))])
                dma('sp', [(fbt[:], fbias_d[o_].partition_broadcast(128))], [], [('c', 'fbt')], 'c3')
                pb = ps_mm()
                for j in range(NT):
                    for kc in range(NCH):
                        mm(PS[pb][:, j * 16:(j + 1) * 16], xn[:, kc, j * 128:(j + 1) * 128], wf[:, kc, :], kc == 0, kc == NCH - 1,
                           [('W', s0), ('xn', j // 4, kc)], [kps(pb)])
                for j in range(NT):
                    dve_tt(zf[:, j, :], PS[pb][:, j * 16:(j + 1) * 16], fbt[:], ALU.add, [kps(pb), ('c', 'fbt')], [('c', 'cum')])
                zf2 = zf[:].rearrange("p j h -> p (j h)")
                lf2 = lf[:].rearrange("p j h -> p (j h)")
                act(zf2, zf2, AF.Exp, [('c', 'cum')], [('c', 'cum')], scale=-1.0)
                act(lf2, zf2, AF.Ln, [('c', 'cum')], [('c', 'lf')], bias=1.0)
                dve_ts(lf2, lf2, -1.0, None, ALU.mult, ALU.bypass, [('c', 'lf')], [('c', 'lf')])
                pc = ps_acc()
                for j in range(NT):
                    mm(PS[pc][:, j * 16:(j + 1) * 16], tri_f, lf[:, j, :], True, j == 0, [('c', 'lf'), CG], [kps(pc)])
                    for i in range(j):
                        mm(PS[pc][:, j * 16:(j + 1) * 16], ones_f[:], lf[:, i, :], False, i == j - 1, [('c', 'lf'), ('c', 'onesf')], [kps(pc)])
                dve_copy(cum[:].rearrange("p j h -> p (j h)"), PS[pc][:, 0:256], [kps(pc)], [('c', 'cum')])
                pr = ps_acc()
                for qb in range(NTB):
                    n = 4 * qb + 4
                    for i in range(n):
                        mm(PS[pr][:, qb * 16:(qb + 1) * 16], ones_f[:], lf[:, i, :], i == 0, i == n - 1, [('c', 'lf'), ('c', 'onesf')], [kps(pr)])
                dve_copy(rcar[:].rearrange("p q h -> p (q h)"), PS[pr][:, 0:64], [kps(pr)], [('c', 'rcar')])
                for qb in range(NTB):
                    for j in range(4 * qb + 4):
                        dve_tt(fbtab[:, qb, j, :], rcar[:, qb, :], cum[:, j, :], ALU.subtract, [('c', 'rcar'), ('c', 'cum')], [('c', 'fbtab')])
                cum48 = WK[:, 0:2048].rearrange("p (j c) -> p j c", j=16)
                CT = WK[:, 4096:6144]
                CHL = wk_bf(7168, 2048)
                rec.op('pool', lambda e, cum48=cum48: e.memset(cum48, 0.0), [], [('WK', 'cum48')])
                dve_copy(cum48[:, :, 0:16], cum[:], [('c', 'cum'), ('WK', 'cum48')], [('WK', 'cum48')])
                dve_copy(cum48[:, :, 32:48], cum[:], [('c', 'cum'), ('WK', 'cum48')], [('WK', 'cum48')])
                dve_copy(cum48[:, :, 64:80], cum[:], [('c', 'cum'), ('WK', 'cum48')], [('WK', 'cum48')])
                dve_copy(cum48[:, :, 96:112], cum[:], [('c', 'cum'), ('WK', 'cum48')], [('WK', 'cum48')])
                for j4 in range(4):
                    pb = ps_mm()
                    for jj in range(4):
                        j = j4 * 4 + jj
                        rec.op('pe', lambda e, pb=pb, jj=jj, j=j, cum48=cum48: e.transpose(PS[pb][:, jj * 128:(jj + 1) * 128], cum48[:, j, :], ident),
                               [('WK', 'cum48'), CG], [kps(pb)])
                    dve_copy(CT[:, j4 * 512:(j4 + 1) * 512], PS[pb][:, :], [kps(pb)], [('WK', 'ct', j4)])
                for qb in range(NTB):
                    blk = slice(qb * 512, (qb + 1) * 512)
                    tm = TMP[rotate('tmp', 2)]
                    ktm = ('tmp', TMP.index(tm))
                    dve_ts(tm[:, :], CT[:, blk], CT[:, qb * 512 + 511:qb * 512 + 512], None, ALU.subtract, ALU.bypass, [('WK', 'ct', qb)], [ktm])
                    dve_copy(CHL[:, blk], tm[:, :], [ktm], [('WK', 'chl', qb)])
                    dve_tt(CHL[32:48, blk], tm[32:48, :], CHL[32:48, blk], ALU.subtract, [ktm, ('WK', 'chl', qb)], [('WK', 'chl', qb)])
                    dve_tt(CHL[96:112, blk], tm[96:112, :], CHL[96:112, blk], ALU.subtract, [ktm, ('WK', 'chl', qb)], [('WK', 'chl', qb)])
                rec.fence('WK')
                Vaug = wk_bf(5120, 4096).rearrange("p (j h d) -> p j h d", j=16, h=2)
                rec.op('pool', lambda e, Vaug=Vaug: e.memset(Vaug[:, :, :, 64:128], 1.0), [], [('WK', 'vones')])
                for hh in range(2):
                    rec.op('pool', lambda e, t=Kh[hh]: e.memset(t[64:128, :], 0.0), [], [('WK', 'kaug', hh)])
                    rec.op('pool', lambda e, t=Kh[hh]: e.memset(t[64:65, :], 1.0), [('WK', 'kaug', hh)], [('WK', 'kaug', hh)])
                    rec.op('pool', lambda e, t=Kh[hh]: e.memset(t[96:97, :], 1.0), [('WK', 'kaug', hh)], [('WK', 'kaug', hh)])
                nxt = wslot()
                def fox_w(slot, p):
                    wv = wview(slot, 0, 8, 384)
                    wo = wview(slot, 3072, 1, 1024)
                    load_w(slot, [(wv[:, :, 0:128], wsrc_cols(w_in_c[o_], p * 128, 128)),
                                  (wv[:, :, 128:256], wsrc_cols(w_in_c[o_], 1024 + p * 128, 128)),
                                  (wv[:, :, 256:384], wsrc_cols(w_in_c[o_], 2048 + p * 128, 128)),
                                  (wo, wsrc_rows(w_out_c[o_], p * 128, 1))])
                    return wv, wo
                wcur = fox_w(nxt, 0)
                for p in range(8):
                    slot = nxt
                    wv, wo = wcur
                    if p + 1 < 8:
                        nxt = wslot()
                        wcur = fox_w(nxt, p + 1)
                    for tb in range(NTB):
                        sl = slice(tb * 512, (tb + 1) * 512)
                        pb = projT(wv, 0, tb, slot)
                        for hh in range(2):
                            act(Qh[hh][0:64, sl], PS[pb][hh * 64:(hh + 1) * 64, :], AF.Copy, [kps(pb)], [('WK', 'q', hh, tb)], scale=0.125)
                        pb = projT(wv, 128, tb, slot)
                        for hh in range(2):
                            dve_copy(Kh[hh][0:64, sl], PS[pb][hh * 64:(hh + 1) * 64, :], [kps(pb)], [('WK', 'k', hh, tb)])
                        for hh in range(2):
                            pb = ps_mm()
                            mm(PS[pb][:], SEL[:, 2 * p + hh, :], CHL[0:64, sl], True, True, [('c', 'sel'), ('WK', 'chl', tb)], [kps(pb)])
                            dve_copy(Qh[hh][64:128, sl], PS[pb][64:128, :], [kps(pb)], [('WK', 'qaug', hh, tb)])
                    for j4 in range(4):
                        pb = ps_mm()
                        for jj in range(4):
                            j = j4 * 4 + jj
                            for kc in range(NCH):
                                mm(PS[pb][:, jj * 128:(jj + 1) * 128], xn[:, kc, j * 128:(j + 1) * 128], wv[:, kc, 256:384], kc == 0, kc == NCH - 1,
                                   [('W', slot), ('xn', j4, kc)], [kps(pb)])
                        act(Vaug[:, j4 * 4:(j4 + 1) * 4, :, 0:64], PS[pb][:].rearrange("p (j h d) -> p j h d", j=4, h=2), AF.Copy, [kps(pb), ('WK', 'vones')], [('WK', 'v', j4)])
                    tiles = []
                    for hh in range(2):
                        for qb in range(NTB):
                            nj = 4 * qb + 4
                            blk = {}
                            for j in range(nj):
                                def score(hh=hh, qb=qb, j=j, h=2 * p + hh):
                                    n0 = max(0, j - 4 * qb) * 128
                                    N = 512 - n0
                                    sb_ = ps_mm()
                                    mm(PS[sb_][:, 0:N], Kh[hh][:, j * 128:(j + 1) * 128], Qh[hh][:, qb * 512 + n0:(qb + 1) * 512], True, True,
                                       [('WK', 'k', hh, j // 4), ('WK', 'kaug', hh), ('WK', 'q', hh, qb), ('WK', 'qaug', hh, qb)], [kps(sb_)])
                                    pt = PT[rotate('pt', 4)]
                                    kpt = ('pt', id(pt))
                                    act(pt[:, 0:N], PS[sb_][:, 0:N], AF.Exp, [kps(sb_), ('c', 'fbtab')], [kpt], bias=fbtab[:, qb, j, h:h + 1])
                                    if j >= 4 * qb:
                                        dve_tt(pt[:, 0:128], pt[:, 0:128], tri_bf[:], ALU.mult, [kpt, ('c', 'tribf')], [kpt], eng='pool')
                                    return (n0, N, pt, kpt)

                                def pv(info, hh=hh, j=j, nj=nj, blk=blk):
                                    n0, N, pt, kpt = info
                                    mm(PS[blk['ob']][:, n0:512], Vaug[:, j, hh, :], pt[:, 0:N], j == 0, j == nj - 1, [('WK', 'v', j // 4), ('WK', 'vones'), kpt], [kps(blk['ob'])])
                                t = {'score': score, 'pv': pv}
                                if j == 0:
                                    t['first'] = lambda blk=blk: blk.__setitem__('ob', ps_acc())
                                if j == nj - 1:
                                    t['fin'] = lambda blk=blk, hh=hh, qb=qb: attn_finalize(blk['ob'], OTv[64 * hh:64 * hh + 64, qb * 512:(qb + 1) * 512], [('WK', 'ot', qb, hh)])
                                tiles.append(t)
                    run_tiles(tiles)
                    outproj_acc(wo, 1, lambda kc, tb: OTv[:, tb * 512:(tb + 1) * 512], lambda kc, tb: [('WK', 'ot', tb, 0), ('WK', 'ot', tb, 1)], slot)
            else:
                e_ = l // 2
                if e_ == 0:
                    rec.op('pool', lambda e: e.memset(oml[:], 1.0), [], [('c', 'oml')])
                else:
                    dma('sp', [(TMP[0][:], lbl_d[0].partition_broadcast(128))], [], [('tmp', 0)], 'c1')
                    dma('sp', [(TMP[1][:], lbl_d[1].partition_broadcast(128))], [], [('tmp', 1)], 'c4')
                    act(TMP[0][:], TMP[0][:], AF.Exp, [('tmp', 0)], [('tmp', 0)])
                    act(TMP[1][:], TMP[1][:], AF.Exp, [('tmp', 1)], [('tmp', 1)])
                    dve_tt(TMP[0][:], TMP[0][:], TMP[1][:], ALU.add, [('tmp', 0), ('tmp', 1)], [('tmp', 0)])
                    dve_recip(TMP[0][:], TMP[0][:], [('tmp', 0)], [('tmp', 0)])
                    dve_tt(TMP[0][:], TMP[1][:], TMP[0][:], ALU.mult, [('tmp', 0), ('tmp', 1)], [('tmp', 0)])
                    dve_ts(oml[:], TMP[0][:], -1.0, 1.0, ALU.mult, ALU.add, [('tmp', 0)], [('c', 'oml')])
                Qh = [wk_bf(0, 2048), wk_bf(1024, 2048)]
                Kh = [wk_bf(2048, 2048), wk_bf(3072, 2048)]
                OTv = wk_bf(4096, 2048)
                Vaug = wk_bf(6144, 4096).rearrange("p (j h d) -> p j h d", j=16, h=2)
                EB = [wk_bf(5760, 640), EB1[:]]
                rec.op('pool', lambda e, Vaug=Vaug: e.memset(Vaug[:, :, :, 64:128], 1.0), [], [('WK', 'vones')])
                for hh in range(2):
                    rec.op('pool', lambda e, t=Kh[hh]: e.memset(t[64:128, :], 0.0), [], [('WK', 'kaug', hh)])
                    rec.op('pool', lambda e, t=Qh[hh]: e.memset(t[64:128, :], 0.0), [], [('WK', 'qaug', hh)])
                nxt = wslot()
                def b_w(slot, p):
                    wv = wview(slot, 0, 8, 384)
                    wo = wview(slot, 3072, 1, 1024)
                    load_w(slot, [(wv[:, :, 0:128], wsrc_cols(w_in_ab[e_], 2048 + p * 128, 128)),
                                  (wv[:, :, 128:256], wsrc_cols(w_in_ab[e_], 2560 + p * 128, 128)),
                                  (wv[:, :, 256:384], wsrc_cols(w_in_ab[e_], 3072 + p * 128, 128)),
                                  (wo, wsrc_rows(w_out_ab[e_], 512 + p * 128, 1))])
                    return wv, wo
                wcur = b_w(nxt, 0)
                for p in range(4):
                    slot = nxt
                    wv, wo = wcur
                    if p + 1 < 4:
                        nxt = wslot()
                        wcur = b_w(nxt, p + 1)
                    for tb in range(NTB):
                        sl = slice(tb * 512, (tb + 1) * 512)
                        pb = projT(wv, 0, tb, slot)
                        for hh in range(2):
                            act(Qh[hh][0:64, sl], PS[pb][hh * 64:(hh + 1) * 64, :], AF.Copy, [kps(pb)], [('WK', 'q', hh, tb)], scale=0.125)
                        pb = projT(wv, 128, tb, slot)
                        for hh in range(2):
                            dve_copy(Kh[hh][0:64, sl], PS[pb][hh * 64:(hh + 1) * 64, :], [kps(pb)], [('WK', 'k', hh, tb)])
                    for j4 in range(4):
                        pb = ps_mm()
                        for jj in range(4):
                            j = j4 * 4 + jj
                            for kc in range(NCH):
                                mm(PS[pb][:, jj * 128:(jj + 1) * 128], xn[:, kc, j * 128:(j + 1) * 128], wv[:, kc, 256:384], kc == 0, kc == NCH - 1,
                                   [('W', slot), ('xn', j4, kc)], [kps(pb)])
                        act(Vaug[:, j4 * 4:(j4 + 1) * 4, :, 0:64], PS[pb][:].rearrange("p (j h d) -> p j h d", j=4, h=2), AF.Copy, [kps(pb), ('WK', 'vones')], [('WK', 'v', j4)])
                    tiles = []
                    for hh in range(2):
                        h = 2 * p + hh
                        bh = BH[0]
                        kbh = ('WK', 'bh', 0)
                        eb = EB[hh]
                        keb = ('WK', 'eb', hh)
                        dma('sp', [(bh, relb_d[e_, h])], [], [kbh], ('bh', 0))
                        rec.op('pool', lambda e, bh=bh: e.memset(bh[0:64, 576:640], NEG), [kbh], [kbh])
                        rec.op('pool', lambda e, bh=bh: e.memset(bh[64:128, 0:64], NEG), [kbh], [kbh])
                        act(eb, bh, AF.Exp, [kbh], [keb])
                        for qb in range(NTB):
                            ds = [0] + [d for d in (-512, -384, -256, -128, 128, 256, 384) if 0 <= qb * 512 + d < T]
                            nd = len(ds)
                            blk = {}
                            for ii, d in enumerate(ds):
                                def score(hh=hh, qb=qb, d=d, eb=eb, keb=keb):
                                    kb = qb * 512 + d
                                    j = kb // 128
                                    lo = max(0, d)
                                    hi = 512 if d >= -128 else d + 640
                                    N = hi - lo
                                    sb_ = ps_mm()
                                    mm(PS[sb_][:, 0:N], Kh[hh][:, kb:kb + 128], Qh[hh][:, qb * 512 + lo:qb * 512 + hi], True, True,
                                       [('WK', 'k', hh, j // 4), ('WK', 'kaug', hh), ('WK', 'q', hh, qb), ('WK', 'qaug', hh)], [kps(sb_)])
                                    pt = PT[rotate('pt', 4)]
                                    kpt = ('pt', id(pt))
                                    act(pt[:, 0:N], PS[sb_][:, 0:N], AF.Exp, [kps(sb_)], [kpt])
                                    dve_tt(pt[:, 0:N], pt[:, 0:N], eb[:, lo - d:hi - d], ALU.mult, [kpt, keb], [kpt], eng='pool')
                                    return (j, lo, hi, N, pt, kpt)

                                def pv(info, hh=hh, ii=ii, nd=nd, blk=blk):
                                    j, lo, hi, N, pt, kpt = info
                                    mm(PS[blk['ob']][:, lo:hi], Vaug[:, j, hh, :], pt[:, 0:N], ii == 0, ii == nd - 1, [('WK', 'v', j // 4), ('WK', 'vones'), kpt], [kps(blk['ob'])])
                                t = {'score': score, 'pv': pv}
                                if ii == 0:
                                    t['first'] = lambda blk=blk: blk.__setitem__('ob', ps_acc())
                                if ii == nd - 1:
                                    t['fin'] = lambda blk=blk, hh=hh, qb=qb: attn_finalize(blk['ob'], OTv[64 * hh:64 * hh + 64, qb * 512:(qb + 1) * 512], [('WK', 'ot', qb, hh)])
                                tiles.append(t)
                    run_tiles(tiles)
                    outproj_acc(wo, 1, lambda kc, tb: OTv[:, tb * 512:(tb + 1) * 512], lambda kc, tb: [('WK', 'ot', tb, 0), ('WK', 'ot', tb, 1)], slot)

                rec.fence('WK')

                def a_views(s_):
                    o = s_ * 3072
                    return dict(QA=WK[:, o:o + 512], GT=wk_bf(o + 512, 512),
                                KA=WK[:, o + 768:o + 1280].rearrange("p (j d) -> p j d", j=4),
                                LF=WK[:, o + 1280:o + 1792].rearrange("p (j d) -> p j d", j=4),
                                VA=wk_bf(o + 1792, 512).rearrange("p (j d) -> p j d", j=4),
                                OTA=wk_bf(o + 2048, 2048))

                def a_w(slot, h):
                    wv = wview(slot, 0, 8, 512)
                    wo = wview(slot, 4096, 1, 1024)
                    load_w(slot, [(wv[:, :, i * 128:(i + 1) * 128], wsrc_cols(w_in_ab[e_], i * 512 + h * 128, 128)) for i in range(4)]
                           + [(wo, wsrc_rows(w_out_ab[e_], h * 128, 1))])
                    return wv, wo

                def a_head_gen(h, s_, slot, wv):
                    V = a_views(s_)
                    QA, GT, KA, LF, VA, OTA = V['QA'], V['GT'], V['KA'], V['LF'], V['VA'], V['OTA']
                    W_ = lambda n, *a: ('WK', n, s_) + a
                    hs = slice(h * 128, (h + 1) * 128)
                    er, e0, ep, em, mp, mn = ER[s_], E0[s_], EP[s_], EM[s_], MP[s_], MN[s_]
                    qi, qh, kh, ktc, am = QI[s_], QH[s_], KH[s_], KTC[s_], AM[s_]
                    sf = SFS[s_]
                    sbs = [SB[2 * s_], SB[2 * s_ + 1]]
                    K = lambda n: ('a', n, s_)
                    st = {'first': True, 'sb': 0}
                    for tb in range(NTB):
                        sl = slice(tb * 512, (tb + 1) * 512)
                        pb = projT(wv, 0, tb, slot)
                        sg = TMP[s_]
                        ksg = ('tmp', s_)
                        act(sg[:], PS[pb][:], AF.Sigmoid, [kps(pb)], [ksg])
                        dve_stt(QA, PS[pb][:], float(128 ** -0.5), sg[:], ALU.mult, ALU.mult, [kps(pb), ksg], [W_('qa')])
                        yield
                        pb = projT(wv, 384, tb, slot)
                        act(sg[:], PS[pb][:], AF.Sigmoid, [kps(pb)], [ksg])
                        dve_tt(GT, PS[pb][:], sg[:], ALU.mult, [kps(pb), ksg], [W_('gt')])
                        yield
                        pv = ps_mm()
                        for jj in range(4):
                            j = tb * 4 + jj
                            for kc in range(NCH):
                                mm(PS[pv][:, jj * 128:(jj + 1) * 128], xn[:, kc, j * 128:(j + 1) * 128], wv[:, kc, 256:384], kc == 0, kc == NCH - 1,
                                   [('W', slot), ('xn', tb, kc)], [kps(pv)])
                        act(VA, PS[pv][:].rearrange("p (j d) -> p j d", j=4), AF.Copy, [kps(pv)], [W_('va')])
                        yield
                        pz = ps_mm()
                        for jj in range(4):
                            j = tb * 4 + jj
                            for kc in range(NCH):
                                mm(PS[pz][:, jj * 128:(jj + 1) * 128], xn[:, kc, j * 128:(j + 1) * 128], wv[:, kc, 128:256], kc == 0, kc == NCH - 1,
                                   [('W', slot), ('xn', tb, kc)], [kps(pz)])
                        act(sg[:], PS[pz][:], AF.Sigmoid, [kps(pz)], [ksg], scale=-1.0)
                        for jj in range(4):
                            dve_tt(KA[:, jj, :], sg[:, jj * 128:(jj + 1) * 128], oml[:, hs], ALU.mult, [ksg, ('c', 'oml')], [W_('ka')])
                        act(LF, KA, AF.Ln, [W_('ka')], [W_('lf')], bias=1.0, scale=-1.0)
                        yield
                        po = ps_acc()
                        for jj in range(4):
                            ts = slice(jj * 128, (jj + 1) * 128)
                            pbT = ps_mm()
                            mm(PS[pbT][:, 0:128], LF[:, jj, :], maskBD, True, True, [W_('lf'), CG], [kps(pbT)])
                            act(e0[:], PS[pbT][:, 0:128], AF.Exp, [kps(pbT)], [K('e0')])
                            dve_copy(mp[:], PS[pbT][:, 0:128].rearrange("p (c t) -> p c t", c=4)[:, :, 15], [kps(pbT), K('e0')], [K('mp')])
                            dve_ts(mn[:], mp[:], -1.0, None, ALU.mult, ALU.bypass, [K('mp')], [K('mn')])
                            for c in range(4):
                                cs = slice(c * 32, (c + 1) * 32)
                                act(ep[:, cs], PS[pbT][:, cs], AF.Exp, [kps(pbT), K('mn')], [K('ep')], bias=mn[:, c:c + 1])
                                act(em[:, cs], PS[pbT][:, cs], AF.Exp, [kps(pbT), K('mp')], [K('em')], bias=mp[:, c:c + 1], scale=-1.0)
                            yield
                            prv = ps_mm()
                            mm(PS[prv][:, 0:128], triRev, LF[:, jj, :], True, True, [W_('lf'), CG], [kps(prv)])
                            act(er[:], PS[prv][:, 0:128], AF.Exp, [kps(prv)], [K('er')])
                            for c in range(4):
                                dve_stt(ktc[:, c, :], KA[:, jj, :], cmask[:, c:c + 1], er[:], ALU.mult, ALU.mult, [W_('ka'), K('er'), CG], [K('ktc')])
                            yield
                            pkT = ps_mm()
                            rec.op('pe', lambda e, pkT=pkT, jj=jj, KA=KA: e.transpose(PS[pkT][:, 0:128], KA[:, jj, :], ident), [W_('ka'), CG], [kps(pkT)])
                            dve_tt(kh[:], PS[pkT][:, 0:128], em[:], ALU.mult, [kps(pkT), K('em')], [K('kh')])
                            dve_tt(qi[:], QA[:, ts], e0[:], ALU.mult, [W_('qa'), K('e0')], [K('qi')])
                            dve_tt(qh[:], QA[:, ts], ep[:], ALU.mult, [W_('qa'), K('ep')], [K('qh')])
                            yield
                            pu = ps_mm()
                            for c in range(4):
                                mm(PS[pu][:, c * 128:(c + 1) * 128], ktc[:, c, :], VA[:, jj, :], True, True, [K('ktc'), W_('va')], [kps(pu)])
                            pA = ps_mm()
                            mm(PS[pA][:, 0:128], kh[:], qh[:], True, True, [K('kh'), K('qh')], [kps(pA)])
                            dve_tt(am[:], PS[pA][:, 0:128], maskBD, ALU.mult, [kps(pA), CG], [K('am')])
                            mm(PS[po][:, ts], VA[:, jj, :], am[:], True, False, [W_('va'), K('am')], [kps(po)])
                            for c in range(4):
                                cs = slice(c * 32, (c + 1) * 32)
                                if not st['first']:
                                    sbc = sbs[st['sb'] % 2]
                                    mm(PS[po][:, jj * 128 + c * 32:jj * 128 + (c + 1) * 32], sbc[:], qi[:, cs], False, c == 3, [('sbf', id(sbc)), K('qi')], [kps(po)])
                                if st['first']:
                                    dve_copy(sf[:], PS[pu][:, c * 128:(c + 1) * 128], [kps(pu)], [K('sf')])
                                else:
                                    dve_stt(sf[:], sf[:], e0[:, c * 32 + 31:c * 32 + 32], PS[pu][:, c * 128:(c + 1) * 128], ALU.mult, ALU.add, [K('sf'), K('e0'), kps(pu)], [K('sf')])
                                st['sb'] += 1
                                sbn = sbs[st['sb'] % 2]
                                dve_copy(sbn[:], sf[:], [K('sf')], [('sbf', id(sbn))])
                                st['first'] = False
                            yield
                        q = SQ[s_]
                        act(q[:], PS[po][:], AF.Square, [kps(po)], [('sq', id(q))])
                        pn = ps_acc()
                        mm(PS[pn][:], ones_bf[:], q[:], True, True, [('sq', id(q)), ('c', 'ones')], [kps(pn)])
                        rs = RS[s_]
                        rstd_from_ps(pn, 512, rs, [], 1.0 / 128)
                        on = REC[s_]
                        dve_stt(on[:], PS[po][:], again[:, e_ * 4 + h:e_ * 4 + h + 1], rs[:], ALU.mult, ALU.mult, [kps(po), ('rs', id(rs)), CG], [('rec', id(on))])
                        dve_tt(OTA[:, sl], on[:], GT, ALU.mult, [('rec', id(on)), W_('gt')], [W_('ota', tb)])
                        yield

                for hp in range(2):
                    heads = (2 * hp, 2 * hp + 1)
                    ws = [a_w(s_, heads[s_]) for s_ in range(2)]
                    gens = [a_head_gen(heads[s_], s_, s_, ws[s_][0]) for s_ in range(2)]
                    alive = [True, True]
                    if DBG == 2:
                        for s_ in range(2):
                            for _ in gens[s_]:
                                pass
                        alive = [False, False]
                    if KSTOP > 0:
                        for s_ in range(2):
                            for _i in range(KSTOP):
                                next(gens[s_])
                        alive = [False, False]
                    while any(alive):
                        for s_ in range(2):
                            if alive[s_]:
                                try:
                                    next(gens[s_])
                                except StopIteration:
                                    alive[s_] = False
                    OT2 = [a_views(0)['OTA'], a_views(1)['OTA']]
                    wos = [ws[0][1], ws[1][1]]
                    for tb in range(NTB):
                        for oc in range(NCH):
                            sl = slice(tb * 512, (tb + 1) * 512)
                            if DBG == 3:
                                for s_ in range(2):
                                    pb = ps_mm()
                                    mm(PS[pb][:], wos[s_][:, 0, oc * 128:(oc + 1) * 128], OT2[s_][:, tb * 512:(tb + 1) * 512], True, True,
                                       [('W', s_), ('WK', 'ota', s_, tb)], [kps(pb)])
                                    dve_tt(hT[:, oc, sl], hT[:, oc, sl], PS[pb][:], ALU.add, [('hT', oc, tb), kps(pb)], [('hT', oc, tb)])
                                continue
                            pb = ps_mm()
                            for s_ in range(2):
                                mm(PS[pb][:], wos[s_][:, 0, oc * 128:(oc + 1) * 128], OT2[s_][:, tb * 512:(tb + 1) * 512], s_ == 0, s_ == 1,
                                   [('W', s_), ('WK', 'ota', s_, tb)], [kps(pb)])
                            dve_tt(hT[:, oc, sl], hT[:, oc, sl], PS[pb][:], ALU.add, [('hT', oc, tb), kps(pb)], [('hT', oc, tb)])
                wstate['i'] = 1

            rec.fence('WK')
            QX = wk_bf(0, 4096).rearrange("p (c t) -> p c t", c=2)
            OX = wk_bf(2048, 4096).rearrange("p (c t) -> p c t", c=2)
            MT = WK[:, 4096:6144].rearrange("p (c t) -> p c t", c=8)
            MN_ = wk_bf(6144, 2048).rearrange("p (c t) -> p c t", c=8)
            KX = wk_bf(7168, 512).rearrange("p (c t) -> p c t", c=2)
            VX = wk_bf(7424, 512).rearrange("p (c t) -> p c t", c=2)
            dma('sp', [(MT[:, c, :], memT_d[b, c * 128:(c + 1) * 128, :]) for c in range(NCH)], [], [('WK', 'mt')], 'mt')
            pb = ps_acc()
            for c in range(NCH):
                q = SQ[rotate('sq', 2)]
                act(q[:, 0:256], MT[:, c, :], AF.Square, [('WK', 'mt')], [('sq', id(q))])
                mm(PS[pb][:, 0:256], ones_bf[:], q[:, 0:256], c == 0, c == NCH - 1, [('sq', id(q)), ('c', 'ones')], [kps(pb)])
            rs = RS[rotate('rs', 2)]
            rstd_from_ps(pb, 256, rs, [], 1.0 / D)
            for c in range(NCH):
                dve_stt(MN_[:, c, :], MT[:, c, :], gains[:, (8 + l) * 8 + c:(8 + l) * 8 + c + 1], rs[:, 0:256], ALU.mult, ALU.mult,
                        [('WK', 'mt'), ('rs', id(rs)), CG], [('WK', 'mn')])
            norm_to_xn(4 + l)
            nxt = wslot()
            def x_w(slot, h):
                wq = wview(slot, 0, 8, 256)
                wkv = wview(slot, 2048, 8, 512)
                wo = wview(slot, 6144, 2, 1024)
                load_w(slot, [(wq, wsrc_cols(w_xq[l], h * 256, 256)),
                              (wkv[:, :, 0:256], wsrc_cols(w_xkv[l], h * 256, 256)),
                              (wkv[:, :, 256:512], wsrc_cols(w_xkv[l], 1024 + h * 256, 256)),
                              (wo, wsrc_rows(w_xo[l], h * 256, 2))])
                return wq, wkv, wo
            wcur = x_w(nxt, 0)
            for h in range(4):
                slot = nxt
                wq, wkv, wo = wcur
                if h + 1 < 4:
                    nxt = wslot()
                    wcur = x_w(nxt, h + 1)
                for dc in range(2):
                    pb = ps_mm()
                    for kc in range(NCH):
                        mm(PS[pb][:, 0:256], wkv[:, kc, dc * 128:(dc + 1) * 128], MN_[:, kc, :], kc == 0, kc == NCH - 1, [('W', slot), ('WK', 'mn')], [kps(pb)])
                    act(KX[:, dc, :], PS[pb][:, 0:256], AF.Copy, [kps(pb)], [('WK', 'kx')], scale=1.0 / 16)
                for mt in range(2):
                    pb = ps_mm()
                    for kc in range(NCH):
                        mm(PS[pb][:, 0:256], MN_[:, kc, mt * 128:(mt + 1) * 128], wkv[:, kc, 256:512], kc == 0, kc == NCH - 1, [('W', slot), ('WK', 'mn')], [kps(pb)])
                    dve_copy(VX[:, mt, :], PS[pb][:, 0:256], [kps(pb)], [('WK', 'vx')])
                for tb in range(NTB):
                    sl = slice(tb * 512, (tb + 1) * 512)
                    for dc in range(2):
                        pb = projT(wq, dc * 128, tb, slot)
                        if dc == 0:
                            act(QX[:, dc, sl], PS[pb][:], AF.Copy, [kps(pb)], [('WK', 'qx', tb, dc)])
                        else:
                            dve_copy(QX[:, dc, sl], PS[pb][:], [kps(pb)], [('WK', 'qx', tb, dc)])
                for qb in range(NTB):
                    sl = slice(qb * 512, (qb + 1) * 512)
                    pts = []
                    for mt in range(2):
                        sb_ = ps_mm()
                        for dc in range(2):
                            mm(PS[sb_][:], KX[:, dc, mt * 128:(mt + 1) * 128], QX[:, dc, sl], dc == 0, dc == 1, [('WK', 'kx'), ('WK', 'qx', qb, dc)], [kps(sb_)])
                        pt = PT[rotate('pt', 4)]
                        act(pt[:], PS[sb_][:], AF.Exp, [kps(sb_)], [('pt', id(pt))])
                        pts.append(pt)
                    db = ps_acc()
                    for mt in range(2):
                        mm(PS[db][:], ones_bf[:], pts[mt][:], mt == 0, mt == 1, [('c', 'ones'), ('pt', id(pts[mt]))], [kps(db)])
                    rc = REC[rotate('rec', 2)]
                    act_recip(rc[:], PS[db][:], [kps(db)], [('rec', id(rc))])
                    for dvc in range(2):
                        ob = ps_acc()
                        for mt in range(2):
                            mm(PS[ob][:], VX[:, mt, dvc * 128:(dvc + 1) * 128], pts[mt][:], mt == 0, mt == 1, [('WK', 'vx'), ('pt', id(pts[mt]))], [kps(ob)])
                        dve_tt(OX[:, dvc, sl], PS[ob][:], rc[:], ALU.mult, [kps(ob), ('rec', id(rc))], [('WK', 'ox', qb, dvc)])
                outproj_acc(wo, 2, lambda kc, tb: OX[:, kc, tb * 512:(tb + 1) * 512], lambda kc, tb: [('WK', 'ox', tb, kc)], slot)

            rec.fence('WK')
            norm_to_xn(12 + l)
            nxt = wslot()
            def m_w(slot, g):
                wu = wview(slot, 0, 8, 512)
                wd = wview(slot, 4096, 4, 1024)
                load_w(slot, [(wu, wsrc_cols(w_up[l], g * 512, 512)), (wd, wsrc_rows(w_down[l], g * 512, 4))])
                return wu, wd
            wcur = m_w(nxt, 0)
            for g in range(8):
                slot = nxt
                wu, wd = wcur
                if g + 1 < 8:
                    nxt = wslot()
                    wcur = m_w(nxt, g + 1)
                hb = g % 2
                HFF = wk_bf(hb * 4096, 8192).rearrange("p (c t) -> p c t", c=4)
                for tb in range(NTB):
                    sl = slice(tb * 512, (tb + 1) * 512)
                    for fc in range(4):
                        pb = projT(wu, fc * 128, tb, slot)
                        tm = PT[rotate('pt', 4)]
                        ktm = ('pt', id(tm))
                        act(tm[:], PS[pb][:], AF.Relu, [kps(pb)], [ktm])
                        dve_tt(HFF[:, fc, sl], tm[:], tm[:], ALU.mult, [ktm], [('WK', 'hff', hb, tb, fc)], eng='pool')
                outproj_acc(wd, 4, lambda kc, tb, HFF=HFF: HFF[:, kc, tb * 512:(tb + 1) * 512], lambda kc, tb, hb=hb: [('WK', 'hff', hb, tb, kc)], slot)

        rec.fence('WK')
        OUTS = [WK[:, 0:4096].rearrange("p (c t) -> p c t", c=8), WK[:, 4096:8192].rearrange("p (c t) -> p c t", c=8)]
        out_v = outT_d[b].rearrange("(c p) t -> p c t", p=128)

        def store(tb, b=b, OUTS=OUTS, out_v=out_v):
            dma('sp', [(out_v[:, :, tb * 512:(tb + 1) * 512], OUTS[tb % 2][:])], [('WK', 'out', tb % 2, c) for c in range(NCH)],
                [('od', b, tb)], ('st', tb % 2))
        if final_norm:
            norm_h(16, lambda c, tb: OUTS[tb % 2][:, c, :], lambda c, tb: [('WK', 'out', tb % 2, c)], after_block=store)
        else:
            for tb in range(NTB):
                for c in range(NCH):
                    dve_copy(OUTS[tb % 2][:, c, :], hT[:, c, tb * 512:(tb + 1) * 512], [('hT', c, tb)], [('WK', 'out', tb % 2, c)])
                store(tb)
    rec.op('sp', None, [('od', b, tb) for b in range(nbatch) for tb in range(NTB)], [])

    rec.plan()
    sems = {}
    for e in Rec.ENGS:
        for ep_ in range(max(1, rec.nepoch[e])):
            sems[(e, ep_)] = es.enter_context(nc.semaphore(f"s_{e}_{ep_}"))
    dsems = {}
    for i, k in enumerate(rec.dma_cnt):
        dsems[k] = es.enter_context(nc.semaphore(f"d_{i}"))
    block = es.enter_context(nc.Block())

    def make_body(eng):
        ops = rec.eng_ops[eng]

        def body(e):
            for o in ops:
                for Dd in o.w_eng:
                    e.wait_ge(sems[(Dd.eng, Dd.epoch)], Dd.sigval)
                for (k, t) in o.w_dma:
                    e.wait_ge(dsems[k], t)
                if o.fn is None:
                    continue
                if o.dma:
                    for ins in o.fn(e):
                        ins.then_inc(dsems[o.dsemkey], 16)
                else:
                    ins = o.fn(e)
                    if o.signal:
                        ins.then_inc(sems[(eng, o.epoch)], 1)
        return body

    block.sync(make_body('sp'))
    block.tensor(make_body('pe'))
    block.scalar(make_body('act'))
    block.vector(make_body('dve'))
    block.gpsimd(make_body('pool'))
    es.close()
    return nc, rec


def _consts():
    c = np.zeros((128, 516), np.float32)
    i = np.arange(128)
    c[:, 0:128] = np.eye(128, dtype=np.float32)
    c[:, 128:256] = (i[:, None] <= i[None, :])
    same = (i[:, None] // 32) == (i[None, :] // 32)
    c[:, 256:384] = same & (i[:, None] <= i[None, :])
    c[:, 384:512] = same & (i[:, None] > i[None, :])
    c[:, 512:516] = (i[:, None] // 32) == np.arange(4)[None, :]
    return c


def _prep_shared(inp):
    f = lambda a: np.ascontiguousarray(np.asarray(a, dtype=np.float32))
    g = np.concatenate([f(inp['norm_mix']), f(inp['norm_xattn']), f(inp['norm_mem']), f(inp['norm_mlp']),
                        f(inp['norm_final'])[None, :]], axis=0)
    gains = np.ascontiguousarray(g.reshape(17, 8, 128).transpose(2, 0, 1).reshape(128, 136))
    again = np.ascontiguousarray(f(inp['a_out_gain']).reshape(2, 4, 128).transpose(2, 0, 1).reshape(128, 8))
    sidx = np.arange(128)[:, None]
    uidx = np.arange(640)[None, :]
    ridx = np.clip(uidx - sidx, -128, 128) + 128
    relb = np.ascontiguousarray(f(inp['b_rel_bias'])[:, :, ridx])
    selc = np.zeros((64, 16, 128), np.float32)
    for hh_ in range(16):
        selc[hh_, hh_, 64] = 1.0
        selc[32 + hh_, hh_, 96] = 1.0
    sh = dict(selc=selc.reshape(64, 2048), gains=gains, again=again, lbl=f(inp['a_lb_logits']), fbias=f(inp['c_fgate_bias']), relb=relb, consts=_consts())
    for k in ('w_in_ab', 'w_out_ab', 'w_in_c', 'w_out_c', 'w_xq', 'w_xkv', 'w_xo', 'w_up', 'w_down'):
        sh[k] = f(inp[k])
    return sh


_CACHE = {}


def kernel(**inp):
    ncores = 8
    x = np.asarray(inp['x'], dtype=np.float32)
    mem = np.asarray(inp['mem'], dtype=np.float32)
    sh = _prep_shared(inp)
    in_maps = []
    for c in range(ncores):
        m = dict(sh)
        m['xT'] = np.ascontiguousarray(x[2 * c:2 * c + 2].transpose(0, 2, 1))
        m['memT'] = np.ascontiguousarray(mem[2 * c:2 * c + 2].transpose(0, 2, 1))
        in_maps.append(m)
    if 'nc' not in _CACHE:
        _CACHE['nc'] = build()[0]
    res = run_bass_kernel_spmd(_CACHE['nc'], in_maps, core_ids=list(range(ncores)))
    out = np.empty((16, T, D), np.float32)
    for c in range(ncores):
        out[2 * c:2 * c + 2] = res.results[c]['outT'].transpose(0, 2, 1)
    return out
```

```python
import numpy as np
from contextlib import ExitStack
import concourse.bass as bass
import concourse.mybir as mybir
from concourse.bass_utils import run_bass_kernel_spmd

F32 = mybir.dt.float32
BF16 = mybir.dt.bfloat16
ALU = mybir.AluOpType
AF = mybir.ActivationFunctionType

D = 1024
T = 2048
NCH = 8
NTB = 4
NT = 16
MEM = 256
EPS = 1e-6
EPOCH = 20000
NEG = -30000.0
import os
DBG = int(os.environ.get('KDBG', '0'))
KSTOP = int(os.environ.get('KSTOP', '0'))


class Op:
    pass


class Rec:
    ENGS = ('sp', 'pe', 'act', 'dve', 'pool')

    def __init__(s):
        s.ops = []
        s.lastw = {}
        s.rd_eng = {}
        s.rd_dma = {}
        s.eng_ops = {e: [] for e in s.ENGS}
        s.dma_cnt = {}
        s.fdeps = {}

    def fence(s, region):
        deps = set(s.fdeps.get(region, ()))
        for k in list(s.lastw):
            if k[0] == region:
                deps.add(s.lastw.pop(k))
        for k in list(s.rd_eng):
            if k[0] == region:
                deps.update(s.rd_eng.pop(k).values())
        for k in list(s.rd_dma):
            if k[0] == region:
                deps.update(s.rd_dma.pop(k))
        best = {}
        for di in deps:
            o = s.ops[di]
            kk = ('d', o.dsemkey) if o.dma else ('e', o.eng)
            if kk not in best or best[kk] < di:
                best[kk] = di
        s.fdeps[region] = set(best.values())

    def op(s, eng, fn, r=(), w=(), dma=None):
        o = Op()
        o.eng = eng
        o.fn = fn
        o.idx = len(s.ops)
        o.pos = len(s.eng_ops[eng])
        o.signal = False
        o.dma = dma is not None
        deps = set()
        raw = set()
        for k in r:
            d = s.lastw.get(k)
            if d is not None:
                deps.add(d)
                raw.add(d)
            elif k[0] in s.fdeps:
                deps.update(s.fdeps[k[0]])
                raw.update(s.fdeps[k[0]])
            if k[0] == 'ps':
                for e2, d2 in s.rd_eng.get(k, {}).items():
                    if e2 != eng:
                        deps.add(d2)
        for k in w:
            d = s.lastw.get(k)
            if d is not None:
                deps.add(d)
            elif k[0] in s.fdeps:
                deps.update(s.fdeps[k[0]])
                raw.update(s.fdeps[k[0]])
            deps.update(s.rd_eng.get(k, {}).values())
            deps.update(s.rd_dma.get(k, ()))
        for k in r:
            if o.dma:
                s.rd_dma.setdefault(k, []).append(o.idx)
            else:
                s.rd_eng.setdefault(k, {})[eng] = o.idx
        for k in w:
            s.lastw[k] = o.idx
            s.rd_eng[k] = {}
            s.rd_dma[k] = []
        o.deps = deps
        o.raw = raw
        if o.dma:
            semkey, n = dma
            s.dma_cnt[semkey] = s.dma_cnt.get(semkey, 0) + n
            o.dsemkey = semkey
            o.dtarget = 16 * s.dma_cnt[semkey]
        s.ops.append(o)
        s.eng_ops[eng].append(o)
        return o

    def plan(s):
        waited = {e: {f: -1 for f in s.ENGS} for e in s.ENGS}
        waited_dma = {e: {} for e in s.ENGS}
        for o in s.ops:
            o.w_eng = []
            o.w_dma = []
            best = {}
            bestd = {}
            for di in o.deps:
                Dd = s.ops[di]
                if Dd.dma:
                    if waited_dma[o.eng].get(Dd.dsemkey, 0) >= Dd.dtarget:
                        continue
                    if bestd.get(Dd.dsemkey, 0) < Dd.dtarget:
                        bestd[Dd.dsemkey] = Dd.dtarget
                else:
                    if Dd.eng == o.eng and (o.eng == 'pe' or di not in o.raw):
                        continue
                    if Dd.pos <= waited[o.eng][Dd.eng]:
                        continue
                    if Dd.eng not in best or best[Dd.eng].pos < Dd.pos:
                        best[Dd.eng] = Dd
            for f, Dd in best.items():
                Dd.signal = True
                waited[o.eng][f] = Dd.pos
                o.w_eng.append(Dd)
            for k, t in bestd.items():
                waited_dma[o.eng][k] = t
                o.w_dma.append((k, t))
        s.nepoch = {}
        for e in s.ENGS:
            c = 0
            for o in s.eng_ops[e]:
                if o.signal:
                    o.epoch = c // EPOCH
                    o.sigval = c % EPOCH + 1
                    c += 1
            s.nepoch[e] = (c + EPOCH - 1) // EPOCH


def build(nbatch=2, layers=(0, 1, 2, 3), final_norm=True):
    nc = bass.Bass("TRN2", target_bir_lowering=False)
    rec = Rec()

    def din(name, shape):
        return nc.dram_tensor(name, list(shape), F32, kind="ExternalInput").ap()

    xT_d = din("xT", [2, D, T])
    memT_d = din("memT", [2, D, MEM])
    gains_d = din("gains", [128, 136])
    again_d = din("again", [128, 8])
    lbl_d = din("lbl", [2, 512])
    fbias_d = din("fbias", [2, 16])
    relb_d = din("relb", [2, 8, 128, 640])
    consts_d = din("consts", [128, 516])
    selc_d = din("selc", [64, 2048])
    w_in_ab = din("w_in_ab", [2, D, 3584])
    w_out_ab = din("w_out_ab", [2, D, D])
    w_in_c = din("w_in_c", [2, D, 3088])
    w_out_c = din("w_out_c", [2, D, D])
    w_xq = din("w_xq", [4, D, D])
    w_xkv = din("w_xkv", [4, D, 2 * D])
    w_xo = din("w_xo", [4, D, D])
    w_up = din("w_up", [4, D, 4 * D])
    w_down = din("w_down", [4, 4 * D, D])
    outT_d = nc.dram_tensor("outT", [2, D, T], F32, kind="ExternalOutput").ap()

    es = ExitStack()

    def sb(name, shape, dt):
        return es.enter_context(nc.sbuf_tensor(name, list(shape), dt))

    hT = sb("hT", [128, NCH, T], F32)
    xn = sb("xn", [128, NCH, T], BF16)
    Wt = [sb("W0", [128, 8192], BF16), sb("W1", [128, 8192], BF16)]
    WK = sb("WK", [128, 8192], F32)
    gains = sb("gains_s", [128, 136], F32)
    again = sb("again_s", [128, 8], F32)
    consts = sb("consts_s", [128, 516], F32)
    tri_bf = sb("tri_bf", [128, 128], BF16)
    ones_bf = sb("ones_bf", [128, 128], BF16)
    ones_f = sb("ones_f", [128, 128], F32)
    SEL = sb("sel", [64, 16, 128], BF16)
    oml = sb("oml", [128, 512], F32)
    fbt = sb("fbt", [128, 16], F32)
    lf = sb("lf", [128, 16, 16], F32)
    cum = sb("cum", [128, 16, 16], F32)
    zf = cum
    rcar = sb("rcar", [128, 4, 16], F32)
    fbtab = sb("fbtab", [128, 4, 16, 16], F32)
    SQ = [sb(f"sq{i}", [128, 512], BF16) for i in range(2)]
    RS = [sb(f"rs{i}", [128, 512], F32) for i in range(2)]
    PT = [sb(f"pt{i}", [128, 512], BF16) for i in range(4)]
    TMP = [sb(f"tmp{i}", [128, 512], F32) for i in range(2)]
    REC = [sb(f"rec{i}", [128, 512], F32) for i in range(2)]
    BH = [WK[:, 5120:5760]]
    ER = [sb(f"er{i}", [128, 128], F32) for i in range(2)]
    E0 = [sb(f"e0{i}", [128, 128], F32) for i in range(2)]
    EP = [sb(f"ep{i}", [128, 128], F32) for i in range(2)]
    EM = [sb(f"em{i}", [128, 128], F32) for i in range(2)]
    MP = [sb(f"mp{i}", [128, 4], F32) for i in range(2)]
    MN = [sb(f"mn{i}", [128, 4], F32) for i in range(2)]
    QI = [sb(f"qi{i}", [128, 128], BF16) for i in range(2)]
    QH = [sb(f"qh{i}", [128, 128], BF16) for i in range(2)]
    KH = [sb(f"kh{i}", [128, 128], BF16) for i in range(2)]
    KTC = [sb(f"ktc{i}", [128, 4, 128], BF16) for i in range(2)]
    AM = [sb(f"am{i}", [128, 128], BF16) for i in range(2)]
    SFS = [sb(f"sf{i}", [128, 128], F32) for i in range(2)]
    EB1 = sb("eb1", [128, 640], BF16)
    SB = [sb(f"sbf{i}", [128, 128], BF16) for i in range(4)]

    PS = [es.enter_context(nc.psum_tensor(f"ps{i}", [128, 512], F32)) for i in range(8)]

    ident = consts[:, 0:128]
    tri_f = consts[:, 128:256]
    maskBD = consts[:, 256:384]
    triRev = consts[:, 384:512]
    cmask = consts[:, 512:516]

    cnt = {'mm': 0, 'acc': 0}

    def ps_mm():
        cnt['mm'] += 1
        return cnt['mm'] % 4

    def ps_acc():
        cnt['acc'] += 1
        return 4 + cnt['acc'] % 4

    rot = {}

    def rotate(name, n):
        rot[name] = rot.get(name, -1) + 1
        return rot[name] % n

    def mm(out, lhsT, rhs, start, stop, r, w):
        rec.op('pe', lambda e: e.matmul(out, lhsT, rhs, start=start, stop=stop), r, w)

    def act(out, in_, func, r, w, bias=0.0, scale=1.0):
        rec.op('act', lambda e: e.activation(out, in_, func, bias=bias, scale=scale), r, w)

    def dve_tt(out, a, b, op, r, w, eng='dve'):
        rec.op(eng, lambda e: e.tensor_tensor(out, a, b, op), r, w)

    def dve_ts(out, a, s1, s2, op0, op1, r, w, eng='dve'):
        rec.op(eng, lambda e: e.tensor_scalar(out, a, s1, s2, op0, op1), r, w)

    def dve_stt(out, a, sc, b, op0, op1, r, w):
        rec.op('dve', lambda e: e.scalar_tensor_tensor(out, a, sc, b, op0, op1), r, w)

    def dve_copy(out, a, r, w, eng='dve'):
        rec.op(eng, lambda e: e.tensor_copy(out, a), r, w)

    def dma(eng, pairs, r, w, semkey):
        pairs = list(pairs)
        rec.op(eng, lambda e: [e.dma_start(out=o, in_=i) for (o, i) in pairs], r, w, dma=(semkey, len(pairs)))

    def kps(b):
        return ('ps', b)

    dma('sp', [(gains[:], gains_d), (again[:], again_d), (consts[:], consts_d)], [], [('c', 'g')], 'c0')
    dma('pool', [(tri_bf[:], consts_d[:, 128:256])], [], [('c', 'tribf')], 'c2')
    dma('pool', [(SEL[:].rearrange("p h m -> p (h m)"), selc_d)], [], [('c', 'sel')], 'c5')
    rec.op('pool', lambda e: e.memset(ones_bf[:], 1.0), [], [('c', 'ones')])
    rec.op('pool', lambda e: e.memset(ones_f[:], 1.0), [], [('c', 'onesf')])
    CG = ('c', 'g')

    wstate = {'i': 0}

    def wslot():
        wstate['i'] += 1
        return wstate['i'] % 2

    def wview(slot, off, kc, n):
        return Wt[slot][:, off:off + kc * n].rearrange("p (kc n) -> p kc n", kc=kc)

    def wsrc_cols(wd, c0, n):
        return wd.rearrange("(kc p) n -> p kc n", p=128)[:, :, c0:c0 + n]

    def wsrc_rows(wd, r0, kc):
        return wd[r0:r0 + kc * 128, :].rearrange("(kc p) n -> p kc n", p=128)

    def load_w(slot, pairs):
        dma('pool', pairs, [], [('W', slot)], ('W', slot))

    def rstd_from_ps(psb, ncols, rs, rkeys, scale):
        act(rs[:, 0:ncols], PS[psb][:, 0:ncols], AF.Ln, rkeys + [kps(psb)], [('rs', id(rs))], bias=EPS, scale=scale)
        act(rs[:, 0:ncols], rs[:, 0:ncols], AF.Exp, [('rs', id(rs))], [('rs', id(rs))], scale=-0.5)

    def norm_h(gidx, out_fn, out_keys_fn, after_block=None):
        for tb in range(NTB):
            sl = slice(tb * 512, (tb + 1) * 512)
            pb = ps_acc()
            for c in range(NCH):
                q = SQ[rotate('sq', 2)]
                act(q[:], hT[:, c, sl], AF.Square, [('hT', c, tb)], [('sq', id(q))])
                mm(PS[pb][:], ones_bf[:], q[:], c == 0, c == NCH - 1, [('sq', id(q)), ('c', 'ones')], [kps(pb)])
            rs = RS[rotate('rs', 2)]
            rstd_from_ps(pb, 512, rs, [], 1.0 / D)
            for c in range(NCH):
                dve_stt(out_fn(c, tb), hT[:, c, sl], gains[:, gidx * 8 + c:gidx * 8 + c + 1], rs[:], ALU.mult, ALU.mult,
                        [('hT', c, tb), ('rs', id(rs)), CG], out_keys_fn(c, tb))
            if after_block is not None:
                after_block(tb)

    def norm_to_xn(gidx):
        norm_h(gidx, lambda c, tb: xn[:, c, tb * 512:(tb + 1) * 512], lambda c, tb: [('xn', tb, c)])

    def xn_keys(tb):
        return [('xn', tb, c) for c in range(NCH)]

    def projT(wv, c0, tb, slot):
        pb = ps_mm()
        for kc in range(NCH):
            mm(PS[pb][:], wv[:, kc, c0:c0 + 128], xn[:, kc, tb * 512:(tb + 1) * 512], kc == 0, kc == NCH - 1,
               [('W', slot), ('xn', tb, kc)], [kps(pb)])
        return pb

    def outproj_acc(wo, nkc, src_fn, src_keys_fn, slot):
        for tb in range(NTB):
            for oc in range(NCH):
                pb = ps_mm()
                for kc in range(nkc):
                    mm(PS[pb][:], wo[:, kc, oc * 128:(oc + 1) * 128], src_fn(kc, tb), kc == 0, kc == nkc - 1,
                       [('W', slot)] + src_keys_fn(kc, tb), [kps(pb)])
                sl = slice(tb * 512, (tb + 1) * 512)
                dve_tt(hT[:, oc, sl], hT[:, oc, sl], PS[pb][:], ALU.add, [('hT', oc, tb), kps(pb)], [('hT', oc, tb)])

    def wk_bf(off, n):
        return WK[:, off:off + n // 2].bitcast(BF16)

    def dve_recip(out, a, r, w):
        rec.op('dve', lambda e: e.reciprocal(out, a), r, w)

    def act_recip(out, a, r, w):
        act(out, a, AF.Ln, r, w)
        act(out, out, AF.Exp, w, w, scale=-1.0)

    def attn_finalize(ob, out_ap, out_keys, on_dve=False):
        rc = REC[rotate('rec', 2)]
        if on_dve:
            dve_recip(rc[0:64, :], PS[ob][64:128, :], [kps(ob)], [('rec', id(rc))])
        else:
            act_recip(rc[0:64, :], PS[ob][64:128, :], [kps(ob)], [('rec', id(rc))])
        dve_tt(out_ap, PS[ob][0:64, :], rc[0:64, :], ALU.mult, [kps(ob), ('rec', id(rc))], out_keys)

    SKEW = 3

    def run_tiles(tiles):
        pend = []
        for t in tiles + [None] * SKEW:
            if t is not None:
                if t.get('first') is not None:
                    t['first']()
                pend.append((t, t['score']()))
            if pend and (len(pend) > SKEW or t is None):
                pt_, info = pend.pop(0)
                pt_['pv'](info)
                if pt_.get('fin') is not None:
                    pt_['fin']()
        assert not pend

    for b in range(nbatch):
        for c in range(NCH):
            dma('sp', [(hT[:, c, :], xT_d[b, c * 128:(c + 1) * 128, :])], [],
                [('hT', c, tb) for tb in range(NTB)], ('ld', c))

        for l in layers:
            rec.fence('WK')
            norm_to_xn(l)
            if l % 2 == 1:
                o_ = l // 2
                Qh = [wk_bf(0, 2048), wk_bf(1024, 2048)]
                Kh = [wk_bf(2048, 2048), wk_bf(3072, 2048)]
                OTv = wk_bf(4096, 2048)
                s0 = wslot()
                wf = wview(s0, 0, 8, 16)
                load_w(s0, [(wf, wsrc_cols(w_in_c[o_], 3072, 16))])
                dma('sp', [(fbt[:], fbias_d[o_].partition_broadcast(128))], [], [('c', 'fbt')], 'c3')
                pb = ps_mm()
                for j in range(NT):
                    for kc in range(NCH):
                        mm(PS[pb][:, j * 16:(j + 1) * 16], xn[:, kc, j * 128:(j + 1) * 128], wf[:, kc, :], kc == 0, kc == NCH - 1,
                           [('W', s0), ('xn', j // 4, kc)], [kps(pb)])
                for j in range(NT):
                    dve_tt(zf[:, j, :], PS[pb][:, j * 16:(j + 1) * 16], fbt[:], ALU.add, [kps(pb), ('c', 'fbt')], [('c', 'cum')])
                zf2 = zf[:].rearrange("p j h -> p (j h)")
                lf2 = lf[:].rearrange("p j h -> p (j h)")
                act(zf2, zf2, AF.Exp, [('c', 'cum')], [('c', 'cum')], scale=-1.0)
                act(lf2, zf2, AF.Ln, [('c', 'cum')], [('c', 'lf')], bias=1.0)
                dve_ts(lf2, lf2, -1.0, None, ALU.mult, ALU.bypass, [('c', 'lf')], [('c', 'lf')])
                pc = ps_acc()
                for j in range(NT):
                    mm(PS[pc][:, j * 16:(j + 1) * 16], tri_f, lf[:, j, :], True, j == 0, [('c', 'lf'), CG], [kps(pc)])
                    for i in range(j):
                        mm(PS[pc][:, j * 16:(j + 1) * 16], ones_f[:], lf[:, i, :], False, i == j - 1, [('c', 'lf'), ('c', 'onesf')], [kps(pc)])
                dve_copy(cum[:].rearrange("p j h -> p (j h)"), PS[pc][:, 0:256], [kps(pc)], [('c', 'cum')])
                pr = ps_acc()
                for qb in range(NTB):
                    n = 4 * qb + 4
                    for i in range(n):
                        mm(PS[pr][:, qb * 16:(qb + 1) * 16], ones_f[:], lf[:, i, :], i == 0, i == n - 1, [('c', 'lf'), ('c', 'onesf')], [kps(pr)])
                dve_copy(rcar[:].rearrange("p q h -> p (q h)"), PS[pr][:, 0:64], [kps(pr)], [('c', 'rcar')])
                for qb in range(NTB):
                    for j in range(4 * qb + 4):
                        dve_tt(fbtab[:, qb, j, :], rcar[:, qb, :], cum[:, j, :], ALU.subtract, [('c', 'rcar'), ('c', 'cum')], [('c', 'fbtab')])
                cum48 = WK[:, 0:2048].rearrange("p (j c) -> p j c", j=16)
                CT = WK[:, 4096:6144]
                CHL = wk_bf(7168, 2048)
                rec.op('pool', lambda e, cum48=cum48: e.memset(cum48, 0.0), [], [('WK', 'cum48')])
                dve_copy(cum48[:, :, 0:16], cum[:], [('c', 'cum'), ('WK', 'cum48')], [('WK', 'cum48')])
                dve_copy(cum48[:, :, 32:48], cum[:], [('c', 'cum'), ('WK', 'cum48')], [('WK', 'cum48')])
                dve_copy(cum48[:, :, 64:80], cum[:], [('c', 'cum'), ('WK', 'cum48')], [('WK', 'cum48')])
                dve_copy(cum48[:, :, 96:112], cum[:], [('c', 'cum'), ('WK', 'cum48')], [('WK', 'cum48')])
                for j4 in range(4):
                    pb = ps_mm()
                    for jj in range(4):
                        j = j4 * 4 + jj
                        rec.op('pe', lambda e, pb=pb, jj=jj, j=j, cum48=cum48: e.transpose(PS[pb][:, jj * 128:(jj + 1) * 128], cum48[:, j, :], ident),
                               [('WK', 'cum48'), CG], [kps(pb)])
                    dve_copy(CT[:, j4 * 512:(j4 + 1) * 512], PS[pb][:, :], [kps(pb)], [('WK', 'ct', j4)])
                for qb in range(NTB):
                    blk = slice(qb * 512, (qb + 1) * 512)
                    tm = TMP[rotate('tmp', 2)]
                    ktm = ('tmp', TMP.index(tm))
                    dve_ts(tm[:, :], CT[:, blk], CT[:, qb * 512 + 511:qb * 512 + 512], None, ALU.subtract, ALU.bypass, [('WK', 'ct', qb)], [ktm])
                    dve_copy(CHL[:, blk], tm[:, :], [ktm], [('WK', 'chl', qb)])
                    dve_tt(CHL[32:48, blk], tm[32:48, :], CHL[32:48, blk], ALU.subtract, [ktm, ('WK', 'chl', qb)], [('WK', 'chl', qb)])
                    dve_tt(CHL[96:112, blk], tm[96:112, :], CHL[96:112, blk], ALU.subtract, [ktm, ('WK', 'chl', qb)], [('WK', 'chl', qb)])
                rec.fence('WK')
                Vaug = wk_bf(5120, 4096).rearrange("p (j h d) -> p j h d", j=16, h=2)
                rec.op('pool', lambda e, Vaug=Vaug: e.memset(Vaug[:, :, :, 64:128], 1.0), [], [('WK', 'vones')])
                for hh in range(2):
                    rec.op('pool', lambda e, t=Kh[hh]: e.memset(t[64:128, :], 0.0), [], [('WK', 'kaug', hh)])
                    rec.op('pool', lambda e, t=Kh[hh]: e.memset(t[64:65, :], 1.0), [('WK', 'kaug', hh)], [('WK', 'kaug', hh)])
                    rec.op('pool', lambda e, t=Kh[hh]: e.memset(t[96:97, :], 1.0), [('WK', 'kaug', hh)], [('WK', 'kaug', hh)])
                nxt = wslot()
                def fox_w(slot, p):
                    wv = wview(slot, 0, 8, 384)
                    wo = wview(slot, 3072, 1, 1024)
                    load_w(slot, [(wv[:, :, 0:128], wsrc_cols(w_in_c[o_], p * 128, 128)),
                                  (wv[:, :, 128:256], wsrc_cols(w_in_c[o_], 1024 + p * 128, 128)),
                                  (wv[:, :, 256:384], wsrc_cols(w_in_c[o_], 2048 + p * 128, 128)),
                                  (wo, wsrc_rows(w_out_c[o_], p * 128, 1))])
                    return wv, wo
                wcur = fox_w(nxt, 0)
                for p in range(8):
                    slot = nxt
                    wv, wo = wcur
                    if p + 1 < 8:
                        nxt = wslot()
                        wcur = fox_w(nxt, p + 1)
                    for tb in range(NTB):
                        sl = slice(tb * 512, (tb + 1) * 512)
                        pb = projT(wv, 0, tb, slot)
                        act(Qh[0][0:64, sl], PS[pb][0:64, :], AF.Copy, [kps(pb)], [('WK', 'q', 0, tb)], scale=0.125)
                        dve_ts(Qh[1][0:64, sl], PS[pb][64:128, :], 0.125, None, ALU.mult, ALU.bypass, [kps(pb)], [('WK', 'q', 1, tb)])
                        pb = projT(wv, 128, tb, slot)
                        for hh in range(2):
                            dve_copy(Kh[hh][0:64, sl], PS[pb][hh * 64:(hh + 1) * 64, :], [kps(pb)], [('WK', 'k', hh, tb)])
                        for hh in range(2):
                            pb = ps_mm()
                            mm(PS[pb][:], SEL[:, 2 * p + hh, :], CHL[0:64, sl], True, True, [('c', 'sel'), ('WK', 'chl', tb)], [kps(pb)])
                            dve_copy(Qh[hh][64:128, sl], PS[pb][64:128, :], [kps(pb)], [('WK', 'qaug', hh, tb)])
                    for j4 in range(4):
                        pb = ps_mm()
                        for jj in range(4):
                            j = j4 * 4 + jj
                            for kc in range(NCH):
                                mm(PS[pb][:, jj * 128:(jj + 1) * 128], xn[:, kc, j * 128:(j + 1) * 128], wv[:, kc, 256:384], kc == 0, kc == NCH - 1,
                                   [('W', slot), ('xn', j4, kc)], [kps(pb)])
                        act(Vaug[:, j4 * 4:(j4 + 1) * 4, :, 0:64], PS[pb][:].rearrange("p (j h d) -> p j h d", j=4, h=2), AF.Copy, [kps(pb), ('WK', 'vones')], [('WK', 'v', j4)])
                    tiles = []
                    for hh in range(2):
                        for qb in range(NTB):
                            nj = 4 * qb + 4
                            blk = {}
                            for j in range(nj):
                                def score(hh=hh, qb=qb, j=j, h=2 * p + hh):
                                    n0 = max(0, j - 4 * qb) * 128
                                    N = 512 - n0
                                    sb_ = ps_mm()
                                    mm(PS[sb_][:, 0:N], Kh[hh][:, j * 128:(j + 1) * 128], Qh[hh][:, qb * 512 + n0:(qb + 1) * 512], True, True,
                                       [('WK', 'k', hh, j // 4), ('WK', 'kaug', hh), ('WK', 'q', hh, qb), ('WK', 'qaug', hh, qb)], [kps(sb_)])
                                    pt = PT[rotate('pt', 4)]
                                    kpt = ('pt', id(pt))
                                    act(pt[:, 0:N], PS[sb_][:, 0:N], AF.Exp, [kps(sb_), ('c', 'fbtab')], [kpt], bias=fbtab[:, qb, j, h:h + 1])
                                    if j >= 4 * qb:
                                        dve_tt(pt[:, 0:128], pt[:, 0:128], tri_bf[:], ALU.mult, [kpt, ('c', 'tribf')], [kpt], eng='pool')
                                    return (n0, N, pt, kpt)

                                def pv(info, hh=hh, j=j, nj=nj, blk=blk):
                                    n0, N, pt, kpt = info
                                    mm(PS[blk['ob']][:, n0:512], Vaug[:, j, hh, :], pt[:, 0:N], j == 0, j == nj - 1, [('WK', 'v', j // 4), ('WK', 'vones'), kpt], [kps(blk['ob'])])
                                t = {'score': score, 'pv': pv}
                                if j == 0:
                                    t['first'] = lambda blk=blk: blk.__setitem__('ob', ps_acc())
                                if j == nj - 1:
                                    t['fin'] = lambda blk=blk, hh=hh, qb=qb: attn_finalize(blk['ob'], OTv[64 * hh:64 * hh + 64, qb * 512:(qb + 1) * 512], [('WK', 'ot', qb, hh)], on_dve=True)
                                tiles.append(t)
                    run_tiles(tiles)
                    outproj_acc(wo, 1, lambda kc, tb: OTv[:, tb * 512:(tb + 1) * 512], lambda kc, tb: [('WK', 'ot', tb, 0), ('WK', 'ot', tb, 1)], slot)
            else:
                e_ = l // 2
                if e_ == 0:
                    rec.op('pool', lambda e: e.memset(oml[:], 1.0), [], [('c', 'oml')])
                else:
                    dma('sp', [(TMP[0][:], lbl_d[0].partition_broadcast(128))], [], [('tmp', 0)], 'c1')
                    dma('sp', [(TMP[1][:], lbl_d[1].partition_broadcast(128))], [], [('tmp', 1)], 'c4')
                    act(TMP[0][:], TMP[0][:], AF.Exp, [('tmp', 0)], [('tmp', 0)])
                    act(TMP[1][:], TMP[1][:], AF.Exp, [('tmp', 1)], [('tmp', 1)])
                    dve_tt(TMP[0][:], TMP[0][:], TMP[1][:], ALU.add, [('tmp', 0), ('tmp', 1)], [('tmp', 0)])
                    dve_recip(TMP[0][:], TMP[0][:], [('tmp', 0)], [('tmp', 0)])
                    dve_tt(TMP[0][:], TMP[1][:], TMP[0][:], ALU.mult, [('tmp', 0), ('tmp', 1)], [('tmp', 0)])
                    dve_ts(oml[:], TMP[0][:], -1.0, 1.0, ALU.mult, ALU.add, [('tmp', 0)], [('c', 'oml')])
                Qh = [wk_bf(0, 2048), wk_bf(1024, 2048)]
                Kh = [wk_bf(2048, 2048), wk_bf(3072, 2048)]
                OTv = wk_bf(4096, 2048)
                Vaug = wk_bf(6144, 4096).rearrange("p (j h d) -> p j h d", j=16, h=2)
                EB = [wk_bf(5760, 640), EB1[:]]
                rec.op('pool', lambda e, Vaug=Vaug: e.memset(Vaug[:, :, :, 64:128], 1.0), [], [('WK', 'vones')])
                for hh in range(2):
                    rec.op('pool', lambda e, t=Kh[hh]: e.memset(t[64:128, :], 0.0), [], [('WK', 'kaug', hh)])
                    rec.op('pool', lambda e, t=Qh[hh]: e.memset(t[64:128, :], 0.0), [], [('WK', 'qaug', hh)])
                nxt = wslot()
                def b_w(slot, p):
                    wv = wview(slot, 0, 8, 384)
                    wo = wview(slot, 3072, 1, 1024)
                    load_w(slot, [(wv[:, :, 0:128], wsrc_cols(w_in_ab[e_], 2048 + p * 128, 128)),
                                  (wv[:, :, 128:256], wsrc_cols(w_in_ab[e_], 2560 + p * 128, 128)),
                                  (wv[:, :, 256:384], wsrc_cols(w_in_ab[e_], 3072 + p * 128, 128)),
                                  (wo, wsrc_rows(w_out_ab[e_], 512 + p * 128, 1))])
                    return wv, wo
                wcur = b_w(nxt, 0)
                for p in range(4):
                    slot = nxt
                    wv, wo = wcur
                    if p + 1 < 4:
                        nxt = wslot()
                        wcur = b_w(nxt, p + 1)
                    for tb in range(NTB):
                        sl = slice(tb * 512, (tb + 1) * 512)
                        pb = projT(wv, 0, tb, slot)
                        for hh in range(2):
                            act(Qh[hh][0:64, sl], PS[pb][hh * 64:(hh + 1) * 64, :], AF.Copy, [kps(pb)], [('WK', 'q', hh, tb)], scale=0.125)
                        pb = projT(wv, 128, tb, slot)
                        for hh in range(2):
                            dve_copy(Kh[hh][0:64, sl], PS[pb][hh * 64:(hh + 1) * 64, :], [kps(pb)], [('WK', 'k', hh, tb)])
                    for j4 in range(4):
                        pb = ps_mm()
                        for jj in range(4):
                            j = j4 * 4 + jj
                            for kc in range(NCH):
                                mm(PS[pb][:, jj * 128:(jj + 1) * 128], xn[:, kc, j * 128:(j + 1) * 128], wv[:, kc, 256:384], kc == 0, kc == NCH - 1,
                                   [('W', slot), ('xn', j4, kc)], [kps(pb)])
                        act(Vaug[:, j4 * 4:(j4 + 1) * 4, :, 0:64], PS[pb][:].rearrange("p (j h d) -> p j h d", j=4, h=2), AF.Copy, [kps(pb), ('WK', 'vones')], [('WK', 'v', j4)])
                    tiles = []
                    for hh in range(2):
                        h = 2 * p + hh
                        bh = BH[0]
                        kbh = ('WK', 'bh', 0)
                        eb = EB[hh]
                        keb = ('WK', 'eb', hh)
                        dma('sp', [(bh, relb_d[e_, h])], [], [kbh], ('bh', 0))
                        rec.op('pool', lambda e, bh=bh: e.memset(bh[0:64, 576:640], NEG), [kbh], [kbh])
                        rec.op('pool', lambda e, bh=bh: e.memset(bh[64:128, 0:64], NEG), [kbh], [kbh])
                        act(eb, bh, AF.Exp, [kbh], [keb])
                        for qb in range(NTB):
                            ds = [0] + [d for d in (-512, -384, -256, -128, 128, 256, 384) if 0 <= qb * 512 + d < T]
                            nd = len(ds)
                            blk = {}
                            for ii, d in enumerate(ds):
                                def score(hh=hh, qb=qb, d=d, eb=eb, keb=keb):
                                    kb = qb * 512 + d
                                    j = kb // 128
                                    lo = max(0, d)
                                    hi = 512 if d >= -128 else d + 640
                                    N = hi - lo
                                    sb_ = ps_mm()
                                    mm(PS[sb_][:, 0:N], Kh[hh][:, kb:kb + 128], Qh[hh][:, qb * 512 + lo:qb * 512 + hi], True, True,
                                       [('WK', 'k', hh, j // 4), ('WK', 'kaug', hh), ('WK', 'q', hh, qb), ('WK', 'qaug', hh)], [kps(sb_)])
                                    pt = PT[rotate('pt', 4)]
                                    kpt = ('pt', id(pt))
                                    act(pt[:, 0:N], PS[sb_][:, 0:N], AF.Exp, [kps(sb_)], [kpt])
                                    dve_tt(pt[:, 0:N], pt[:, 0:N], eb[:, lo - d:hi - d], ALU.mult, [kpt, keb], [kpt])
                                    return (j, lo, hi, N, pt, kpt)

                                def pv(info, hh=hh, ii=ii, nd=nd, blk=blk):
                                    j, lo, hi, N, pt, kpt = info
                                    mm(PS[blk['ob']][:, lo:hi], Vaug[:, j, hh, :], pt[:, 0:N], ii == 0, ii == nd - 1, [('WK', 'v', j // 4), ('WK', 'vones'), kpt], [kps(blk['ob'])])
                                t = {'score': score, 'pv': pv}
                                if ii == 0:
                                    t['first'] = lambda blk=blk: blk.__setitem__('ob', ps_acc())
                                if ii == nd - 1:
                                    t['fin'] = lambda blk=blk, hh=hh, qb=qb: attn_finalize(blk['ob'], OTv[64 * hh:64 * hh + 64, qb * 512:(qb + 1) * 512], [('WK', 'ot', qb, hh)])
                                tiles.append(t)
                    run_tiles(tiles)
                    outproj_acc(wo, 1, lambda kc, tb: OTv[:, tb * 512:(tb + 1) * 512], lambda kc, tb: [('WK', 'ot', tb, 0), ('WK', 'ot', tb, 1)], slot)

                rec.fence('WK')

                def a_views(s_):
                    o = s_ * 3072
                    return dict(QA=WK[:, o:o + 512], GT=wk_bf(o + 512, 512),
                                KA=WK[:, o + 768:o + 1280].rearrange("p (j d) -> p j d", j=4),
                                LF=WK[:, o + 1280:o + 1792].rearrange("p (j d) -> p j d", j=4),
                                VA=wk_bf(o + 1792, 512).rearrange("p (j d) -> p j d", j=4),
                                OTA=wk_bf(o + 2048, 2048))

                def a_w(slot, h):
                    wv = wview(slot, 0, 8, 512)
                    wo = wview(slot, 4096, 1, 1024)
                    load_w(slot, [(wv[:, :, i * 128:(i + 1) * 128], wsrc_cols(w_in_ab[e_], i * 512 + h * 128, 128)) for i in range(4)]
                           + [(wo, wsrc_rows(w_out_ab[e_], h * 128, 1))])
                    return wv, wo

                def a_head_gen(h, s_, slot, wv):
                    V = a_views(s_)
                    QA, GT, KA, LF, VA, OTA = V['QA'], V['GT'], V['KA'], V['LF'], V['VA'], V['OTA']
                    W_ = lambda n, *a: ('WK', n, s_) + a
                    hs = slice(h * 128, (h + 1) * 128)
                    er, e0, ep, em, mp, mn = ER[s_], E0[s_], EP[s_], EM[s_], MP[s_], MN[s_]
                    qi, qh, kh, ktc, am = QI[s_], QH[s_], KH[s_], KTC[s_], AM[s_]
                    sf = SFS[s_]
                    sbs = [SB[2 * s_], SB[2 * s_ + 1]]
                    K = lambda n: ('a', n, s_)
                    st = {'first': True, 'sb': 0}
                    for tb in range(NTB):
                        sl = slice(tb * 512, (tb + 1) * 512)
                        pb = projT(wv, 0, tb, slot)
                        sg = TMP[s_]
                        ksg = ('tmp', s_)
                        act(sg[:], PS[pb][:], AF.Sigmoid, [kps(pb)], [ksg])
                        dve_stt(QA, PS[pb][:], float(128 ** -0.5), sg[:], ALU.mult, ALU.mult, [kps(pb), ksg], [W_('qa')])
                        yield
                        pb = projT(wv, 384, tb, slot)
                        act(sg[:], PS[pb][:], AF.Sigmoid, [kps(pb)], [ksg])
                        dve_tt(GT, PS[pb][:], sg[:], ALU.mult, [kps(pb), ksg], [W_('gt')])
                        yield
                        pv = ps_mm()
                        for jj in range(4):
                            j = tb * 4 + jj
                            for kc in range(NCH):
                                mm(PS[pv][:, jj * 128:(jj + 1) * 128], xn[:, kc, j * 128:(j + 1) * 128], wv[:, kc, 256:384], kc == 0, kc == NCH - 1,
                                   [('W', slot), ('xn', tb, kc)], [kps(pv)])
                        act(VA, PS[pv][:].rearrange("p (j d) -> p j d", j=4), AF.Copy, [kps(pv)], [W_('va')])
                        yield
                        pz = ps_mm()
                        for jj in range(4):
                            j = tb * 4 + jj
                            for kc in range(NCH):
                                mm(PS[pz][:, jj * 128:(jj + 1) * 128], xn[:, kc, j * 128:(j + 1) * 128], wv[:, kc, 128:256], kc == 0, kc == NCH - 1,
                                   [('W', slot), ('xn', tb, kc)], [kps(pz)])
                        act(sg[:], PS[pz][:], AF.Sigmoid, [kps(pz)], [ksg], scale=-1.0)
                        for jj in range(4):
                            dve_tt(KA[:, jj, :], sg[:, jj * 128:(jj + 1) * 128], oml[:, hs], ALU.mult, [ksg, ('c', 'oml')], [W_('ka')])
                        act(LF, KA, AF.Ln, [W_('ka')], [W_('lf')], bias=1.0, scale=-1.0)
                        yield
                        po = 4 + s_
                        for jj in range(4):
                            ts = slice(jj * 128, (jj + 1) * 128)
                            pbT = ps_mm()
                            mm(PS[pbT][:, 0:128], LF[:, jj, :], maskBD, True, True, [W_('lf'), CG], [kps(pbT)])
                            act(e0[:], PS[pbT][:, 0:128], AF.Exp, [kps(pbT)], [K('e0')])
                            dve_copy(mp[:], PS[pbT][:, 0:128].rearrange("p (c t) -> p c t", c=4)[:, :, 15], [kps(pbT), K('e0')], [K('mp')])
                            dve_ts(mn[:], mp[:], -1.0, None, ALU.mult, ALU.bypass, [K('mp')], [K('mn')])
                            for c in range(4):
                                cs = slice(c * 32, (c + 1) * 32)
                                act(ep[:, cs], PS[pbT][:, cs], AF.Exp, [kps(pbT), K('mn')], [K('ep')], bias=mn[:, c:c + 1])
                                act(em[:, cs], PS[pbT][:, cs], AF.Exp, [kps(pbT), K('mp')], [K('em')], bias=mp[:, c:c + 1], scale=-1.0)
                            yield
                            prv = ps_mm()
                            mm(PS[prv][:, 0:128], triRev, LF[:, jj, :], True, True, [W_('lf'), CG], [kps(prv)])
                            act(er[:], PS[prv][:, 0:128], AF.Exp, [kps(prv)], [K('er')])
                            for c in range(4):
                                dve_stt(ktc[:, c, :], KA[:, jj, :], cmask[:, c:c + 1], er[:], ALU.mult, ALU.mult, [W_('ka'), K('er'), CG], [K('ktc')])
                            yield
                            pkT = ps_mm()
                            rec.op('pe', lambda e, pkT=pkT, jj=jj, KA=KA: e.transpose(PS[pkT][:, 0:128], KA[:, jj, :], ident), [W_('ka'), CG], [kps(pkT)])
                            dve_tt(kh[:], PS[pkT][:, 0:128], em[:], ALU.mult, [kps(pkT), K('em')], [K('kh')])
                            dve_tt(qi[:], QA[:, ts], e0[:], ALU.mult, [W_('qa'), K('e0')], [K('qi')])
                            dve_tt(qh[:], QA[:, ts], ep[:], ALU.mult, [W_('qa'), K('ep')], [K('qh')])
                            yield
                            pu = 6 + s_
                            for c in range(4):
                                mm(PS[pu][:, c * 128:(c + 1) * 128], ktc[:, c, :], VA[:, jj, :], True, True, [K('ktc'), W_('va')], [kps(pu)])
                            pA = ps_mm()
                            mm(PS[pA][:, 0:128], kh[:], qh[:], True, True, [K('kh'), K('qh')], [kps(pA)])
                            dve_tt(am[:], PS[pA][:, 0:128], maskBD, ALU.mult, [kps(pA), CG], [K('am')])
                            mm(PS[po][:, ts], VA[:, jj, :], am[:], True, False, [W_('va'), K('am')], [kps(po)])
                            yield
                            for c in range(4):
                                cs = slice(c * 32, (c + 1) * 32)
                                if not st['first']:
                                    sbc = sbs[st['sb'] % 2]
                                    mm(PS[po][:, jj * 128 + c * 32:jj * 128 + (c + 1) * 32], sbc[:], qi[:, cs], False, c == 3, [('sbf', id(sbc)), K('qi')], [kps(po)])
                                if st['first']:
                                    dve_copy(sf[:], PS[pu][:, c * 128:(c + 1) * 128], [kps(pu)], [K('sf')])
                                else:
                                    dve_stt(sf[:], sf[:], e0[:, c * 32 + 31:c * 32 + 32], PS[pu][:, c * 128:(c + 1) * 128], ALU.mult, ALU.add, [K('sf'), K('e0'), kps(pu)], [K('sf')])
                                st['sb'] += 1
                                sbn = sbs[st['sb'] % 2]
                                dve_copy(sbn[:], sf[:], [K('sf')], [('sbf', id(sbn))])
                                st['first'] = False
                                yield
                        q = SQ[s_]
                        act(q[:], PS[po][:], AF.Square, [kps(po)], [('sq', id(q))])
                        pn = ps_mm()
                        mm(PS[pn][:], ones_bf[:], q[:], True, True, [('sq', id(q)), ('c', 'ones')], [kps(pn)])
                        rs = RS[s_]
                        rstd_from_ps(pn, 512, rs, [], 1.0 / 128)
                        on = REC[s_]
                        dve_stt(on[:], PS[po][:], again[:, e_ * 4 + h:e_ * 4 + h + 1], rs[:], ALU.mult, ALU.mult, [kps(po), ('rs', id(rs)), CG], [('rec', id(on))])
                        dve_tt(OTA[:, sl], on[:], GT, ALU.mult, [('rec', id(on)), W_('gt')], [W_('ota', tb)])
                        yield

                for hp in range(2):
                    heads = (2 * hp, 2 * hp + 1)
                    ws = [a_w(s_, heads[s_]) for s_ in range(2)]
                    gens = [a_head_gen(heads[s_], s_, s_, ws[s_][0]) for s_ in range(2)]
                    alive = [True, True]
                    if DBG == 2:
                        for s_ in range(2):
                            for _ in gens[s_]:
                                pass
                        alive = [False, False]
                    if KSTOP > 0:
                        for s_ in range(2):
                            for _i in range(KSTOP):
                                next(gens[s_])
                        alive = [False, False]
                    while any(alive):
                        for s_ in range(2):
                            if alive[s_]:
                                try:
                                    next(gens[s_])
                                except StopIteration:
                                    alive[s_] = False
                    OT2 = [a_views(0)['OTA'], a_views(1)['OTA']]
                    wos = [ws[0][1], ws[1][1]]
                    for tb in range(NTB):
                        for oc in range(NCH):
                            sl = slice(tb * 512, (tb + 1) * 512)
                            if DBG == 3:
                                for s_ in range(2):
                                    pb = ps_mm()
                                    mm(PS[pb][:], wos[s_][:, 0, oc * 128:(oc + 1) * 128], OT2[s_][:, tb * 512:(tb + 1) * 512], True, True,
                                       [('W', s_), ('WK', 'ota', s_, tb)], [kps(pb)])
                                    dve_tt(hT[:, oc, sl], hT[:, oc, sl], PS[pb][:], ALU.add, [('hT', oc, tb), kps(pb)], [('hT', oc, tb)])
                                continue
                            pb = ps_mm()
                            for s_ in range(2):
                                mm(PS[pb][:], wos[s_][:, 0, oc * 128:(oc + 1) * 128], OT2[s_][:, tb * 512:(tb + 1) * 512], s_ == 0, s_ == 1,
                                   [('W', s_), ('WK', 'ota', s_, tb)], [kps(pb)])
                            dve_tt(hT[:, oc, sl], hT[:, oc, sl], PS[pb][:], ALU.add, [('hT', oc, tb), kps(pb)], [('hT', oc, tb)])
                wstate['i'] = 1

            rec.fence('WK')
            QX = wk_bf(0, 4096).rearrange("p (c t) -> p c t", c=2)
            OX = wk_bf(2048, 4096).rearrange("p (c t) -> p c t", c=2)
            MT = WK[:, 4096:6144].rearrange("p (c t) -> p c t", c=8)
            MN_ = wk_bf(6144, 2048).rearrange("p (c t) -> p c t", c=8)
            KX = wk_bf(7168, 512).rearrange("p (c t) -> p c t", c=2)
            VX = wk_bf(7424, 512).rearrange("p (c t) -> p c t", c=2)
            dma('sp', [(MT[:, c, :], memT_d[b, c * 128:(c + 1) * 128, :]) for c in range(NCH)], [], [('WK', 'mt')], 'mt')
            pb = ps_acc()
            for c in range(NCH):
                q = SQ[rotate('sq', 2)]
                act(q[:, 0:256], MT[:, c, :], AF.Square, [('WK', 'mt')], [('sq', id(q))])
                mm(PS[pb][:, 0:256], ones_bf[:], q[:, 0:256], c == 0, c == NCH - 1, [('sq', id(q)), ('c', 'ones')], [kps(pb)])
            rs = RS[rotate('rs', 2)]
            rstd_from_ps(pb, 256, rs, [], 1.0 / D)
            for c in range(NCH):
                dve_stt(MN_[:, c, :], MT[:, c, :], gains[:, (8 + l) * 8 + c:(8 + l) * 8 + c + 1], rs[:, 0:256], ALU.mult, ALU.mult,
                        [('WK', 'mt'), ('rs', id(rs)), CG], [('WK', 'mn')])
            norm_to_xn(4 + l)
            nxt = wslot()
            def x_w(slot, h):
                wq = wview(slot, 0, 8, 256)
                wkv = wview(slot, 2048, 8, 512)
                wo = wview(slot, 6144, 2, 1024)
                load_w(slot, [(wq, wsrc_cols(w_xq[l], h * 256, 256)),
                              (wkv[:, :, 0:256], wsrc_cols(w_xkv[l], h * 256, 256)),
                              (wkv[:, :, 256:512], wsrc_cols(w_xkv[l], 1024 + h * 256, 256)),
                              (wo, wsrc_rows(w_xo[l], h * 256, 2))])
                return wq, wkv, wo
            wcur = x_w(nxt, 0)
            for h in range(4):
                slot = nxt
                wq, wkv, wo = wcur
                if h + 1 < 4:
                    nxt = wslot()
                    wcur = x_w(nxt, h + 1)
                for dc in range(2):
                    pb = ps_mm()
                    for kc in range(NCH):
                        mm(PS[pb][:, 0:256], wkv[:, kc, dc * 128:(dc + 1) * 128], MN_[:, kc, :], kc == 0, kc == NCH - 1, [('W', slot), ('WK', 'mn')], [kps(pb)])
                    act(KX[:, dc, :], PS[pb][:, 0:256], AF.Copy, [kps(pb)], [('WK', 'kx')], scale=1.0 / 16)
                for mt in range(2):
                    pb = ps_mm()
                    for kc in range(NCH):
                        mm(PS[pb][:, 0:256], MN_[:, kc, mt * 128:(mt + 1) * 128], wkv[:, kc, 256:512], kc == 0, kc == NCH - 1, [('W', slot), ('WK', 'mn')], [kps(pb)])
                    dve_copy(VX[:, mt, :], PS[pb][:, 0:256], [kps(pb)], [('WK', 'vx')])
                for tb in range(NTB):
                    sl = slice(tb * 512, (tb + 1) * 512)
                    for dc in range(2):
                        pb = projT(wq, dc * 128, tb, slot)
                        if dc == 0:
                            act(QX[:, dc, sl], PS[pb][:], AF.Copy, [kps(pb)], [('WK', 'qx', tb, dc)])
                        else:
                            dve_copy(QX[:, dc, sl], PS[pb][:], [kps(pb)], [('WK', 'qx', tb, dc)])
                for qb in range(NTB):
                    sl = slice(qb * 512, (qb + 1) * 512)
                    pts = []
                    for mt in range(2):
                        sb_ = ps_mm()
                        for dc in range(2):
                            mm(PS[sb_][:], KX[:, dc, mt * 128:(mt + 1) * 128], QX[:, dc, sl], dc == 0, dc == 1, [('WK', 'kx'), ('WK', 'qx', qb, dc)], [kps(sb_)])
                        pt = PT[rotate('pt', 4)]
                        act(pt[:], PS[sb_][:], AF.Exp, [kps(sb_)], [('pt', id(pt))])
                        pts.append(pt)
                    db = ps_acc()
                    for mt in range(2):
                        mm(PS[db][:], ones_bf[:], pts[mt][:], mt == 0, mt == 1, [('c', 'ones'), ('pt', id(pts[mt]))], [kps(db)])
                    rc = REC[rotate('rec', 2)]
                    act_recip(rc[:], PS[db][:], [kps(db)], [('rec', id(rc))])
                    for dvc in range(2):
                        ob = ps_acc()
                        for mt in range(2):
                            mm(PS[ob][:], VX[:, mt, dvc * 128:(dvc + 1) * 128], pts[mt][:], mt == 0, mt == 1, [('WK', 'vx'), ('pt', id(pts[mt]))], [kps(ob)])
                        dve_tt(OX[:, dvc, sl], PS[ob][:], rc[:], ALU.mult, [kps(ob), ('rec', id(rc))], [('WK', 'ox', qb, dvc)])
                outproj_acc(wo, 2, lambda kc, tb: OX[:, kc, tb * 512:(tb + 1) * 512], lambda kc, tb: [('WK', 'ox', tb, kc)], slot)

            rec.fence('WK')
            norm_to_xn(12 + l)
            nxt = wslot()
            def m_w(slot, g):
                wu = wview(slot, 0, 8, 512)
                wd = wview(slot, 4096, 4, 1024)
                load_w(slot, [(wu, wsrc_cols(w_up[l], g * 512, 512)), (wd, wsrc_rows(w_down[l], g * 512, 4))])
                return wu, wd
            wcur = m_w(nxt, 0)
            for g in range(8):
                slot = nxt
                wu, wd = wcur
                if g + 1 < 8:
                    nxt = wslot()
                    wcur = m_w(nxt, g + 1)
                hb = g % 2
                HFF = wk_bf(hb * 4096, 8192).rearrange("p (c t) -> p c t", c=4)
                for tb in range(NTB):
                    sl = slice(tb * 512, (tb + 1) * 512)
                    for fc in range(4):
                        pb = projT(wu, fc * 128, tb, slot)
                        tm = PT[rotate('pt', 4)]
                        ktm = ('pt', id(tm))
                        act(tm[:], PS[pb][:], AF.Relu, [kps(pb)], [ktm])
                        dve_tt(HFF[:, fc, sl], tm[:], tm[:], ALU.mult, [ktm], [('WK', 'hff', hb, tb, fc)], eng='pool')
                outproj_acc(wd, 4, lambda kc, tb, HFF=HFF: HFF[:, kc, tb * 512:(tb + 1) * 512], lambda kc, tb, hb=hb: [('WK', 'hff', hb, tb, kc)], slot)

        rec.fence('WK')
        OUTS = [WK[:, 0:4096].rearrange("p (c t) -> p c t", c=8), WK[:, 4096:8192].rearrange("p (c t) -> p c t", c=8)]
        out_v = outT_d[b].rearrange("(c p) t -> p c t", p=128)

        def store(tb, b=b, OUTS=OUTS, out_v=out_v):
            dma('sp', [(out_v[:, :, tb * 512:(tb + 1) * 512], OUTS[tb % 2][:])], [('WK', 'out', tb % 2, c) for c in range(NCH)],
                [('od', b, tb)], ('st', tb % 2))
        if final_norm:
            norm_h(16, lambda c, tb: OUTS[tb % 2][:, c, :], lambda c, tb: [('WK', 'out', tb % 2, c)], after_block=store)
        else:
            for tb in range(NTB):
                for c in range(NCH):
                    dve_copy(OUTS[tb % 2][:, c, :], hT[:, c, tb * 512:(tb + 1) * 512], [('hT', c, tb)], [('WK', 'out', tb % 2, c)])
                store(tb)
    rec.op('sp', None, [('od', b, tb) for b in range(nbatch) for tb in range(NTB)], [])

    rec.plan()
    sems = {}
    for e in Rec.ENGS:
        for ep_ in range(max(1, rec.nepoch[e])):
            sems[(e, ep_)] = es.enter_context(nc.semaphore(f"s_{e}_{ep_}"))
    dsems = {}
    for i, k in enumerate(rec.dma_cnt):
        dsems[k] = es.enter_context(nc.semaphore(f"d_{i}"))
    block = es.enter_context(nc.Block())

    def make_body(eng):
        ops = rec.eng_ops[eng]

        def body(e):
            for o in ops:
                for Dd in o.w_eng:
                    e.wait_ge(sems[(Dd.eng, Dd.epoch)], Dd.sigval)
                for (k, t) in o.w_dma:
                    e.wait_ge(dsems[k], t)
                if o.fn is None:
                    continue
                if o.dma:
                    for ins in o.fn(e):
                        ins.then_inc(dsems[o.dsemkey], 16)
                else:
                    ins = o.fn(e)
                    if o.signal:
                        ins.then_inc(sems[(eng, o.epoch)], 1)
        return body

    block.sync(make_body('sp'))
    block.tensor(make_body('pe'))
    block.scalar(make_body('act'))
    block.vector(make_body('dve'))
    block.gpsimd(make_body('pool'))
    es.close()
    return nc, rec


def _consts():
    c = np.zeros((128, 516), np.float32)
    i = np.arange(128)
    c[:, 0:128] = np.eye(128, dtype=np.float32)
    c[:, 128:256] = (i[:, None] <= i[None, :])
    same = (i[:, None] // 32) == (i[None, :] // 32)
    c[:, 256:384] = same & (i[:, None] <= i[None, :])
    c[:, 384:512] = same & (i[:, None] > i[None, :])
    c[:, 512:516] = (i[:, None] // 32) == np.arange(4)[None, :]
    return c


def _prep_shared(inp):
    f = lambda a: np.ascontiguousarray(np.asarray(a, dtype=np.float32))
    g = np.concatenate([f(inp['norm_mix']), f(inp['norm_xattn']), f(inp['norm_mem']), f(inp['norm_mlp']),
                        f(inp['norm_final'])[None, :]], axis=0)
    gains = np.ascontiguousarray(g.reshape(17, 8, 128).transpose(2, 0, 1).reshape(128, 136))
    again = np.ascontiguousarray(f(inp['a_out_gain']).reshape(2, 4, 128).transpose(2, 0, 1).reshape(128, 8))
    sidx = np.arange(128)[:, None]
    uidx = np.arange(640)[None, :]
    ridx = np.clip(uidx - sidx, -128, 128) + 128
    relb = np.ascontiguousarray(f(inp['b_rel_bias'])[:, :, ridx])
    selc = np.zeros((64, 16, 128), np.float32)
    for hh_ in range(16):
        selc[hh_, hh_, 64] = 1.0
        selc[32 + hh_, hh_, 96] = 1.0
    sh = dict(selc=selc.reshape(64, 2048), gains=gains, again=again, lbl=f(inp['a_lb_logits']), fbias=f(inp['c_fgate_bias']), relb=relb, consts=_consts())
    for k in ('w_in_ab', 'w_out_ab', 'w_in_c', 'w_out_c', 'w_xq', 'w_xkv', 'w_xo', 'w_up', 'w_down'):
        sh[k] = f(inp[k])
    return sh


_CACHE = {}


def kernel(**inp):
    ncores = 8
    x = np.asarray(inp['x'], dtype=np.float32)
    mem = np.asarray(inp['mem'], dtype=np.float32)
    sh = _prep_shared(inp)
    in_maps = []
    for c in range(ncores):
        m = dict(sh)
        m['xT'] = np.ascontiguousarray(x[2 * c:2 * c + 2].transpose(0, 2, 1))
        m['memT'] = np.ascontiguousarray(mem[2 * c:2 * c + 2].transpose(0, 2, 1))
        in_maps.append(m)
    if 'nc' not in _CACHE:
        _CACHE['nc'] = build()[0]
    res = run_bass_kernel_spmd(_CACHE['nc'], in_maps, core_ids=list(range(ncores)))
    out = np.empty((16, T, D), np.float32)
    for c in range(ncores):
        out[2 * c:2 * c + 2] = res.results[c]['outT'].transpose(0, 2, 1)
    return out
```

```python
import numpy as np
from contextlib import ExitStack
import concourse.bass as bass
import concourse.mybir as mybir
from concourse.bass_utils import run_bass_kernel_spmd

F32 = mybir.dt.float32
BF16 = mybir.dt.bfloat16
ALU = mybir.AluOpType
AF = mybir.ActivationFunctionType

D = 1024
T = 2048
NCH = 8
NTB = 4
NT = 16
MEM = 256
EPS = 1e-6
EPOCH = 20000
NEG = -30000.0
CH = 64
NCK = 128 // CH
import os
DBG = int(os.environ.get('KDBG', '0'))
KSTOP = int(os.environ.get('KSTOP', '0'))


class Op:
    pass


class Rec:
    ENGS = ('sp', 'pe', 'act', 'dve', 'pool')

    def __init__(s):
        s.ops = []
        s.lastw = {}
        s.rd_eng = {}
        s.rd_dma = {}
        s.eng_ops = {e: [] for e in s.ENGS}
        s.dma_cnt = {}
        s.fdeps = {}

    def fence(s, region):
        deps = set(s.fdeps.get(region, ()))
        for k in list(s.lastw):
            if k[0] == region:
                deps.add(s.lastw.pop(k))
        for k in list(s.rd_eng):
            if k[0] == region:
                deps.update(s.rd_eng.pop(k).values())
        for k in list(s.rd_dma):
            if k[0] == region:
                deps.update(s.rd_dma.pop(k))
        best = {}
        for di in deps:
            o = s.ops[di]
            kk = ('d', o.dsemkey) if o.dma else ('e', o.eng)
            if kk not in best or best[kk] < di:
                best[kk] = di
        s.fdeps[region] = set(best.values())

    def op(s, eng, fn, r=(), w=(), dma=None):
        o = Op()
        o.eng = eng
        o.fn = fn
        o.idx = len(s.ops)
        o.pos = len(s.eng_ops[eng])
        o.signal = False
        o.dma = dma is not None
        deps = set()
        raw = set()
        for k in r:
            d = s.lastw.get(k)
            if d is not None:
                deps.add(d)
                raw.add(d)
            elif k[0] in s.fdeps:
                deps.update(s.fdeps[k[0]])
                raw.update(s.fdeps[k[0]])
            if k[0] == 'ps':
                for e2, d2 in s.rd_eng.get(k, {}).items():
                    if e2 != eng:
                        deps.add(d2)
        for k in w:
            d = s.lastw.get(k)
            if d is not None:
                deps.add(d)
            elif k[0] in s.fdeps:
                deps.update(s.fdeps[k[0]])
                raw.update(s.fdeps[k[0]])
            deps.update(s.rd_eng.get(k, {}).values())
            deps.update(s.rd_dma.get(k, ()))
        for k in r:
            if o.dma:
                s.rd_dma.setdefault(k, []).append(o.idx)
            else:
                s.rd_eng.setdefault(k, {})[eng] = o.idx
        for k in w:
            s.lastw[k] = o.idx
            s.rd_eng[k] = {}
            s.rd_dma[k] = []
        o.deps = deps
        o.raw = raw
        if o.dma:
            semkey, n = dma
            s.dma_cnt[semkey] = s.dma_cnt.get(semkey, 0) + n
            o.dsemkey = semkey
            o.dtarget = 16 * s.dma_cnt[semkey]
        s.ops.append(o)
        s.eng_ops[eng].append(o)
        return o

    def plan(s):
        waited = {e: {f: -1 for f in s.ENGS} for e in s.ENGS}
        waited_dma = {e: {} for e in s.ENGS}
        for o in s.ops:
            o.w_eng = []
            o.w_dma = []
            best = {}
            bestd = {}
            for di in o.deps:
                Dd = s.ops[di]
                if Dd.dma:
                    if waited_dma[o.eng].get(Dd.dsemkey, 0) >= Dd.dtarget:
                        continue
                    if bestd.get(Dd.dsemkey, 0) < Dd.dtarget:
                        bestd[Dd.dsemkey] = Dd.dtarget
                else:
                    if Dd.eng == o.eng and (o.eng == 'pe' or di not in o.raw):
                        continue
                    if Dd.pos <= waited[o.eng][Dd.eng]:
                        continue
                    if Dd.eng not in best or best[Dd.eng].pos < Dd.pos:
                        best[Dd.eng] = Dd
            for f, Dd in best.items():
                Dd.signal = True
                waited[o.eng][f] = Dd.pos
                o.w_eng.append(Dd)
            for k, t in bestd.items():
                waited_dma[o.eng][k] = t
                o.w_dma.append((k, t))
        s.nepoch = {}
        for e in s.ENGS:
            c = 0
            for o in s.eng_ops[e]:
                if o.signal:
                    o.epoch = c // EPOCH
                    o.sigval = c % EPOCH + 1
                    c += 1
            s.nepoch[e] = (c + EPOCH - 1) // EPOCH


def build(nbatch=2, layers=(0, 1, 2, 3), final_norm=True):
    nc = bass.Bass("TRN2", target_bir_lowering=False)
    rec = Rec()

    def din(name, shape):
        return nc.dram_tensor(name, list(shape), F32, kind="ExternalInput").ap()

    xT_d = din("xT", [2, D, T])
    memT_d = din("memT", [2, D, MEM])
    gains_d = din("gains", [128, 136])
    again_d = din("again", [128, 8])
    lbl_d = din("lbl", [2, 512])
    fbias_d = din("fbias", [2, 16])
    relb_d = din("relb", [2, 8, 128, 640])
    consts_d = din("consts", [128, 516])
    selc_d = din("selc", [64, 2048])
    w_in_ab = din("w_in_ab", [2, D, 3584])
    w_out_ab = din("w_out_ab", [2, D, D])
    w_in_c = din("w_in_c", [2, D, 3088])
    w_out_c = din("w_out_c", [2, D, D])
    w_xq = din("w_xq", [4, D, D])
    w_xkv = din("w_xkv", [4, D, 2 * D])
    w_xo = din("w_xo", [4, D, D])
    w_up = din("w_up", [4, D, 4 * D])
    w_down = din("w_down", [4, 4 * D, D])
    outT_d = nc.dram_tensor("outT", [2, D, T], F32, kind="ExternalOutput").ap()

    es = ExitStack()

    def sb(name, shape, dt):
        return es.enter_context(nc.sbuf_tensor(name, list(shape), dt))

    hT = sb("hT", [128, NCH, T], F32)
    xn = sb("xn", [128, NCH, T], BF16)
    Wt = [sb("W0", [128, 8192], BF16), sb("W1", [128, 8192], BF16)]
    WK = sb("WK", [128, 8192], F32)
    gains = sb("gains_s", [128, 136], F32)
    again = sb("again_s", [128, 8], F32)
    consts = sb("consts_s", [128, 516], F32)
    tri_bf = sb("tri_bf", [128, 128], BF16)
    ones_bf = sb("ones_bf", [128, 128], BF16)
    ones_f = sb("ones_f", [128, 128], F32)
    SEL = sb("sel", [64, 16, 128], BF16)
    oml = sb("oml", [128, 512], F32)
    fbt = sb("fbt", [128, 16], F32)
    lf = sb("lf", [128, 16, 16], F32)
    cum = sb("cum", [128, 16, 16], F32)
    zf = cum
    rcar = sb("rcar", [128, 4, 16], F32)
    fbtab = sb("fbtab", [128, 4, 16, 16], F32)
    SQ = [sb(f"sq{i}", [128, 512], BF16) for i in range(2)]
    RS = [sb(f"rs{i}", [128, 512], F32) for i in range(2)]
    PT = [sb(f"pt{i}", [128, 512], BF16) for i in range(4)]
    TMP = [sb(f"tmp{i}", [128, 512], F32) for i in range(2)]
    REC = [sb(f"rec{i}", [128, 512], F32) for i in range(2)]
    BH = [WK[:, 5120:5760]]
    ER = [sb(f"er{i}", [128, 128], F32) for i in range(2)]
    E0 = [sb(f"e0{i}", [128, 128], F32) for i in range(2)]
    EP = [sb(f"ep{i}", [128, 128], F32) for i in range(2)]
    EM = [sb(f"em{i}", [128, 128], F32) for i in range(2)]
    MP = [sb(f"mp{i}", [128, 4], F32) for i in range(2)]
    MN = [sb(f"mn{i}", [128, 4], F32) for i in range(2)]
    QI = [sb(f"qi{i}", [128, 128], BF16) for i in range(2)]
    QH = [sb(f"qh{i}", [128, 128], BF16) for i in range(2)]
    KH = [sb(f"kh{i}", [128, 128], BF16) for i in range(2)]
    KTC = [sb(f"ktc{i}", [128, 4, 128], BF16) for i in range(2)]
    AM = [sb(f"am{i}", [128, 128], BF16) for i in range(2)]
    SFS = [sb(f"sf{i}", [128, 128], F32) for i in range(2)]
    EB1 = sb("eb1", [128, 640], BF16)
    SB = [sb(f"sbf{i}", [128, 128], BF16) for i in range(4)]

    PS = [es.enter_context(nc.psum_tensor(f"ps{i}", [128, 512], F32)) for i in range(8)]

    ident = consts[:, 0:128]
    tri_f = consts[:, 128:256]
    maskBD = consts[:, 256:384]
    triRev = consts[:, 384:512]
    cmask = consts[:, 512:516]

    cnt = {'mm': 0, 'acc': 0}

    def ps_mm():
        cnt['mm'] += 1
        return cnt['mm'] % 4

    def ps_acc():
        cnt['acc'] += 1
        return 4 + cnt['acc'] % 4

    rot = {}

    def rotate(name, n):
        rot[name] = rot.get(name, -1) + 1
        return rot[name] % n

    def mm(out, lhsT, rhs, start, stop, r, w):
        rec.op('pe', lambda e: e.matmul(out, lhsT, rhs, start=start, stop=stop), r, w)

    def act(out, in_, func, r, w, bias=0.0, scale=1.0):
        rec.op('act', lambda e: e.activation(out, in_, func, bias=bias, scale=scale), r, w)

    def dve_tt(out, a, b, op, r, w, eng='dve'):
        rec.op(eng, lambda e: e.tensor_tensor(out, a, b, op), r, w)

    def dve_ts(out, a, s1, s2, op0, op1, r, w, eng='dve'):
        rec.op(eng, lambda e: e.tensor_scalar(out, a, s1, s2, op0, op1), r, w)

    def dve_stt(out, a, sc, b, op0, op1, r, w):
        rec.op('dve', lambda e: e.scalar_tensor_tensor(out, a, sc, b, op0, op1), r, w)

    def dve_copy(out, a, r, w, eng='dve'):
        rec.op(eng, lambda e: e.tensor_copy(out, a), r, w)

    def dma(eng, pairs, r, w, semkey):
        pairs = list(pairs)
        rec.op(eng, lambda e: [e.dma_start(out=o, in_=i) for (o, i) in pairs], r, w, dma=(semkey, len(pairs)))

    def kps(b):
        return ('ps', b)

    dma('sp', [(gains[:], gains_d), (again[:], again_d), (consts[:], consts_d)], [], [('c', 'g')], 'c0')
    dma('pool', [(tri_bf[:], consts_d[:, 128:256])], [], [('c', 'tribf')], 'c2')
    dma('pool', [(SEL[:].rearrange("p h m -> p (h m)"), selc_d)], [], [('c', 'sel')], 'c5')
    rec.op('pool', lambda e: e.memset(ones_bf[:], 1.0), [], [('c', 'ones')])
    rec.op('pool', lambda e: e.memset(ones_f[:], 1.0), [], [('c', 'onesf')])
    CG = ('c', 'g')

    wstate = {'i': 0}

    def wslot():
        wstate['i'] += 1
        return wstate['i'] % 2

    def wview(slot, off, kc, n):
        return Wt[slot][:, off:off + kc * n].rearrange("p (kc n) -> p kc n", kc=kc)

    def wsrc_cols(wd, c0, n):
        return wd.rearrange("(kc p) n -> p kc n", p=128)[:, :, c0:c0 + n]

    def wsrc_rows(wd, r0, kc):
        return wd[r0:r0 + kc * 128, :].rearrange("(kc p) n -> p kc n", p=128)

    def load_w(slot, pairs):
        dma('pool', pairs, [], [('W', slot)], ('W', slot))

    def rstd_from_ps(psb, ncols, rs, rkeys, scale):
        act(rs[:, 0:ncols], PS[psb][:, 0:ncols], AF.Ln, rkeys + [kps(psb)], [('rs', id(rs))], bias=EPS, scale=scale)
        act(rs[:, 0:ncols], rs[:, 0:ncols], AF.Exp, [('rs', id(rs))], [('rs', id(rs))], scale=-0.5)

    def norm_h(gidx, out_fn, out_keys_fn, after_block=None):
        for tb in range(NTB):
            sl = slice(tb * 512, (tb + 1) * 512)
            pb = ps_acc()
            for c in range(NCH):
                q = SQ[rotate('sq', 2)]
                act(q[:], hT[:, c, sl], AF.Square, [('hT', c, tb)], [('sq', id(q))])
                mm(PS[pb][:], ones_bf[:], q[:], c == 0, c == NCH - 1, [('sq', id(q)), ('c', 'ones')], [kps(pb)])
            rs = RS[rotate('rs', 2)]
            rstd_from_ps(pb, 512, rs, [], 1.0 / D)
            for c in range(NCH):
                dve_stt(out_fn(c, tb), hT[:, c, sl], gains[:, gidx * 8 + c:gidx * 8 + c + 1], rs[:], ALU.mult, ALU.mult,
                        [('hT', c, tb), ('rs', id(rs)), CG], out_keys_fn(c, tb))
            if after_block is not None:
                after_block(tb)

    def norm_to_xn(gidx):
        norm_h(gidx, lambda c, tb: xn[:, c, tb * 512:(tb + 1) * 512], lambda c, tb: [('xn', tb, c)])

    def xn_keys(tb):
        return [('xn', tb, c) for c in range(NCH)]

    def projT(wv, c0, tb, slot):
        pb = ps_mm()
        for kc in range(NCH):
            mm(PS[pb][:], wv[:, kc, c0:c0 + 128], xn[:, kc, tb * 512:(tb + 1) * 512], kc == 0, kc == NCH - 1,
               [('W', slot), ('xn', tb, kc)], [kps(pb)])
        return pb

    def outproj_acc(wo, nkc, src_fn, src_keys_fn, slot):
        for tb in range(NTB):
            for oc in range(NCH):
                pb = ps_mm()
                for kc in range(nkc):
                    mm(PS[pb][:], wo[:, kc, oc * 128:(oc + 1) * 128], src_fn(kc, tb), kc == 0, kc == nkc - 1,
                       [('W', slot)] + src_keys_fn(kc, tb), [kps(pb)])
                sl = slice(tb * 512, (tb + 1) * 512)
                dve_tt(hT[:, oc, sl], hT[:, oc, sl], PS[pb][:], ALU.add, [('hT', oc, tb), kps(pb)], [('hT', oc, tb)])

    def wk_bf(off, n):
        return WK[:, off:off + n // 2].bitcast(BF16)

    def dve_recip(out, a, r, w):
        rec.op('dve', lambda e: e.reciprocal(out, a), r, w)

    def act_recip(out, a, r, w):
        act(out, a, AF.Ln, r, w)
        act(out, out, AF.Exp, w, w, scale=-1.0)

    def attn_finalize(ob, out_ap, out_keys, on_dve=False):
        rc = REC[rotate('rec', 2)]
        if on_dve:
            dve_recip(rc[0:64, :], PS[ob][64:128, :], [kps(ob)], [('rec', id(rc))])
        else:
            act_recip(rc[0:64, :], PS[ob][64:128, :], [kps(ob)], [('rec', id(rc))])
        dve_tt(out_ap, PS[ob][0:64, :], rc[0:64, :], ALU.mult, [kps(ob), ('rec', id(rc))], out_keys)

    SKEW = 3

    def run_tiles(tiles):
        pend = []
        for t in tiles + [None] * SKEW:
            if t is not None:
                if t.get('first') is not None:
                    t['first']()
                pend.append((t, t['score']()))
            if pend and (len(pend) > SKEW or t is None):
                pt_, info = pend.pop(0)
                pt_['pv'](info)
                if pt_.get('fin') is not None:
                    pt_['fin']()
        assert not pend

    for b in range(nbatch):
        for c in range(NCH):
            dma('sp', [(hT[:, c, :], xT_d[b, c * 128:(c + 1) * 128, :])], [],
                [('hT', c, tb) for tb in range(NTB)], ('ld', c))

        for l in layers:
            rec.fence('WK')
            norm_to_xn(l)
            if l % 2 == 1:
                o_ = l // 2
                Qh = [wk_bf(0, 2048), wk_bf(1024, 2048)]
                Kh = [wk_bf(2048, 2048), wk_bf(3072, 2048)]
                OTv = wk_bf(4096, 2048)
                s0 = wslot()
                wf = wview(s0, 0, 8, 16)
                load_w(s0, [(wf, wsrc_cols(w_in_c[o_], 3072, 16))])
                dma('sp', [(fbt[:], fbias_d[o_].partition_broadcast(128))], [], [('c', 'fbt')], 'c3')
                pb = ps_mm()
                for j in range(NT):
                    for kc in range(NCH):
                        mm(PS[pb][:, j * 16:(j + 1) * 16], xn[:, kc, j * 128:(j + 1) * 128], wf[:, kc, :], kc == 0, kc == NCH - 1,
                           [('W', s0), ('xn', j // 4, kc)], [kps(pb)])
                for j in range(NT):
                    dve_tt(zf[:, j, :], PS[pb][:, j * 16:(j + 1) * 16], fbt[:], ALU.add, [kps(pb), ('c', 'fbt')], [('c', 'cum')])
                zf2 = zf[:].rearrange("p j h -> p (j h)")
                lf2 = lf[:].rearrange("p j h -> p (j h)")
                act(zf2, zf2, AF.Exp, [('c', 'cum')], [('c', 'cum')], scale=-1.0)
                act(lf2, zf2, AF.Ln, [('c', 'cum')], [('c', 'lf')], bias=1.0)
                dve_ts(lf2, lf2, -1.0, None, ALU.mult, ALU.bypass, [('c', 'lf')], [('c', 'lf')])
                pc = ps_acc()
                for j in range(NT):
                    mm(PS[pc][:, j * 16:(j + 1) * 16], tri_f, lf[:, j, :], True, j == 0, [('c', 'lf'), CG], [kps(pc)])
                    for i in range(j):
                        mm(PS[pc][:, j * 16:(j + 1) * 16], ones_f[:], lf[:, i, :], False, i == j - 1, [('c', 'lf'), ('c', 'onesf')], [kps(pc)])
                dve_copy(cum[:].rearrange("p j h -> p (j h)"), PS[pc][:, 0:256], [kps(pc)], [('c', 'cum')])
                pr = ps_acc()
                for qb in range(NTB):
                    n = 4 * qb + 4
                    for i in range(n):
                        mm(PS[pr][:, qb * 16:(qb + 1) * 16], ones_f[:], lf[:, i, :], i == 0, i == n - 1, [('c', 'lf'), ('c', 'onesf')], [kps(pr)])
                dve_copy(rcar[:].rearrange("p q h -> p (q h)"), PS[pr][:, 0:64], [kps(pr)], [('c', 'rcar')])
                for qb in range(NTB):
                    for j in range(4 * qb + 4):
                        dve_tt(fbtab[:, qb, j, :], rcar[:, qb, :], cum[:, j, :], ALU.subtract, [('c', 'rcar'), ('c', 'cum')], [('c', 'fbtab')])
                cum48 = WK[:, 0:2048].rearrange("p (j c) -> p j c", j=16)
                CT = WK[:, 4096:6144]
                CHL = wk_bf(7168, 2048)
                rec.op('pool', lambda e, cum48=cum48: e.memset(cum48, 0.0), [], [('WK', 'cum48')])
                dve_copy(cum48[:, :, 0:16], cum[:], [('c', 'cum'), ('WK', 'cum48')], [('WK', 'cum48')])
                dve_copy(cum48[:, :, 32:48], cum[:], [('c', 'cum'), ('WK', 'cum48')], [('WK', 'cum48')])
                dve_copy(cum48[:, :, 64:80], cum[:], [('c', 'cum'), ('WK', 'cum48')], [('WK', 'cum48')])
                dve_copy(cum48[:, :, 96:112], cum[:], [('c', 'cum'), ('WK', 'cum48')], [('WK', 'cum48')])
                for j4 in range(4):
                    pb = ps_mm()
                    for jj in range(4):
                        j = j4 * 4 + jj
                        rec.op('pe', lambda e, pb=pb, jj=jj, j=j, cum48=cum48: e.transpose(PS[pb][:, jj * 128:(jj + 1) * 128], cum48[:, j, :], ident),
                               [('WK', 'cum48'), CG], [kps(pb)])
                    dve_copy(CT[:, j4 * 512:(j4 + 1) * 512], PS[pb][:, :], [kps(pb)], [('WK', 'ct', j4)])
                for qb in range(NTB):
                    blk = slice(qb * 512, (qb + 1) * 512)
                    tm = TMP[rotate('tmp', 2)]
                    ktm = ('tmp', TMP.index(tm))
                    dve_ts(tm[:, :], CT[:, blk], CT[:, qb * 512 + 511:qb * 512 + 512], None, ALU.subtract, ALU.bypass, [('WK', 'ct', qb)], [ktm])
                    dve_copy(CHL[:, blk], tm[:, :], [ktm], [('WK', 'chl', qb)])
                    dve_tt(CHL[32:48, blk], tm[32:48, :], CHL[32:48, blk], ALU.subtract, [ktm, ('WK', 'chl', qb)], [('WK', 'chl', qb)])
                    dve_tt(CHL[96:112, blk], tm[96:112, :], CHL[96:112, blk], ALU.subtract, [ktm, ('WK', 'chl', qb)], [('WK', 'chl', qb)])
                rec.fence('WK')
                Vaug = wk_bf(5120, 4096).rearrange("p (j h d) -> p j h d", j=16, h=2)
                rec.op('pool', lambda e, Vaug=Vaug: e.memset(Vaug[:, :, :, 64:128], 1.0), [], [('WK', 'vones')])
                for hh in range(2):
                    rec.op('pool', lambda e, t=Kh[hh]: e.memset(t[64:128, :], 0.0), [], [('WK', 'kaug', hh)])
                    rec.op('pool', lambda e, t=Kh[hh]: e.memset(t[64:65, :], 1.0), [('WK', 'kaug', hh)], [('WK', 'kaug', hh)])
                    rec.op('pool', lambda e, t=Kh[hh]: e.memset(t[96:97, :], 1.0), [('WK', 'kaug', hh)], [('WK', 'kaug', hh)])
                nxt = wslot()
                def fox_w(slot, p):
                    wv = wview(slot, 0, 8, 384)
                    wo = wview(slot, 3072, 1, 1024)
                    load_w(slot, [(wv[:, :, 0:128], wsrc_cols(w_in_c[o_], p * 128, 128)),
                                  (wv[:, :, 128:256], wsrc_cols(w_in_c[o_], 1024 + p * 128, 128)),
                                  (wv[:, :, 256:384], wsrc_cols(w_in_c[o_], 2048 + p * 128, 128)),
                                  (wo, wsrc_rows(w_out_c[o_], p * 128, 1))])
                    return wv, wo
                wcur = fox_w(nxt, 0)
                for p in range(8):
                    slot = nxt
                    wv, wo = wcur
                    if p + 1 < 8:
                        nxt = wslot()
                        wcur = fox_w(nxt, p + 1)
                    for tb in range(NTB):
                        sl = slice(tb * 512, (tb + 1) * 512)
                        pb = projT(wv, 0, tb, slot)
                        act(Qh[0][0:64, sl], PS[pb][0:64, :], AF.Copy, [kps(pb)], [('WK', 'q', 0, tb)], scale=0.125)
                        dve_ts(Qh[1][0:64, sl], PS[pb][64:128, :], 0.125, None, ALU.mult, ALU.bypass, [kps(pb)], [('WK', 'q', 1, tb)])
                        pb = projT(wv, 128, tb, slot)
                        for hh in range(2):
                            dve_copy(Kh[hh][0:64, sl], PS[pb][hh * 64:(hh + 1) * 64, :], [kps(pb)], [('WK', 'k', hh, tb)])
                        for hh in range(2):
                            pb = ps_mm()
                            mm(PS[pb][:], SEL[:, 2 * p + hh, :], CHL[0:64, sl], True, True, [('c', 'sel'), ('WK', 'chl', tb)], [kps(pb)])
                            dve_copy(Qh[hh][64:128, sl], PS[pb][64:128, :], [kps(pb)], [('WK', 'qaug', hh, tb)])
                    for j4 in range(4):
                        pb = ps_mm()
                        for jj in range(4):
                            j = j4 * 4 + jj
                            for kc in range(NCH):
                                mm(PS[pb][:, jj * 128:(jj + 1) * 128], xn[:, kc, j * 128:(j + 1) * 128], wv[:, kc, 256:384], kc == 0, kc == NCH - 1,
                                   [('W', slot), ('xn', j4, kc)], [kps(pb)])
                        act(Vaug[:, j4 * 4:(j4 + 1) * 4, :, 0:64], PS[pb][:].rearrange("p (j h d) -> p j h d", j=4, h=2), AF.Copy, [kps(pb), ('WK', 'vones')], [('WK', 'v', j4)])
                    tiles = []
                    for hh in range(2):
                        for qb in range(NTB):
                            nj = 4 * qb + 4
                            blk = {}
                            for j in range(nj):
                                def score(hh=hh, qb=qb, j=j, h=2 * p + hh):
                                    n0 = max(0, j - 4 * qb) * 128
                                    N = 512 - n0
                                    sb_ = ps_mm()
                                    mm(PS[sb_][:, 0:N], Kh[hh][:, j * 128:(j + 1) * 128], Qh[hh][:, qb * 512 + n0:(qb + 1) * 512], True, True,
                                       [('WK', 'k', hh, j // 4), ('WK', 'kaug', hh), ('WK', 'q', hh, qb), ('WK', 'qaug', hh, qb)], [kps(sb_)])
                                    pt = PT[rotate('pt', 4)]
                                    kpt = ('pt', id(pt))
                                    act(pt[:, 0:N], PS[sb_][:, 0:N], AF.Exp, [kps(sb_), ('c', 'fbtab')], [kpt], bias=fbtab[:, qb, j, h:h + 1])
                                    if j >= 4 * qb:
                                        dve_tt(pt[:, 0:128], pt[:, 0:128], tri_bf[:], ALU.mult, [kpt, ('c', 'tribf')], [kpt], eng='pool')
                                    return (n0, N, pt, kpt)

                                def pv(info, hh=hh, j=j, nj=nj, blk=blk):
                                    n0, N, pt, kpt = info
                                    mm(PS[blk['ob']][:, n0:512], Vaug[:, j, hh, :], pt[:, 0:N], j == 0, j == nj - 1, [('WK', 'v', j // 4), ('WK', 'vones'), kpt], [kps(blk['ob'])])
                                t = {'score': score, 'pv': pv}
                                if j == 0:
                                    t['first'] = lambda blk=blk: blk.__setitem__('ob', ps_acc())
                                if j == nj - 1:
                                    t['fin'] = lambda blk=blk, hh=hh, qb=qb: attn_finalize(blk['ob'], OTv[64 * hh:64 * hh + 64, qb * 512:(qb + 1) * 512], [('WK', 'ot', qb, hh)], on_dve=True)
                                tiles.append(t)
                    run_tiles(tiles)
                    outproj_acc(wo, 1, lambda kc, tb: OTv[:, tb * 512:(tb + 1) * 512], lambda kc, tb: [('WK', 'ot', tb, 0), ('WK', 'ot', tb, 1)], slot)
            else:
                e_ = l // 2
                if e_ == 0:
                    rec.op('pool', lambda e: e.memset(oml[:], 1.0), [], [('c', 'oml')])
                else:
                    dma('sp', [(TMP[0][:], lbl_d[0].partition_broadcast(128))], [], [('tmp', 0)], 'c1')
                    dma('sp', [(TMP[1][:], lbl_d[1].partition_broadcast(128))], [], [('tmp', 1)], 'c4')
                    act(TMP[0][:], TMP[0][:], AF.Exp, [('tmp', 0)], [('tmp', 0)])
                    act(TMP[1][:], TMP[1][:], AF.Exp, [('tmp', 1)], [('tmp', 1)])
                    dve_tt(TMP[0][:], TMP[0][:], TMP[1][:], ALU.add, [('tmp', 0), ('tmp', 1)], [('tmp', 0)])
                    dve_recip(TMP[0][:], TMP[0][:], [('tmp', 0)], [('tmp', 0)])
                    dve_tt(TMP[0][:], TMP[1][:], TMP[0][:], ALU.mult, [('tmp', 0), ('tmp', 1)], [('tmp', 0)])
                    dve_ts(oml[:], TMP[0][:], -1.0, 1.0, ALU.mult, ALU.add, [('tmp', 0)], [('c', 'oml')])
                Qh = [wk_bf(0, 2048), wk_bf(1024, 2048)]
                Kh = [wk_bf(2048, 2048), wk_bf(3072, 2048)]
                OTv = wk_bf(4096, 2048)
                Vaug = wk_bf(6144, 4096).rearrange("p (j h d) -> p j h d", j=16, h=2)
                EB = [wk_bf(5760, 640), EB1[:]]
                rec.op('pool', lambda e, Vaug=Vaug: e.memset(Vaug[:, :, :, 64:128], 1.0), [], [('WK', 'vones')])
                for hh in range(2):
                    rec.op('pool', lambda e, t=Kh[hh]: e.memset(t[64:128, :], 0.0), [], [('WK', 'kaug', hh)])
                    rec.op('pool', lambda e, t=Qh[hh]: e.memset(t[64:128, :], 0.0), [], [('WK', 'qaug', hh)])
                nxt = wslot()
                def b_w(slot, p):
                    wv = wview(slot, 0, 8, 384)
                    wo = wview(slot, 3072, 1, 1024)
                    load_w(slot, [(wv[:, :, 0:128], wsrc_cols(w_in_ab[e_], 2048 + p * 128, 128)),
                                  (wv[:, :, 128:256], wsrc_cols(w_in_ab[e_], 2560 + p * 128, 128)),
                                  (wv[:, :, 256:384], wsrc_cols(w_in_ab[e_], 3072 + p * 128, 128)),
                                  (wo, wsrc_rows(w_out_ab[e_], 512 + p * 128, 1))])
                    return wv, wo
                wcur = b_w(nxt, 0)
                for p in range(4):
                    slot = nxt
                    wv, wo = wcur
                    if p + 1 < 4:
                        nxt = wslot()
                        wcur = b_w(nxt, p + 1)
                    for tb in range(NTB):
                        sl = slice(tb * 512, (tb + 1) * 512)
                        pb = projT(wv, 0, tb, slot)
                        for hh in range(2):
                            act(Qh[hh][0:64, sl], PS[pb][hh * 64:(hh + 1) * 64, :], AF.Copy, [kps(pb)], [('WK', 'q', hh, tb)], scale=0.125)
                        pb = projT(wv, 128, tb, slot)
                        for hh in range(2):
                            dve_copy(Kh[hh][0:64, sl], PS[pb][hh * 64:(hh + 1) * 64, :], [kps(pb)], [('WK', 'k', hh, tb)])
                    for j4 in range(4):
                        pb = ps_mm()
                        for jj in range(4):
                            j = j4 * 4 + jj
                            for kc in range(NCH):
                                mm(PS[pb][:, jj * 128:(jj + 1) * 128], xn[:, kc, j * 128:(j + 1) * 128], wv[:, kc, 256:384], kc == 0, kc == NCH - 1,
                                   [('W', slot), ('xn', j4, kc)], [kps(pb)])
                        act(Vaug[:, j4 * 4:(j4 + 1) * 4, :, 0:64], PS[pb][:].rearrange("p (j h d) -> p j h d", j=4, h=2), AF.Copy, [kps(pb), ('WK', 'vones')], [('WK', 'v', j4)])
                    tiles = []
                    for hh in range(2):
                        h = 2 * p + hh
                        bh = BH[0]
                        kbh = ('WK', 'bh', 0)
                        eb = EB[hh]
                        keb = ('WK', 'eb', hh)
                        dma('sp', [(bh, relb_d[e_, h])], [], [kbh], ('bh', 0))
                        rec.op('pool', lambda e, bh=bh: e.memset(bh[0:64, 576:640], NEG), [kbh], [kbh])
                        rec.op('pool', lambda e, bh=bh: e.memset(bh[64:128, 0:64], NEG), [kbh], [kbh])
                        act(eb, bh, AF.Exp, [kbh], [keb])
                        for qb in range(NTB):
                            ds = [0] + [d for d in (-512, -384, -256, -128, 128, 256, 384) if 0 <= qb * 512 + d < T]
                            nd = len(ds)
                            blk = {}
                            for ii, d in enumerate(ds):
                                def score(hh=hh, qb=qb, d=d, eb=eb, keb=keb):
                                    kb = qb * 512 + d
                                    j = kb // 128
                                    lo = max(0, d)
                                    hi = 512 if d >= -128 else d + 640
                                    N = hi - lo
                                    sb_ = ps_mm()
                                    mm(PS[sb_][:, 0:N], Kh[hh][:, kb:kb + 128], Qh[hh][:, qb * 512 + lo:qb * 512 + hi], True, True,
                                       [('WK', 'k', hh, j // 4), ('WK', 'kaug', hh), ('WK', 'q', hh, qb), ('WK', 'qaug', hh)], [kps(sb_)])
                                    pt = PT[rotate('pt', 4)]
                                    kpt = ('pt', id(pt))
                                    act(pt[:, 0:N], PS[sb_][:, 0:N], AF.Exp, [kps(sb_)], [kpt])
                                    dve_tt(pt[:, 0:N], pt[:, 0:N], eb[:, lo - d:hi - d], ALU.mult, [kpt, keb], [kpt])
                                    return (j, lo, hi, N, pt, kpt)

                                def pv(info, hh=hh, ii=ii, nd=nd, blk=blk):
                                    j, lo, hi, N, pt, kpt = info
                                    mm(PS[blk['ob']][:, lo:hi], Vaug[:, j, hh, :], pt[:, 0:N], ii == 0, ii == nd - 1, [('WK', 'v', j // 4), ('WK', 'vones'), kpt], [kps(blk['ob'])])
                                t = {'score': score, 'pv': pv}
                                if ii == 0:
                                    t['first'] = lambda blk=blk: blk.__setitem__('ob', ps_acc())
                                if ii == nd - 1:
                                    t['fin'] = lambda blk=blk, hh=hh, qb=qb: attn_finalize(blk['ob'], OTv[64 * hh:64 * hh + 64, qb * 512:(qb + 1) * 512], [('WK', 'ot', qb, hh)])
                                tiles.append(t)
                    run_tiles(tiles)
                    outproj_acc(wo, 1, lambda kc, tb: OTv[:, tb * 512:(tb + 1) * 512], lambda kc, tb: [('WK', 'ot', tb, 0), ('WK', 'ot', tb, 1)], slot)

                rec.fence('WK')

                def a_views(s_):
                    o = s_ * 3072
                    return dict(QA=WK[:, o:o + 512], GT=wk_bf(o + 512, 512),
                                KA=WK[:, o + 768:o + 1280].rearrange("p (j d) -> p j d", j=4),
                                LF=WK[:, o + 1280:o + 1792].rearrange("p (j d) -> p j d", j=4),
                                VA=wk_bf(o + 1792, 512).rearrange("p (j d) -> p j d", j=4),
                                OTA=wk_bf(o + 2048, 2048))

                def a_w(slot, h):
                    wv = wview(slot, 0, 8, 512)
                    wo = wview(slot, 4096, 1, 1024)
                    load_w(slot, [(wv[:, :, i * 128:(i + 1) * 128], wsrc_cols(w_in_ab[e_], i * 512 + h * 128, 128)) for i in range(4)]
                           + [(wo, wsrc_rows(w_out_ab[e_], h * 128, 1))])
                    return wv, wo

                def a_head_gen(h, s_, slot, wv):
                    V = a_views(s_)
                    QA, GT, KA, LF, VA, OTA = V['QA'], V['GT'], V['KA'], V['LF'], V['VA'], V['OTA']
                    W_ = lambda n, *a: ('WK', n, s_) + a
                    hs = slice(h * 128, (h + 1) * 128)
                    er, e0, ep, em, mp, mn = ER[s_], E0[s_], EP[s_], EM[s_], MP[s_], MN[s_]
                    qi, qh, kh, ktc, am = QI[s_], QH[s_], KH[s_], KTC[s_], AM[s_]
                    sf = SFS[s_]
                    sbs = [SB[2 * s_], SB[2 * s_ + 1]]
                    K = lambda n: ('a', n, s_)
                    st = {'first': True, 'sb': 0}
                    for tb in range(NTB):
                        sl = slice(tb * 512, (tb + 1) * 512)
                        pb = projT(wv, 0, tb, slot)
                        sg = TMP[s_]
                        ksg = ('tmp', s_)
                        act(sg[:], PS[pb][:], AF.Sigmoid, [kps(pb)], [ksg])
                        dve_stt(QA, PS[pb][:], float(128 ** -0.5), sg[:], ALU.mult, ALU.mult, [kps(pb), ksg], [W_('qa')])
                        yield
                        pb = projT(wv, 384, tb, slot)
                        act(sg[:], PS[pb][:], AF.Sigmoid, [kps(pb)], [ksg])
                        dve_tt(GT, PS[pb][:], sg[:], ALU.mult, [kps(pb), ksg], [W_('gt')])
                        yield
                        pv = ps_mm()
                        for jj in range(4):
                            j = tb * 4 + jj
                            for kc in range(NCH):
                                mm(PS[pv][:, jj * 128:(jj + 1) * 128], xn[:, kc, j * 128:(j + 1) * 128], wv[:, kc, 256:384], kc == 0, kc == NCH - 1,
                                   [('W', slot), ('xn', tb, kc)], [kps(pv)])
                        act(VA, PS[pv][:].rearrange("p (j d) -> p j d", j=4), AF.Copy, [kps(pv)], [W_('va')])
                        yield
                        pz = ps_mm()
                        for jj in range(4):
                            j = tb * 4 + jj
                            for kc in range(NCH):
                                mm(PS[pz][:, jj * 128:(jj + 1) * 128], xn[:, kc, j * 128:(j + 1) * 128], wv[:, kc, 128:256], kc == 0, kc == NCH - 1,
                                   [('W', slot), ('xn', tb, kc)], [kps(pz)])
                        act(sg[:], PS[pz][:], AF.Sigmoid, [kps(pz)], [ksg], scale=-1.0)
                        for jj in range(4):
                            dve_tt(KA[:, jj, :], sg[:, jj * 128:(jj + 1) * 128], oml[:, hs], ALU.mult, [ksg, ('c', 'oml')], [W_('ka')])
                        act(LF, KA, AF.Ln, [W_('ka')], [W_('lf')], bias=1.0, scale=-1.0)
                        yield
                        po = 4 + s_
                        for jj in range(4):
                            ts = slice(jj * 128, (jj + 1) * 128)
                            pbT = ps_mm()
                            mm(PS[pbT][:, 0:128], LF[:, jj, :], maskBD, True, True, [W_('lf'), CG], [kps(pbT)])
                            act(e0[:], PS[pbT][:, 0:128], AF.Exp, [kps(pbT)], [K('e0')])
                            dve_copy(mp[:, 0:NCK], PS[pbT][:, 0:128].rearrange("p (c t) -> p c t", c=NCK)[:, :, CH // 2 - 1], [kps(pbT), K('e0')], [K('mp')])
                            dve_ts(mn[:, 0:NCK], mp[:, 0:NCK], -1.0, None, ALU.mult, ALU.bypass, [K('mp')], [K('mn')])
                            for c in range(NCK):
                                cs = slice(c * CH, (c + 1) * CH)
                                act(ep[:, cs], PS[pbT][:, cs], AF.Exp, [kps(pbT), K('mn')], [K('ep')], bias=mn[:, c:c + 1])
                                act(em[:, cs], PS[pbT][:, cs], AF.Exp, [kps(pbT), K('mp')], [K('em')], bias=mp[:, c:c + 1], scale=-1.0)
                            yield
                            prv = ps_mm()
                            mm(PS[prv][:, 0:128], triRev, LF[:, jj, :], True, True, [W_('lf'), CG], [kps(prv)])
                            act(er[:], PS[prv][:, 0:128], AF.Exp, [kps(prv)], [K('er')])
                            for c in range(NCK):
                                dve_stt(ktc[:, c, :], KA[:, jj, :], cmask[:, c:c + 1], er[:], ALU.mult, ALU.mult, [W_('ka'), K('er'), CG], [K('ktc')])
                            yield
                            pkT = ps_mm()
                            rec.op('pe', lambda e, pkT=pkT, jj=jj, KA=KA: e.transpose(PS[pkT][:, 0:128], KA[:, jj, :], ident), [W_('ka'), CG], [kps(pkT)])
                            dve_tt(kh[:], PS[pkT][:, 0:128], em[:], ALU.mult, [kps(pkT), K('em')], [K('kh')])
                            dve_tt(qi[:], QA[:, ts], e0[:], ALU.mult, [W_('qa'), K('e0')], [K('qi')])
                            dve_tt(qh[:], QA[:, ts], ep[:], ALU.mult, [W_('qa'), K('ep')], [K('qh')])
                            yield
                            pu = 6 + s_
                            for c in range(NCK):
                                mm(PS[pu][:, c * 128:(c + 1) * 128], ktc[:, c, :], VA[:, jj, :], True, True, [K('ktc'), W_('va')], [kps(pu)])
                            pA = ps_mm()
                            mm(PS[pA][:, 0:128], kh[:], qh[:], True, True, [K('kh'), K('qh')], [kps(pA)])
                            dve_tt(am[:], PS[pA][:, 0:128], maskBD, ALU.mult, [kps(pA), CG], [K('am')])
                            mm(PS[po][:, ts], VA[:, jj, :], am[:], True, False, [W_('va'), K('am')], [kps(po)])
                            yield
                            for c in range(NCK):
                                cs = slice(c * CH, (c + 1) * CH)
                                if not st['first']:
                                    sbc = sbs[st['sb'] % 2]
                                    mm(PS[po][:, jj * 128 + c * CH:jj * 128 + (c + 1) * CH], sbc[:], qi[:, cs], False, c == NCK - 1, [('sbf', id(sbc)), K('qi')], [kps(po)])
                                if st['first']:
                                    dve_copy(sf[:], PS[pu][:, c * 128:(c + 1) * 128], [kps(pu)], [K('sf')])
                                else:
                                    dve_stt(sf[:], sf[:], e0[:, c * CH + CH - 1:c * CH + CH], PS[pu][:, c * 128:(c + 1) * 128], ALU.mult, ALU.add, [K('sf'), K('e0'), kps(pu)], [K('sf')])
                                st['sb'] += 1
                                sbn = sbs[st['sb'] % 2]
                                dve_copy(sbn[:], sf[:], [K('sf')], [('sbf', id(sbn))])
                                st['first'] = False
                                yield
                        q = SQ[s_]
                        act(q[:], PS[po][:], AF.Square, [kps(po)], [('sq', id(q))])
                        pn = ps_mm()
                        mm(PS[pn][:], ones_bf[:], q[:], True, True, [('sq', id(q)), ('c', 'ones')], [kps(pn)])
                        rs = RS[s_]
                        rstd_from_ps(pn, 512, rs, [], 1.0 / 128)
                        on = REC[s_]
                        dve_stt(on[:], PS[po][:], again[:, e_ * 4 + h:e_ * 4 + h + 1], rs[:], ALU.mult, ALU.mult, [kps(po), ('rs', id(rs)), CG], [('rec', id(on))])
                        dve_tt(OTA[:, sl], on[:], GT, ALU.mult, [('rec', id(on)), W_('gt')], [W_('ota', tb)])
                        yield

                for hp in range(2):
                    heads = (2 * hp, 2 * hp + 1)
                    ws = [a_w(s_, heads[s_]) for s_ in range(2)]
                    gens = [a_head_gen(heads[s_], s_, s_, ws[s_][0]) for s_ in range(2)]
                    alive = [True, True]
                    if DBG == 2:
                        for s_ in range(2):
                            for _ in gens[s_]:
                                pass
                        alive = [False, False]
                    if KSTOP > 0:
                        for s_ in range(2):
                            for _i in range(KSTOP):
                                next(gens[s_])
                        alive = [False, False]
                    while any(alive):
                        for s_ in range(2):
                            if alive[s_]:
                                try:
                                    next(gens[s_])
                                except StopIteration:
                                    alive[s_] = False
                    OT2 = [a_views(0)['OTA'], a_views(1)['OTA']]
                    wos = [ws[0][1], ws[1][1]]
                    for tb in range(NTB):
                        for oc in range(NCH):
                            sl = slice(tb * 512, (tb + 1) * 512)
                            if DBG == 3:
                                for s_ in range(2):
                                    pb = ps_mm()
                                    mm(PS[pb][:], wos[s_][:, 0, oc * 128:(oc + 1) * 128], OT2[s_][:, tb * 512:(tb + 1) * 512], True, True,
                                       [('W', s_), ('WK', 'ota', s_, tb)], [kps(pb)])
                                    dve_tt(hT[:, oc, sl], hT[:, oc, sl], PS[pb][:], ALU.add, [('hT', oc, tb), kps(pb)], [('hT', oc, tb)])
                                continue
                            pb = ps_mm()
                            for s_ in range(2):
                                mm(PS[pb][:], wos[s_][:, 0, oc * 128:(oc + 1) * 128], OT2[s_][:, tb * 512:(tb + 1) * 512], s_ == 0, s_ == 1,
                                   [('W', s_), ('WK', 'ota', s_, tb)], [kps(pb)])
                            dve_tt(hT[:, oc, sl], hT[:, oc, sl], PS[pb][:], ALU.add, [('hT', oc, tb), kps(pb)], [('hT', oc, tb)])
                wstate['i'] = 1

            rec.fence('WK')
            QX = wk_bf(0, 4096).rearrange("p (c t) -> p c t", c=2)
            OX = wk_bf(2048, 4096).rearrange("p (c t) -> p c t", c=2)
            MT = WK[:, 4096:6144].rearrange("p (c t) -> p c t", c=8)
            MN_ = wk_bf(6144, 2048).rearrange("p (c t) -> p c t", c=8)
            KX = wk_bf(7168, 512).rearrange("p (c t) -> p c t", c=2)
            VX = wk_bf(7424, 512).rearrange("p (c t) -> p c t", c=2)
            dma('sp', [(MT[:, c, :], memT_d[b, c * 128:(c + 1) * 128, :]) for c in range(NCH)], [], [('WK', 'mt')], 'mt')
            pb = ps_acc()
            for c in range(NCH):
                q = SQ[rotate('sq', 2)]
                act(q[:, 0:256], MT[:, c, :], AF.Square, [('WK', 'mt')], [('sq', id(q))])
                mm(PS[pb][:, 0:256], ones_bf[:], q[:, 0:256], c == 0, c == NCH - 1, [('sq', id(q)), ('c', 'ones')], [kps(pb)])
            rs = RS[rotate('rs', 2)]
            rstd_from_ps(pb, 256, rs, [], 1.0 / D)
            for c in range(NCH):
                dve_stt(MN_[:, c, :], MT[:, c, :], gains[:, (8 + l) * 8 + c:(8 + l) * 8 + c + 1], rs[:, 0:256], ALU.mult, ALU.mult,
                        [('WK', 'mt'), ('rs', id(rs)), CG], [('WK', 'mn')])
            norm_to_xn(4 + l)
            nxt = wslot()
            def x_w(slot, h):
                wq = wview(slot, 0, 8, 256)
                wkv = wview(slot, 2048, 8, 512)
                wo = wview(slot, 6144, 2, 1024)
                load_w(slot, [(wq, wsrc_cols(w_xq[l], h * 256, 256)),
                              (wkv[:, :, 0:256], wsrc_cols(w_xkv[l], h * 256, 256)),
                              (wkv[:, :, 256:512], wsrc_cols(w_xkv[l], 1024 + h * 256, 256)),
                              (wo, wsrc_rows(w_xo[l], h * 256, 2))])
                return wq, wkv, wo
            wcur = x_w(nxt, 0)
            for h in range(4):
                slot = nxt
                wq, wkv, wo = wcur
                if h + 1 < 4:
                    nxt = wslot()
                    wcur = x_w(nxt, h + 1)
                for dc in range(2):
                    pb = ps_mm()
                    for kc in range(NCH):
                        mm(PS[pb][:, 0:256], wkv[:, kc, dc * 128:(dc + 1) * 128], MN_[:, kc, :], kc == 0, kc == NCH - 1, [('W', slot), ('WK', 'mn')], [kps(pb)])
                    act(KX[:, dc, :], PS[pb][:, 0:256], AF.Copy, [kps(pb)], [('WK', 'kx')], scale=1.0 / 16)
                for mt in range(2):
                    pb = ps_mm()
                    for kc in range(NCH):
                        mm(PS[pb][:, 0:256], MN_[:, kc, mt * 128:(mt + 1) * 128], wkv[:, kc, 256:512], kc == 0, kc == NCH - 1, [('W', slot), ('WK', 'mn')], [kps(pb)])
                    dve_copy(VX[:, mt, :], PS[pb][:, 0:256], [kps(pb)], [('WK', 'vx')])
                for tb in range(NTB):
                    sl = slice(tb * 512, (tb + 1) * 512)
                    for dc in range(2):
                        pb = projT(wq, dc * 128, tb, slot)
                        if dc == 0:
                            act(QX[:, dc, sl], PS[pb][:], AF.Copy, [kps(pb)], [('WK', 'qx', tb, dc)])
                        else:
                            dve_copy(QX[:, dc, sl], PS[pb][:], [kps(pb)], [('WK', 'qx', tb, dc)])
                for qb in range(NTB):
                    sl = slice(qb * 512, (qb + 1) * 512)
                    pts = []
                    for mt in range(2):
                        sb_ = ps_mm()
                        for dc in range(2):
                            mm(PS[sb_][:], KX[:, dc, mt * 128:(mt + 1) * 128], QX[:, dc, sl], dc == 0, dc == 1, [('WK', 'kx'), ('WK', 'qx', qb, dc)], [kps(sb_)])
                        pt = PT[rotate('pt', 4)]
                        act(pt[:], PS[sb_][:], AF.Exp, [kps(sb_)], [('pt', id(pt))])
                        pts.append(pt)
                    db = ps_acc()
                    for mt in range(2):
                        mm(PS[db][:], ones_bf[:], pts[mt][:], mt == 0, mt == 1, [('c', 'ones'), ('pt', id(pts[mt]))], [kps(db)])
                    rc = REC[rotate('rec', 2)]
                    act_recip(rc[:], PS[db][:], [kps(db)], [('rec', id(rc))])
                    for dvc in range(2):
                        ob = ps_acc()
                        for mt in range(2):
                            mm(PS[ob][:], VX[:, mt, dvc * 128:(dvc + 1) * 128], pts[mt][:], mt == 0, mt == 1, [('WK', 'vx'), ('pt', id(pts[mt]))], [kps(ob)])
                        dve_tt(OX[:, dvc, sl], PS[ob][:], rc[:], ALU.mult, [kps(ob), ('rec', id(rc))], [('WK', 'ox', qb, dvc)])
                outproj_acc(wo, 2, lambda kc, tb: OX[:, kc, tb * 512:(tb + 1) * 512], lambda kc, tb: [('WK', 'ox', tb, kc)], slot)

            rec.fence('WK')
            norm_to_xn(12 + l)
            nxt = wslot()
            def m_w(slot, g):
                wu = wview(slot, 0, 8, 512)
                wd = wview(slot, 4096, 4, 1024)
                load_w(slot, [(wu, wsrc_cols(w_up[l], g * 512, 512)), (wd, wsrc_rows(w_down[l], g * 512, 4))])
                return wu, wd
            wcur = m_w(nxt, 0)
            for g in range(8):
                slot = nxt
                wu, wd = wcur
                if g + 1 < 8:
                    nxt = wslot()
                    wcur = m_w(nxt, g + 1)
                hb = g % 2
                HFF = wk_bf(hb * 4096, 8192).rearrange("p (c t) -> p c t", c=4)
                for tb in range(NTB):
                    sl = slice(tb * 512, (tb + 1) * 512)
                    for fc in range(4):
                        pb = projT(wu, fc * 128, tb, slot)
                        tm = PT[rotate('pt', 4)]
                        ktm = ('pt', id(tm))
                        act(tm[:], PS[pb][:], AF.Relu, [kps(pb)], [ktm])
                        dve_tt(HFF[:, fc, sl], tm[:], tm[:], ALU.mult, [ktm], [('WK', 'hff', hb, tb, fc)], eng='pool')
                outproj_acc(wd, 4, lambda kc, tb, HFF=HFF: HFF[:, kc, tb * 512:(tb + 1) * 512], lambda kc, tb, hb=hb: [('WK', 'hff', hb, tb, kc)], slot)

        rec.fence('WK')
        OUTS = [WK[:, 0:4096].rearrange("p (c t) -> p c t", c=8), WK[:, 4096:8192].rearrange("p (c t) -> p c t", c=8)]
        out_v = outT_d[b].rearrange("(c p) t -> p c t", p=128)

        def store(tb, b=b, OUTS=OUTS, out_v=out_v):
            dma('sp', [(out_v[:, :, tb * 512:(tb + 1) * 512], OUTS[tb % 2][:])], [('WK', 'out', tb % 2, c) for c in range(NCH)],
                [('od', b, tb)], ('st', tb % 2))
        if final_norm:
            norm_h(16, lambda c, tb: OUTS[tb % 2][:, c, :], lambda c, tb: [('WK', 'out', tb % 2, c)], after_block=store)
        else:
            for tb in range(NTB):
                for c in range(NCH):
                    dve_copy(OUTS[tb % 2][:, c, :], hT[:, c, tb * 512:(tb + 1) * 512], [('hT', c, tb)], [('WK', 'out', tb % 2, c)])
                store(tb)
    rec.op('sp', None, [('od', b, tb) for b in range(nbatch) for tb in range(NTB)], [])

    rec.plan()
    sems = {}
    for e in Rec.ENGS:
        for ep_ in range(max(1, rec.nepoch[e])):
            sems[(e, ep_)] = es.enter_context(nc.semaphore(f"s_{e}_{ep_}"))
    dsems = {}
    for i, k in enumerate(rec.dma_cnt):
        dsems[k] = es.enter_context(nc.semaphore(f"d_{i}"))
    block = es.enter_context(nc.Block())

    def make_body(eng):
        ops = rec.eng_ops[eng]

        def body(e):
            for o in ops:
                for Dd in o.w_eng:
                    e.wait_ge(sems[(Dd.eng, Dd.epoch)], Dd.sigval)
                for (k, t) in o.w_dma:
                    e.wait_ge(dsems[k], t)
                if o.fn is None:
                    continue
                if o.dma:
                    for ins in o.fn(e):
                        ins.then_inc(dsems[o.dsemkey], 16)
                else:
                    ins = o.fn(e)
                    if o.signal:
                        ins.then_inc(sems[(eng, o.epoch)], 1)
        return body

    block.sync(make_body('sp'))
    block.tensor(make_body('pe'))
    block.scalar(make_body('act'))
    block.vector(make_body('dve'))
    block.gpsimd(make_body('pool'))
    es.close()
    return nc, rec


def _consts():
    c = np.zeros((128, 516), np.float32)
    i = np.arange(128)
    c[:, 0:128] = np.eye(128, dtype=np.float32)
    c[:, 128:256] = (i[:, None] <= i[None, :])
    same = (i[:, None] // CH) == (i[None, :] // CH)
    c[:, 256:384] = same & (i[:, None] <= i[None, :])
    c[:, 384:512] = same & (i[:, None] > i[None, :])
    c[:, 512:516] = (i[:, None] // CH) == np.arange(4)[None, :]
    return c


def _prep_shared(inp):
    f = lambda a: np.ascontiguousarray(np.asarray(a, dtype=np.float32))
    g = np.concatenate([f(inp['norm_mix']), f(inp['norm_xattn']), f(inp['norm_mem']), f(inp['norm_mlp']),
                        f(inp['norm_final'])[None, :]], axis=0)
    gains = np.ascontiguousarray(g.reshape(17, 8, 128).transpose(2, 0, 1).reshape(128, 136))
    again = np.ascontiguousarray(f(inp['a_out_gain']).reshape(2, 4, 128).transpose(2, 0, 1).reshape(128, 8))
    sidx = np.arange(128)[:, None]
    uidx = np.arange(640)[None, :]
    ridx = np.clip(uidx - sidx, -128, 128) + 128
    relb = np.ascontiguousarray(f(inp['b_rel_bias'])[:, :, ridx])
    selc = np.zeros((64, 16, 128), np.float32)
    for hh_ in range(16):
        selc[hh_, hh_, 64] = 1.0
        selc[32 + hh_, hh_, 96] = 1.0
    sh = dict(selc=selc.reshape(64, 2048), gains=gains, again=again, lbl=f(inp['a_lb_logits']), fbias=f(inp['c_fgate_bias']), relb=relb, consts=_consts())
    for k in ('w_in_ab', 'w_out_ab', 'w_in_c', 'w_out_c', 'w_xq', 'w_xkv', 'w_xo', 'w_up', 'w_down'):
        sh[k] = f(inp[k])
    return sh


_CACHE = {}


def kernel(**inp):
    ncores = 8
    x = np.asarray(inp['x'], dtype=np.float32)
    mem = np.asarray(inp['mem'], dtype=np.float32)
    sh = _prep_shared(inp)
    in_maps = []
    for c in range(ncores):
        m = dict(sh)
        m['xT'] = np.ascontiguousarray(x[2 * c:2 * c + 2].transpose(0, 2, 1))
        m['memT'] = np.ascontiguousarray(mem[2 * c:2 * c + 2].transpose(0, 2, 1))
        in_maps.append(m)
    if 'nc' not in _CACHE:
        _CACHE['nc'] = build()[0]
    res = run_bass_kernel_spmd(_CACHE['nc'], in_maps, core_ids=list(range(ncores)))
    out = np.empty((16, T, D), np.float32)
    for c in range(ncores):
        out[2 * c:2 * c + 2] = res.results[c]['outT'].transpose(0, 2, 1)
    return out
```

```python
import numpy as np
from contextlib import ExitStack
import concourse.bass as bass
import concourse.mybir as mybir
from concourse.bass_utils import run_bass_kernel_spmd

F32 = mybir.dt.float32
BF16 = mybir.dt.bfloat16
ALU = mybir.AluOpType
AF = mybir.ActivationFunctionType

D = 1024
T = 2048
NCH = 8
NTB = 4
NT = 16
MEM = 256
EPS = 1e-6
EPOCH = 20000
NEG = -30000.0
CH = 64
NCK = 128 // CH
import os
DBG = int(os.environ.get('KDBG', '0'))
KSTOP = int(os.environ.get('KSTOP', '0'))


class Op:
    pass


class Rec:
    ENGS = ('sp', 'pe', 'act', 'dve', 'pool')

    def __init__(s):
        s.ops = []
        s.lastw = {}
        s.rd_eng = {}
        s.rd_dma = {}
        s.eng_ops = {e: [] for e in s.ENGS}
        s.dma_cnt = {}
        s.fdeps = {}

    def fence(s, region):
        deps = set(s.fdeps.get(region, ()))
        for k in list(s.lastw):
            if k[0] == region:
                deps.add(s.lastw.pop(k))
        for k in list(s.rd_eng):
            if k[0] == region:
                deps.update(s.rd_eng.pop(k).values())
        for k in list(s.rd_dma):
            if k[0] == region:
                deps.update(s.rd_dma.pop(k))
        best = {}
        for di in deps:
            o = s.ops[di]
            kk = ('d', o.dsemkey) if o.dma else ('e', o.eng)
            if kk not in best or best[kk] < di:
                best[kk] = di
        s.fdeps[region] = set(best.values())

    def op(s, eng, fn, r=(), w=(), dma=None):
        o = Op()
        o.eng = eng
        o.fn = fn
        o.idx = len(s.ops)
        o.pos = len(s.eng_ops[eng])
        o.signal = False
        o.dma = dma is not None
        deps = set()
        raw = set()
        for k in r:
            d = s.lastw.get(k)
            if d is not None:
                deps.add(d)
                raw.add(d)
            elif k[0] in s.fdeps:
                deps.update(s.fdeps[k[0]])
                raw.update(s.fdeps[k[0]])
            if k[0] == 'ps':
                for e2, d2 in s.rd_eng.get(k, {}).items():
                    if e2 != eng:
                        deps.add(d2)
        for k in w:
            d = s.lastw.get(k)
            if d is not None:
                deps.add(d)
            elif k[0] in s.fdeps:
                deps.update(s.fdeps[k[0]])
                raw.update(s.fdeps[k[0]])
            deps.update(s.rd_eng.get(k, {}).values())
            deps.update(s.rd_dma.get(k, ()))
        for k in r:
            if o.dma:
                s.rd_dma.setdefault(k, []).append(o.idx)
            else:
                s.rd_eng.setdefault(k, {})[eng] = o.idx
        for k in w:
            s.lastw[k] = o.idx
            s.rd_eng[k] = {}
            s.rd_dma[k] = []
        o.deps = deps
        o.raw = raw
        if o.dma:
            semkey, n = dma
            s.dma_cnt[semkey] = s.dma_cnt.get(semkey, 0) + n
            o.dsemkey = semkey
            o.dtarget = 16 * s.dma_cnt[semkey]
        s.ops.append(o)
        s.eng_ops[eng].append(o)
        return o

    def plan(s):
        waited = {e: {f: -1 for f in s.ENGS} for e in s.ENGS}
        waited_dma = {e: {} for e in s.ENGS}
        for o in s.ops:
            o.w_eng = []
            o.w_dma = []
            best = {}
            bestd = {}
            for di in o.deps:
                Dd = s.ops[di]
                if Dd.dma:
                    if waited_dma[o.eng].get(Dd.dsemkey, 0) >= Dd.dtarget:
                        continue
                    if bestd.get(Dd.dsemkey, 0) < Dd.dtarget:
                        bestd[Dd.dsemkey] = Dd.dtarget
                else:
                    if Dd.eng == o.eng and (o.eng == 'pe' or di not in o.raw):
                        continue
                    if Dd.pos <= waited[o.eng][Dd.eng]:
                        continue
                    if Dd.eng not in best or best[Dd.eng].pos < Dd.pos:
                        best[Dd.eng] = Dd
            for f, Dd in best.items():
                Dd.signal = True
                waited[o.eng][f] = Dd.pos
                o.w_eng.append(Dd)
            for k, t in bestd.items():
                waited_dma[o.eng][k] = t
                o.w_dma.append((k, t))
        s.nepoch = {}
        for e in s.ENGS:
            c = 0
            for o in s.eng_ops[e]:
                if o.signal:
                    o.epoch = c // EPOCH
                    o.sigval = c % EPOCH + 1
                    c += 1
            s.nepoch[e] = (c + EPOCH - 1) // EPOCH


def build(nbatch=2, layers=(0, 1, 2, 3), final_norm=True):
    nc = bass.Bass("TRN2", target_bir_lowering=False)
    rec = Rec()

    def din(name, shape):
        return nc.dram_tensor(name, list(shape), F32, kind="ExternalInput").ap()

    xT_d = din("xT", [2, D, T])
    memT_d = din("memT", [2, D, MEM])
    gains_d = din("gains", [128, 136])
    again_d = din("again", [128, 8])
    lbl_d = din("lbl", [2, 512])
    fbias_d = din("fbias", [2, 16])
    relb_d = din("relb", [2, 8, 128, 640])
    consts_d = din("consts", [128, 516])
    selc_d = din("selc", [64, 2048])
    w_in_ab = din("w_in_ab", [2, D, 3584])
    w_out_ab = din("w_out_ab", [2, D, D])
    w_in_c = din("w_in_c", [2, D, 3088])
    w_out_c = din("w_out_c", [2, D, D])
    w_xq = din("w_xq", [4, D, D])
    w_xkv = din("w_xkv", [4, D, 2 * D])
    w_xo = din("w_xo", [4, D, D])
    w_up = din("w_up", [4, D, 4 * D])
    w_down = din("w_down", [4, 4 * D, D])
    outT_d = nc.dram_tensor("outT", [2, D, T], F32, kind="ExternalOutput").ap()

    es = ExitStack()

    def sb(name, shape, dt):
        return es.enter_context(nc.sbuf_tensor(name, list(shape), dt))

    hT = sb("hT", [128, NCH, T], F32)
    xn = sb("xn", [128, NCH, T], BF16)
    Wt = [sb("W0", [128, 8192], BF16), sb("W1", [128, 8192], BF16)]
    WK = sb("WK", [128, 8192], F32)
    gains = sb("gains_s", [128, 136], F32)
    again = sb("again_s", [128, 8], F32)
    consts = sb("consts_s", [128, 516], F32)
    tri_bf = sb("tri_bf", [128, 128], BF16)
    ones_bf = sb("ones_bf", [128, 128], BF16)
    ones_f = sb("ones_f", [128, 128], F32)
    SEL = sb("sel", [64, 16, 128], BF16)
    oml = sb("oml", [128, 512], F32)
    fbt = sb("fbt", [128, 16], F32)
    lf = sb("lf", [128, 16, 16], F32)
    cum = sb("cum", [128, 16, 16], F32)
    zf = cum
    rcar = sb("rcar", [128, 4, 16], F32)
    fbtab = sb("fbtab", [128, 4, 16, 16], F32)
    SQ = [sb(f"sq{i}", [128, 512], BF16) for i in range(2)]
    RS = [sb(f"rs{i}", [128, 512], F32) for i in range(2)]
    PT = [sb(f"pt{i}", [128, 512], BF16) for i in range(4)]
    TMP = [sb(f"tmp{i}", [128, 512], F32) for i in range(2)]
    REC = [sb(f"rec{i}", [128, 512], F32) for i in range(2)]
    BH = [WK[:, 5120:5760]]
    ER = [sb(f"er{i}", [128, 128], F32) for i in range(2)]
    E0 = [sb(f"e0{i}", [128, 128], F32) for i in range(2)]
    EP = [sb(f"ep{i}", [128, 128], F32) for i in range(2)]
    EM = [sb(f"em{i}", [128, 128], F32) for i in range(2)]
    MP = [sb(f"mp{i}", [128, 4], F32) for i in range(2)]
    MN = [sb(f"mn{i}", [128, 4], F32) for i in range(2)]
    QI = [sb(f"qi{i}", [128, 128], BF16) for i in range(2)]
    QH = [sb(f"qh{i}", [128, 128], BF16) for i in range(2)]
    KH = [sb(f"kh{i}", [128, 128], BF16) for i in range(2)]
    KTC = [sb(f"ktc{i}", [128, 4, 128], BF16) for i in range(2)]
    AM = [sb(f"am{i}", [128, 128], BF16) for i in range(2)]
    SFS = [sb(f"sf{i}", [128, 128], F32) for i in range(2)]
    EB1 = sb("eb1", [128, 640], BF16)
    SB = [sb(f"sbf{i}", [128, 128], BF16) for i in range(4)]

    PS = [es.enter_context(nc.psum_tensor(f"ps{i}", [128, 512], F32)) for i in range(8)]

    ident = consts[:, 0:128]
    tri_f = consts[:, 128:256]
    maskBD = consts[:, 256:384]
    triRev = consts[:, 384:512]
    cmask = consts[:, 512:516]

    cnt = {'mm': 0, 'acc': 0}

    def ps_mm():
        cnt['mm'] += 1
        return cnt['mm'] % 4

    def ps_acc():
        cnt['acc'] += 1
        return 4 + cnt['acc'] % 4

    rot = {}

    def rotate(name, n):
        rot[name] = rot.get(name, -1) + 1
        return rot[name] % n

    def mm(out, lhsT, rhs, start, stop, r, w):
        rec.op('pe', lambda e: e.matmul(out, lhsT, rhs, start=start, stop=stop), r, w)

    def act(out, in_, func, r, w, bias=0.0, scale=1.0):
        rec.op('act', lambda e: e.activation(out, in_, func, bias=bias, scale=scale), r, w)

    def dve_tt(out, a, b, op, r, w, eng='dve'):
        rec.op(eng, lambda e: e.tensor_tensor(out, a, b, op), r, w)

    def dve_ts(out, a, s1, s2, op0, op1, r, w, eng='dve'):
        rec.op(eng, lambda e: e.tensor_scalar(out, a, s1, s2, op0, op1), r, w)

    def dve_stt(out, a, sc, b, op0, op1, r, w):
        rec.op('dve', lambda e: e.scalar_tensor_tensor(out, a, sc, b, op0, op1), r, w)

    def dve_copy(out, a, r, w, eng='dve'):
        rec.op(eng, lambda e: e.tensor_copy(out, a), r, w)

    def dma(eng, pairs, r, w, semkey):
        pairs = list(pairs)
        rec.op(eng, lambda e: [e.dma_start(out=o, in_=i) for (o, i) in pairs], r, w, dma=(semkey, len(pairs)))

    def kps(b):
        return ('ps', b)

    dma('sp', [(gains[:], gains_d), (again[:], again_d), (consts[:], consts_d)], [], [('c', 'g')], 'c0')
    dma('pool', [(tri_bf[:], consts_d[:, 128:256])], [], [('c', 'tribf')], 'c2')
    dma('pool', [(SEL[:].rearrange("p h m -> p (h m)"), selc_d)], [], [('c', 'sel')], 'c5')
    rec.op('pool', lambda e: e.memset(ones_bf[:], 1.0), [], [('c', 'ones')])
    rec.op('pool', lambda e: e.memset(ones_f[:], 1.0), [], [('c', 'onesf')])
    CG = ('c', 'g')

    wstate = {'i': 0}

    def wslot():
        wstate['i'] += 1
        return wstate['i'] % 2

    def wview(slot, off, kc, n):
        return Wt[slot][:, off:off + kc * n].rearrange("p (kc n) -> p kc n", kc=kc)

    def wsrc_cols(wd, c0, n):
        return wd.rearrange("(kc p) n -> p kc n", p=128)[:, :, c0:c0 + n]

    def wsrc_rows(wd, r0, kc):
        return wd[r0:r0 + kc * 128, :].rearrange("(kc p) n -> p kc n", p=128)

    def load_w(slot, pairs):
        dma('pool', pairs, [], [('W', slot)], ('W', slot))

    def rstd_from_ps(psb, ncols, rs, rkeys, scale):
        act(rs[:, 0:ncols], PS[psb][:, 0:ncols], AF.Ln, rkeys + [kps(psb)], [('rs', id(rs))], bias=EPS, scale=scale)
        act(rs[:, 0:ncols], rs[:, 0:ncols], AF.Exp, [('rs', id(rs))], [('rs', id(rs))], scale=-0.5)

    def norm_h(gidx, out_fn, out_keys_fn, after_block=None):
        for tb in range(NTB):
            sl = slice(tb * 512, (tb + 1) * 512)
            pb = ps_acc()
            for c in range(NCH):
                q = SQ[rotate('sq', 2)]
                act(q[:], hT[:, c, sl], AF.Square, [('hT', c, tb)], [('sq', id(q))])
                mm(PS[pb][:], ones_bf[:], q[:], c == 0, c == NCH - 1, [('sq', id(q)), ('c', 'ones')], [kps(pb)])
            rs = RS[rotate('rs', 2)]
            rstd_from_ps(pb, 512, rs, [], 1.0 / D)
            for c in range(NCH):
                dve_stt(out_fn(c, tb), hT[:, c, sl], gains[:, gidx * 8 + c:gidx * 8 + c + 1], rs[:], ALU.mult, ALU.mult,
                        [('hT', c, tb), ('rs', id(rs)), CG], out_keys_fn(c, tb))
            if after_block is not None:
                after_block(tb)

    def norm_to_xn(gidx):
        norm_h(gidx, lambda c, tb: xn[:, c, tb * 512:(tb + 1) * 512], lambda c, tb: [('xn', tb, c)])

    def xn_keys(tb):
        return [('xn', tb, c) for c in range(NCH)]

    def projT(wv, c0, tb, slot):
        pb = ps_mm()
        for kc in range(NCH):
            mm(PS[pb][:], wv[:, kc, c0:c0 + 128], xn[:, kc, tb * 512:(tb + 1) * 512], kc == 0, kc == NCH - 1,
               [('W', slot), ('xn', tb, kc)], [kps(pb)])
        return pb

    def outproj_acc(wo, nkc, src_fn, src_keys_fn, slot):
        for tb in range(NTB):
            for oc in range(NCH):
                pb = ps_mm()
                for kc in range(nkc):
                    mm(PS[pb][:], wo[:, kc, oc * 128:(oc + 1) * 128], src_fn(kc, tb), kc == 0, kc == nkc - 1,
                       [('W', slot)] + src_keys_fn(kc, tb), [kps(pb)])
                sl = slice(tb * 512, (tb + 1) * 512)
                dve_tt(hT[:, oc, sl], hT[:, oc, sl], PS[pb][:], ALU.add, [('hT', oc, tb), kps(pb)], [('hT', oc, tb)])

    def wk_bf(off, n):
        return WK[:, off:off + n // 2].bitcast(BF16)

    def dve_recip(out, a, r, w):
        rec.op('dve', lambda e: e.reciprocal(out, a), r, w)

    def act_recip(out, a, r, w):
        act(out, a, AF.Ln, r, w)
        act(out, out, AF.Exp, w, w, scale=-1.0)

    def attn_finalize(ob, out_ap, out_keys, on_dve=False):
        rc = REC[rotate('rec', 2)]
        if on_dve:
            dve_recip(rc[0:64, :], PS[ob][64:128, :], [kps(ob)], [('rec', id(rc))])
        else:
            act_recip(rc[0:64, :], PS[ob][64:128, :], [kps(ob)], [('rec', id(rc))])
        dve_tt(out_ap, PS[ob][0:64, :], rc[0:64, :], ALU.mult, [kps(ob), ('rec', id(rc))], out_keys)

    SKEW = 3

    class Fill:
        def __init__(self):
            self.p = {}

        def add(self, tb, fn):
            self.p.setdefault(tb, []).append(fn)

        def one(self):
            for tb in sorted(self.p):
                if self.p[tb]:
                    self.p[tb].pop(0)()
                    return

        def flush(self, tb=None):
            for t in sorted(self.p):
                if tb is None or t == tb:
                    while self.p[t]:
                        self.p[t].pop(0)()

    def outproj_fill(fill, wos, srcs_fn, keys_fn, slots):
        for tb in range(NTB):
            for oc in range(NCH):
                def f(tb=tb, oc=oc):
                    pb = ps_mm()
                    n = len(wos)
                    for i in range(n):
                        mm(PS[pb][:], wos[i][:, 0, oc * 128:(oc + 1) * 128], srcs_fn(i, tb), i == 0, i == n - 1,
                           [('W', slots[i])] + keys_fn(i, tb), [kps(pb)])
                    sl = slice(tb * 512, (tb + 1) * 512)
                    dve_tt(hT[:, oc, sl], hT[:, oc, sl], PS[pb][:], ALU.add, [('hT', oc, tb), kps(pb)], [('hT', oc, tb)])
                fill.add(tb, f)

    def run_tiles(tiles, fill=None):
        pend = []
        for t in tiles + [None] * SKEW:
            if t is not None:
                if t.get('first') is not None:
                    t['first']()
                pend.append((t, t['score']()))
            if pend and (len(pend) > SKEW or t is None):
                pt_, info = pend.pop(0)
                pt_['pv'](info)
                if pt_.get('fin') is not None:
                    if fill is not None:
                        fill.flush(pt_['tb'])
                    pt_['fin']()
                elif fill is not None:
                    fill.one()
        assert not pend

    for b in range(nbatch):
        for c in range(NCH):
            dma('sp', [(hT[:, c, :], xT_d[b, c * 128:(c + 1) * 128, :])], [],
                [('hT', c, tb) for tb in range(NTB)], ('ld', c))

        for l in layers:
            rec.fence('WK')
            norm_to_xn(l)
            if l % 2 == 1:
                o_ = l // 2
                Qh = [wk_bf(0, 2048), wk_bf(1024, 2048)]
                Kh = [wk_bf(2048, 2048), wk_bf(3072, 2048)]
                OTv = wk_bf(4096, 2048)
                s0 = wslot()
                wf = wview(s0, 0, 8, 16)
                load_w(s0, [(wf, wsrc_cols(w_in_c[o_], 3072, 16))])
                dma('sp', [(fbt[:], fbias_d[o_].partition_broadcast(128))], [], [('c', 'fbt')], 'c3')
                pb = ps_mm()
                for j in range(NT):
                    for kc in range(NCH):
                        mm(PS[pb][:, j * 16:(j + 1) * 16], xn[:, kc, j * 128:(j + 1) * 128], wf[:, kc, :], kc == 0, kc == NCH - 1,
                           [('W', s0), ('xn', j // 4, kc)], [kps(pb)])
                for j in range(NT):
                    dve_tt(zf[:, j, :], PS[pb][:, j * 16:(j + 1) * 16], fbt[:], ALU.add, [kps(pb), ('c', 'fbt')], [('c', 'cum')])
                zf2 = zf[:].rearrange("p j h -> p (j h)")
                lf2 = lf[:].rearrange("p j h -> p (j h)")
                act(zf2, zf2, AF.Exp, [('c', 'cum')], [('c', 'cum')], scale=-1.0)
                act(lf2, zf2, AF.Ln, [('c', 'cum')], [('c', 'lf')], bias=1.0)
                dve_ts(lf2, lf2, -1.0, None, ALU.mult, ALU.bypass, [('c', 'lf')], [('c', 'lf')])
                pc = ps_acc()
                for j in range(NT):
                    mm(PS[pc][:, j * 16:(j + 1) * 16], tri_f, lf[:, j, :], True, j == 0, [('c', 'lf'), CG], [kps(pc)])
                    for i in range(j):
                        mm(PS[pc][:, j * 16:(j + 1) * 16], ones_f[:], lf[:, i, :], False, i == j - 1, [('c', 'lf'), ('c', 'onesf')], [kps(pc)])
                dve_copy(cum[:].rearrange("p j h -> p (j h)"), PS[pc][:, 0:256], [kps(pc)], [('c', 'cum')])
                pr = ps_acc()
                for qb in range(NTB):
                    n = 4 * qb + 4
                    for i in range(n):
                        mm(PS[pr][:, qb * 16:(qb + 1) * 16], ones_f[:], lf[:, i, :], i == 0, i == n - 1, [('c', 'lf'), ('c', 'onesf')], [kps(pr)])
                dve_copy(rcar[:].rearrange("p q h -> p (q h)"), PS[pr][:, 0:64], [kps(pr)], [('c', 'rcar')])
                for qb in range(NTB):
                    for j in range(4 * qb + 4):
                        dve_tt(fbtab[:, qb, j, :], rcar[:, qb, :], cum[:, j, :], ALU.subtract, [('c', 'rcar'), ('c', 'cum')], [('c', 'fbtab')])
                cum48 = WK[:, 0:2048].rearrange("p (j c) -> p j c", j=16)
                CT = WK[:, 4096:6144]
                CHL = wk_bf(7168, 2048)
                rec.op('pool', lambda e, cum48=cum48: e.memset(cum48, 0.0), [], [('WK', 'cum48')])
                dve_copy(cum48[:, :, 0:16], cum[:], [('c', 'cum'), ('WK', 'cum48')], [('WK', 'cum48')])
                dve_copy(cum48[:, :, 32:48], cum[:], [('c', 'cum'), ('WK', 'cum48')], [('WK', 'cum48')])
                dve_copy(cum48[:, :, 64:80], cum[:], [('c', 'cum'), ('WK', 'cum48')], [('WK', 'cum48')])
                dve_copy(cum48[:, :, 96:112], cum[:], [('c', 'cum'), ('WK', 'cum48')], [('WK', 'cum48')])
                for j4 in range(4):
                    pb = ps_mm()
                    for jj in range(4):
                        j = j4 * 4 + jj
                        rec.op('pe', lambda e, pb=pb, jj=jj, j=j, cum48=cum48: e.transpose(PS[pb][:, jj * 128:(jj + 1) * 128], cum48[:, j, :], ident),
                               [('WK', 'cum48'), CG], [kps(pb)])
                    dve_copy(CT[:, j4 * 512:(j4 + 1) * 512], PS[pb][:, :], [kps(pb)], [('WK', 'ct', j4)])
                for qb in range(NTB):
                    blk = slice(qb * 512, (qb + 1) * 512)
                    tm = TMP[rotate('tmp', 2)]
                    ktm = ('tmp', TMP.index(tm))
                    dve_ts(tm[:, :], CT[:, blk], CT[:, qb * 512 + 511:qb * 512 + 512], None, ALU.subtract, ALU.bypass, [('WK', 'ct', qb)], [ktm])
                    dve_copy(CHL[:, blk], tm[:, :], [ktm], [('WK', 'chl', qb)])
                    dve_tt(CHL[32:48, blk], tm[32:48, :], CHL[32:48, blk], ALU.subtract, [ktm, ('WK', 'chl', qb)], [('WK', 'chl', qb)])
                    dve_tt(CHL[96:112, blk], tm[96:112, :], CHL[96:112, blk], ALU.subtract, [ktm, ('WK', 'chl', qb)], [('WK', 'chl', qb)])
                rec.fence('WK')
                Vaug = wk_bf(5120, 4096).rearrange("p (j h d) -> p j h d", j=16, h=2)
                rec.op('pool', lambda e, Vaug=Vaug: e.memset(Vaug[:, :, :, 64:128], 1.0), [], [('WK', 'vones')])
                for hh in range(2):
                    rec.op('pool', lambda e, t=Kh[hh]: e.memset(t[64:128, :], 0.0), [], [('WK', 'kaug', hh)])
                    rec.op('pool', lambda e, t=Kh[hh]: e.memset(t[64:65, :], 1.0), [('WK', 'kaug', hh)], [('WK', 'kaug', hh)])
                    rec.op('pool', lambda e, t=Kh[hh]: e.memset(t[96:97, :], 1.0), [('WK', 'kaug', hh)], [('WK', 'kaug', hh)])
                nxt = wslot()
                def fox_w(slot, p):
                    par = (p // 2) % 2
                    wv = wview(slot, 0, 8, 384)
                    wo = wview(slot, 3072 + 1024 * par, 1, 1024)
                    dma('pool', [(wv[:, :, 0:128], wsrc_cols(w_in_c[o_], p * 128, 128)),
                                 (wv[:, :, 128:256], wsrc_cols(w_in_c[o_], 1024 + p * 128, 128)),
                                 (wv[:, :, 256:384], wsrc_cols(w_in_c[o_], 2048 + p * 128, 128)),
                                 (wo, wsrc_rows(w_out_c[o_], p * 128, 1))], [], [('W', (slot, 'v')), ('W', (slot, 'o', par))], ('W', slot))
                    return wv, wo, (slot, 'v'), (slot, 'o', par)
                rec.fence('W')
                wcur = fox_w(nxt, 0)
                fillF = Fill()
                for p in range(8):
                    wv, wo, slot, okey = wcur
                    if p + 1 < 8:
                        nxt = wslot()
                        wcur = fox_w(nxt, p + 1)
                    for tb in range(NTB):
                        sl = slice(tb * 512, (tb + 1) * 512)
                        pb = projT(wv, 0, tb, slot)
                        act(Qh[0][0:64, sl], PS[pb][0:64, :], AF.Copy, [kps(pb)], [('WK', 'q', 0, tb)], scale=0.125)
                        dve_ts(Qh[1][0:64, sl], PS[pb][64:128, :], 0.125, None, ALU.mult, ALU.bypass, [kps(pb)], [('WK', 'q', 1, tb)])
                        pb = projT(wv, 128, tb, slot)
                        for hh in range(2):
                            dve_copy(Kh[hh][0:64, sl], PS[pb][hh * 64:(hh + 1) * 64, :], [kps(pb)], [('WK', 'k', hh, tb)])
                        for hh in range(2):
                            pb = ps_mm()
                            mm(PS[pb][:], SEL[:, 2 * p + hh, :], CHL[0:64, sl], True, True, [('c', 'sel'), ('WK', 'chl', tb)], [kps(pb)])
                            dve_copy(Qh[hh][64:128, sl], PS[pb][64:128, :], [kps(pb)], [('WK', 'qaug', hh, tb)])
                    for j4 in range(4):
                        pb = ps_mm()
                        for jj in range(4):
                            j = j4 * 4 + jj
                            for kc in range(NCH):
                                mm(PS[pb][:, jj * 128:(jj + 1) * 128], xn[:, kc, j * 128:(j + 1) * 128], wv[:, kc, 256:384], kc == 0, kc == NCH - 1,
                                   [('W', slot), ('xn', j4, kc)], [kps(pb)])
                        act(Vaug[:, j4 * 4:(j4 + 1) * 4, :, 0:64], PS[pb][:].rearrange("p (j h d) -> p j h d", j=4, h=2), AF.Copy, [kps(pb), ('WK', 'vones')], [('WK', 'v', j4)])
                    tiles = []
                    for hh in range(2):
                        for qb in range(NTB):
                            nj = 4 * qb + 4
                            blk = {}
                            for j in range(nj):
                                def score(hh=hh, qb=qb, j=j, h=2 * p + hh):
                                    n0 = max(0, j - 4 * qb) * 128
                                    N = 512 - n0
                                    sb_ = ps_mm()
                                    mm(PS[sb_][:, 0:N], Kh[hh][:, j * 128:(j + 1) * 128], Qh[hh][:, qb * 512 + n0:(qb + 1) * 512], True, True,
                                       [('WK', 'k', hh, j // 4), ('WK', 'kaug', hh), ('WK', 'q', hh, qb), ('WK', 'qaug', hh, qb)], [kps(sb_)])
                                    pt = PT[rotate('pt', 4)]
                                    kpt = ('pt', id(pt))
                                    act(pt[:, 0:N], PS[sb_][:, 0:N], AF.Exp, [kps(sb_), ('c', 'fbtab')], [kpt], bias=fbtab[:, qb, j, h:h + 1])
                                    if j >= 4 * qb:
                                        dve_tt(pt[:, 0:128], pt[:, 0:128], tri_bf[:], ALU.mult, [kpt, ('c', 'tribf')], [kpt], eng='pool')
                                    return (n0, N, pt, kpt)

                                def pv(info, hh=hh, j=j, nj=nj, blk=blk):
                                    n0, N, pt, kpt = info
                                    mm(PS[blk['ob']][:, n0:512], Vaug[:, j, hh, :], pt[:, 0:N], j == 0, j == nj - 1, [('WK', 'v', j // 4), ('WK', 'vones'), kpt], [kps(blk['ob'])])
                                t = {'score': score, 'pv': pv}
                                if j == 0:
                                    t['first'] = lambda blk=blk: blk.__setitem__('ob', ps_acc())
                                if j == nj - 1:
                                    t['fin'] = lambda blk=blk, hh=hh, qb=qb: attn_finalize(blk['ob'], OTv[64 * hh:64 * hh + 64, qb * 512:(qb + 1) * 512], [('WK', 'ot', qb, hh)], on_dve=True)
                                    t['tb'] = qb
                                tiles.append(t)
                    run_tiles(tiles, fillF)
                    fillF.flush()
                    outproj_fill(fillF, [wo], lambda i, tb: OTv[:, tb * 512:(tb + 1) * 512], lambda i, tb: [('WK', 'ot', tb, 0), ('WK', 'ot', tb, 1)], [okey])
                    fillF.flush(0)
                fillF.flush()
                rec.fence('W')
            else:
                e_ = l // 2
                if e_ == 0:
                    rec.op('pool', lambda e: e.memset(oml[:], 1.0), [], [('c', 'oml')])
                else:
                    dma('sp', [(TMP[0][:], lbl_d[0].partition_broadcast(128))], [], [('tmp', 0)], 'c1')
                    dma('sp', [(TMP[1][:], lbl_d[1].partition_broadcast(128))], [], [('tmp', 1)], 'c4')
                    act(TMP[0][:], TMP[0][:], AF.Exp, [('tmp', 0)], [('tmp', 0)])
                    act(TMP[1][:], TMP[1][:], AF.Exp, [('tmp', 1)], [('tmp', 1)])
                    dve_tt(TMP[0][:], TMP[0][:], TMP[1][:], ALU.add, [('tmp', 0), ('tmp', 1)], [('tmp', 0)])
                    dve_recip(TMP[0][:], TMP[0][:], [('tmp', 0)], [('tmp', 0)])
                    dve_tt(TMP[0][:], TMP[1][:], TMP[0][:], ALU.mult, [('tmp', 0), ('tmp', 1)], [('tmp', 0)])
                    dve_ts(oml[:], TMP[0][:], -1.0, 1.0, ALU.mult, ALU.add, [('tmp', 0)], [('c', 'oml')])
                Qh = [wk_bf(0, 2048), wk_bf(1024, 2048)]
                Kh = [wk_bf(2048, 2048), wk_bf(3072, 2048)]
                OTv = wk_bf(4096, 2048)
                Vaug = wk_bf(6144, 4096).rearrange("p (j h d) -> p j h d", j=16, h=2)
                EB = [wk_bf(5760, 640), EB1[:]]
                rec.op('pool', lambda e, Vaug=Vaug: e.memset(Vaug[:, :, :, 64:128], 1.0), [], [('WK', 'vones')])
                for hh in range(2):
                    rec.op('pool', lambda e, t=Kh[hh]: e.memset(t[64:128, :], 0.0), [], [('WK', 'kaug', hh)])
                    rec.op('pool', lambda e, t=Qh[hh]: e.memset(t[64:128, :], 0.0), [], [('WK', 'qaug', hh)])
                nxt = wslot()
                def b_w(slot, p):
                    par = (p // 2) % 2
                    wv = wview(slot, 0, 8, 384)
                    wo = wview(slot, 3072 + 1024 * par, 1, 1024)
                    dma('pool', [(wv[:, :, 0:128], wsrc_cols(w_in_ab[e_], 2048 + p * 128, 128)),
                                 (wv[:, :, 128:256], wsrc_cols(w_in_ab[e_], 2560 + p * 128, 128)),
                                 (wv[:, :, 256:384], wsrc_cols(w_in_ab[e_], 3072 + p * 128, 128)),
                                 (wo, wsrc_rows(w_out_ab[e_], 512 + p * 128, 1))], [], [('W', (slot, 'v')), ('W', (slot, 'o', par))], ('W', slot))
                    return wv, wo, (slot, 'v'), (slot, 'o', par)
                rec.fence('W')
                wcur = b_w(nxt, 0)
                fillB = Fill()
                for p in range(4):
                    wv, wo, slot, okey = wcur
                    if p + 1 < 4:
                        nxt = wslot()
                        wcur = b_w(nxt, p + 1)
                    for tb in range(NTB):
                        sl = slice(tb * 512, (tb + 1) * 512)
                        pb = projT(wv, 0, tb, slot)
                        for hh in range(2):
                            act(Qh[hh][0:64, sl], PS[pb][hh * 64:(hh + 1) * 64, :], AF.Copy, [kps(pb)], [('WK', 'q', hh, tb)], scale=0.125)
                        pb = projT(wv, 128, tb, slot)
                        for hh in range(2):
                            dve_copy(Kh[hh][0:64, sl], PS[pb][hh * 64:(hh + 1) * 64, :], [kps(pb)], [('WK', 'k', hh, tb)])
                    for j4 in range(4):
                        pb = ps_mm()
                        for jj in range(4):
                            j = j4 * 4 + jj
                            for kc in range(NCH):
                                mm(PS[pb][:, jj * 128:(jj + 1) * 128], xn[:, kc, j * 128:(j + 1) * 128], wv[:, kc, 256:384], kc == 0, kc == NCH - 1,
                                   [('W', slot), ('xn', j4, kc)], [kps(pb)])
                        act(Vaug[:, j4 * 4:(j4 + 1) * 4, :, 0:64], PS[pb][:].rearrange("p (j h d) -> p j h d", j=4, h=2), AF.Copy, [kps(pb), ('WK', 'vones')], [('WK', 'v', j4)])
                    tiles = []
                    for hh in range(2):
                        h = 2 * p + hh
                        bh = BH[0]
                        kbh = ('WK', 'bh', 0)
                        eb = EB[hh]
                        keb = ('WK', 'eb', hh)
                        dma('sp', [(bh, relb_d[e_, h])], [], [kbh], ('bh', 0))
                        rec.op('pool', lambda e, bh=bh: e.memset(bh[0:64, 576:640], NEG), [kbh], [kbh])
                        rec.op('pool', lambda e, bh=bh: e.memset(bh[64:128, 0:64], NEG), [kbh], [kbh])
                        act(eb, bh, AF.Exp, [kbh], [keb])
                        for qb in range(NTB):
                            ds = [0] + [d for d in (-512, -384, -256, -128, 128, 256, 384) if 0 <= qb * 512 + d < T]
                            nd = len(ds)
                            blk = {}
                            for ii, d in enumerate(ds):
                                def score(hh=hh, qb=qb, d=d, eb=eb, keb=keb):
                                    kb = qb * 512 + d
                                    j = kb // 128
                                    lo = max(0, d)
                                    hi = 512 if d >= -128 else d + 640
                                    N = hi - lo
                                    sb_ = ps_mm()
                                    mm(PS[sb_][:, 0:N], Kh[hh][:, kb:kb + 128], Qh[hh][:, qb * 512 + lo:qb * 512 + hi], True, True,
                                       [('WK', 'k', hh, j // 4), ('WK', 'kaug', hh), ('WK', 'q', hh, qb), ('WK', 'qaug', hh)], [kps(sb_)])
                                    pt = PT[rotate('pt', 4)]
                                    kpt = ('pt', id(pt))
                                    act(pt[:, 0:N], PS[sb_][:, 0:N], AF.Exp, [kps(sb_)], [kpt])
                                    dve_tt(pt[:, 0:N], pt[:, 0:N], eb[:, lo - d:hi - d], ALU.mult, [kpt, keb], [kpt])
                                    return (j, lo, hi, N, pt, kpt)

                                def pv(info, hh=hh, ii=ii, nd=nd, blk=blk):
                                    j, lo, hi, N, pt, kpt = info
                                    mm(PS[blk['ob']][:, lo:hi], Vaug[:, j, hh, :], pt[:, 0:N], ii == 0, ii == nd - 1, [('WK', 'v', j // 4), ('WK', 'vones'), kpt], [kps(blk['ob'])])
                                t = {'score': score, 'pv': pv}
                                if ii == 0:
                                    t['first'] = lambda blk=blk: blk.__setitem__('ob', ps_acc())
                                if ii == nd - 1:
                                    t['fin'] = lambda blk=blk, hh=hh, qb=qb: attn_finalize(blk['ob'], OTv[64 * hh:64 * hh + 64, qb * 512:(qb + 1) * 512], [('WK', 'ot', qb, hh)])
                                    t['tb'] = qb
                                tiles.append(t)
                    run_tiles(tiles, fillB)
                    fillB.flush()
                    outproj_fill(fillB, [wo], lambda i, tb: OTv[:, tb * 512:(tb + 1) * 512], lambda i, tb: [('WK', 'ot', tb, 0), ('WK', 'ot', tb, 1)], [okey])
                    fillB.flush(0)
                fillB.flush()
                rec.fence('W')

                rec.fence('WK')

                def a_views(s_):
                    o = s_ * 3072
                    return dict(QA=WK[:, o:o + 512], GT=wk_bf(o + 512, 512),
                                KA=WK[:, o + 768:o + 1280].rearrange("p (j d) -> p j d", j=4),
                                LF=WK[:, o + 1280:o + 1792].rearrange("p (j d) -> p j d", j=4),
                                VA=wk_bf(o + 1792, 512).rearrange("p (j d) -> p j d", j=4),
                                OTA=wk_bf(o + 2048, 2048))

                def a_w(slot, h):
                    wv = wview(slot, 0, 8, 512)
                    wo = wview(slot, 4096, 1, 1024)
                    load_w(slot, [(wv[:, :, i * 128:(i + 1) * 128], wsrc_cols(w_in_ab[e_], i * 512 + h * 128, 128)) for i in range(4)]
                           + [(wo, wsrc_rows(w_out_ab[e_], h * 128, 1))])
                    return wv, wo

                def a_head_gen(h, s_, slot, wv):
                    V = a_views(s_)
                    QA, GT, KA, LF, VA, OTA = V['QA'], V['GT'], V['KA'], V['LF'], V['VA'], V['OTA']
                    W_ = lambda n, *a: ('WK', n, s_) + a
                    hs = slice(h * 128, (h + 1) * 128)
                    er, e0, ep, em, mp, mn = ER[s_], E0[s_], EP[s_], EM[s_], MP[s_], MN[s_]
                    qi, qh, kh, ktc, am = QI[s_], QH[s_], KH[s_], KTC[s_], AM[s_]
                    sf = SFS[s_]
                    sbs = [SB[2 * s_], SB[2 * s_ + 1]]
                    K = lambda n: ('a', n, s_)
                    st = {'first': True, 'sb': 0}
                    for tb in range(NTB):
                        sl = slice(tb * 512, (tb + 1) * 512)
                        pb = projT(wv, 0, tb, slot)
                        sg = TMP[s_]
                        ksg = ('tmp', s_)
                        act(sg[:], PS[pb][:], AF.Sigmoid, [kps(pb)], [ksg])
                        dve_stt(QA, PS[pb][:], float(128 ** -0.5), sg[:], ALU.mult, ALU.mult, [kps(pb), ksg], [W_('qa')])
                        yield
                        pb = projT(wv, 384, tb, slot)
                        act(sg[:], PS[pb][:], AF.Sigmoid, [kps(pb)], [ksg])
                        dve_tt(GT, PS[pb][:], sg[:], ALU.mult, [kps(pb), ksg], [W_('gt')])
                        yield
                        pv = ps_mm()
                        for jj in range(4):
                            j = tb * 4 + jj
                            for kc in range(NCH):
                                mm(PS[pv][:, jj * 128:(jj + 1) * 128], xn[:, kc, j * 128:(j + 1) * 128], wv[:, kc, 256:384], kc == 0, kc == NCH - 1,
                                   [('W', slot), ('xn', tb, kc)], [kps(pv)])
                        act(VA, PS[pv][:].rearrange("p (j d) -> p j d", j=4), AF.Copy, [kps(pv)], [W_('va')])
                        yield
                        pz = ps_mm()
                        for jj in range(4):
                            j = tb * 4 + jj
                            for kc in range(NCH):
                                mm(PS[pz][:, jj * 128:(jj + 1) * 128], xn[:, kc, j * 128:(j + 1) * 128], wv[:, kc, 128:256], kc == 0, kc == NCH - 1,
                                   [('W', slot), ('xn', tb, kc)], [kps(pz)])
                        act(sg[:], PS[pz][:], AF.Sigmoid, [kps(pz)], [ksg], scale=-1.0)
                        for jj in range(4):
                            dve_tt(KA[:, jj, :], sg[:, jj * 128:(jj + 1) * 128], oml[:, hs], ALU.mult, [ksg, ('c', 'oml')], [W_('ka')])
                        act(LF, KA, AF.Ln, [W_('ka')], [W_('lf')], bias=1.0, scale=-1.0)
                        yield
                        po = 4 + s_
                        for jj in range(4):
                            ts = slice(jj * 128, (jj + 1) * 128)
                            pbT = ps_mm()
                            mm(PS[pbT][:, 0:128], LF[:, jj, :], maskBD, True, True, [W_('lf'), CG], [kps(pbT)])
                            act(e0[:], PS[pbT][:, 0:128], AF.Exp, [kps(pbT)], [K('e0')])
                            dve_copy(mp[:, 0:NCK], PS[pbT][:, 0:128].rearrange("p (c t) -> p c t", c=NCK)[:, :, CH // 2 - 1], [kps(pbT), K('e0')], [K('mp')])
                            dve_ts(mn[:, 0:NCK], mp[:, 0:NCK], -1.0, None, ALU.mult, ALU.bypass, [K('mp')], [K('mn')])
                            for c in range(NCK):
                                cs = slice(c * CH, (c + 1) * CH)
                                act(ep[:, cs], PS[pbT][:, cs], AF.Exp, [kps(pbT), K('mn')], [K('ep')], bias=mn[:, c:c + 1])
                                act(em[:, cs], PS[pbT][:, cs], AF.Exp, [kps(pbT), K('mp')], [K('em')], bias=mp[:, c:c + 1], scale=-1.0)
                            yield
                            prv = ps_mm()
                            mm(PS[prv][:, 0:128], triRev, LF[:, jj, :], True, True, [W_('lf'), CG], [kps(prv)])
                            act(er[:], PS[prv][:, 0:128], AF.Exp, [kps(prv)], [K('er')])
                            for c in range(NCK):
                                dve_stt(ktc[:, c, :], KA[:, jj, :], cmask[:, c:c + 1], er[:], ALU.mult, ALU.mult, [W_('ka'), K('er'), CG], [K('ktc')])
                            yield
                            pkT = ps_mm()
                            rec.op('pe', lambda e, pkT=pkT, jj=jj, KA=KA: e.transpose(PS[pkT][:, 0:128], KA[:, jj, :], ident), [W_('ka'), CG], [kps(pkT)])
                            dve_tt(kh[:], PS[pkT][:, 0:128], em[:], ALU.mult, [kps(pkT), K('em')], [K('kh')])
                            dve_tt(qi[:], QA[:, ts], e0[:], ALU.mult, [W_('qa'), K('e0')], [K('qi')])
                            dve_tt(qh[:], QA[:, ts], ep[:], ALU.mult, [W_('qa'), K('ep')], [K('qh')])
                            yield
                            pu = 6 + s_
                            for c in range(NCK):
                                mm(PS[pu][:, c * 128:(c + 1) * 128], ktc[:, c, :], VA[:, jj, :], True, True, [K('ktc'), W_('va')], [kps(pu)])
                            pA = ps_mm()
                            mm(PS[pA][:, 0:128], kh[:], qh[:], True, True, [K('kh'), K('qh')], [kps(pA)])
                            dve_tt(am[:], PS[pA][:, 0:128], maskBD, ALU.mult, [kps(pA), CG], [K('am')])
                            mm(PS[po][:, ts], VA[:, jj, :], am[:], True, False, [W_('va'), K('am')], [kps(po)])
                            yield
                            for c in range(NCK):
                                cs = slice(c * CH, (c + 1) * CH)
                                if not st['first']:
                                    sbc = sbs[st['sb'] % 2]
                                    mm(PS[po][:, jj * 128 + c * CH:jj * 128 + (c + 1) * CH], sbc[:], qi[:, cs], False, c == NCK - 1, [('sbf', id(sbc)), K('qi')], [kps(po)])
                                if st['first']:
                                    dve_copy(sf[:], PS[pu][:, c * 128:(c + 1) * 128], [kps(pu)], [K('sf')])
                                else:
                                    dve_stt(sf[:], sf[:], e0[:, c * CH + CH - 1:c * CH + CH], PS[pu][:, c * 128:(c + 1) * 128], ALU.mult, ALU.add, [K('sf'), K('e0'), kps(pu)], [K('sf')])
                                st['sb'] += 1
                                sbn = sbs[st['sb'] % 2]
                                dve_copy(sbn[:], sf[:], [K('sf')], [('sbf', id(sbn))])
                                st['first'] = False
                                yield
                        q = SQ[s_]
                        act(q[:], PS[po][:], AF.Square, [kps(po)], [('sq', id(q))])
                        pn = ps_mm()
                        mm(PS[pn][:], ones_bf[:], q[:], True, True, [('sq', id(q)), ('c', 'ones')], [kps(pn)])
                        rs = RS[s_]
                        rstd_from_ps(pn, 512, rs, [], 1.0 / 128)
                        on = REC[s_]
                        dve_stt(on[:], PS[po][:], again[:, e_ * 4 + h:e_ * 4 + h + 1], rs[:], ALU.mult, ALU.mult, [kps(po), ('rs', id(rs)), CG], [('rec', id(on))])
                        dve_tt(OTA[:, sl], on[:], GT, ALU.mult, [('rec', id(on)), W_('gt')], [W_('ota', tb)])
                        yield

                for hp in range(2):
                    heads = (2 * hp, 2 * hp + 1)
                    ws = [a_w(s_, heads[s_]) for s_ in range(2)]
                    gens = [a_head_gen(heads[s_], s_, s_, ws[s_][0]) for s_ in range(2)]
                    alive = [True, True]
                    if DBG == 2:
                        for s_ in range(2):
                            for _ in gens[s_]:
                                pass
                        alive = [False, False]
                    if KSTOP > 0:
                        for s_ in range(2):
                            for _i in range(KSTOP):
                                next(gens[s_])
                        alive = [False, False]
                    while any(alive):
                        for s_ in range(2):
                            if alive[s_]:
                                try:
                                    next(gens[s_])
                                except StopIteration:
                                    alive[s_] = False
                    OT2 = [a_views(0)['OTA'], a_views(1)['OTA']]
                    wos = [ws[0][1], ws[1][1]]
                    for tb in range(NTB):
                        for oc in range(NCH):
                            sl = slice(tb * 512, (tb + 1) * 512)
                            if DBG == 3:
                                for s_ in range(2):
                                    pb = ps_mm()
                                    mm(PS[pb][:], wos[s_][:, 0, oc * 128:(oc + 1) * 128], OT2[s_][:, tb * 512:(tb + 1) * 512], True, True,
                                       [('W', s_), ('WK', 'ota', s_, tb)], [kps(pb)])
                                    dve_tt(hT[:, oc, sl], hT[:, oc, sl], PS[pb][:], ALU.add, [('hT', oc, tb), kps(pb)], [('hT', oc, tb)])
                                continue
                            pb = ps_mm()
                            for s_ in range(2):
                                mm(PS[pb][:], wos[s_][:, 0, oc * 128:(oc + 1) * 128], OT2[s_][:, tb * 512:(tb + 1) * 512], s_ == 0, s_ == 1,
                                   [('W', s_), ('WK', 'ota', s_, tb)], [kps(pb)])
                            dve_tt(hT[:, oc, sl], hT[:, oc, sl], PS[pb][:], ALU.add, [('hT', oc, tb), kps(pb)], [('hT', oc, tb)])
                wstate['i'] = 1

            rec.fence('WK')
            QX = wk_bf(0, 4096).rearrange("p (c t) -> p c t", c=2)
            OX = wk_bf(2048, 4096).rearrange("p (c t) -> p c t", c=2)
            MT = WK[:, 4096:6144].rearrange("p (c t) -> p c t", c=8)
            MN_ = wk_bf(6144, 2048).rearrange("p (c t) -> p c t", c=8)
            KX = wk_bf(7168, 512).rearrange("p (c t) -> p c t", c=2)
            VX = wk_bf(7424, 512).rearrange("p (c t) -> p c t", c=2)
            dma('sp', [(MT[:, c, :], memT_d[b, c * 128:(c + 1) * 128, :]) for c in range(NCH)], [], [('WK', 'mt')], 'mt')
            pb = ps_acc()
            for c in range(NCH):
                q = SQ[rotate('sq', 2)]
                act(q[:, 0:256], MT[:, c, :], AF.Square, [('WK', 'mt')], [('sq', id(q))])
                mm(PS[pb][:, 0:256], ones_bf[:], q[:, 0:256], c == 0, c == NCH - 1, [('sq', id(q)), ('c', 'ones')], [kps(pb)])
            rs = RS[rotate('rs', 2)]
            rstd_from_ps(pb, 256, rs, [], 1.0 / D)
            for c in range(NCH):
                dve_stt(MN_[:, c, :], MT[:, c, :], gains[:, (8 + l) * 8 + c:(8 + l) * 8 + c + 1], rs[:, 0:256], ALU.mult, ALU.mult,
                        [('WK', 'mt'), ('rs', id(rs)), CG], [('WK', 'mn')])
            norm_to_xn(4 + l)
            nxt = wslot()
            def x_w(slot, h):
                wq = wview(slot, 0, 8, 256)
                wkv = wview(slot, 2048, 8, 512)
                wo = wview(slot, 6144, 2, 1024)
                load_w(slot, [(wq, wsrc_cols(w_xq[l], h * 256, 256)),
                              (wkv[:, :, 0:256], wsrc_cols(w_xkv[l], h * 256, 256)),
                              (wkv[:, :, 256:512], wsrc_cols(w_xkv[l], 1024 + h * 256, 256)),
                              (wo, wsrc_rows(w_xo[l], h * 256, 2))])
                return wq, wkv, wo
            wcur = x_w(nxt, 0)
            for h in range(4):
                slot = nxt
                wq, wkv, wo = wcur
                if h + 1 < 4:
                    nxt = wslot()
                    wcur = x_w(nxt, h + 1)
                for dc in range(2):
                    pb = ps_mm()
                    for kc in range(NCH):
                        mm(PS[pb][:, 0:256], wkv[:, kc, dc * 128:(dc + 1) * 128], MN_[:, kc, :], kc == 0, kc == NCH - 1, [('W', slot), ('WK', 'mn')], [kps(pb)])
                    act(KX[:, dc, :], PS[pb][:, 0:256], AF.Copy, [kps(pb)], [('WK', 'kx')], scale=1.0 / 16)
                for mt in range(2):
                    pb = ps_mm()
                    for kc in range(NCH):
                        mm(PS[pb][:, 0:256], MN_[:, kc, mt * 128:(mt + 1) * 128], wkv[:, kc, 256:512], kc == 0, kc == NCH - 1, [('W', slot), ('WK', 'mn')], [kps(pb)])
                    dve_copy(VX[:, mt, :], PS[pb][:, 0:256], [kps(pb)], [('WK', 'vx')])
                for tb in range(NTB):
                    sl = slice(tb * 512, (tb + 1) * 512)
                    for dc in range(2):
                        pb = projT(wq, dc * 128, tb, slot)
                        if dc == 0:
                            act(QX[:, dc, sl], PS[pb][:], AF.Copy, [kps(pb)], [('WK', 'qx', tb, dc)])
                        else:
                            dve_copy(QX[:, dc, sl], PS[pb][:], [kps(pb)], [('WK', 'qx', tb, dc)])
                for qb in range(NTB):
                    sl = slice(qb * 512, (qb + 1) * 512)
                    pts = []
                    for mt in range(2):
                        sb_ = ps_mm()
                        for dc in range(2):
                            mm(PS[sb_][:], KX[:, dc, mt * 128:(mt + 1) * 128], QX[:, dc, sl], dc == 0, dc == 1, [('WK', 'kx'), ('WK', 'qx', qb, dc)], [kps(sb_)])
                        pt = PT[rotate('pt', 4)]
                        act(pt[:], PS[sb_][:], AF.Exp, [kps(sb_)], [('pt', id(pt))])
                        pts.append(pt)
                    db = ps_acc()
                    for mt in range(2):
                        mm(PS[db][:], ones_bf[:], pts[mt][:], mt == 0, mt == 1, [('c', 'ones'), ('pt', id(pts[mt]))], [kps(db)])
                    rc = REC[rotate('rec', 2)]
                    act_recip(rc[:], PS[db][:], [kps(db)], [('rec', id(rc))])
                    for dvc in range(2):
                        ob = ps_acc()
                        for mt in range(2):
                            mm(PS[ob][:], VX[:, mt, dvc * 128:(dvc + 1) * 128], pts[mt][:], mt == 0, mt == 1, [('WK', 'vx'), ('pt', id(pts[mt]))], [kps(ob)])
                        dve_tt(OX[:, dvc, sl], PS[ob][:], rc[:], ALU.mult, [kps(ob), ('rec', id(rc))], [('WK', 'ox', qb, dvc)])
                outproj_acc(wo, 2, lambda kc, tb: OX[:, kc, tb * 512:(tb + 1) * 512], lambda kc, tb: [('WK', 'ox', tb, kc)], slot)

            rec.fence('WK')
            norm_to_xn(12 + l)
            nxt = wslot()
            def m_w(slot, g):
                wu = wview(slot, 0, 8, 512)
                wd = wview(slot, 4096, 4, 1024)
                load_w(slot, [(wu, wsrc_cols(w_up[l], g * 512, 512)), (wd, wsrc_rows(w_down[l], g * 512, 4))])
                return wu, wd
            wcur = m_w(nxt, 0)
            for g in range(8):
                slot = nxt
                wu, wd = wcur
                if g + 1 < 8:
                    nxt = wslot()
                    wcur = m_w(nxt, g + 1)
                hb = g % 2
                HFF = wk_bf(hb * 4096, 8192).rearrange("p (c t) -> p c t", c=4)
                for tb in range(NTB):
                    sl = slice(tb * 512, (tb + 1) * 512)
                    for fc in range(4):
                        pb = projT(wu, fc * 128, tb, slot)
                        tm = PT[rotate('pt', 4)]
                        ktm = ('pt', id(tm))
                        act(tm[:], PS[pb][:], AF.Relu, [kps(pb)], [ktm])
                        dve_tt(HFF[:, fc, sl], tm[:], tm[:], ALU.mult, [ktm], [('WK', 'hff', hb, tb, fc)], eng='pool')
                outproj_acc(wd, 4, lambda kc, tb, HFF=HFF: HFF[:, kc, tb * 512:(tb + 1) * 512], lambda kc, tb, hb=hb: [('WK', 'hff', hb, tb, kc)], slot)

        rec.fence('WK')
        OUTS = [WK[:, 0:4096].rearrange("p (c t) -> p c t", c=8), WK[:, 4096:8192].rearrange("p (c t) -> p c t", c=8)]
        out_v = outT_d[b].rearrange("(c p) t -> p c t", p=128)

        def store(tb, b=b, OUTS=OUTS, out_v=out_v):
            dma('sp', [(out_v[:, :, tb * 512:(tb + 1) * 512], OUTS[tb % 2][:])], [('WK', 'out', tb % 2, c) for c in range(NCH)],
                [('od', b, tb)], ('st', tb % 2))
        if final_norm:
            norm_h(16, lambda c, tb: OUTS[tb % 2][:, c, :], lambda c, tb: [('WK', 'out', tb % 2, c)], after_block=store)
        else:
            for tb in range(NTB):
                for c in range(NCH):
                    dve_copy(OUTS[tb % 2][:, c, :], hT[:, c, tb * 512:(tb + 1) * 512], [('hT', c, tb)], [('WK', 'out', tb % 2, c)])
                store(tb)
    rec.op('sp', None, [('od', b, tb) for b in range(nbatch) for tb in range(NTB)], [])

    rec.plan()
    sems = {}
    for e in Rec.ENGS:
        for ep_ in range(max(1, rec.nepoch[e])):
            sems[(e, ep_)] = es.enter_context(nc.semaphore(f"s_{e}_{ep_}"))
    dsems = {}
    for i, k in enumerate(rec.dma_cnt):
        dsems[k] = es.enter_context(nc.semaphore(f"d_{i}"))
    block = es.enter_context(nc.Block())

    def make_body(eng):
        ops = rec.eng_ops[eng]

        def body(e):
            for o in ops:
                for Dd in o.w_eng:
                    e.wait_ge(sems[(Dd.eng, Dd.epoch)], Dd.sigval)
                for (k, t) in o.w_dma:
                    e.wait_ge(dsems[k], t)
                if o.fn is None:
                    continue
                if o.dma:
                    for ins in o.fn(e):
                        ins.then_inc(dsems[o.dsemkey], 16)
                else:
                    ins = o.fn(e)
                    if o.signal:
                        ins.then_inc(sems[(eng, o.epoch)], 1)
        return body

    block.sync(make_body('sp'))
    block.tensor(make_body('pe'))
    block.scalar(make_body('act'))
    block.vector(make_body('dve'))
    block.gpsimd(make_body('pool'))
    es.close()
    return nc, rec


def _consts():
    c = np.zeros((128, 516), np.float32)
    i = np.arange(128)
    c[:, 0:128] = np.eye(128, dtype=np.float32)
    c[:, 128:256] = (i[:, None] <= i[None, :])
    same = (i[:, None] // CH) == (i[None, :] // CH)
    c[:, 256:384] = same & (i[:, None] <= i[None, :])
    c[:, 384:512] = same & (i[:, None] > i[None, :])
    c[:, 512:516] = (i[:, None] // CH) == np.arange(4)[None, :]
    return c


def _prep_shared(inp):
    f = lambda a: np.ascontiguousarray(np.asarray(a, dtype=np.float32))
    g = np.concatenate([f(inp['norm_mix']), f(inp['norm_xattn']), f(inp['norm_mem']), f(inp['norm_mlp']),
                        f(inp['norm_final'])[None, :]], axis=0)
    gains = np.ascontiguousarray(g.reshape(17, 8, 128).transpose(2, 0, 1).reshape(128, 136))
    again = np.ascontiguousarray(f(inp['a_out_gain']).reshape(2, 4, 128).transpose(2, 0, 1).reshape(128, 8))
    sidx = np.arange(128)[:, None]
    uidx = np.arange(640)[None, :]
    ridx = np.clip(uidx - sidx, -128, 128) + 128
    relb = np.ascontiguousarray(f(inp['b_rel_bias'])[:, :, ridx])
    selc = np.zeros((64, 16, 128), np.float32)
    for hh_ in range(16):
        selc[hh_, hh_, 64] = 1.0
        selc[32 + hh_, hh_, 96] = 1.0
    sh = dict(selc=selc.reshape(64, 2048), gains=gains, again=again, lbl=f(inp['a_lb_logits']), fbias=f(inp['c_fgate_bias']), relb=relb, consts=_consts())
    for k in ('w_in_ab', 'w_out_ab', 'w_in_c', 'w_out_c', 'w_xq', 'w_xkv', 'w_xo', 'w_up', 'w_down'):
        sh[k] = f(inp[k])
    return sh


_CACHE = {}


def kernel(**inp):
    ncores = 8
    x = np.asarray(inp['x'], dtype=np.float32)
    mem = np.asarray(inp['mem'], dtype=np.float32)
    sh = _prep_shared(inp)
    in_maps = []
    for c in range(ncores):
        m = dict(sh)
        m['xT'] = np.ascontiguousarray(x[2 * c:2 * c + 2].transpose(0, 2, 1))
        m['memT'] = np.ascontiguousarray(mem[2 * c:2 * c + 2].transpose(0, 2, 1))
        in_maps.append(m)
    if 'nc' not in _CACHE:
        _CACHE['nc'] = build()[0]
    res = run_bass_kernel_spmd(_CACHE['nc'], in_maps, core_ids=list(range(ncores)))
    out = np.empty((16, T, D), np.float32)
    for c in range(ncores):
        out[2 * c:2 * c + 2] = res.results[c]['outT'].transpose(0, 2, 1)
    return out
```

```python
import os
import numpy as np
from contextlib import ExitStack
import concourse.bass as bass
import concourse.mybir as mybir
from concourse.bass_utils import run_bass_kernel_spmd

F32 = mybir.dt.float32
BF16 = mybir.dt.bfloat16
ALU = mybir.AluOpType
AF = mybir.ActivationFunctionType

D = 1024
T = 2048
NCH = 8
NTB = 4
NT = 16
MEM = 256
EPS = 1e-6
EPOCH = 20000
NEG = -30000.0
CH = 64
NCK = 128 // CH
AOFF = int(os.environ.get('KAOFF', '14'))
DBG = int(os.environ.get('KDBG', '0'))
KSTOP = int(os.environ.get('KSTOP', '0'))


class Op:
    pass


class Rec:
    ENGS = ('sp', 'pe', 'act', 'dve', 'pool')

    def __init__(s):
        s.ops = []
        s.lastw = {}
        s.rd_eng = {}
        s.rd_dma = {}
        s.eng_ops = {e: [] for e in s.ENGS}
        s.dma_cnt = {}
        s.fdeps = {}

    def fence(s, region):
        deps = set(s.fdeps.get(region, ()))
        for k in list(s.lastw):
            if k[0] == region:
                deps.add(s.lastw.pop(k))
        for k in list(s.rd_eng):
            if k[0] == region:
                deps.update(s.rd_eng.pop(k).values())
        for k in list(s.rd_dma):
            if k[0] == region:
                deps.update(s.rd_dma.pop(k))
        best = {}
        for di in deps:
            o = s.ops[di]
            kk = ('d', o.dsemkey) if o.dma else ('e', o.eng)
            if kk not in best or best[kk] < di:
                best[kk] = di
        s.fdeps[region] = set(best.values())

    def op(s, eng, fn, r=(), w=(), dma=None):
        o = Op()
        o.eng = eng
        o.fn = fn
        o.idx = len(s.ops)
        o.pos = len(s.eng_ops[eng])
        o.signal = False
        o.dma = dma is not None
        deps = set()
        raw = set()
        for k in r:
            d = s.lastw.get(k)
            if d is not None:
                deps.add(d)
                raw.add(d)
            elif k[0] in s.fdeps:
                deps.update(s.fdeps[k[0]])
                raw.update(s.fdeps[k[0]])
            if k[0] == 'ps':
                for e2, d2 in s.rd_eng.get(k, {}).items():
                    if e2 != eng:
                        deps.add(d2)
        for k in w:
            d = s.lastw.get(k)
            if d is not None:
                deps.add(d)
            elif k[0] in s.fdeps:
                deps.update(s.fdeps[k[0]])
                raw.update(s.fdeps[k[0]])
            deps.update(s.rd_eng.get(k, {}).values())
            deps.update(s.rd_dma.get(k, ()))
        for k in r:
            if o.dma:
                s.rd_dma.setdefault(k, []).append(o.idx)
            else:
                s.rd_eng.setdefault(k, {})[eng] = o.idx
        for k in w:
            s.lastw[k] = o.idx
            s.rd_eng[k] = {}
            s.rd_dma[k] = []
        o.deps = deps
        o.raw = raw
        if o.dma:
            semkey, n = dma
            s.dma_cnt[semkey] = s.dma_cnt.get(semkey, 0) + n
            o.dsemkey = semkey
            o.dtarget = 16 * s.dma_cnt[semkey]
        s.ops.append(o)
        s.eng_ops[eng].append(o)
        return o

    def plan(s):
        waited = {e: {f: -1 for f in s.ENGS} for e in s.ENGS}
        waited_dma = {e: {} for e in s.ENGS}
        for o in s.ops:
            o.w_eng = []
            o.w_dma = []
            best = {}
            bestd = {}
            for di in o.deps:
                Dd = s.ops[di]
                if Dd.dma:
                    if waited_dma[o.eng].get(Dd.dsemkey, 0) >= Dd.dtarget:
                        continue
                    if bestd.get(Dd.dsemkey, 0) < Dd.dtarget:
                        bestd[Dd.dsemkey] = Dd.dtarget
                else:
                    if Dd.eng == o.eng and (o.eng == 'pe' or di not in o.raw):
                        continue
                    if Dd.pos <= waited[o.eng][Dd.eng]:
                        continue
                    if Dd.eng not in best or best[Dd.eng].pos < Dd.pos:
                        best[Dd.eng] = Dd
            for f, Dd in best.items():
                Dd.signal = True
                waited[o.eng][f] = Dd.pos
                o.w_eng.append(Dd)
            for k, t in bestd.items():
                waited_dma[o.eng][k] = t
                o.w_dma.append((k, t))
        s.nepoch = {}
        for e in s.ENGS:
            c = 0
            for o in s.eng_ops[e]:
                if o.signal:
                    o.epoch = c // EPOCH
                    o.sigval = c % EPOCH + 1
                    c += 1
            s.nepoch[e] = (c + EPOCH - 1) // EPOCH


def build(nbatch=2, layers=(0, 1, 2, 3), final_norm=True):
    nc = bass.Bass("TRN2", target_bir_lowering=False)
    rec = Rec()

    def din(name, shape):
        return nc.dram_tensor(name, list(shape), F32, kind="ExternalInput").ap()

    xT_d = din("xT", [2, D, T])
    memT_d = din("memT", [2, D, MEM])
    gains_d = din("gains", [128, 136])
    again_d = din("again", [128, 8])
    lbl_d = din("lbl", [2, 512])
    fbias_d = din("fbias", [2, 16])
    relb_d = din("relb", [2, 8, 128, 640])
    consts_d = din("consts", [128, 516])
    selc_d = din("selc", [64, 2048])
    w_in_ab = din("w_in_ab", [2, D, 3584])
    w_out_ab = din("w_out_ab", [2, D, D])
    w_in_c = din("w_in_c", [2, D, 3088])
    w_out_c = din("w_out_c", [2, D, D])
    w_xq = din("w_xq", [4, D, D])
    w_xkv = din("w_xkv", [4, D, 2 * D])
    w_xo = din("w_xo", [4, D, D])
    w_up = din("w_up", [4, D, 4 * D])
    w_down = din("w_down", [4, 4 * D, D])
    outT_d = nc.dram_tensor("outT", [2, D, T], F32, kind="ExternalOutput").ap()

    es = ExitStack()

    def sb(name, shape, dt):
        return es.enter_context(nc.sbuf_tensor(name, list(shape), dt))

    hT = sb("hT", [128, NCH, T], F32)
    xn = sb("xn", [128, NCH, T], BF16)
    Wt = [sb("W0", [128, 8192], BF16), sb("W1", [128, 8192], BF16)]
    WK = sb("WK", [128, 8192], F32)
    gains = sb("gains_s", [128, 136], F32)
    again = sb("again_s", [128, 8], F32)
    consts = sb("consts_s", [128, 516], F32)
    tri_bf = sb("tri_bf", [128, 128], BF16)
    ones_bf = sb("ones_bf", [128, 128], BF16)
    ones_f = sb("ones_f", [128, 128], F32)
    SEL = sb("sel", [64, 16, 128], BF16)
    oml = sb("oml", [128, 512], F32)
    fbt = sb("fbt", [128, 16], F32)
    lf = sb("lf", [128, 16, 16], F32)
    cum = sb("cum", [128, 16, 16], F32)
    zf = cum
    rcar = sb("rcar", [128, 4, 16], F32)
    fbtab = sb("fbtab", [128, 4, 16, 16], F32)
    SQ = [sb(f"sq{i}", [128, 512], BF16) for i in range(2)]
    RS = [sb(f"rs{i}", [128, 512], F32) for i in range(2)]
    PT = [sb(f"pt{i}", [128, 512], BF16) for i in range(4)]
    TMP = [sb(f"tmp{i}", [128, 512], F32) for i in range(2)]
    REC = [sb(f"rec{i}", [128, 512], F32) for i in range(2)]
    BH = [WK[:, 5120:5760]]
    ER = [sb(f"er{i}", [128, 128], F32) for i in range(2)]
    E0 = [sb(f"e0{i}", [128, 128], F32) for i in range(2)]
    EP = [sb(f"ep{i}", [128, 128], F32) for i in range(2)]
    EM = [sb(f"em{i}", [128, 128], F32) for i in range(2)]
    MP = [sb(f"mp{i}", [128, 4], F32) for i in range(2)]
    MN = [sb(f"mn{i}", [128, 4], F32) for i in range(2)]
    QI = [sb(f"qi{i}", [128, 128], BF16) for i in range(2)]
    QH = [sb(f"qh{i}", [128, 128], BF16) for i in range(2)]
    KH = [sb(f"kh{i}", [128, 128], BF16) for i in range(2)]
    KTC = [sb(f"ktc{i}", [128, 4, 128], BF16) for i in range(2)]
    AM = [sb(f"am{i}", [128, 128], BF16) for i in range(2)]
    SFS = [sb(f"sf{i}", [128, 128], F32) for i in range(2)]
    EB1 = sb("eb1", [128, 640], BF16)
    SB = [sb(f"sbf{i}", [128, 128], BF16) for i in range(4)]

    PS = [es.enter_context(nc.psum_tensor(f"ps{i}", [128, 512], F32)) for i in range(8)]

    ident = consts[:, 0:128]
    tri_f = consts[:, 128:256]
    maskBD = consts[:, 256:384]
    triRev = consts[:, 384:512]
    cmask = consts[:, 512:516]

    cnt = {'mm': 0, 'acc': 0}

    def ps_mm():
        cnt['mm'] += 1
        return cnt['mm'] % 4

    def ps_acc():
        cnt['acc'] += 1
        return 4 + cnt['acc'] % 4

    rot = {}

    def rotate(name, n):
        rot[name] = rot.get(name, -1) + 1
        return rot[name] % n

    def mm(out, lhsT, rhs, start, stop, r, w):
        rec.op('pe', lambda e: e.matmul(out, lhsT, rhs, start=start, stop=stop), r, w)

    def act(out, in_, func, r, w, bias=0.0, scale=1.0):
        rec.op('act', lambda e: e.activation(out, in_, func, bias=bias, scale=scale), r, w)

    def dve_tt(out, a, b, op, r, w, eng='dve'):
        rec.op(eng, lambda e: e.tensor_tensor(out, a, b, op), r, w)

    def dve_ts(out, a, s1, s2, op0, op1, r, w, eng='dve'):
        rec.op(eng, lambda e: e.tensor_scalar(out, a, s1, s2, op0, op1), r, w)

    def dve_stt(out, a, sc, b, op0, op1, r, w):
        rec.op('dve', lambda e: e.scalar_tensor_tensor(out, a, sc, b, op0, op1), r, w)

    def dve_copy(out, a, r, w, eng='dve'):
        rec.op(eng, lambda e: e.tensor_copy(out, a), r, w)

    def dma(eng, pairs, r, w, semkey):
        pairs = list(pairs)
        rec.op(eng, lambda e: [e.dma_start(out=o, in_=i) for (o, i) in pairs], r, w, dma=(semkey, len(pairs)))

    def kps(b):
        return ('ps', b)

    dma('sp', [(gains[:], gains_d), (again[:], again_d), (consts[:], consts_d)], [], [('c', 'g')], 'c0')
    dma('pool', [(tri_bf[:], consts_d[:, 128:256])], [], [('c', 'tribf')], 'c2')
    dma('pool', [(SEL[:].rearrange("p h m -> p (h m)"), selc_d)], [], [('c', 'sel')], 'c5')
    rec.op('pool', lambda e: e.memset(ones_bf[:], 1.0), [], [('c', 'ones')])
    rec.op('pool', lambda e: e.memset(ones_f[:], 1.0), [], [('c', 'onesf')])
    CG = ('c', 'g')

    wstate = {'i': 0}

    def wslot():
        wstate['i'] += 1
        return wstate['i'] % 2

    def wview(slot, off, kc, n):
        return Wt[slot][:, off:off + kc * n].rearrange("p (kc n) -> p kc n", kc=kc)

    def wsrc_cols(wd, c0, n):
        return wd.rearrange("(kc p) n -> p kc n", p=128)[:, :, c0:c0 + n]

    def wsrc_rows(wd, r0, kc):
        return wd[r0:r0 + kc * 128, :].rearrange("(kc p) n -> p kc n", p=128)

    def load_w(slot, pairs):
        dma('pool', pairs, [], [('W', slot)], ('W', slot))

    def rstd_from_ps(psb, ncols, rs, rkeys, scale):
        act(rs[:, 0:ncols], PS[psb][:, 0:ncols], AF.Ln, rkeys + [kps(psb)], [('rs', id(rs))], bias=EPS, scale=scale)
        act(rs[:, 0:ncols], rs[:, 0:ncols], AF.Exp, [('rs', id(rs))], [('rs', id(rs))], scale=-0.5)

    def norm_h(gidx, out_fn, out_keys_fn, after_block=None):
        for tb in range(NTB):
            sl = slice(tb * 512, (tb + 1) * 512)
            pb = ps_acc()
            for c in range(NCH):
                q = SQ[rotate('sq', 2)]
                act(q[:], hT[:, c, sl], AF.Square, [('hT', c, tb)], [('sq', id(q))])
                mm(PS[pb][:], ones_bf[:], q[:], c == 0, c == NCH - 1, [('sq', id(q)), ('c', 'ones')], [kps(pb)])
            rs = RS[rotate('rs', 2)]
            rstd_from_ps(pb, 512, rs, [], 1.0 / D)
            for c in range(NCH):
                dve_stt(out_fn(c, tb), hT[:, c, sl], gains[:, gidx * 8 + c:gidx * 8 + c + 1], rs[:], ALU.mult, ALU.mult,
                        [('hT', c, tb), ('rs', id(rs)), CG], out_keys_fn(c, tb))
            if after_block is not None:
                after_block(tb)

    def norm_to_xn(gidx):
        norm_h(gidx, lambda c, tb: xn[:, c, tb * 512:(tb + 1) * 512], lambda c, tb: [('xn', tb, c)])

    def xn_keys(tb):
        return [('xn', tb, c) for c in range(NCH)]

    def projT(wv, c0, tb, slot):
        pb = ps_mm()
        for kc in range(NCH):
            mm(PS[pb][:], wv[:, kc, c0:c0 + 128], xn[:, kc, tb * 512:(tb + 1) * 512], kc == 0, kc == NCH - 1,
               [('W', slot), ('xn', tb, kc)], [kps(pb)])
        return pb

    def outproj_acc(wo, nkc, src_fn, src_keys_fn, slot):
        for tb in range(NTB):
            for oc in range(NCH):
                pb = ps_mm()
                for kc in range(nkc):
                    mm(PS[pb][:], wo[:, kc, oc * 128:(oc + 1) * 128], src_fn(kc, tb), kc == 0, kc == nkc - 1,
                       [('W', slot)] + src_keys_fn(kc, tb), [kps(pb)])
                sl = slice(tb * 512, (tb + 1) * 512)
                dve_tt(hT[:, oc, sl], hT[:, oc, sl], PS[pb][:], ALU.add, [('hT', oc, tb), kps(pb)], [('hT', oc, tb)])

    def wk_bf(off, n):
        return WK[:, off:off + n // 2].bitcast(BF16)

    def dve_recip(out, a, r, w):
        rec.op('dve', lambda e: e.reciprocal(out, a), r, w)

    def act_recip(out, a, r, w):
        act(out, a, AF.Ln, r, w)
        act(out, out, AF.Exp, w, w, scale=-1.0)

    def attn_finalize(ob, out_ap, out_keys, on_dve=False):
        rc = REC[rotate('rec', 2)]
        if on_dve:
            dve_recip(rc[0:64, :], PS[ob][64:128, :], [kps(ob)], [('rec', id(rc))])
        else:
            act_recip(rc[0:64, :], PS[ob][64:128, :], [kps(ob)], [('rec', id(rc))])
        dve_tt(out_ap, PS[ob][0:64, :], rc[0:64, :], ALU.mult, [kps(ob), ('rec', id(rc))], out_keys)

    SKEW = 3

    class Fill:
        def __init__(self):
            self.p = {}

        def add(self, tb, fn):
            self.p.setdefault(tb, []).append(fn)

        def one(self):
            for tb in sorted(self.p):
                if self.p[tb]:
                    self.p[tb].pop(0)()
                    return

        def flush(self, tb=None):
            for t in sorted(self.p):
                if tb is None or t == tb:
                    while self.p[t]:
                        self.p[t].pop(0)()

    def outproj_fill(fill, wos, srcs_fn, keys_fn, slots):
        for tb in range(NTB):
            for oc in range(NCH):
                def f(tb=tb, oc=oc):
                    pb = ps_mm()
                    n = len(wos)
                    for i in range(n):
                        mm(PS[pb][:], wos[i][:, 0, oc * 128:(oc + 1) * 128], srcs_fn(i, tb), i == 0, i == n - 1,
                           [('W', slots[i])] + keys_fn(i, tb), [kps(pb)])
                    sl = slice(tb * 512, (tb + 1) * 512)
                    dve_tt(hT[:, oc, sl], hT[:, oc, sl], PS[pb][:], ALU.add, [('hT', oc, tb), kps(pb)], [('hT', oc, tb)])
                fill.add(tb, f)

    def run_tiles(tiles, fill=None):
        pend = []
        for t in tiles + [None] * SKEW:
            if t is not None:
                if t.get('first') is not None:
                    t['first']()
                pend.append((t, t['score']()))
            if pend and (len(pend) > SKEW or t is None):
                pt_, info = pend.pop(0)
                pt_['pv'](info)
                if pt_.get('fin') is not None:
                    if fill is not None:
                        fill.flush(pt_['tb'])
                    pt_['fin']()
                elif fill is not None:
                    fill.one()
        assert not pend

    for b in range(nbatch):
        for c in range(NCH):
            dma('sp', [(hT[:, c, :], xT_d[b, c * 128:(c + 1) * 128, :])], [],
                [('hT', c, tb) for tb in range(NTB)], ('ld', c))

        for l in layers:
            rec.fence('WK')
            norm_to_xn(l)
            if l % 2 == 1:
                o_ = l // 2
                Qh = [wk_bf(0, 2048), wk_bf(1024, 2048)]
                Kh = [wk_bf(2048, 2048), wk_bf(3072, 2048)]
                OTv = wk_bf(4096, 2048)
                s0 = wslot()
                wf = wview(s0, 0, 8, 16)
                load_w(s0, [(wf, wsrc_cols(w_in_c[o_], 3072, 16))])
                dma('sp', [(fbt[:], fbias_d[o_].partition_broadcast(128))], [], [('c', 'fbt')], 'c3')
                pb = ps_mm()
                for j in range(NT):
                    for kc in range(NCH):
                        mm(PS[pb][:, j * 16:(j + 1) * 16], xn[:, kc, j * 128:(j + 1) * 128], wf[:, kc, :], kc == 0, kc == NCH - 1,
                           [('W', s0), ('xn', j // 4, kc)], [kps(pb)])
                for j in range(NT):
                    dve_tt(zf[:, j, :], PS[pb][:, j * 16:(j + 1) * 16], fbt[:], ALU.add, [kps(pb), ('c', 'fbt')], [('c', 'cum')])
                zf2 = zf[:].rearrange("p j h -> p (j h)")
                lf2 = lf[:].rearrange("p j h -> p (j h)")
                act(zf2, zf2, AF.Exp, [('c', 'cum')], [('c', 'cum')], scale=-1.0)
                act(lf2, zf2, AF.Ln, [('c', 'cum')], [('c', 'lf')], bias=1.0)
                dve_ts(lf2, lf2, -1.0, None, ALU.mult, ALU.bypass, [('c', 'lf')], [('c', 'lf')])
                pc = ps_acc()
                for j in range(NT):
                    mm(PS[pc][:, j * 16:(j + 1) * 16], tri_f, lf[:, j, :], True, j == 0, [('c', 'lf'), CG], [kps(pc)])
                    for i in range(j):
                        mm(PS[pc][:, j * 16:(j + 1) * 16], ones_f[:], lf[:, i, :], False, i == j - 1, [('c', 'lf'), ('c', 'onesf')], [kps(pc)])
                dve_copy(cum[:].rearrange("p j h -> p (j h)"), PS[pc][:, 0:256], [kps(pc)], [('c', 'cum')])
                pr = ps_acc()
                for qb in range(NTB):
                    n = 4 * qb + 4
                    for i in range(n):
                        mm(PS[pr][:, qb * 16:(qb + 1) * 16], ones_f[:], lf[:, i, :], i == 0, i == n - 1, [('c', 'lf'), ('c', 'onesf')], [kps(pr)])
                dve_copy(rcar[:].rearrange("p q h -> p (q h)"), PS[pr][:, 0:64], [kps(pr)], [('c', 'rcar')])
                for qb in range(NTB):
                    for j in range(4 * qb + 4):
                        dve_tt(fbtab[:, qb, j, :], rcar[:, qb, :], cum[:, j, :], ALU.subtract, [('c', 'rcar'), ('c', 'cum')], [('c', 'fbtab')])
                cum48 = WK[:, 0:2048].rearrange("p (j c) -> p j c", j=16)
                CT = WK[:, 4096:6144]
                CHL = wk_bf(7168, 2048)
                rec.op('pool', lambda e, cum48=cum48: e.memset(cum48, 0.0), [], [('WK', 'cum48')])
                dve_copy(cum48[:, :, 0:16], cum[:], [('c', 'cum'), ('WK', 'cum48')], [('WK', 'cum48')])
                dve_copy(cum48[:, :, 32:48], cum[:], [('c', 'cum'), ('WK', 'cum48')], [('WK', 'cum48')])
                dve_copy(cum48[:, :, 64:80], cum[:], [('c', 'cum'), ('WK', 'cum48')], [('WK', 'cum48')])
                dve_copy(cum48[:, :, 96:112], cum[:], [('c', 'cum'), ('WK', 'cum48')], [('WK', 'cum48')])
                for j4 in range(4):
                    pb = ps_mm()
                    for jj in range(4):
                        j = j4 * 4 + jj
                        rec.op('pe', lambda e, pb=pb, jj=jj, j=j, cum48=cum48: e.transpose(PS[pb][:, jj * 128:(jj + 1) * 128], cum48[:, j, :], ident),
                               [('WK', 'cum48'), CG], [kps(pb)])
                    dve_copy(CT[:, j4 * 512:(j4 + 1) * 512], PS[pb][:, :], [kps(pb)], [('WK', 'ct', j4)])
                for qb in range(NTB):
                    blk = slice(qb * 512, (qb + 1) * 512)
                    tm = TMP[rotate('tmp', 2)]
                    ktm = ('tmp', TMP.index(tm))
                    dve_ts(tm[:, :], CT[:, blk], CT[:, qb * 512 + 511:qb * 512 + 512], None, ALU.subtract, ALU.bypass, [('WK', 'ct', qb)], [ktm])
                    dve_copy(CHL[:, blk], tm[:, :], [ktm], [('WK', 'chl', qb)])
                    dve_tt(CHL[32:48, blk], tm[32:48, :], CHL[32:48, blk], ALU.subtract, [ktm, ('WK', 'chl', qb)], [('WK', 'chl', qb)])
                    dve_tt(CHL[96:112, blk], tm[96:112, :], CHL[96:112, blk], ALU.subtract, [ktm, ('WK', 'chl', qb)], [('WK', 'chl', qb)])
                rec.fence('WK')
                Vaug = wk_bf(5120, 4096).rearrange("p (j h d) -> p j h d", j=16, h=2)
                rec.op('pool', lambda e, Vaug=Vaug: e.memset(Vaug[:, :, :, 64:128], 1.0), [], [('WK', 'vones')])
                for hh in range(2):
                    rec.op('pool', lambda e, t=Kh[hh]: e.memset(t[64:128, :], 0.0), [], [('WK', 'kaug', hh)])
                    rec.op('pool', lambda e, t=Kh[hh]: e.memset(t[64:65, :], 1.0), [('WK', 'kaug', hh)], [('WK', 'kaug', hh)])
                    rec.op('pool', lambda e, t=Kh[hh]: e.memset(t[96:97, :], 1.0), [('WK', 'kaug', hh)], [('WK', 'kaug', hh)])
                nxt = wslot()
                def fox_w(slot, p):
                    par = (p // 2) % 2
                    wv = wview(slot, 0, 8, 384)
                    wo = wview(slot, 3072 + 1024 * par, 1, 1024)
                    dma('pool', [(wv[:, :, 0:128], wsrc_cols(w_in_c[o_], p * 128, 128)),
                                 (wv[:, :, 128:256], wsrc_cols(w_in_c[o_], 1024 + p * 128, 128)),
                                 (wv[:, :, 256:384], wsrc_cols(w_in_c[o_], 2048 + p * 128, 128)),
                                 (wo, wsrc_rows(w_out_c[o_], p * 128, 1))], [], [('W', (slot, 'v')), ('W', (slot, 'o', par))], ('W', slot))
                    return wv, wo, (slot, 'v'), (slot, 'o', par)
                rec.fence('W')
                wcur = fox_w(nxt, 0)
                fillF = Fill()
                for p in range(8):
                    wv, wo, slot, okey = wcur
                    if p + 1 < 8:
                        nxt = wslot()
                        wcur = fox_w(nxt, p + 1)
                    for tb in range(NTB):
                        sl = slice(tb * 512, (tb + 1) * 512)
                        pb = projT(wv, 0, tb, slot)
                        act(Qh[0][0:64, sl], PS[pb][0:64, :], AF.Copy, [kps(pb)], [('WK', 'q', 0, tb)], scale=0.125)
                        dve_ts(Qh[1][0:64, sl], PS[pb][64:128, :], 0.125, None, ALU.mult, ALU.bypass, [kps(pb)], [('WK', 'q', 1, tb)])
                        pb = projT(wv, 128, tb, slot)
                        for hh in range(2):
                            dve_copy(Kh[hh][0:64, sl], PS[pb][hh * 64:(hh + 1) * 64, :], [kps(pb)], [('WK', 'k', hh, tb)])
                        for hh in range(2):
                            pb = ps_mm()
                            mm(PS[pb][:], SEL[:, 2 * p + hh, :], CHL[0:64, sl], True, True, [('c', 'sel'), ('WK', 'chl', tb)], [kps(pb)])
                            dve_copy(Qh[hh][64:128, sl], PS[pb][64:128, :], [kps(pb)], [('WK', 'qaug', hh, tb)])
                    for j4 in range(4):
                        pb = ps_mm()
                        for jj in range(4):
                            j = j4 * 4 + jj
                            for kc in range(NCH):
                                mm(PS[pb][:, jj * 128:(jj + 1) * 128], xn[:, kc, j * 128:(j + 1) * 128], wv[:, kc, 256:384], kc == 0, kc == NCH - 1,
                                   [('W', slot), ('xn', j4, kc)], [kps(pb)])
                        act(Vaug[:, j4 * 4:(j4 + 1) * 4, :, 0:64], PS[pb][:].rearrange("p (j h d) -> p j h d", j=4, h=2), AF.Copy, [kps(pb), ('WK', 'vones')], [('WK', 'v', j4)])
                    tiles = []
                    for hh in range(2):
                        for qb in range(NTB):
                            nj = 4 * qb + 4
                            blk = {}
                            for j in range(nj):
                                def score(hh=hh, qb=qb, j=j, h=2 * p + hh):
                                    n0 = max(0, j - 4 * qb) * 128
                                    N = 512 - n0
                                    sb_ = ps_mm()
                                    mm(PS[sb_][:, 0:N], Kh[hh][:, j * 128:(j + 1) * 128], Qh[hh][:, qb * 512 + n0:(qb + 1) * 512], True, True,
                                       [('WK', 'k', hh, j // 4), ('WK', 'kaug', hh), ('WK', 'q', hh, qb), ('WK', 'qaug', hh, qb)], [kps(sb_)])
                                    pt = PT[rotate('pt', 4)]
                                    kpt = ('pt', id(pt))
                                    act(pt[:, 0:N], PS[sb_][:, 0:N], AF.Exp, [kps(sb_), ('c', 'fbtab')], [kpt], bias=fbtab[:, qb, j, h:h + 1])
                                    if j >= 4 * qb:
                                        dve_tt(pt[:, 0:128], pt[:, 0:128], tri_bf[:], ALU.mult, [kpt, ('c', 'tribf')], [kpt], eng='pool')
                                    return (n0, N, pt, kpt)

                                def pv(info, hh=hh, j=j, nj=nj, blk=blk):
                                    n0, N, pt, kpt = info
                                    mm(PS[blk['ob']][:, n0:512], Vaug[:, j, hh, :], pt[:, 0:N], j == 0, j == nj - 1, [('WK', 'v', j // 4), ('WK', 'vones'), kpt], [kps(blk['ob'])])
                                t = {'score': score, 'pv': pv}
                                if j == 0:
                                    t['first'] = lambda blk=blk: blk.__setitem__('ob', ps_acc())
                                if j == nj - 1:
                                    t['fin'] = lambda blk=blk, hh=hh, qb=qb: attn_finalize(blk['ob'], OTv[64 * hh:64 * hh + 64, qb * 512:(qb + 1) * 512], [('WK', 'ot', qb, hh)], on_dve=(qb == 1))
                                    t['tb'] = qb
                                tiles.append(t)
                    run_tiles(tiles, fillF)
                    fillF.flush()
                    outproj_fill(fillF, [wo], lambda i, tb: OTv[:, tb * 512:(tb + 1) * 512], lambda i, tb: [('WK', 'ot', tb, 0), ('WK', 'ot', tb, 1)], [okey])
                    fillF.flush(0)
                fillF.flush()
                rec.fence('W')
            else:
                e_ = l // 2
                if e_ == 0:
                    rec.op('pool', lambda e: e.memset(oml[:], 1.0), [], [('c', 'oml')])
                else:
                    dma('sp', [(TMP[0][:], lbl_d[0].partition_broadcast(128))], [], [('tmp', 0)], 'c1')
                    dma('sp', [(TMP[1][:], lbl_d[1].partition_broadcast(128))], [], [('tmp', 1)], 'c4')
                    act(TMP[0][:], TMP[0][:], AF.Exp, [('tmp', 0)], [('tmp', 0)])
                    act(TMP[1][:], TMP[1][:], AF.Exp, [('tmp', 1)], [('tmp', 1)])
                    dve_tt(TMP[0][:], TMP[0][:], TMP[1][:], ALU.add, [('tmp', 0), ('tmp', 1)], [('tmp', 0)])
                    dve_recip(TMP[0][:], TMP[0][:], [('tmp', 0)], [('tmp', 0)])
                    dve_tt(TMP[0][:], TMP[1][:], TMP[0][:], ALU.mult, [('tmp', 0), ('tmp', 1)], [('tmp', 0)])
                    dve_ts(oml[:], TMP[0][:], -1.0, 1.0, ALU.mult, ALU.add, [('tmp', 0)], [('c', 'oml')])
                Qh = [wk_bf(0, 2048), wk_bf(1024, 2048)]
                Kh = [wk_bf(2048, 2048), wk_bf(3072, 2048)]
                OTv = wk_bf(4096, 2048)
                Vaug = wk_bf(6144, 4096).rearrange("p (j h d) -> p j h d", j=16, h=2)
                EB = [wk_bf(5760, 640), EB1[:]]
                rec.op('pool', lambda e, Vaug=Vaug: e.memset(Vaug[:, :, :, 64:128], 1.0), [], [('WK', 'vones')])
                for hh in range(2):
                    rec.op('pool', lambda e, t=Kh[hh]: e.memset(t[64:128, :], 0.0), [], [('WK', 'kaug', hh)])
                    rec.op('pool', lambda e, t=Qh[hh]: e.memset(t[64:128, :], 0.0), [], [('WK', 'qaug', hh)])
                nxt = wslot()
                def b_w(slot, p):
                    par = (p // 2) % 2
                    wv = wview(slot, 0, 8, 384)
                    wo = wview(slot, 3072 + 1024 * par, 1, 1024)
                    dma('pool', [(wv[:, :, 0:128], wsrc_cols(w_in_ab[e_], 2048 + p * 128, 128)),
                                 (wv[:, :, 128:256], wsrc_cols(w_in_ab[e_], 2560 + p * 128, 128)),
                                 (wv[:, :, 256:384], wsrc_cols(w_in_ab[e_], 3072 + p * 128, 128)),
                                 (wo, wsrc_rows(w_out_ab[e_], 512 + p * 128, 1))], [], [('W', (slot, 'v')), ('W', (slot, 'o', par))], ('W', slot))
                    return wv, wo, (slot, 'v'), (slot, 'o', par)
                rec.fence('W')
                wcur = b_w(nxt, 0)
                fillB = Fill()
                for p in range(4):
                    wv, wo, slot, okey = wcur
                    if p + 1 < 4:
                        nxt = wslot()
                        wcur = b_w(nxt, p + 1)
                    for tb in range(NTB):
                        sl = slice(tb * 512, (tb + 1) * 512)
                        pb = projT(wv, 0, tb, slot)
                        for hh in range(2):
                            act(Qh[hh][0:64, sl], PS[pb][hh * 64:(hh + 1) * 64, :], AF.Copy, [kps(pb)], [('WK', 'q', hh, tb)], scale=0.125)
                        pb = projT(wv, 128, tb, slot)
                        for hh in range(2):
                            dve_copy(Kh[hh][0:64, sl], PS[pb][hh * 64:(hh + 1) * 64, :], [kps(pb)], [('WK', 'k', hh, tb)])
                    for j4 in range(4):
                        pb = ps_mm()
                        for jj in range(4):
                            j = j4 * 4 + jj
                            for kc in range(NCH):
                                mm(PS[pb][:, jj * 128:(jj + 1) * 128], xn[:, kc, j * 128:(j + 1) * 128], wv[:, kc, 256:384], kc == 0, kc == NCH - 1,
                                   [('W', slot), ('xn', j4, kc)], [kps(pb)])
                        act(Vaug[:, j4 * 4:(j4 + 1) * 4, :, 0:64], PS[pb][:].rearrange("p (j h d) -> p j h d", j=4, h=2), AF.Copy, [kps(pb), ('WK', 'vones')], [('WK', 'v', j4)])
                    tiles = []
                    for hh in range(2):
                        h = 2 * p + hh
                        bh = BH[0]
                        kbh = ('WK', 'bh', 0)
                        eb = EB[hh]
                        keb = ('WK', 'eb', hh)
                        dma('sp', [(bh, relb_d[e_, h])], [], [kbh], ('bh', 0))
                        rec.op('pool', lambda e, bh=bh: e.memset(bh[0:64, 576:640], NEG), [kbh], [kbh])
                        rec.op('pool', lambda e, bh=bh: e.memset(bh[64:128, 0:64], NEG), [kbh], [kbh])
                        act(eb, bh, AF.Exp, [kbh], [keb])
                        for qb in range(NTB):
                            ds = [0] + [d for d in (-512, -384, -256, -128, 128, 256, 384) if 0 <= qb * 512 + d < T]
                            nd = len(ds)
                            blk = {}
                            for ii, d in enumerate(ds):
                                def score(hh=hh, qb=qb, d=d, eb=eb, keb=keb):
                                    kb = qb * 512 + d
                                    j = kb // 128
                                    lo = max(0, d)
                                    hi = 512 if d >= -128 else d + 640
                                    N = hi - lo
                                    sb_ = ps_mm()
                                    mm(PS[sb_][:, 0:N], Kh[hh][:, kb:kb + 128], Qh[hh][:, qb * 512 + lo:qb * 512 + hi], True, True,
                                       [('WK', 'k', hh, j // 4), ('WK', 'kaug', hh), ('WK', 'q', hh, qb), ('WK', 'qaug', hh)], [kps(sb_)])
                                    pt = PT[rotate('pt', 4)]
                                    kpt = ('pt', id(pt))
                                    act(pt[:, 0:N], PS[sb_][:, 0:N], AF.Exp, [kps(sb_)], [kpt])
                                    dve_tt(pt[:, 0:N], pt[:, 0:N], eb[:, lo - d:hi - d], ALU.mult, [kpt, keb], [kpt])
                                    return (j, lo, hi, N, pt, kpt)

                                def pv(info, hh=hh, ii=ii, nd=nd, blk=blk):
                                    j, lo, hi, N, pt, kpt = info
                                    mm(PS[blk['ob']][:, lo:hi], Vaug[:, j, hh, :], pt[:, 0:N], ii == 0, ii == nd - 1, [('WK', 'v', j // 4), ('WK', 'vones'), kpt], [kps(blk['ob'])])
                                t = {'score': score, 'pv': pv}
                                if ii == 0:
                                    t['first'] = lambda blk=blk: blk.__setitem__('ob', ps_acc())
                                if ii == nd - 1:
                                    t['fin'] = lambda blk=blk, hh=hh, qb=qb: attn_finalize(blk['ob'], OTv[64 * hh:64 * hh + 64, qb * 512:(qb + 1) * 512], [('WK', 'ot', qb, hh)])
                                    t['tb'] = qb
                                tiles.append(t)
                    run_tiles(tiles, fillB)
                    fillB.flush()
                    outproj_fill(fillB, [wo], lambda i, tb: OTv[:, tb * 512:(tb + 1) * 512], lambda i, tb: [('WK', 'ot', tb, 0), ('WK', 'ot', tb, 1)], [okey])
                    fillB.flush(0)
                fillB.flush()
                rec.fence('W')

                rec.fence('WK')

                def a_views(s_):
                    o = s_ * 3072
                    return dict(QA=WK[:, o:o + 512], GT=wk_bf(o + 512, 512),
                                KA=WK[:, o + 768:o + 1280].rearrange("p (j d) -> p j d", j=4),
                                LF=WK[:, o + 1280:o + 1792].rearrange("p (j d) -> p j d", j=4),
                                VA=wk_bf(o + 1792, 512).rearrange("p (j d) -> p j d", j=4),
                                OTA=wk_bf(o + 2048, 2048))

                def a_w(slot, h):
                    wv = wview(slot, 0, 8, 512)
                    wo = wview(slot, 4096, 1, 1024)
                    load_w(slot, [(wv[:, :, i * 128:(i + 1) * 128], wsrc_cols(w_in_ab[e_], i * 512 + h * 128, 128)) for i in range(4)]
                           + [(wo, wsrc_rows(w_out_ab[e_], h * 128, 1))])
                    return wv, wo

                def a_head_gen(h, s_, slot, wv):
                    V = a_views(s_)
                    QA, GT, KA, LF, VA, OTA = V['QA'], V['GT'], V['KA'], V['LF'], V['VA'], V['OTA']
                    W_ = lambda n, *a: ('WK', n, s_) + a
                    hs = slice(h * 128, (h + 1) * 128)
                    er, e0, ep, em, mp, mn = ER[s_], E0[s_], EP[s_], EM[s_], MP[s_], MN[s_]
                    qi, qh, kh, ktc, am = QI[s_], QH[s_], KH[s_], KTC[s_], AM[s_]
                    sf = SFS[s_]
                    sbs = [SB[2 * s_], SB[2 * s_ + 1]]
                    K = lambda n: ('a', n, s_)
                    st = {'first': True, 'sb': 0}
                    for tb in range(NTB):
                        sl = slice(tb * 512, (tb + 1) * 512)
                        pb = projT(wv, 0, tb, slot)
                        sg = TMP[s_]
                        ksg = ('tmp', s_)
                        act(sg[:], PS[pb][:], AF.Sigmoid, [kps(pb)], [ksg])
                        dve_stt(QA, PS[pb][:], float(128 ** -0.5), sg[:], ALU.mult, ALU.mult, [kps(pb), ksg], [W_('qa')])
                        yield
                        pb = projT(wv, 384, tb, slot)
                        act(sg[:], PS[pb][:], AF.Sigmoid, [kps(pb)], [ksg])
                        dve_tt(GT, PS[pb][:], sg[:], ALU.mult, [kps(pb), ksg], [W_('gt')])
                        yield
                        pv = ps_mm()
                        for jj in range(4):
                            j = tb * 4 + jj
                            for kc in range(NCH):
                                mm(PS[pv][:, jj * 128:(jj + 1) * 128], xn[:, kc, j * 128:(j + 1) * 128], wv[:, kc, 256:384], kc == 0, kc == NCH - 1,
                                   [('W', slot), ('xn', tb, kc)], [kps(pv)])
                        act(VA, PS[pv][:].rearrange("p (j d) -> p j d", j=4), AF.Copy, [kps(pv)], [W_('va')])
                        yield
                        pz = ps_mm()
                        for jj in range(4):
                            j = tb * 4 + jj
                            for kc in range(NCH):
                                mm(PS[pz][:, jj * 128:(jj + 1) * 128], xn[:, kc, j * 128:(j + 1) * 128], wv[:, kc, 128:256], kc == 0, kc == NCH - 1,
                                   [('W', slot), ('xn', tb, kc)], [kps(pz)])
                        act(sg[:], PS[pz][:], AF.Sigmoid, [kps(pz)], [ksg], scale=-1.0)
                        for jj in range(4):
                            dve_tt(KA[:, jj, :], sg[:, jj * 128:(jj + 1) * 128], oml[:, hs], ALU.mult, [ksg, ('c', 'oml')], [W_('ka')])
                        act(LF, KA, AF.Ln, [W_('ka')], [W_('lf')], bias=1.0, scale=-1.0)
                        yield
                        po = 4 + s_
                        for jj in range(4):
                            ts = slice(jj * 128, (jj + 1) * 128)
                            pbT = ps_mm()
                            mm(PS[pbT][:, 0:128], LF[:, jj, :], maskBD, True, True, [W_('lf'), CG], [kps(pbT)])
                            act(e0[:], PS[pbT][:, 0:128], AF.Exp, [kps(pbT)], [K('e0')])
                            dve_copy(mp[:, 0:NCK], PS[pbT][:, 0:128].rearrange("p (c t) -> p c t", c=NCK)[:, :, CH // 2 - 1], [kps(pbT), K('e0')], [K('mp')])
                            dve_ts(mn[:, 0:NCK], mp[:, 0:NCK], -1.0, None, ALU.mult, ALU.bypass, [K('mp')], [K('mn')])
                            for c in range(NCK):
                                cs = slice(c * CH, (c + 1) * CH)
                                act(ep[:, cs], PS[pbT][:, cs], AF.Exp, [kps(pbT), K('mn')], [K('ep')], bias=mn[:, c:c + 1])
                                act(em[:, cs], PS[pbT][:, cs], AF.Exp, [kps(pbT), K('mp')], [K('em')], bias=mp[:, c:c + 1], scale=-1.0)
                            yield
                            prv = ps_mm()
                            mm(PS[prv][:, 0:128], triRev, LF[:, jj, :], True, True, [W_('lf'), CG], [kps(prv)])
                            act(er[:], PS[prv][:, 0:128], AF.Exp, [kps(prv)], [K('er')])
                            for c in range(NCK):
                                dve_stt(ktc[:, c, :], KA[:, jj, :], cmask[:, c:c + 1], er[:], ALU.mult, ALU.mult, [W_('ka'), K('er'), CG], [K('ktc')])
                            yield
                            pkT = ps_mm()
                            rec.op('pe', lambda e, pkT=pkT, jj=jj, KA=KA: e.transpose(PS[pkT][:, 0:128], KA[:, jj, :], ident), [W_('ka'), CG], [kps(pkT)])
                            dve_tt(kh[:], PS[pkT][:, 0:128], em[:], ALU.mult, [kps(pkT), K('em')], [K('kh')])
                            dve_tt(qi[:], QA[:, ts], e0[:], ALU.mult, [W_('qa'), K('e0')], [K('qi')])
                            dve_tt(qh[:], QA[:, ts], ep[:], ALU.mult, [W_('qa'), K('ep')], [K('qh')])
                            yield
                            pu = 6 + s_
                            for c in range(NCK):
                                mm(PS[pu][:, c * 128:(c + 1) * 128], ktc[:, c, :], VA[:, jj, :], True, True, [K('ktc'), W_('va')], [kps(pu)])
                            pA = ps_mm()
                            mm(PS[pA][:, 0:128], kh[:], qh[:], True, True, [K('kh'), K('qh')], [kps(pA)])
                            dve_tt(am[:], PS[pA][:, 0:128], maskBD, ALU.mult, [kps(pA), CG], [K('am')])
                            mm(PS[po][:, ts], VA[:, jj, :], am[:], True, False, [W_('va'), K('am')], [kps(po)])
                            yield
                            for c in range(NCK):
                                cs = slice(c * CH, (c + 1) * CH)
                                if not st['first']:
                                    sbc = sbs[st['sb'] % 2]
                                    mm(PS[po][:, jj * 128 + c * CH:jj * 128 + (c + 1) * CH], sbc[:], qi[:, cs], False, c == NCK - 1, [('sbf', id(sbc)), K('qi')], [kps(po)])
                                if st['first']:
                                    dve_copy(sf[:], PS[pu][:, c * 128:(c + 1) * 128], [kps(pu)], [K('sf')])
                                else:
                                    dve_stt(sf[:], sf[:], e0[:, c * CH + CH - 1:c * CH + CH], PS[pu][:, c * 128:(c + 1) * 128], ALU.mult, ALU.add, [K('sf'), K('e0'), kps(pu)], [K('sf')])
                                st['sb'] += 1
                                sbn = sbs[st['sb'] % 2]
                                dve_copy(sbn[:], sf[:], [K('sf')], [('sbf', id(sbn))])
                                st['first'] = False
                                yield
                        q = SQ[s_]
                        act(q[:], PS[po][:], AF.Square, [kps(po)], [('sq', id(q))])
                        pn = ps_mm()
                        mm(PS[pn][:], ones_bf[:], q[:], True, True, [('sq', id(q)), ('c', 'ones')], [kps(pn)])
                        rs = RS[s_]
                        rstd_from_ps(pn, 512, rs, [], 1.0 / 128)
                        on = REC[s_]
                        dve_stt(on[:], PS[po][:], again[:, e_ * 4 + h:e_ * 4 + h + 1], rs[:], ALU.mult, ALU.mult, [kps(po), ('rs', id(rs)), CG], [('rec', id(on))])
                        dve_tt(OTA[:, sl], on[:], GT, ALU.mult, [('rec', id(on)), W_('gt')], [W_('ota', tb)])
                        yield

                for hp in range(2):
                    heads = (2 * hp, 2 * hp + 1)
                    ws = [a_w(s_, heads[s_]) for s_ in range(2)]
                    gens = [a_head_gen(heads[s_], s_, s_, ws[s_][0]) for s_ in range(2)]
                    alive = [True, True]
                    for _ in range(AOFF):
                        next(gens[0])
                    if DBG == 2:
                        for s_ in range(2):
                            for _ in gens[s_]:
                                pass
                        alive = [False, False]
                    if KSTOP > 0:
                        for s_ in range(2):
                            for _i in range(KSTOP):
                                next(gens[s_])
                        alive = [False, False]
                    while any(alive):
                        for s_ in range(2):
                            if alive[s_]:
                                try:
                                    next(gens[s_])
                                except StopIteration:
                                    alive[s_] = False
                    OT2 = [a_views(0)['OTA'], a_views(1)['OTA']]
                    wos = [ws[0][1], ws[1][1]]
                    for tb in range(NTB):
                        for oc in range(NCH):
                            sl = slice(tb * 512, (tb + 1) * 512)
                            if DBG == 3:
                                for s_ in range(2):
                                    pb = ps_mm()
                                    mm(PS[pb][:], wos[s_][:, 0, oc * 128:(oc + 1) * 128], OT2[s_][:, tb * 512:(tb + 1) * 512], True, True,
                                       [('W', s_), ('WK', 'ota', s_, tb)], [kps(pb)])
                                    dve_tt(hT[:, oc, sl], hT[:, oc, sl], PS[pb][:], ALU.add, [('hT', oc, tb), kps(pb)], [('hT', oc, tb)])
                                continue
                            pb = ps_mm()
                            for s_ in range(2):
                                mm(PS[pb][:], wos[s_][:, 0, oc * 128:(oc + 1) * 128], OT2[s_][:, tb * 512:(tb + 1) * 512], s_ == 0, s_ == 1,
                                   [('W', s_), ('WK', 'ota', s_, tb)], [kps(pb)])
                            dve_tt(hT[:, oc, sl], hT[:, oc, sl], PS[pb][:], ALU.add, [('hT', oc, tb), kps(pb)], [('hT', oc, tb)])
                wstate['i'] = 1

            rec.fence('WK')
            QX = wk_bf(0, 4096).rearrange("p (c t) -> p c t", c=2)
            OX = wk_bf(2048, 4096).rearrange("p (c t) -> p c t", c=2)
            MT = WK[:, 4096:6144].rearrange("p (c t) -> p c t", c=8)
            MN_ = wk_bf(6144, 2048).rearrange("p (c t) -> p c t", c=8)
            KX = wk_bf(7168, 512).rearrange("p (c t) -> p c t", c=2)
            VX = wk_bf(7424, 512).rearrange("p (c t) -> p c t", c=2)
            dma('sp', [(MT[:, c, :], memT_d[b, c * 128:(c + 1) * 128, :]) for c in range(NCH)], [], [('WK', 'mt')], 'mt')
            pb = ps_acc()
            for c in range(NCH):
                q = SQ[rotate('sq', 2)]
                act(q[:, 0:256], MT[:, c, :], AF.Square, [('WK', 'mt')], [('sq', id(q))])
                mm(PS[pb][:, 0:256], ones_bf[:], q[:, 0:256], c == 0, c == NCH - 1, [('sq', id(q)), ('c', 'ones')], [kps(pb)])
            rs = RS[rotate('rs', 2)]
            rstd_from_ps(pb, 256, rs, [], 1.0 / D)
            for c in range(NCH):
                dve_stt(MN_[:, c, :], MT[:, c, :], gains[:, (8 + l) * 8 + c:(8 + l) * 8 + c + 1], rs[:, 0:256], ALU.mult, ALU.mult,
                        [('WK', 'mt'), ('rs', id(rs)), CG], [('WK', 'mn')])
            norm_to_xn(4 + l)
            nxt = wslot()
            def x_w(slot, h):
                wq = wview(slot, 0, 8, 256)
                wkv = wview(slot, 2048, 8, 512)
                wo = wview(slot, 6144, 2, 1024)
                load_w(slot, [(wq, wsrc_cols(w_xq[l], h * 256, 256)),
                              (wkv[:, :, 0:256], wsrc_cols(w_xkv[l], h * 256, 256)),
                              (wkv[:, :, 256:512], wsrc_cols(w_xkv[l], 1024 + h * 256, 256)),
                              (wo, wsrc_rows(w_xo[l], h * 256, 2))])
                return wq, wkv, wo
            wcur = x_w(nxt, 0)
            for h in range(4):
                slot = nxt
                wq, wkv, wo = wcur
                if h + 1 < 4:
                    nxt = wslot()
                    wcur = x_w(nxt, h + 1)
                for dc in range(2):
                    pb = ps_mm()
                    for kc in range(NCH):
                        mm(PS[pb][:, 0:256], wkv[:, kc, dc * 128:(dc + 1) * 128], MN_[:, kc, :], kc == 0, kc == NCH - 1, [('W', slot), ('WK', 'mn')], [kps(pb)])
                    act(KX[:, dc, :], PS[pb][:, 0:256], AF.Copy, [kps(pb)], [('WK', 'kx')], scale=1.0 / 16)
                for mt in range(2):
                    pb = ps_mm()
                    for kc in range(NCH):
                        mm(PS[pb][:, 0:256], MN_[:, kc, mt * 128:(mt + 1) * 128], wkv[:, kc, 256:512], kc == 0, kc == NCH - 1, [('W', slot), ('WK', 'mn')], [kps(pb)])
                    dve_copy(VX[:, mt, :], PS[pb][:, 0:256], [kps(pb)], [('WK', 'vx')])
                for tb in range(NTB):
                    sl = slice(tb * 512, (tb + 1) * 512)
                    for dc in range(2):
                        pb = projT(wq, dc * 128, tb, slot)
                        if dc == 0:
                            act(QX[:, dc, sl], PS[pb][:], AF.Copy, [kps(pb)], [('WK', 'qx', tb, dc)])
                        else:
                            dve_copy(QX[:, dc, sl], PS[pb][:], [kps(pb)], [('WK', 'qx', tb, dc)])
                for qb in range(NTB):
                    sl = slice(qb * 512, (qb + 1) * 512)
                    pts = []
                    for mt in range(2):
                        sb_ = ps_mm()
                        for dc in range(2):
                            mm(PS[sb_][:], KX[:, dc, mt * 128:(mt + 1) * 128], QX[:, dc, sl], dc == 0, dc == 1, [('WK', 'kx'), ('WK', 'qx', qb, dc)], [kps(sb_)])
                        pt = PT[rotate('pt', 4)]
                        act(pt[:], PS[sb_][:], AF.Exp, [kps(sb_)], [('pt', id(pt))])
                        pts.append(pt)
                    db = ps_acc()
                    for mt in range(2):
                        mm(PS[db][:], ones_bf[:], pts[mt][:], mt == 0, mt == 1, [('c', 'ones'), ('pt', id(pts[mt]))], [kps(db)])
                    rc = REC[rotate('rec', 2)]
                    act_recip(rc[:], PS[db][:], [kps(db)], [('rec', id(rc))])
                    for dvc in range(2):
                        ob = ps_acc()
                        for mt in range(2):
                            mm(PS[ob][:], VX[:, mt, dvc * 128:(dvc + 1) * 128], pts[mt][:], mt == 0, mt == 1, [('WK', 'vx'), ('pt', id(pts[mt]))], [kps(ob)])
                        dve_tt(OX[:, dvc, sl], PS[ob][:], rc[:], ALU.mult, [kps(ob), ('rec', id(rc))], [('WK', 'ox', qb, dvc)])
                outproj_acc(wo, 2, lambda kc, tb: OX[:, kc, tb * 512:(tb + 1) * 512], lambda kc, tb: [('WK', 'ox', tb, kc)], slot)

            rec.fence('WK')
            norm_to_xn(12 + l)
            nxt = wslot()
            def m_w(slot, g):
                wu = wview(slot, 0, 8, 512)
                wd = wview(slot, 4096, 4, 1024)
                load_w(slot, [(wu, wsrc_cols(w_up[l], g * 512, 512)), (wd, wsrc_rows(w_down[l], g * 512, 4))])
                return wu, wd
            wcur = m_w(nxt, 0)
            for g in range(8):
                slot = nxt
                wu, wd = wcur
                if g + 1 < 8:
                    nxt = wslot()
                    wcur = m_w(nxt, g + 1)
                hb = g % 2
                HFF = wk_bf(hb * 4096, 8192).rearrange("p (c t) -> p c t", c=4)
                for tb in range(NTB):
                    sl = slice(tb * 512, (tb + 1) * 512)
                    for fc in range(4):
                        pb = projT(wu, fc * 128, tb, slot)
                        tm = PT[rotate('pt', 4)]
                        ktm = ('pt', id(tm))
                        act(tm[:], PS[pb][:], AF.Relu, [kps(pb)], [ktm])
                        dve_tt(HFF[:, fc, sl], tm[:], tm[:], ALU.mult, [ktm], [('WK', 'hff', hb, tb, fc)], eng='pool')
                outproj_acc(wd, 4, lambda kc, tb, HFF=HFF: HFF[:, kc, tb * 512:(tb + 1) * 512], lambda kc, tb, hb=hb: [('WK', 'hff', hb, tb, kc)], slot)

        rec.fence('WK')
        OUTS = [WK[:, 0:4096].rearrange("p (c t) -> p c t", c=8), WK[:, 4096:8192].rearrange("p (c t) -> p c t", c=8)]
        out_v = outT_d[b].rearrange("(c p) t -> p c t", p=128)

        def store(tb, b=b, OUTS=OUTS, out_v=out_v):
            dma('sp', [(out_v[:, :, tb * 512:(tb + 1) * 512], OUTS[tb % 2][:])], [('WK', 'out', tb % 2, c) for c in range(NCH)],
                [('od', b, tb)], ('st', tb % 2))
        if final_norm:
            norm_h(16, lambda c, tb: OUTS[tb % 2][:, c, :], lambda c, tb: [('WK', 'out', tb % 2, c)], after_block=store)
        else:
            for tb in range(NTB):
                for c in range(NCH):
                    dve_copy(OUTS[tb % 2][:, c, :], hT[:, c, tb * 512:(tb + 1) * 512], [('hT', c, tb)], [('WK', 'out', tb % 2, c)])
                store(tb)
    rec.op('sp', None, [('od', b, tb) for b in range(nbatch) for tb in range(NTB)], [])

    rec.plan()
    sems = {}
    for e in Rec.ENGS:
        for ep_ in range(max(1, rec.nepoch[e])):
            sems[(e, ep_)] = es.enter_context(nc.semaphore(f"s_{e}_{ep_}"))
    dsems = {}
    for i, k in enumerate(rec.dma_cnt):
        dsems[k] = es.enter_context(nc.semaphore(f"d_{i}"))
    block = es.enter_context(nc.Block())

    def make_body(eng):
        ops = rec.eng_ops[eng]

        def body(e):
            for o in ops:
                for Dd in o.w_eng:
                    e.wait_ge(sems[(Dd.eng, Dd.epoch)], Dd.sigval)
                for (k, t) in o.w_dma:
                    e.wait_ge(dsems[k], t)
                if o.fn is None:
                    continue
                if o.dma:
                    for ins in o.fn(e):
                        ins.then_inc(dsems[o.dsemkey], 16)
                else:
                    ins = o.fn(e)
                    if o.signal:
                        ins.then_inc(sems[(eng, o.epoch)], 1)
        return body

    block.sync(make_body('sp'))
    block.tensor(make_body('pe'))
    block.scalar(make_body('act'))
    block.vector(make_body('dve'))
    block.gpsimd(make_body('pool'))
    es.close()
    return nc, rec


def _consts():
    c = np.zeros((128, 516), np.float32)
    i = np.arange(128)
    c[:, 0:128] = np.eye(128, dtype=np.float32)
    c[:, 128:256] = (i[:, None] <= i[None, :])
    same = (i[:, None] // CH) == (i[None, :] // CH)
    c[:, 256:384] = same & (i[:, None] <= i[None, :])
    c[:, 384:512] = same & (i[:, None] > i[None, :])
    c[:, 512:516] = (i[:, None] // CH) == np.arange(4)[None, :]
    return c


def _prep_shared(inp):
    f = lambda a: np.ascontiguousarray(np.asarray(a, dtype=np.float32))
    g = np.concatenate([f(inp['norm_mix']), f(inp['norm_xattn']), f(inp['norm_mem']), f(inp['norm_mlp']),
                        f(inp['norm_final'])[None, :]], axis=0)
    gains = np.ascontiguousarray(g.reshape(17, 8, 128).transpose(2, 0, 1).reshape(128, 136))
    again = np.ascontiguousarray(f(inp['a_out_gain']).reshape(2, 4, 128).transpose(2, 0, 1).reshape(128, 8))
    sidx = np.arange(128)[:, None]
    uidx = np.arange(640)[None, :]
    ridx = np.clip(uidx - sidx, -128, 128) + 128
    relb = np.ascontiguousarray(f(inp['b_rel_bias'])[:, :, ridx])
    selc = np.zeros((64, 16, 128), np.float32)
    for hh_ in range(16):
        selc[hh_, hh_, 64] = 1.0
        selc[32 + hh_, hh_, 96] = 1.0
    sh = dict(selc=selc.reshape(64, 2048), gains=gains, again=again, lbl=f(inp['a_lb_logits']), fbias=f(inp['c_fgate_bias']), relb=relb, consts=_consts())
    for k in ('w_in_ab', 'w_out_ab', 'w_in_c', 'w_out_c', 'w_xq', 'w_xkv', 'w_xo', 'w_up', 'w_down'):
        sh[k] = f(inp[k])
    return sh


_CACHE = {}


def kernel(**inp):
    ncores = 8
    x = np.asarray(inp['x'], dtype=np.float32)
    mem = np.asarray(inp['mem'], dtype=np.float32)
    sh = _prep_shared(inp)
    in_maps = []
    for c in range(ncores):
        m = dict(sh)
        m['xT'] = np.ascontiguousarray(x[2 * c:2 * c + 2].transpose(0, 2, 1))
        m['memT'] = np.ascontiguousarray(mem[2 * c:2 * c + 2].transpose(0, 2, 1))
        in_maps.append(m)
    if 'nc' not in _CACHE:
        _CACHE['nc'] = build()[0]
    res = run_bass_kernel_spmd(_CACHE['nc'], in_maps, core_ids=list(range(ncores)))
    out = np.empty((16, T, D), np.float32)
    for c in range(ncores):
        out[2 * c:2 * c + 2] = res.results[c]['outT'].transpose(0, 2, 1)
    return out
```

```python
import os
import numpy as np
from contextlib import ExitStack
import concourse.bass as bass
import concourse.mybir as mybir
from concourse.bass_utils import run_bass_kernel_spmd

F32 = mybir.dt.float32
BF16 = mybir.dt.bfloat16
ALU = mybir.AluOpType
AF = mybir.ActivationFunctionType

D = 1024
T = 2048
NCH = 8
NTB = 4
NT = 16
MEM = 256
EPS = 1e-6
EPOCH = 20000
NEG = -30000.0
CH = 64
NCK = 128 // CH
AOFF = int(os.environ.get('KAOFF', '14'))
DBG = int(os.environ.get('KDBG', '0'))
KSTOP = int(os.environ.get('KSTOP', '0'))


class Op:
    pass


class Rec:
    ENGS = ('sp', 'pe', 'act', 'dve', 'pool')

    def __init__(s):
        s.ops = []
        s.lastw = {}
        s.rd_eng = {}
        s.rd_dma = {}
        s.eng_ops = {e: [] for e in s.ENGS}
        s.dma_cnt = {}
        s.fdeps = {}

    def fence(s, region):
        deps = set(s.fdeps.get(region, ()))
        for k in list(s.lastw):
            if k[0] == region:
                deps.add(s.lastw.pop(k))
        for k in list(s.rd_eng):
            if k[0] == region:
                deps.update(s.rd_eng.pop(k).values())
        for k in list(s.rd_dma):
            if k[0] == region:
                deps.update(s.rd_dma.pop(k))
        best = {}
        for di in deps:
            o = s.ops[di]
            kk = ('d', o.dsemkey) if o.dma else ('e', o.eng)
            if kk not in best or best[kk] < di:
                best[kk] = di
        s.fdeps[region] = set(best.values())

    def op(s, eng, fn, r=(), w=(), dma=None):
        o = Op()
        o.eng = eng
        o.fn = fn
        o.idx = len(s.ops)
        o.pos = len(s.eng_ops[eng])
        o.signal = False
        o.dma = dma is not None
        deps = set()
        raw = set()
        for k in r:
            d = s.lastw.get(k)
            if d is not None:
                deps.add(d)
                raw.add(d)
            elif k[0] in s.fdeps:
                deps.update(s.fdeps[k[0]])
                raw.update(s.fdeps[k[0]])
            if k[0] == 'ps':
                for e2, d2 in s.rd_eng.get(k, {}).items():
                    if e2 != eng:
                        deps.add(d2)
        for k in w:
            d = s.lastw.get(k)
            if d is not None:
                deps.add(d)
            elif k[0] in s.fdeps:
                deps.update(s.fdeps[k[0]])
                raw.update(s.fdeps[k[0]])
            deps.update(s.rd_eng.get(k, {}).values())
            deps.update(s.rd_dma.get(k, ()))
        for k in r:
            if o.dma:
                s.rd_dma.setdefault(k, []).append(o.idx)
            else:
                s.rd_eng.setdefault(k, {})[eng] = o.idx
        for k in w:
            s.lastw[k] = o.idx
            s.rd_eng[k] = {}
            s.rd_dma[k] = []
        o.deps = deps
        o.raw = raw
        if o.dma:
            semkey, n = dma
            s.dma_cnt[semkey] = s.dma_cnt.get(semkey, 0) + n
            o.dsemkey = semkey
            o.dtarget = 16 * s.dma_cnt[semkey]
        s.ops.append(o)
        s.eng_ops[eng].append(o)
        return o

    def plan(s):
        waited = {e: {f: -1 for f in s.ENGS} for e in s.ENGS}
        waited_dma = {e: {} for e in s.ENGS}
        for o in s.ops:
            o.w_eng = []
            o.w_dma = []
            best = {}
            bestd = {}
            for di in o.deps:
                Dd = s.ops[di]
                if Dd.dma:
                    if waited_dma[o.eng].get(Dd.dsemkey, 0) >= Dd.dtarget:
                        continue
                    if bestd.get(Dd.dsemkey, 0) < Dd.dtarget:
                        bestd[Dd.dsemkey] = Dd.dtarget
                else:
                    if Dd.eng == o.eng and (o.eng == 'pe' or di not in o.raw):
                        continue
                    if Dd.pos <= waited[o.eng][Dd.eng]:
                        continue
                    if Dd.eng not in best or best[Dd.eng].pos < Dd.pos:
                        best[Dd.eng] = Dd
            for f, Dd in best.items():
                Dd.signal = True
                waited[o.eng][f] = Dd.pos
                o.w_eng.append(Dd)
            for k, t in bestd.items():
                waited_dma[o.eng][k] = t
                o.w_dma.append((k, t))
        s.nepoch = {}
        for e in s.ENGS:
            c = 0
            for o in s.eng_ops[e]:
                if o.signal:
                    o.epoch = c // EPOCH
                    o.sigval = c % EPOCH + 1
                    c += 1
            s.nepoch[e] = (c + EPOCH - 1) // EPOCH


def build(nbatch=2, layers=(0, 1, 2, 3), final_norm=True):
    nc = bass.Bass("TRN2", target_bir_lowering=False)
    rec = Rec()

    def din(name, shape):
        return nc.dram_tensor(name, list(shape), F32, kind="ExternalInput").ap()

    xT_d = din("xT", [2, D, T])
    memT_d = din("memT", [2, D, MEM])
    gains_d = din("gains", [128, 136])
    again_d = din("again", [128, 8])
    lbl_d = din("lbl", [2, 512])
    fbias_d = din("fbias", [2, 16])
    relb_d = din("relb", [2, 8, 128, 640])
    consts_d = din("consts", [128, 516])
    selc_d = din("selc", [64, 2048])
    w_in_ab = din("w_in_ab", [2, D, 3584])
    w_out_ab = din("w_out_ab", [2, D, D])
    w_in_c = din("w_in_c", [2, D, 3088])
    w_out_c = din("w_out_c", [2, D, D])
    w_xq = din("w_xq", [4, D, D])
    w_xkv = din("w_xkv", [4, D, 2 * D])
    w_xo = din("w_xo", [4, D, D])
    w_up = din("w_up", [4, D, 4 * D])
    w_down = din("w_down", [4, 4 * D, D])
    outT_d = nc.dram_tensor("outT", [2, D, T], F32, kind="ExternalOutput").ap()

    es = ExitStack()

    def sb(name, shape, dt):
        return es.enter_context(nc.sbuf_tensor(name, list(shape), dt))

    hT = sb("hT", [128, NCH, T], F32)
    xn = sb("xn", [128, NCH, T], BF16)
    Wt = [sb("W0", [128, 8192], BF16), sb("W1", [128, 8192], BF16)]
    WK = sb("WK", [128, 8192], F32)
    gains = sb("gains_s", [128, 136], F32)
    again = sb("again_s", [128, 8], F32)
    consts = sb("consts_s", [128, 516], F32)
    tri_bf = sb("tri_bf", [128, 128], BF16)
    ones_bf = sb("ones_bf", [128, 128], BF16)
    ones_f = sb("ones_f", [128, 128], F32)
    SEL = sb("sel", [64, 16, 128], BF16)
    oml = sb("oml", [128, 512], F32)
    fbt = sb("fbt", [128, 16], F32)
    lf = sb("lf", [128, 16, 16], F32)
    cum = sb("cum", [128, 16, 16], F32)
    zf = cum
    rcar = sb("rcar", [128, 4, 16], F32)
    fbtab = sb("fbtab", [128, 4, 16, 16], F32)
    SQ = [sb(f"sq{i}", [128, 512], BF16) for i in range(2)]
    RS = [sb(f"rs{i}", [128, 512], F32) for i in range(2)]
    PT = [sb(f"pt{i}", [128, 512], BF16) for i in range(4)]
    TMP = [sb(f"tmp{i}", [128, 512], F32) for i in range(2)]
    REC = [sb(f"rec{i}", [128, 512], F32) for i in range(2)]
    BH = [WK[:, 5120:5760]]
    ER = [sb(f"er{i}", [128, 128], F32) for i in range(2)]
    E0 = [sb(f"e0{i}", [128, 128], F32) for i in range(2)]
    EP = [sb(f"ep{i}", [128, 128], F32) for i in range(2)]
    EM = [sb(f"em{i}", [128, 128], F32) for i in range(2)]
    MP = [sb(f"mp{i}", [128, 4], F32) for i in range(2)]
    MN = [sb(f"mn{i}", [128, 4], F32) for i in range(2)]
    QI = [sb(f"qi{i}", [128, 128], BF16) for i in range(2)]
    QH = [sb(f"qh{i}", [128, 128], BF16) for i in range(2)]
    KH = [sb(f"kh{i}", [128, 128], BF16) for i in range(2)]
    KTC = [sb(f"ktc{i}", [128, 4, 128], BF16) for i in range(2)]
    AM = [sb(f"am{i}", [128, 128], BF16) for i in range(2)]
    SFS = [sb(f"sf{i}", [128, 128], F32) for i in range(2)]
    EB1 = sb("eb1", [128, 640], BF16)
    SB = [sb(f"sbf{i}", [128, 128], BF16) for i in range(4)]

    PS = [es.enter_context(nc.psum_tensor(f"ps{i}", [128, 512], F32)) for i in range(8)]

    ident = consts[:, 0:128]
    tri_f = consts[:, 128:256]
    maskBD = consts[:, 256:384]
    triRev = consts[:, 384:512]
    cmask = consts[:, 512:516]

    cnt = {'mm': 0, 'acc': 0}

    def ps_mm():
        cnt['mm'] += 1
        return cnt['mm'] % 4

    def ps_acc():
        cnt['acc'] += 1
        return 4 + cnt['acc'] % 4

    rot = {}

    def rotate(name, n):
        rot[name] = rot.get(name, -1) + 1
        return rot[name] % n

    def mm(out, lhsT, rhs, start, stop, r, w):
        rec.op('pe', lambda e: e.matmul(out, lhsT, rhs, start=start, stop=stop), r, w)

    def act(out, in_, func, r, w, bias=0.0, scale=1.0):
        rec.op('act', lambda e: e.activation(out, in_, func, bias=bias, scale=scale), r, w)

    def dve_tt(out, a, b, op, r, w, eng='dve'):
        rec.op(eng, lambda e: e.tensor_tensor(out, a, b, op), r, w)

    def dve_ts(out, a, s1, s2, op0, op1, r, w, eng='dve'):
        rec.op(eng, lambda e: e.tensor_scalar(out, a, s1, s2, op0, op1), r, w)

    def dve_stt(out, a, sc, b, op0, op1, r, w):
        rec.op('dve', lambda e: e.scalar_tensor_tensor(out, a, sc, b, op0, op1), r, w)

    def dve_copy(out, a, r, w, eng='dve'):
        rec.op(eng, lambda e: e.tensor_copy(out, a), r, w)

    def dma(eng, pairs, r, w, semkey):
        pairs = list(pairs)
        rec.op(eng, lambda e: [e.dma_start(out=o, in_=i) for (o, i) in pairs], r, w, dma=(semkey, len(pairs)))

    def kps(b):
        return ('ps', b)

    dma('sp', [(gains[:], gains_d), (again[:], again_d), (consts[:], consts_d)], [], [('c', 'g')], 'c0')
    dma('pool', [(tri_bf[:], consts_d[:, 128:256])], [], [('c', 'tribf')], 'c2')
    dma('pool', [(SEL[:].rearrange("p h m -> p (h m)"), selc_d)], [], [('c', 'sel')], 'c5')
    rec.op('pool', lambda e: e.memset(ones_bf[:], 1.0), [], [('c', 'ones')])
    rec.op('pool', lambda e: e.memset(ones_f[:], 1.0), [], [('c', 'onesf')])
    CG = ('c', 'g')

    wstate = {'i': 0}

    def wslot():
        wstate['i'] += 1
        return wstate['i'] % 2

    def wview(slot, off, kc, n):
        return Wt[slot][:, off:off + kc * n].rearrange("p (kc n) -> p kc n", kc=kc)

    def wsrc_cols(wd, c0, n):
        return wd.rearrange("(kc p) n -> p kc n", p=128)[:, :, c0:c0 + n]

    def wsrc_rows(wd, r0, kc):
        return wd[r0:r0 + kc * 128, :].rearrange("(kc p) n -> p kc n", p=128)

    def load_w(slot, pairs):
        dma('pool', pairs, [], [('W', slot)], ('W', slot))

    def rstd_from_ps(psb, ncols, rs, rkeys, scale):
        act(rs[:, 0:ncols], PS[psb][:, 0:ncols], AF.Ln, rkeys + [kps(psb)], [('rs', id(rs))], bias=EPS, scale=scale)
        act(rs[:, 0:ncols], rs[:, 0:ncols], AF.Exp, [('rs', id(rs))], [('rs', id(rs))], scale=-0.5)

    def norm_h(gidx, out_fn, out_keys_fn, after_block=None):
        for tb in range(NTB):
            sl = slice(tb * 512, (tb + 1) * 512)
            pb = ps_acc()
            for c in range(NCH):
                q = SQ[rotate('sq', 2)]
                act(q[:], hT[:, c, sl], AF.Square, [('hT', c, tb)], [('sq', id(q))])
                mm(PS[pb][:], ones_bf[:], q[:], c == 0, c == NCH - 1, [('sq', id(q)), ('c', 'ones')], [kps(pb)])
            rs = RS[rotate('rs', 2)]
            rstd_from_ps(pb, 512, rs, [], 1.0 / D)
            for c in range(NCH):
                dve_stt(out_fn(c, tb), hT[:, c, sl], gains[:, gidx * 8 + c:gidx * 8 + c + 1], rs[:], ALU.mult, ALU.mult,
                        [('hT', c, tb), ('rs', id(rs)), CG], out_keys_fn(c, tb))
            if after_block is not None:
                after_block(tb)

    def norm_to_xn(gidx):
        norm_h(gidx, lambda c, tb: xn[:, c, tb * 512:(tb + 1) * 512], lambda c, tb: [('xn', tb, c)])

    def xn_keys(tb):
        return [('xn', tb, c) for c in range(NCH)]

    def projT(wv, c0, tb, slot):
        pb = ps_mm()
        for kc in range(NCH):
            mm(PS[pb][:], wv[:, kc, c0:c0 + 128], xn[:, kc, tb * 512:(tb + 1) * 512], kc == 0, kc == NCH - 1,
               [('W', slot), ('xn', tb, kc)], [kps(pb)])
        return pb

    def outproj_acc(wo, nkc, src_fn, src_keys_fn, slot):
        for tb in range(NTB):
            for oc in range(NCH):
                pb = ps_mm()
                for kc in range(nkc):
                    mm(PS[pb][:], wo[:, kc, oc * 128:(oc + 1) * 128], src_fn(kc, tb), kc == 0, kc == nkc - 1,
                       [('W', slot)] + src_keys_fn(kc, tb), [kps(pb)])
                sl = slice(tb * 512, (tb + 1) * 512)
                dve_tt(hT[:, oc, sl], hT[:, oc, sl], PS[pb][:], ALU.add, [('hT', oc, tb), kps(pb)], [('hT', oc, tb)])

    def wk_bf(off, n):
        return WK[:, off:off + n // 2].bitcast(BF16)

    def dve_recip(out, a, r, w):
        rec.op('dve', lambda e: e.reciprocal(out, a), r, w)

    def act_recip(out, a, r, w):
        act(out, a, AF.Ln, r, w)
        act(out, out, AF.Exp, w, w, scale=-1.0)

    def attn_finalize(ob, out_ap, out_keys, on_dve=False):
        rc = REC[rotate('rec', 2)]
        if on_dve:
            dve_recip(rc[0:64, :], PS[ob][64:128, :], [kps(ob)], [('rec', id(rc))])
        else:
            act_recip(rc[0:64, :], PS[ob][64:128, :], [kps(ob)], [('rec', id(rc))])
        dve_tt(out_ap, PS[ob][0:64, :], rc[0:64, :], ALU.mult, [kps(ob), ('rec', id(rc))], out_keys)

    SKEW = 3

    class Fill:
        def __init__(self):
            self.p = {}

        def add(self, tb, fn):
            self.p.setdefault(tb, []).append(fn)

        def one(self):
            for tb in sorted(self.p):
                if self.p[tb]:
                    self.p[tb].pop(0)()
                    return

        def flush(self, tb=None):
            for t in sorted(self.p):
                if tb is None or t == tb:
                    while self.p[t]:
                        self.p[t].pop(0)()

    def outproj_fill(fill, wos, srcs_fn, keys_fn, slots):
        for tb in range(NTB):
            for oc in range(NCH):
                def f(tb=tb, oc=oc):
                    pb = ps_mm()
                    n = len(wos)
                    for i in range(n):
                        mm(PS[pb][:], wos[i][:, 0, oc * 128:(oc + 1) * 128], srcs_fn(i, tb), i == 0, i == n - 1,
                           [('W', slots[i])] + keys_fn(i, tb), [kps(pb)])
                    sl = slice(tb * 512, (tb + 1) * 512)
                    dve_tt(hT[:, oc, sl], hT[:, oc, sl], PS[pb][:], ALU.add, [('hT', oc, tb), kps(pb)], [('hT', oc, tb)])
                fill.add(tb, f)

    def run_tiles(tiles, fill=None):
        pend = []
        for t in tiles + [None] * SKEW:
            if t is not None:
                if t.get('first') is not None:
                    t['first']()
                pend.append((t, t['score']()))
            if pend and (len(pend) > SKEW or t is None):
                pt_, info = pend.pop(0)
                pt_['pv'](info)
                if pt_.get('fin') is not None:
                    if fill is not None:
                        fill.flush(pt_['tb'])
                    pt_['fin']()
                elif fill is not None:
                    fill.one()
        assert not pend

    for b in range(nbatch):
        for c in range(NCH):
            dma('sp', [(hT[:, c, :], xT_d[b, c * 128:(c + 1) * 128, :])], [],
                [('hT', c, tb) for tb in range(NTB)], ('ld', c))

        for l in layers:
            rec.fence('WK')
            norm_to_xn(l)
            if l % 2 == 1:
                o_ = l // 2
                Qh = [wk_bf(0, 2048), wk_bf(1024, 2048)]
                Kh = [wk_bf(2048, 2048), wk_bf(3072, 2048)]
                OTv = wk_bf(4096, 2048)
                s0 = wslot()
                wf = wview(s0, 0, 8, 16)
                load_w(s0, [(wf, wsrc_cols(w_in_c[o_], 3072, 16))])
                dma('sp', [(fbt[:], fbias_d[o_].partition_broadcast(128))], [], [('c', 'fbt')], 'c3')
                pb = ps_mm()
                for j in range(NT):
                    for kc in range(NCH):
                        mm(PS[pb][:, j * 16:(j + 1) * 16], xn[:, kc, j * 128:(j + 1) * 128], wf[:, kc, :], kc == 0, kc == NCH - 1,
                           [('W', s0), ('xn', j // 4, kc)], [kps(pb)])
                for j in range(NT):
                    dve_tt(zf[:, j, :], PS[pb][:, j * 16:(j + 1) * 16], fbt[:], ALU.add, [kps(pb), ('c', 'fbt')], [('c', 'cum')])
                zf2 = zf[:].rearrange("p j h -> p (j h)")
                lf2 = lf[:].rearrange("p j h -> p (j h)")
                act(zf2, zf2, AF.Exp, [('c', 'cum')], [('c', 'cum')], scale=-1.0)
                act(lf2, zf2, AF.Ln, [('c', 'cum')], [('c', 'lf')], bias=1.0)
                dve_ts(lf2, lf2, -1.0, None, ALU.mult, ALU.bypass, [('c', 'lf')], [('c', 'lf')])
                pc = ps_acc()
                for j in range(NT):
                    mm(PS[pc][:, j * 16:(j + 1) * 16], tri_f, lf[:, j, :], True, j == 0, [('c', 'lf'), CG], [kps(pc)])
                    for i in range(j):
                        mm(PS[pc][:, j * 16:(j + 1) * 16], ones_f[:], lf[:, i, :], False, i == j - 1, [('c', 'lf'), ('c', 'onesf')], [kps(pc)])
                dve_copy(cum[:].rearrange("p j h -> p (j h)"), PS[pc][:, 0:256], [kps(pc)], [('c', 'cum')])
                pr = ps_acc()
                for qb in range(NTB):
                    n = 4 * qb + 4
                    for i in range(n):
                        mm(PS[pr][:, qb * 16:(qb + 1) * 16], ones_f[:], lf[:, i, :], i == 0, i == n - 1, [('c', 'lf'), ('c', 'onesf')], [kps(pr)])
                dve_copy(rcar[:].rearrange("p q h -> p (q h)"), PS[pr][:, 0:64], [kps(pr)], [('c', 'rcar')])
                for qb in range(NTB):
                    for j in range(4 * qb + 4):
                        dve_tt(fbtab[:, qb, j, :], rcar[:, qb, :], cum[:, j, :], ALU.subtract, [('c', 'rcar'), ('c', 'cum')], [('c', 'fbtab')])
                cum48 = WK[:, 0:2048].rearrange("p (j c) -> p j c", j=16)
                CT = WK[:, 4096:6144]
                CHL = wk_bf(7168, 2048)
                rec.op('pool', lambda e, cum48=cum48: e.memset(cum48, 0.0), [], [('WK', 'cum48')])
                dve_copy(cum48[:, :, 0:16], cum[:], [('c', 'cum'), ('WK', 'cum48')], [('WK', 'cum48')])
                dve_copy(cum48[:, :, 32:48], cum[:], [('c', 'cum'), ('WK', 'cum48')], [('WK', 'cum48')])
                dve_copy(cum48[:, :, 64:80], cum[:], [('c', 'cum'), ('WK', 'cum48')], [('WK', 'cum48')])
                dve_copy(cum48[:, :, 96:112], cum[:], [('c', 'cum'), ('WK', 'cum48')], [('WK', 'cum48')])
                for j4 in range(4):
                    pb = ps_mm()
                    for jj in range(4):
                        j = j4 * 4 + jj
                        rec.op('pe', lambda e, pb=pb, jj=jj, j=j, cum48=cum48: e.transpose(PS[pb][:, jj * 128:(jj + 1) * 128], cum48[:, j, :], ident),
                               [('WK', 'cum48'), CG], [kps(pb)])
                    dve_copy(CT[:, j4 * 512:(j4 + 1) * 512], PS[pb][:, :], [kps(pb)], [('WK', 'ct', j4)])
                for qb in range(NTB):
                    blk = slice(qb * 512, (qb + 1) * 512)
                    tm = TMP[rotate('tmp', 2)]
                    ktm = ('tmp', TMP.index(tm))
                    dve_ts(tm[:, :], CT[:, blk], CT[:, qb * 512 + 511:qb * 512 + 512], None, ALU.subtract, ALU.bypass, [('WK', 'ct', qb)], [ktm])
                    dve_copy(CHL[:, blk], tm[:, :], [ktm], [('WK', 'chl', qb)])
                    dve_tt(CHL[32:48, blk], tm[32:48, :], CHL[32:48, blk], ALU.subtract, [ktm, ('WK', 'chl', qb)], [('WK', 'chl', qb)])
                    dve_tt(CHL[96:112, blk], tm[96:112, :], CHL[96:112, blk], ALU.subtract, [ktm, ('WK', 'chl', qb)], [('WK', 'chl', qb)])
                rec.fence('WK')
                Vaug = wk_bf(5120, 4096).rearrange("p (j h d) -> p j h d", j=16, h=2)
                rec.op('pool', lambda e, Vaug=Vaug: e.memset(Vaug[:, :, :, 64:128], 1.0), [], [('WK', 'vones')])
                for hh in range(2):
                    rec.op('pool', lambda e, t=Kh[hh]: e.memset(t[64:128, :], 0.0), [], [('WK', 'kaug', hh)])
                    rec.op('pool', lambda e, t=Kh[hh]: e.memset(t[64:65, :], 1.0), [('WK', 'kaug', hh)], [('WK', 'kaug', hh)])
                    rec.op('pool', lambda e, t=Kh[hh]: e.memset(t[96:97, :], 1.0), [('WK', 'kaug', hh)], [('WK', 'kaug', hh)])
                nxt = wslot()
                def fox_w(slot, p):
                    par = (p // 2) % 2
                    wv = wview(slot, 0, 8, 384)
                    wo = wview(slot, 3072 + 1024 * par, 1, 1024)
                    dma('pool', [(wv[:, :, 0:128], wsrc_cols(w_in_c[o_], p * 128, 128)),
                                 (wv[:, :, 128:256], wsrc_cols(w_in_c[o_], 1024 + p * 128, 128)),
                                 (wv[:, :, 256:384], wsrc_cols(w_in_c[o_], 2048 + p * 128, 128)),
                                 (wo, wsrc_rows(w_out_c[o_], p * 128, 1))], [], [('W', (slot, 'v')), ('W', (slot, 'o', par))], ('W', slot))
                    return wv, wo, (slot, 'v'), (slot, 'o', par)
                rec.fence('W')
                wcur = fox_w(nxt, 0)
                fillF = Fill()
                for p in range(8):
                    wv, wo, slot, okey = wcur
                    if p + 1 < 8:
                        nxt = wslot()
                        wcur = fox_w(nxt, p + 1)
                    for tb in range(NTB):
                        sl = slice(tb * 512, (tb + 1) * 512)
                        pb = projT(wv, 0, tb, slot)
                        act(Qh[0][0:64, sl], PS[pb][0:64, :], AF.Copy, [kps(pb)], [('WK', 'q', 0, tb)], scale=0.125)
                        dve_ts(Qh[1][0:64, sl], PS[pb][64:128, :], 0.125, None, ALU.mult, ALU.bypass, [kps(pb)], [('WK', 'q', 1, tb)])
                        pb = projT(wv, 128, tb, slot)
                        for hh in range(2):
                            dve_copy(Kh[hh][0:64, sl], PS[pb][hh * 64:(hh + 1) * 64, :], [kps(pb)], [('WK', 'k', hh, tb)])
                        for hh in range(2):
                            pb = ps_mm()
                            mm(PS[pb][:], SEL[:, 2 * p + hh, :], CHL[0:64, sl], True, True, [('c', 'sel'), ('WK', 'chl', tb)], [kps(pb)])
                            dve_copy(Qh[hh][64:128, sl], PS[pb][64:128, :], [kps(pb)], [('WK', 'qaug', hh, tb)])
                    for j4 in range(4):
                        pb = ps_mm()
                        for jj in range(4):
                            j = j4 * 4 + jj
                            for kc in range(NCH):
                                mm(PS[pb][:, jj * 128:(jj + 1) * 128], xn[:, kc, j * 128:(j + 1) * 128], wv[:, kc, 256:384], kc == 0, kc == NCH - 1,
                                   [('W', slot), ('xn', j4, kc)], [kps(pb)])
                        act(Vaug[:, j4 * 4:(j4 + 1) * 4, :, 0:64], PS[pb][:].rearrange("p (j h d) -> p j h d", j=4, h=2), AF.Copy, [kps(pb), ('WK', 'vones')], [('WK', 'v', j4)])
                    tiles = []
                    for hh in range(2):
                        for qb in range(NTB):
                            nj = 4 * qb + 4
                            blk = {}
                            for j in range(nj):
                                def score(hh=hh, qb=qb, j=j, h=2 * p + hh):
                                    n0 = max(0, j - 4 * qb) * 128
                                    N = 512 - n0
                                    sb_ = ps_mm()
                                    mm(PS[sb_][:, 0:N], Kh[hh][:, j * 128:(j + 1) * 128], Qh[hh][:, qb * 512 + n0:(qb + 1) * 512], True, True,
                                       [('WK', 'k', hh, j // 4), ('WK', 'kaug', hh), ('WK', 'q', hh, qb), ('WK', 'qaug', hh, qb)], [kps(sb_)])
                                    pt = PT[rotate('pt', 4)]
                                    kpt = ('pt', id(pt))
                                    act(pt[:, 0:N], PS[sb_][:, 0:N], AF.Exp, [kps(sb_), ('c', 'fbtab')], [kpt], bias=fbtab[:, qb, j, h:h + 1])
                                    if j >= 4 * qb:
                                        dve_tt(pt[:, 0:128], pt[:, 0:128], tri_bf[:], ALU.mult, [kpt, ('c', 'tribf')], [kpt], eng='pool')
                                    return (n0, N, pt, kpt)

                                def pv(info, hh=hh, j=j, nj=nj, blk=blk):
                                    n0, N, pt, kpt = info
                                    mm(PS[blk['ob']][:, n0:512], Vaug[:, j, hh, :], pt[:, 0:N], j == 0, j == nj - 1, [('WK', 'v', j // 4), ('WK', 'vones'), kpt], [kps(blk['ob'])])
                                t = {'score': score, 'pv': pv}
                                if j == 0:
                                    t['first'] = lambda blk=blk: blk.__setitem__('ob', ps_acc())
                                if j == nj - 1:
                                    t['fin'] = lambda blk=blk, hh=hh, qb=qb: attn_finalize(blk['ob'], OTv[64 * hh:64 * hh + 64, qb * 512:(qb + 1) * 512], [('WK', 'ot', qb, hh)], on_dve=True)
                                    t['tb'] = qb
                                tiles.append(t)
                    run_tiles(tiles, fillF)
                    fillF.flush()
                    outproj_fill(fillF, [wo], lambda i, tb: OTv[:, tb * 512:(tb + 1) * 512], lambda i, tb: [('WK', 'ot', tb, 0), ('WK', 'ot', tb, 1)], [okey])
                    fillF.flush(0)
                fillF.flush()
                rec.fence('W')
            else:
                e_ = l // 2
                if e_ == 0:
                    rec.op('pool', lambda e: e.memset(oml[:], 1.0), [], [('c', 'oml')])
                else:
                    dma('sp', [(TMP[0][:], lbl_d[0].partition_broadcast(128))], [], [('tmp', 0)], 'c1')
                    dma('sp', [(TMP[1][:], lbl_d[1].partition_broadcast(128))], [], [('tmp', 1)], 'c4')
                    act(TMP[0][:], TMP[0][:], AF.Exp, [('tmp', 0)], [('tmp', 0)])
                    act(TMP[1][:], TMP[1][:], AF.Exp, [('tmp', 1)], [('tmp', 1)])
                    dve_tt(TMP[0][:], TMP[0][:], TMP[1][:], ALU.add, [('tmp', 0), ('tmp', 1)], [('tmp', 0)])
                    dve_recip(TMP[0][:], TMP[0][:], [('tmp', 0)], [('tmp', 0)])
                    dve_tt(TMP[0][:], TMP[1][:], TMP[0][:], ALU.mult, [('tmp', 0), ('tmp', 1)], [('tmp', 0)])
                    dve_ts(oml[:], TMP[0][:], -1.0, 1.0, ALU.mult, ALU.add, [('tmp', 0)], [('c', 'oml')])
                Qh = [wk_bf(0, 2048), wk_bf(1024, 2048)]
                Kh = [wk_bf(2048, 2048), wk_bf(3072, 2048)]
                OTv = wk_bf(4096, 2048)
                Vaug = wk_bf(6144, 4096).rearrange("p (j h d) -> p j h d", j=16, h=2)
                EB = [wk_bf(5760, 640), EB1[:]]
                rec.op('pool', lambda e, Vaug=Vaug: e.memset(Vaug[:, :, :, 64:128], 1.0), [], [('WK', 'vones')])
                for hh in range(2):
                    rec.op('pool', lambda e, t=Kh[hh]: e.memset(t[64:128, :], 0.0), [], [('WK', 'kaug', hh)])
                    rec.op('pool', lambda e, t=Qh[hh]: e.memset(t[64:128, :], 0.0), [], [('WK', 'qaug', hh)])
                nxt = wslot()
                def b_w(slot, p):
                    par = (p // 2) % 2
                    wv = wview(slot, 0, 8, 384)
                    wo = wview(slot, 3072 + 1024 * par, 1, 1024)
                    dma('pool', [(wv[:, :, 0:128], wsrc_cols(w_in_ab[e_], 2048 + p * 128, 128)),
                                 (wv[:, :, 128:256], wsrc_cols(w_in_ab[e_], 2560 + p * 128, 128)),
                                 (wv[:, :, 256:384], wsrc_cols(w_in_ab[e_], 3072 + p * 128, 128)),
                                 (wo, wsrc_rows(w_out_ab[e_], 512 + p * 128, 1))], [], [('W', (slot, 'v')), ('W', (slot, 'o', par))], ('W', slot))
                    return wv, wo, (slot, 'v'), (slot, 'o', par)
                rec.fence('W')
                wcur = b_w(nxt, 0)
                fillB = Fill()
                for p in range(4):
                    wv, wo, slot, okey = wcur
                    if p + 1 < 4:
                        nxt = wslot()
                        wcur = b_w(nxt, p + 1)
                    for tb in range(NTB):
                        sl = slice(tb * 512, (tb + 1) * 512)
                        pb = projT(wv, 0, tb, slot)
                        for hh in range(2):
                            act(Qh[hh][0:64, sl], PS[pb][hh * 64:(hh + 1) * 64, :], AF.Copy, [kps(pb)], [('WK', 'q', hh, tb)], scale=0.125)
                        pb = projT(wv, 128, tb, slot)
                        for hh in range(2):
                            dve_copy(Kh[hh][0:64, sl], PS[pb][hh * 64:(hh + 1) * 64, :], [kps(pb)], [('WK', 'k', hh, tb)])
                    for j4 in range(4):
                        pb = ps_mm()
                        for jj in range(4):
                            j = j4 * 4 + jj
                            for kc in range(NCH):
                                mm(PS[pb][:, jj * 128:(jj + 1) * 128], xn[:, kc, j * 128:(j + 1) * 128], wv[:, kc, 256:384], kc == 0, kc == NCH - 1,
                                   [('W', slot), ('xn', j4, kc)], [kps(pb)])
                        act(Vaug[:, j4 * 4:(j4 + 1) * 4, :, 0:64], PS[pb][:].rearrange("p (j h d) -> p j h d", j=4, h=2), AF.Copy, [kps(pb), ('WK', 'vones')], [('WK', 'v', j4)])
                    tiles = []
                    for hh in range(2):
                        h = 2 * p + hh
                        bh = BH[0]
                        kbh = ('WK', 'bh', 0)
                        eb = EB[hh]
                        keb = ('WK', 'eb', hh)
                        dma('sp', [(bh, relb_d[e_, h])], [], [kbh], ('bh', 0))
                        rec.op('pool', lambda e, bh=bh: e.memset(bh[0:64, 576:640], NEG), [kbh], [kbh])
                        rec.op('pool', lambda e, bh=bh: e.memset(bh[64:128, 0:64], NEG), [kbh], [kbh])
                        act(eb, bh, AF.Exp, [kbh], [keb])
                        for qb in range(NTB):
                            ds = [0] + [d for d in (-512, -384, -256, -128, 128, 256, 384) if 0 <= qb * 512 + d < T]
                            nd = len(ds)
                            blk = {}
                            for ii, d in enumerate(ds):
                                def score(hh=hh, qb=qb, d=d, eb=eb, keb=keb):
                                    kb = qb * 512 + d
                                    j = kb // 128
                                    lo = max(0, d)
                                    hi = 512 if d >= -128 else d + 640
                                    N = hi - lo
                                    sb_ = ps_mm()
                                    mm(PS[sb_][:, 0:N], Kh[hh][:, kb:kb + 128], Qh[hh][:, qb * 512 + lo:qb * 512 + hi], True, True,
                                       [('WK', 'k', hh, j // 4), ('WK', 'kaug', hh), ('WK', 'q', hh, qb), ('WK', 'qaug', hh)], [kps(sb_)])
                                    pt = PT[rotate('pt', 4)]
                                    kpt = ('pt', id(pt))
                                    act(pt[:, 0:N], PS[sb_][:, 0:N], AF.Exp, [kps(sb_)], [kpt])
                                    dve_tt(pt[:, 0:N], pt[:, 0:N], eb[:, lo - d:hi - d], ALU.mult, [kpt, keb], [kpt])
                                    return (j, lo, hi, N, pt, kpt)

                                def pv(info, hh=hh, ii=ii, nd=nd, blk=blk):
                                    j, lo, hi, N, pt, kpt = info
                                    mm(PS[blk['ob']][:, lo:hi], Vaug[:, j, hh, :], pt[:, 0:N], ii == 0, ii == nd - 1, [('WK', 'v', j // 4), ('WK', 'vones'), kpt], [kps(blk['ob'])])
                                t = {'score': score, 'pv': pv}
                                if ii == 0:
                                    t['first'] = lambda blk=blk: blk.__setitem__('ob', ps_acc())
                                if ii == nd - 1:
                                    t['fin'] = lambda blk=blk, hh=hh, qb=qb: attn_finalize(blk['ob'], OTv[64 * hh:64 * hh + 64, qb * 512:(qb + 1) * 512], [('WK', 'ot', qb, hh)])
                                    t['tb'] = qb
                                tiles.append(t)
                    run_tiles(tiles, fillB)
                    fillB.flush()
                    outproj_fill(fillB, [wo], lambda i, tb: OTv[:, tb * 512:(tb + 1) * 512], lambda i, tb: [('WK', 'ot', tb, 0), ('WK', 'ot', tb, 1)], [okey])
                    fillB.flush(0)
                fillB.flush()
                rec.fence('W')

                rec.fence('WK')

                def a_views(s_):
                    o = s_ * 3072
                    return dict(QA=WK[:, o:o + 512], GT=wk_bf(o + 512, 512),
                                KA=WK[:, o + 768:o + 1280].rearrange("p (j d) -> p j d", j=4),
                                LF=WK[:, o + 1280:o + 1792].rearrange("p (j d) -> p j d", j=4),
                                VA=wk_bf(o + 1792, 512).rearrange("p (j d) -> p j d", j=4),
                                OTA=wk_bf(o + 2048, 2048))

                def a_w(slot, h):
                    wv = wview(slot, 0, 8, 512)
                    wo = wview(slot, 4096, 1, 1024)
                    load_w(slot, [(wv[:, :, i * 128:(i + 1) * 128], wsrc_cols(w_in_ab[e_], i * 512 + h * 128, 128)) for i in range(4)]
                           + [(wo, wsrc_rows(w_out_ab[e_], h * 128, 1))])
                    return wv, wo

                def a_head_gen(h, s_, slot, wv):
                    V = a_views(s_)
                    QA, GT, KA, LF, VA, OTA = V['QA'], V['GT'], V['KA'], V['LF'], V['VA'], V['OTA']
                    W_ = lambda n, *a: ('WK', n, s_) + a
                    hs = slice(h * 128, (h + 1) * 128)
                    er, e0, ep, em, mp, mn = ER[s_], E0[s_], EP[s_], EM[s_], MP[s_], MN[s_]
                    qi, qh, kh, ktc, am = QI[s_], QH[s_], KH[s_], KTC[s_], AM[s_]
                    sf = SFS[s_]
                    sbs = [SB[2 * s_], SB[2 * s_ + 1]]
                    K = lambda n: ('a', n, s_)
                    st = {'first': True, 'sb': 0}
                    for tb in range(NTB):
                        sl = slice(tb * 512, (tb + 1) * 512)
                        pb = projT(wv, 0, tb, slot)
                        sg = TMP[s_]
                        ksg = ('tmp', s_)
                        act(sg[:], PS[pb][:], AF.Sigmoid, [kps(pb)], [ksg])
                        dve_stt(QA, PS[pb][:], float(128 ** -0.5), sg[:], ALU.mult, ALU.mult, [kps(pb), ksg], [W_('qa')])
                        yield
                        pb = projT(wv, 384, tb, slot)
                        act(sg[:], PS[pb][:], AF.Sigmoid, [kps(pb)], [ksg])
                        dve_tt(GT, PS[pb][:], sg[:], ALU.mult, [kps(pb), ksg], [W_('gt')])
                        yield
                        pv = ps_mm()
                        for jj in range(4):
                            j = tb * 4 + jj
                            for kc in range(NCH):
                                mm(PS[pv][:, jj * 128:(jj + 1) * 128], xn[:, kc, j * 128:(j + 1) * 128], wv[:, kc, 256:384], kc == 0, kc == NCH - 1,
                                   [('W', slot), ('xn', tb, kc)], [kps(pv)])
                        act(VA, PS[pv][:].rearrange("p (j d) -> p j d", j=4), AF.Copy, [kps(pv)], [W_('va')])
                        yield
                        pz = ps_mm()
                        for jj in range(4):
                            j = tb * 4 + jj
                            for kc in range(NCH):
                                mm(PS[pz][:, jj * 128:(jj + 1) * 128], xn[:, kc, j * 128:(j + 1) * 128], wv[:, kc, 128:256], kc == 0, kc == NCH - 1,
                                   [('W', slot), ('xn', tb, kc)], [kps(pz)])
                        act(sg[:], PS[pz][:], AF.Sigmoid, [kps(pz)], [ksg], scale=-1.0)
                        for jj in range(4):
                            dve_tt(KA[:, jj, :], sg[:, jj * 128:(jj + 1) * 128], oml[:, hs], ALU.mult, [ksg, ('c', 'oml')], [W_('ka')])
                        act(LF, KA, AF.Ln, [W_('ka')], [W_('lf')], bias=1.0, scale=-1.0)
                        yield
                        po = 4 + s_
                        for jj in range(4):
                            ts = slice(jj * 128, (jj + 1) * 128)
                            pbT = ps_mm()
                            mm(PS[pbT][:, 0:128], LF[:, jj, :], maskBD, True, True, [W_('lf'), CG], [kps(pbT)])
                            act(e0[:], PS[pbT][:, 0:128], AF.Exp, [kps(pbT)], [K('e0')])
                            dve_copy(mp[:, 0:NCK], PS[pbT][:, 0:128].rearrange("p (c t) -> p c t", c=NCK)[:, :, CH // 2 - 1], [kps(pbT), K('e0')], [K('mp')])
                            dve_ts(mn[:, 0:NCK], mp[:, 0:NCK], -1.0, None, ALU.mult, ALU.bypass, [K('mp')], [K('mn')])
                            for c in range(NCK):
                                cs = slice(c * CH, (c + 1) * CH)
                                act(ep[:, cs], PS[pbT][:, cs], AF.Exp, [kps(pbT), K('mn')], [K('ep')], bias=mn[:, c:c + 1])
                                act(em[:, cs], PS[pbT][:, cs], AF.Exp, [kps(pbT), K('mp')], [K('em')], bias=mp[:, c:c + 1], scale=-1.0)
                            yield
                            prv = ps_mm()
                            mm(PS[prv][:, 0:128], triRev, LF[:, jj, :], True, True, [W_('lf'), CG], [kps(prv)])
                            act(er[:], PS[prv][:, 0:128], AF.Exp, [kps(prv)], [K('er')])
                            for c in range(NCK):
                                dve_stt(ktc[:, c, :], KA[:, jj, :], cmask[:, c:c + 1], er[:], ALU.mult, ALU.mult, [W_('ka'), K('er'), CG], [K('ktc')])
                            yield
                            pkT = ps_mm()
                            rec.op('pe', lambda e, pkT=pkT, jj=jj, KA=KA: e.transpose(PS[pkT][:, 0:128], KA[:, jj, :], ident), [W_('ka'), CG], [kps(pkT)])
                            dve_tt(kh[:], PS[pkT][:, 0:128], em[:], ALU.mult, [kps(pkT), K('em')], [K('kh')])
                            dve_tt(qi[:], QA[:, ts], e0[:], ALU.mult, [W_('qa'), K('e0')], [K('qi')])
                            dve_tt(qh[:], QA[:, ts], ep[:], ALU.mult, [W_('qa'), K('ep')], [K('qh')])
                            yield
                            pu = 6 + s_
                            for c in range(NCK):
                                mm(PS[pu][:, c * 128:(c + 1) * 128], ktc[:, c, :], VA[:, jj, :], True, True, [K('ktc'), W_('va')], [kps(pu)])
                            pA = ps_mm()
                            mm(PS[pA][:, 0:128], kh[:], qh[:], True, True, [K('kh'), K('qh')], [kps(pA)])
                            dve_tt(am[:], PS[pA][:, 0:128], maskBD, ALU.mult, [kps(pA), CG], [K('am')])
                            mm(PS[po][:, ts], VA[:, jj, :], am[:], True, False, [W_('va'), K('am')], [kps(po)])
                            yield
                            for c in range(NCK):
                                cs = slice(c * CH, (c + 1) * CH)
                                if not st['first']:
                                    sbc = sbs[st['sb'] % 2]
                                    mm(PS[po][:, jj * 128 + c * CH:jj * 128 + (c + 1) * CH], sbc[:], qi[:, cs], False, c == NCK - 1, [('sbf', id(sbc)), K('qi')], [kps(po)])
                                if st['first']:
                                    dve_copy(sf[:], PS[pu][:, c * 128:(c + 1) * 128], [kps(pu)], [K('sf')])
                                else:
                                    dve_stt(sf[:], sf[:], e0[:, c * CH + CH - 1:c * CH + CH], PS[pu][:, c * 128:(c + 1) * 128], ALU.mult, ALU.add, [K('sf'), K('e0'), kps(pu)], [K('sf')])
                                st['sb'] += 1
                                sbn = sbs[st['sb'] % 2]
                                dve_copy(sbn[:], sf[:], [K('sf')], [('sbf', id(sbn))])
                                st['first'] = False
                                yield
                        q = SQ[s_]
                        act(q[:], PS[po][:], AF.Square, [kps(po)], [('sq', id(q))])
                        pn = ps_mm()
                        mm(PS[pn][:], ones_bf[:], q[:], True, True, [('sq', id(q)), ('c', 'ones')], [kps(pn)])
                        rs = RS[s_]
                        rstd_from_ps(pn, 512, rs, [], 1.0 / 128)
                        on = REC[s_]
                        dve_stt(on[:], PS[po][:], again[:, e_ * 4 + h:e_ * 4 + h + 1], rs[:], ALU.mult, ALU.mult, [kps(po), ('rs', id(rs)), CG], [('rec', id(on))])
                        dve_tt(OTA[:, sl], on[:], GT, ALU.mult, [('rec', id(on)), W_('gt')], [W_('ota', tb)])
                        yield

                for hp in range(2):
                    heads = (2 * hp, 2 * hp + 1)
                    ws = [a_w(s_, heads[s_]) for s_ in range(2)]
                    gens = [a_head_gen(heads[s_], s_, s_, ws[s_][0]) for s_ in range(2)]
                    alive = [True, True]
                    for _ in range(AOFF):
                        next(gens[0])
                    if DBG == 2:
                        for s_ in range(2):
                            for _ in gens[s_]:
                                pass
                        alive = [False, False]
                    if KSTOP > 0:
                        for s_ in range(2):
                            for _i in range(KSTOP):
                                next(gens[s_])
                        alive = [False, False]
                    while any(alive):
                        for s_ in range(2):
                            if alive[s_]:
                                try:
                                    next(gens[s_])
                                except StopIteration:
                                    alive[s_] = False
                    OT2 = [a_views(0)['OTA'], a_views(1)['OTA']]
                    wos = [ws[0][1], ws[1][1]]
                    for tb in range(NTB):
                        for oc in range(NCH):
                            sl = slice(tb * 512, (tb + 1) * 512)
                            if DBG == 3:
                                for s_ in range(2):
                                    pb = ps_mm()
                                    mm(PS[pb][:], wos[s_][:, 0, oc * 128:(oc + 1) * 128], OT2[s_][:, tb * 512:(tb + 1) * 512], True, True,
                                       [('W', s_), ('WK', 'ota', s_, tb)], [kps(pb)])
                                    dve_tt(hT[:, oc, sl], hT[:, oc, sl], PS[pb][:], ALU.add, [('hT', oc, tb), kps(pb)], [('hT', oc, tb)])
                                continue
                            pb = ps_mm()
                            for s_ in range(2):
                                mm(PS[pb][:], wos[s_][:, 0, oc * 128:(oc + 1) * 128], OT2[s_][:, tb * 512:(tb + 1) * 512], s_ == 0, s_ == 1,
                                   [('W', s_), ('WK', 'ota', s_, tb)], [kps(pb)])
                            dve_tt(hT[:, oc, sl], hT[:, oc, sl], PS[pb][:], ALU.add, [('hT', oc, tb), kps(pb)], [('hT', oc, tb)])
                wstate['i'] = 1

            rec.fence('WK')
            QX = wk_bf(0, 4096).rearrange("p (c t) -> p c t", c=2)
            OX = wk_bf(2048, 4096).rearrange("p (c t) -> p c t", c=2)
            MT = WK[:, 4096:6144].rearrange("p (c t) -> p c t", c=8)
            MN_ = wk_bf(6144, 2048).rearrange("p (c t) -> p c t", c=8)
            KX = wk_bf(7168, 512).rearrange("p (c t) -> p c t", c=2)
            VX = wk_bf(7424, 512).rearrange("p (c t) -> p c t", c=2)
            dma('sp', [(MT[:, c, :], memT_d[b, c * 128:(c + 1) * 128, :]) for c in range(NCH)], [], [('WK', 'mt')], 'mt')
            pb = ps_acc()
            for c in range(NCH):
                q = SQ[rotate('sq', 2)]
                act(q[:, 0:256], MT[:, c, :], AF.Square, [('WK', 'mt')], [('sq', id(q))])
                mm(PS[pb][:, 0:256], ones_bf[:], q[:, 0:256], c == 0, c == NCH - 1, [('sq', id(q)), ('c', 'ones')], [kps(pb)])
            rs = RS[rotate('rs', 2)]
            rstd_from_ps(pb, 256, rs, [], 1.0 / D)
            for c in range(NCH):
                dve_stt(MN_[:, c, :], MT[:, c, :], gains[:, (8 + l) * 8 + c:(8 + l) * 8 + c + 1], rs[:, 0:256], ALU.mult, ALU.mult,
                        [('WK', 'mt'), ('rs', id(rs)), CG], [('WK', 'mn')])
            norm_to_xn(4 + l)
            nxt = wslot()
            def x_w(slot, h):
                wq = wview(slot, 0, 8, 256)
                wkv = wview(slot, 2048, 8, 512)
                wo = wview(slot, 6144, 2, 1024)
                load_w(slot, [(wq, wsrc_cols(w_xq[l], h * 256, 256)),
                              (wkv[:, :, 0:256], wsrc_cols(w_xkv[l], h * 256, 256)),
                              (wkv[:, :, 256:512], wsrc_cols(w_xkv[l], 1024 + h * 256, 256)),
                              (wo, wsrc_rows(w_xo[l], h * 256, 2))])
                return wq, wkv, wo
            wcur = x_w(nxt, 0)
            for h in range(4):
                slot = nxt
                wq, wkv, wo = wcur
                if h + 1 < 4:
                    nxt = wslot()
                    wcur = x_w(nxt, h + 1)
                for dc in range(2):
                    pb = ps_mm()
                    for kc in range(NCH):
                        mm(PS[pb][:, 0:256], wkv[:, kc, dc * 128:(dc + 1) * 128], MN_[:, kc, :], kc == 0, kc == NCH - 1, [('W', slot), ('WK', 'mn')], [kps(pb)])
                    act(KX[:, dc, :], PS[pb][:, 0:256], AF.Copy, [kps(pb)], [('WK', 'kx')], scale=1.0 / 16)
                for mt in range(2):
                    pb = ps_mm()
                    for kc in range(NCH):
                        mm(PS[pb][:, 0:256], MN_[:, kc, mt * 128:(mt + 1) * 128], wkv[:, kc, 256:512], kc == 0, kc == NCH - 1, [('W', slot), ('WK', 'mn')], [kps(pb)])
                    dve_copy(VX[:, mt, :], PS[pb][:, 0:256], [kps(pb)], [('WK', 'vx')])
                for tb in range(NTB):
                    sl = slice(tb * 512, (tb + 1) * 512)
                    for dc in range(2):
                        pb = projT(wq, dc * 128, tb, slot)
                        if dc == 0:
                            act(QX[:, dc, sl], PS[pb][:], AF.Copy, [kps(pb)], [('WK', 'qx', tb, dc)])
                        else:
                            dve_copy(QX[:, dc, sl], PS[pb][:], [kps(pb)], [('WK', 'qx', tb, dc)])
                for qb in range(NTB):
                    sl = slice(qb * 512, (qb + 1) * 512)
                    pts = []
                    for mt in range(2):
                        sb_ = ps_mm()
                        for dc in range(2):
                            mm(PS[sb_][:], KX[:, dc, mt * 128:(mt + 1) * 128], QX[:, dc, sl], dc == 0, dc == 1, [('WK', 'kx'), ('WK', 'qx', qb, dc)], [kps(sb_)])
                        pt = PT[rotate('pt', 4)]
                        act(pt[:], PS[sb_][:], AF.Exp, [kps(sb_)], [('pt', id(pt))])
                        pts.append(pt)
                    db = ps_acc()
                    for mt in range(2):
                        mm(PS[db][:], ones_bf[:], pts[mt][:], mt == 0, mt == 1, [('c', 'ones'), ('pt', id(pts[mt]))], [kps(db)])
                    rc = REC[rotate('rec', 2)]
                    act_recip(rc[:], PS[db][:], [kps(db)], [('rec', id(rc))])
                    for dvc in range(2):
                        ob = ps_acc()
                        for mt in range(2):
                            mm(PS[ob][:], VX[:, mt, dvc * 128:(dvc + 1) * 128], pts[mt][:], mt == 0, mt == 1, [('WK', 'vx'), ('pt', id(pts[mt]))], [kps(ob)])
                        dve_tt(OX[:, dvc, sl], PS[ob][:], rc[:], ALU.mult, [kps(ob), ('rec', id(rc))], [('WK', 'ox', qb, dvc)])
                outproj_acc(wo, 2, lambda kc, tb: OX[:, kc, tb * 512:(tb + 1) * 512], lambda kc, tb: [('WK', 'ox', tb, kc)], slot)

            rec.fence('WK')
            norm_to_xn(12 + l)
            nxt = wslot()
            def m_w(slot, g):
                wu = wview(slot, 0, 8, 512)
                wd = wview(slot, 4096, 4, 1024)
                load_w(slot, [(wu, wsrc_cols(w_up[l], g * 512, 512)), (wd, wsrc_rows(w_down[l], g * 512, 4))])
                return wu, wd
            wcur = m_w(nxt, 0)
            for g in range(8):
                slot = nxt
                wu, wd = wcur
                if g + 1 < 8:
                    nxt = wslot()
                    wcur = m_w(nxt, g + 1)
                hb = g % 2
                HFF = wk_bf(hb * 4096, 8192).rearrange("p (c t) -> p c t", c=4)
                for tb in range(NTB):
                    sl = slice(tb * 512, (tb + 1) * 512)
                    for fc in range(4):
                        pb = projT(wu, fc * 128, tb, slot)
                        tm = PT[rotate('pt', 4)]
                        ktm = ('pt', id(tm))
                        act(tm[:], PS[pb][:], AF.Relu, [kps(pb)], [ktm])
                        dve_tt(HFF[:, fc, sl], tm[:], tm[:], ALU.mult, [ktm], [('WK', 'hff', hb, tb, fc)], eng='pool')
                outproj_acc(wd, 4, lambda kc, tb, HFF=HFF: HFF[:, kc, tb * 512:(tb + 1) * 512], lambda kc, tb, hb=hb: [('WK', 'hff', hb, tb, kc)], slot)

        rec.fence('WK')
        OUTS = [WK[:, 0:4096].rearrange("p (c t) -> p c t", c=8), WK[:, 4096:8192].rearrange("p (c t) -> p c t", c=8)]
        out_v = outT_d[b].rearrange("(c p) t -> p c t", p=128)

        def store(tb, b=b, OUTS=OUTS, out_v=out_v):
            dma('sp', [(out_v[:, :, tb * 512:(tb + 1) * 512], OUTS[tb % 2][:])], [('WK', 'out', tb % 2, c) for c in range(NCH)],
                [('od', b, tb)], ('st', tb % 2))
        if final_norm:
            norm_h(16, lambda c, tb: OUTS[tb % 2][:, c, :], lambda c, tb: [('WK', 'out', tb % 2, c)], after_block=store)
        else:
            for tb in range(NTB):
                for c in range(NCH):
                    dve_copy(OUTS[tb % 2][:, c, :], hT[:, c, tb * 512:(tb + 1) * 512], [('hT', c, tb)], [('WK', 'out', tb % 2, c)])
                store(tb)
    rec.op('sp', None, [('od', b, tb) for b in range(nbatch) for tb in range(NTB)], [])

    rec.plan()
    sems = {}
    for e in Rec.ENGS:
        for ep_ in range(max(1, rec.nepoch[e])):
            sems[(e, ep_)] = es.enter_context(nc.semaphore(f"s_{e}_{ep_}"))
    dsems = {}
    for i, k in enumerate(rec.dma_cnt):
        dsems[k] = es.enter_context(nc.semaphore(f"d_{i}"))
    block = es.enter_context(nc.Block())

    def make_body(eng):
        ops = rec.eng_ops[eng]

        def body(e):
            for o in ops:
                for Dd in o.w_eng:
                    e.wait_ge(sems[(Dd.eng, Dd.epoch)], Dd.sigval)
                for (k, t) in o.w_dma:
                    e.wait_ge(dsems[k], t)
                if o.fn is None:
                    continue
                if o.dma:
                    for ins in o.fn(e):
                        ins.then_inc(dsems[o.dsemkey], 16)
                else:
                    ins = o.fn(e)
                    if o.signal:
                        ins.then_inc(sems[(eng, o.epoch)], 1)
        return body

    block.sync(make_body('sp'))
    block.tensor(make_body('pe'))
    block.scalar(make_body('act'))
    block.vector(make_body('dve'))
    block.gpsimd(make_body('pool'))
    es.close()
    return nc, rec


def _consts():
    c = np.zeros((128, 516), np.float32)
    i = np.arange(128)
    c[:, 0:128] = np.eye(128, dtype=np.float32)
    c[:, 128:256] = (i[:, None] <= i[None, :])
    same = (i[:, None] // CH) == (i[None, :] // CH)
    c[:, 256:384] = same & (i[:, None] <= i[None, :])
    c[:, 384:512] = same & (i[:, None] > i[None, :])
    c[:, 512:516] = (i[:, None] // CH) == np.arange(4)[None, :]
    return c


def _prep_shared(inp):
    f = lambda a: np.ascontiguousarray(np.asarray(a, dtype=np.float32))
    g = np.concatenate([f(inp['norm_mix']), f(inp['norm_xattn']), f(inp['norm_mem']), f(inp['norm_mlp']),
                        f(inp['norm_final'])[None, :]], axis=0)
    gains = np.ascontiguousarray(g.reshape(17, 8, 128).transpose(2, 0, 1).reshape(128, 136))
    again = np.ascontiguousarray(f(inp['a_out_gain']).reshape(2, 4, 128).transpose(2, 0, 1).reshape(128, 8))
    sidx = np.arange(128)[:, None]
    uidx = np.arange(640)[None, :]
    ridx = np.clip(uidx - sidx, -128, 128) + 128
    relb = np.ascontiguousarray(f(inp['b_rel_bias'])[:, :, ridx])
    selc = np.zeros((64, 16, 128), np.float32)
    for hh_ in range(16):
        selc[hh_, hh_, 64] = 1.0
        selc[32 + hh_, hh_, 96] = 1.0
    sh = dict(selc=selc.reshape(64, 2048), gains=gains, again=again, lbl=f(inp['a_lb_logits']), fbias=f(inp['c_fgate_bias']), relb=relb, consts=_consts())
    for k in ('w_in_ab', 'w_out_ab', 'w_in_c', 'w_out_c', 'w_xq', 'w_xkv', 'w_xo', 'w_up', 'w_down'):
        sh[k] = f(inp[k])
    return sh


_CACHE = {}


def kernel(**inp):
    ncores = 8
    x = np.asarray(inp['x'], dtype=np.float32)
    mem = np.asarray(inp['mem'], dtype=np.float32)
    sh = _prep_shared(inp)
    in_maps = []
    for c in range(ncores):
        m = dict(sh)
        m['xT'] = np.ascontiguousarray(x[2 * c:2 * c + 2].transpose(0, 2, 1))
        m['memT'] = np.ascontiguousarray(mem[2 * c:2 * c + 2].transpose(0, 2, 1))
        in_maps.append(m)
    if 'nc' not in _CACHE:
        _CACHE['nc'] = build()[0]
    res = run_bass_kernel_spmd(_CACHE['nc'], in_maps, core_ids=list(range(ncores)))
    out = np.empty((16, T, D), np.float32)
    for c in range(ncores):
        out[2 * c:2 * c + 2] = res.results[c]['outT'].transpose(0, 2, 1)
    return out
```

```python
import numpy as np
from contextlib import ExitStack
import concourse.bass as bass
import concourse.mybir as mybir
from concourse.bass_utils import run_bass_kernel_spmd

F32 = mybir.dt.float32
BF16 = mybir.dt.bfloat16
ALU = mybir.AluOpType
AF = mybir.ActivationFunctionType

D = 1024
T = 2048
NCH = 8
NTB = 4
NT = 16
MEM = 256
EPS = 1e-6
EPOCH = 20000
NEG = -30000.0
CH = 64
NCK = 128 // CH
AOFF = 0


class Op:
    pass


class Rec:
    ENGS = ('sp', 'pe', 'act', 'dve', 'pool')

    def __init__(s):
        s.ops = []
        s.lastw = {}
        s.rd_eng = {}
        s.rd_dma = {}
        s.eng_ops = {e: [] for e in s.ENGS}
        s.dma_cnt = {}
        s.fdeps = {}

    def fence(s, region):
        deps = set(s.fdeps.get(region, ()))
        for k in list(s.lastw):
            if k[0] == region:
                deps.add(s.lastw.pop(k))
        for k in list(s.rd_eng):
            if k[0] == region:
                deps.update(s.rd_eng.pop(k).values())
        for k in list(s.rd_dma):
            if k[0] == region:
                deps.update(s.rd_dma.pop(k))
        best = {}
        for di in deps:
            o = s.ops[di]
            kk = ('d', o.dsemkey) if o.dma else ('e', o.eng)
            if kk not in best or best[kk] < di:
                best[kk] = di
        s.fdeps[region] = set(best.values())

    def op(s, eng, fn, r=(), w=(), dma=None):
        o = Op()
        o.eng = eng
        o.fn = fn
        o.idx = len(s.ops)
        o.pos = len(s.eng_ops[eng])
        o.signal = False
        o.dma = dma is not None
        deps = set()
        raw = set()
        for k in r:
            d = s.lastw.get(k)
            if d is not None:
                deps.add(d)
                raw.add(d)
            elif k[0] in s.fdeps:
                deps.update(s.fdeps[k[0]])
                raw.update(s.fdeps[k[0]])
            if k[0] == 'ps':
                for e2, d2 in s.rd_eng.get(k, {}).items():
                    if e2 != eng:
                        deps.add(d2)
        for k in w:
            d = s.lastw.get(k)
            if d is not None:
                deps.add(d)
            elif k[0] in s.fdeps:
                deps.update(s.fdeps[k[0]])
                raw.update(s.fdeps[k[0]])
            deps.update(s.rd_eng.get(k, {}).values())
            deps.update(s.rd_dma.get(k, ()))
        for k in r:
            if o.dma:
                s.rd_dma.setdefault(k, []).append(o.idx)
            else:
                s.rd_eng.setdefault(k, {})[eng] = o.idx
        for k in w:
            s.lastw[k] = o.idx
            s.rd_eng[k] = {}
            s.rd_dma[k] = []
        o.deps = deps
        o.raw = raw
        if o.dma:
            semkey, n = dma
            s.dma_cnt[semkey] = s.dma_cnt.get(semkey, 0) + n
            o.dsemkey = semkey
            o.dtarget = 16 * s.dma_cnt[semkey]
        s.ops.append(o)
        s.eng_ops[eng].append(o)
        return o

    def plan(s):
        waited = {e: {f: -1 for f in s.ENGS} for e in s.ENGS}
        waited_dma = {e: {} for e in s.ENGS}
        for o in s.ops:
            o.w_eng = []
            o.w_dma = []
            best = {}
            bestd = {}
            for di in o.deps:
                Dd = s.ops[di]
                if Dd.dma:
                    if waited_dma[o.eng].get(Dd.dsemkey, 0) >= Dd.dtarget:
                        continue
                    if bestd.get(Dd.dsemkey, 0) < Dd.dtarget:
                        bestd[Dd.dsemkey] = Dd.dtarget
                else:
                    if Dd.eng == o.eng and (o.eng == 'pe' or di not in o.raw):
                        continue
                    if Dd.pos <= waited[o.eng][Dd.eng]:
                        continue
                    if Dd.eng not in best or best[Dd.eng].pos < Dd.pos:
                        best[Dd.eng] = Dd
            for f, Dd in best.items():
                Dd.signal = True
                waited[o.eng][f] = Dd.pos
                o.w_eng.append(Dd)
            for k, t in bestd.items():
                waited_dma[o.eng][k] = t
                o.w_dma.append((k, t))
        s.nepoch = {}
        for e in s.ENGS:
            c = 0
            for o in s.eng_ops[e]:
                if o.signal:
                    o.epoch = c // EPOCH
                    o.sigval = c % EPOCH + 1
                    c += 1
            s.nepoch[e] = (c + EPOCH - 1) // EPOCH


def build(nbatch=2, layers=(0, 1, 2, 3), final_norm=True):
    nc = bass.Bass("TRN2", target_bir_lowering=False)
    rec = Rec()

    def din(name, shape):
        return nc.dram_tensor(name, list(shape), F32, kind="ExternalInput").ap()

    xT_d = din("xT", [2, D, T])
    memT_d = din("memT", [2, D, MEM])
    gains_d = din("gains", [128, 136])
    again_d = din("again", [128, 8])
    lbl_d = din("lbl", [2, 512])
    fbias_d = din("fbias", [2, 16])
    relb_d = din("relb", [2, 8, 128, 640])
    consts_d = din("consts", [128, 516])
    selc_d = din("selc", [64, 2048])
    w_in_ab = din("w_in_ab", [2, D, 3584])
    w_out_ab = din("w_out_ab", [2, D, D])
    w_in_c = din("w_in_c", [2, D, 3088])
    w_out_c = din("w_out_c", [2, D, D])
    w_xq = din("w_xq", [4, D, D])
    w_xkv = din("w_xkv", [4, D, 2 * D])
    w_xo = din("w_xo", [4, D, D])
    w_up = din("w_up", [4, D, 4 * D])
    w_down = din("w_down", [4, 4 * D, D])
    outT_d = nc.dram_tensor("outT", [2, D, T], F32, kind="ExternalOutput").ap()

    es = ExitStack()

    def sb(name, shape, dt):
        return es.enter_context(nc.sbuf_tensor(name, list(shape), dt))

    hT = sb("hT", [128, NCH, T], F32)
    xn = sb("xn", [128, NCH, T], BF16)
    Wt = [sb("W0", [128, 8192], BF16), sb("W1", [128, 8192], BF16)]
    WK = sb("WK", [128, 8192], F32)
    gains = sb("gains_s", [128, 136], F32)
    again = sb("again_s", [128, 8], F32)
    consts = sb("consts_s", [128, 516], F32)
    tri_bf = sb("tri_bf", [128, 128], BF16)
    ones_bf = sb("ones_bf", [128, 128], BF16)
    ones_f = sb("ones_f", [128, 128], F32)
    SEL = sb("sel", [64, 16, 128], BF16)
    oml = sb("oml", [128, 512], F32)
    fbt = sb("fbt", [128, 16], F32)
    lf = sb("lf", [128, 16, 16], F32)
    cum = sb("cum", [128, 16, 16], F32)
    zf = cum
    rcar = sb("rcar", [128, 4, 16], F32)
    fbtab = sb("fbtab", [128, 4, 16, 16], F32)
    SQ = [sb(f"sq{i}", [128, 512], BF16) for i in range(2)]
    RS = [sb(f"rs{i}", [128, 512], F32) for i in range(2)]
    PT = [sb(f"pt{i}", [128, 512], BF16) for i in range(4)]
    TMP = [sb(f"tmp{i}", [128, 512], F32) for i in range(2)]
    REC = [sb(f"rec{i}", [128, 512], F32) for i in range(2)]
    BH = [WK[:, 5120:5760]]
    ER = [sb(f"er{i}", [128, 128], F32) for i in range(2)]
    E0 = [sb(f"e0{i}", [128, 128], F32) for i in range(2)]
    EP = [sb(f"ep{i}", [128, 128], F32) for i in range(2)]
    EM = [sb(f"em{i}", [128, 128], F32) for i in range(2)]
    MP = [sb(f"mp{i}", [128, 4], F32) for i in range(2)]
    MN = [sb(f"mn{i}", [128, 4], F32) for i in range(2)]
    QI = [sb(f"qi{i}", [128, 128], BF16) for i in range(2)]
    QH = [sb(f"qh{i}", [128, 128], BF16) for i in range(2)]
    KH = [sb(f"kh{i}", [128, 128], BF16) for i in range(2)]
    KTC = [sb(f"ktc{i}", [128, 4, 128], BF16) for i in range(2)]
    AM = [sb(f"am{i}", [128, 128], BF16) for i in range(2)]
    SFS = [sb(f"sf{i}", [128, 128], F32) for i in range(2)]
    EB1 = sb("eb1", [128, 640], BF16)
    SB = [sb(f"sbf{i}", [128, 128], BF16) for i in range(4)]

    PS = [es.enter_context(nc.psum_tensor(f"ps{i}", [128, 512], F32)) for i in range(8)]

    ident = consts[:, 0:128]
    tri_f = consts[:, 128:256]
    maskBD = consts[:, 256:384]
    triRev = consts[:, 384:512]
    cmask = consts[:, 512:516]

    cnt = {'mm': 0, 'acc': 0}

    def ps_mm():
        cnt['mm'] += 1
        return cnt['mm'] % 4

    def ps_acc():
        cnt['acc'] += 1
        return 4 + cnt['acc'] % 4

    rot = {}

    def rotate(name, n):
        rot[name] = rot.get(name, -1) + 1
        return rot[name] % n

    def mm(out, lhsT, rhs, start, stop, r, w):
        rec.op('pe', lambda e: e.matmul(out, lhsT, rhs, start=start, stop=stop), r, w)

    def act(out, in_, func, r, w, bias=0.0, scale=1.0):
        rec.op('act', lambda e: e.activation(out, in_, func, bias=bias, scale=scale), r, w)

    def dve_tt(out, a, b, op, r, w, eng='dve'):
        rec.op(eng, lambda e: e.tensor_tensor(out, a, b, op), r, w)

    def dve_ts(out, a, s1, s2, op0, op1, r, w, eng='dve'):
        rec.op(eng, lambda e: e.tensor_scalar(out, a, s1, s2, op0, op1), r, w)

    def dve_stt(out, a, sc, b, op0, op1, r, w):
        rec.op('dve', lambda e: e.scalar_tensor_tensor(out, a, sc, b, op0, op1), r, w)

    def dve_copy(out, a, r, w, eng='dve'):
        rec.op(eng, lambda e: e.tensor_copy(out, a), r, w)

    def dma(eng, pairs, r, w, semkey):
        pairs = list(pairs)
        rec.op(eng, lambda e: [e.dma_start(out=o, in_=i) for (o, i) in pairs], r, w, dma=(semkey, len(pairs)))

    def kps(b):
        return ('ps', b)

    dma('sp', [(gains[:], gains_d), (again[:], again_d), (consts[:], consts_d)], [], [('c', 'g')], 'c0')
    dma('pool', [(tri_bf[:], consts_d[:, 128:256])], [], [('c', 'tribf')], 'c2')
    dma('pool', [(SEL[:].rearrange("p h m -> p (h m)"), selc_d)], [], [('c', 'sel')], 'c5')
    rec.op('pool', lambda e: e.memset(ones_bf[:], 1.0), [], [('c', 'ones')])
    rec.op('pool', lambda e: e.memset(ones_f[:], 1.0), [], [('c', 'onesf')])
    CG = ('c', 'g')

    wstate = {'i': 0}

    def wslot():
        wstate['i'] += 1
        return wstate['i'] % 2

    def wview(slot, off, kc, n):
        return Wt[slot][:, off:off + kc * n].rearrange("p (kc n) -> p kc n", kc=kc)

    def wsrc_cols(wd, c0, n):
        return wd.rearrange("(kc p) n -> p kc n", p=128)[:, :, c0:c0 + n]

    def wsrc_rows(wd, r0, kc):
        return wd[r0:r0 + kc * 128, :].rearrange("(kc p) n -> p kc n", p=128)

    def load_w(slot, pairs):
        dma('pool', pairs, [], [('W', slot)], ('W', slot))

    def rstd_from_ps(psb, ncols, rs, rkeys, scale):
        act(rs[:, 0:ncols], PS[psb][:, 0:ncols], AF.Ln, rkeys + [kps(psb)], [('rs', id(rs))], bias=EPS, scale=scale)
        act(rs[:, 0:ncols], rs[:, 0:ncols], AF.Exp, [('rs', id(rs))], [('rs', id(rs))], scale=-0.5)

    def norm_h(gidx, out_fn, out_keys_fn, after_block=None):
        for tb in range(NTB):
            sl = slice(tb * 512, (tb + 1) * 512)
            pb = ps_acc()
            for c in range(NCH):
                q = SQ[rotate('sq', 2)]
                act(q[:], hT[:, c, sl], AF.Square, [('hT', c, tb)], [('sq', id(q))])
                mm(PS[pb][:], ones_bf[:], q[:], c == 0, c == NCH - 1, [('sq', id(q)), ('c', 'ones')], [kps(pb)])
            rs = RS[rotate('rs', 2)]
            rstd_from_ps(pb, 512, rs, [], 1.0 / D)
            for c in range(NCH):
                dve_stt(out_fn(c, tb), hT[:, c, sl], gains[:, gidx * 8 + c:gidx * 8 + c + 1], rs[:], ALU.mult, ALU.mult,
                        [('hT', c, tb), ('rs', id(rs)), CG], out_keys_fn(c, tb))
            if after_block is not None:
                after_block(tb)

    def norm_to_xn(gidx):
        norm_h(gidx, lambda c, tb: xn[:, c, tb * 512:(tb + 1) * 512], lambda c, tb: [('xn', tb, c)])

    def xn_keys(tb):
        return [('xn', tb, c) for c in range(NCH)]

    def projT(wv, c0, tb, slot):
        pb = ps_mm()
        for kc in range(NCH):
            mm(PS[pb][:], wv[:, kc, c0:c0 + 128], xn[:, kc, tb * 512:(tb + 1) * 512], kc == 0, kc == NCH - 1,
               [('W', slot), ('xn', tb, kc)], [kps(pb)])
        return pb

    def outproj_acc(wo, nkc, src_fn, src_keys_fn, slot):
        for tb in range(NTB):
            for oc in range(NCH):
                pb = ps_mm()
                for kc in range(nkc):
                    mm(PS[pb][:], wo[:, kc, oc * 128:(oc + 1) * 128], src_fn(kc, tb), kc == 0, kc == nkc - 1,
                       [('W', slot)] + src_keys_fn(kc, tb), [kps(pb)])
                sl = slice(tb * 512, (tb + 1) * 512)
                dve_tt(hT[:, oc, sl], hT[:, oc, sl], PS[pb][:], ALU.add, [('hT', oc, tb), kps(pb)], [('hT', oc, tb)])

    def wk_bf(off, n):
        return WK[:, off:off + n // 2].bitcast(BF16)

    def dve_recip(out, a, r, w):
        rec.op('dve', lambda e: e.reciprocal(out, a), r, w)

    def act_recip(out, a, r, w):
        act(out, a, AF.Ln, r, w)
        act(out, out, AF.Exp, w, w, scale=-1.0)

    def attn_finalize(ob, out_ap, out_keys, on_dve=False):
        rc = REC[rotate('rec', 2)]
        if on_dve:
            dve_recip(rc[0:64, :], PS[ob][64:128, :], [kps(ob)], [('rec', id(rc))])
        else:
            act_recip(rc[0:64, :], PS[ob][64:128, :], [kps(ob)], [('rec', id(rc))])
        dve_tt(out_ap, PS[ob][0:64, :], rc[0:64, :], ALU.mult, [kps(ob), ('rec', id(rc))], out_keys)

    SKEW = 3

    class Fill:
        def __init__(self):
            self.p = {}

        def add(self, tb, fn):
            self.p.setdefault(tb, []).append(fn)

        def one(self):
            for tb in sorted(self.p):
                if self.p[tb]:
                    self.p[tb].pop(0)()
                    return

        def flush(self, tb=None):
            for t in sorted(self.p):
                if tb is None or t == tb:
                    while self.p[t]:
                        self.p[t].pop(0)()

    def outproj_fill(fill, wos, srcs_fn, keys_fn, slots):
        for tb in range(NTB):
            for oc in range(NCH):
                def f(tb=tb, oc=oc):
                    pb = ps_mm()
                    n = len(wos)
                    for i in range(n):
                        mm(PS[pb][:], wos[i][:, 0, oc * 128:(oc + 1) * 128], srcs_fn(i, tb), i == 0, i == n - 1,
                           [('W', slots[i])] + keys_fn(i, tb), [kps(pb)])
                    sl = slice(tb * 512, (tb + 1) * 512)
                    dve_tt(hT[:, oc, sl], hT[:, oc, sl], PS[pb][:], ALU.add, [('hT', oc, tb), kps(pb)], [('hT', oc, tb)])
                fill.add(tb, f)

    def run_tiles(tiles, fill=None):
        pend = []
        for t in tiles + [None] * SKEW:
            if t is not None:
                if t.get('first') is not None:
                    t['first']()
                pend.append((t, t['score']()))
            if pend and (len(pend) > SKEW or t is None):
                pt_, info = pend.pop(0)
                pt_['pv'](info)
                if pt_.get('fin') is not None:
                    if fill is not None:
                        fill.flush(pt_['tb'])
                    pt_['fin']()
                elif fill is not None:
                    fill.one()
        assert not pend

    for b in range(nbatch):
        for c in range(NCH):
            dma('sp', [(hT[:, c, :], xT_d[b, c * 128:(c + 1) * 128, :])], [],
                [('hT', c, tb) for tb in range(NTB)], ('ld', c))

        for l in layers:
            rec.fence('WK')
            norm_to_xn(l)
            if l % 2 == 1:
                o_ = l // 2
                Qh = [wk_bf(0, 2048), wk_bf(1024, 2048)]
                Kh = [wk_bf(2048, 2048), wk_bf(3072, 2048)]
                OTv = wk_bf(4096, 2048)
                s0 = wslot()
                wf = wview(s0, 0, 8, 16)
                load_w(s0, [(wf, wsrc_cols(w_in_c[o_], 3072, 16))])
                dma('sp', [(fbt[:], fbias_d[o_].partition_broadcast(128))], [], [('c', 'fbt')], 'c3')
                pb = ps_mm()
                for j in range(NT):
                    for kc in range(NCH):
                        mm(PS[pb][:, j * 16:(j + 1) * 16], xn[:, kc, j * 128:(j + 1) * 128], wf[:, kc, :], kc == 0, kc == NCH - 1,
                           [('W', s0), ('xn', j // 4, kc)], [kps(pb)])
                for j in range(NT):
                    dve_tt(zf[:, j, :], PS[pb][:, j * 16:(j + 1) * 16], fbt[:], ALU.add, [kps(pb), ('c', 'fbt')], [('c', 'cum')])
                zf2 = zf[:].rearrange("p j h -> p (j h)")
                lf2 = lf[:].rearrange("p j h -> p (j h)")
                act(zf2, zf2, AF.Exp, [('c', 'cum')], [('c', 'cum')], scale=-1.0)
                act(lf2, zf2, AF.Ln, [('c', 'cum')], [('c', 'lf')], bias=1.0)
                dve_ts(lf2, lf2, -1.0, None, ALU.mult, ALU.bypass, [('c', 'lf')], [('c', 'lf')])
                pc = ps_acc()
                for j in range(NT):
                    mm(PS[pc][:, j * 16:(j + 1) * 16], tri_f, lf[:, j, :], True, j == 0, [('c', 'lf'), CG], [kps(pc)])
                    for i in range(j):
                        mm(PS[pc][:, j * 16:(j + 1) * 16], ones_f[:], lf[:, i, :], False, i == j - 1, [('c', 'lf'), ('c', 'onesf')], [kps(pc)])
                dve_copy(cum[:].rearrange("p j h -> p (j h)"), PS[pc][:, 0:256], [kps(pc)], [('c', 'cum')])
                pr = ps_acc()
                for qb in range(NTB):
                    n = 4 * qb + 4
                    for i in range(n):
                        mm(PS[pr][:, qb * 16:(qb + 1) * 16], ones_f[:], lf[:, i, :], i == 0, i == n - 1, [('c', 'lf'), ('c', 'onesf')], [kps(pr)])
                dve_copy(rcar[:].rearrange("p q h -> p (q h)"), PS[pr][:, 0:64], [kps(pr)], [('c', 'rcar')])
                for qb in range(NTB):
                    for j in range(4 * qb + 4):
                        dve_tt(fbtab[:, qb, j, :], rcar[:, qb, :], cum[:, j, :], ALU.subtract, [('c', 'rcar'), ('c', 'cum')], [('c', 'fbtab')])
                cum48 = WK[:, 0:2048].rearrange("p (j c) -> p j c", j=16)
                CT = WK[:, 4096:6144]
                CHL = wk_bf(7168, 2048)
                rec.op('pool', lambda e, cum48=cum48: e.memset(cum48, 0.0), [], [('WK', 'cum48')])
                dve_copy(cum48[:, :, 0:16], cum[:], [('c', 'cum'), ('WK', 'cum48')], [('WK', 'cum48')])
                dve_copy(cum48[:, :, 32:48], cum[:], [('c', 'cum'), ('WK', 'cum48')], [('WK', 'cum48')])
                dve_copy(cum48[:, :, 64:80], cum[:], [('c', 'cum'), ('WK', 'cum48')], [('WK', 'cum48')])
                dve_copy(cum48[:, :, 96:112], cum[:], [('c', 'cum'), ('WK', 'cum48')], [('WK', 'cum48')])
                for j4 in range(4):
                    pb = ps_mm()
                    for jj in range(4):
                        j = j4 * 4 + jj
                        rec.op('pe', lambda e, pb=pb, jj=jj, j=j, cum48=cum48: e.transpose(PS[pb][:, jj * 128:(jj + 1) * 128], cum48[:, j, :], ident),
                               [('WK', 'cum48'), CG], [kps(pb)])
                    dve_copy(CT[:, j4 * 512:(j4 + 1) * 512], PS[pb][:, :], [kps(pb)], [('WK', 'ct', j4)])
                for qb in range(NTB):
                    blk = slice(qb * 512, (qb + 1) * 512)
                    tm = TMP[rotate('tmp', 2)]
                    ktm = ('tmp', TMP.index(tm))
                    dve_ts(tm[:, :], CT[:, blk], CT[:, qb * 512 + 511:qb * 512 + 512], None, ALU.subtract, ALU.bypass, [('WK', 'ct', qb)], [ktm])
                    dve_copy(CHL[:, blk], tm[:, :], [ktm], [('WK', 'chl', qb)])
                    dve_tt(CHL[32:48, blk], tm[32:48, :], CHL[32:48, blk], ALU.subtract, [ktm, ('WK', 'chl', qb)], [('WK', 'chl', qb)])
                    dve_tt(CHL[96:112, blk], tm[96:112, :], CHL[96:112, blk], ALU.subtract, [ktm, ('WK', 'chl', qb)], [('WK', 'chl', qb)])
                rec.fence('WK')
                Vaug = wk_bf(5120, 4096).rearrange("p (j h d) -> p j h d", j=16, h=2)
                rec.op('pool', lambda e, Vaug=Vaug: e.memset(Vaug[:, :, :, 64:128], 1.0), [], [('WK', 'vones')])
                for hh in range(2):
                    rec.op('pool', lambda e, t=Kh[hh]: e.memset(t[64:128, :], 0.0), [], [('WK', 'kaug', hh)])
                    rec.op('pool', lambda e, t=Kh[hh]: e.memset(t[64:65, :], 1.0), [('WK', 'kaug', hh)], [('WK', 'kaug', hh)])
                    rec.op('pool', lambda e, t=Kh[hh]: e.memset(t[96:97, :], 1.0), [('WK', 'kaug', hh)], [('WK', 'kaug', hh)])
                nxt = wslot()
                def fox_w(slot, p):
                    par = (p // 2) % 2
                    wv = wview(slot, 0, 8, 384)
                    wo = wview(slot, 3072 + 1024 * par, 1, 1024)
                    dma('pool', [(wv[:, :, 0:128], wsrc_cols(w_in_c[o_], p * 128, 128)),
                                 (wv[:, :, 128:256], wsrc_cols(w_in_c[o_], 1024 + p * 128, 128)),
                                 (wv[:, :, 256:384], wsrc_cols(w_in_c[o_], 2048 + p * 128, 128)),
                                 (wo, wsrc_rows(w_out_c[o_], p * 128, 1))], [], [('W', (slot, 'v')), ('W', (slot, 'o', par))], ('W', slot))
                    return wv, wo, (slot, 'v'), (slot, 'o', par)
                rec.fence('W')
                wcur = fox_w(nxt, 0)
                fillF = Fill()
                for p in range(8):
                    wv, wo, slot, okey = wcur
                    if p + 1 < 8:
                        nxt = wslot()
                        wcur = fox_w(nxt, p + 1)
                    for tb in range(NTB):
                        sl = slice(tb * 512, (tb + 1) * 512)
                        pb = projT(wv, 0, tb, slot)
                        act(Qh[0][0:64, sl], PS[pb][0:64, :], AF.Copy, [kps(pb)], [('WK', 'q', 0, tb)], scale=0.125)
                        dve_ts(Qh[1][0:64, sl], PS[pb][64:128, :], 0.125, None, ALU.mult, ALU.bypass, [kps(pb)], [('WK', 'q', 1, tb)])
                        pb = projT(wv, 128, tb, slot)
                        for hh in range(2):
                            dve_copy(Kh[hh][0:64, sl], PS[pb][hh * 64:(hh + 1) * 64, :], [kps(pb)], [('WK', 'k', hh, tb)])
                        for hh in range(2):
                            pb = ps_mm()
                            mm(PS[pb][:], SEL[:, 2 * p + hh, :], CHL[0:64, sl], True, True, [('c', 'sel'), ('WK', 'chl', tb)], [kps(pb)])
                            dve_copy(Qh[hh][64:128, sl], PS[pb][64:128, :], [kps(pb)], [('WK', 'qaug', hh, tb)])
                    for j4 in range(4):
                        pb = ps_mm()
                        for jj in range(4):
                            j = j4 * 4 + jj
                            for kc in range(NCH):
                                mm(PS[pb][:, jj * 128:(jj + 1) * 128], xn[:, kc, j * 128:(j + 1) * 128], wv[:, kc, 256:384], kc == 0, kc == NCH - 1,
                                   [('W', slot), ('xn', j4, kc)], [kps(pb)])
                        act(Vaug[:, j4 * 4:(j4 + 1) * 4, :, 0:64], PS[pb][:].rearrange("p (j h d) -> p j h d", j=4, h=2), AF.Copy, [kps(pb), ('WK', 'vones')], [('WK', 'v', j4)])
                    tiles = []
                    for hh in range(2):
                        for qb in range(NTB):
                            nj = 4 * qb + 4
                            blk = {}
                            for j in range(nj):
                                def score(hh=hh, qb=qb, j=j, h=2 * p + hh):
                                    n0 = max(0, j - 4 * qb) * 128
                                    N = 512 - n0
                                    sb_ = ps_mm()
                                    mm(PS[sb_][:, 0:N], Kh[hh][:, j * 128:(j + 1) * 128], Qh[hh][:, qb * 512 + n0:(qb + 1) * 512], True, True,
                                       [('WK', 'k', hh, j // 4), ('WK', 'kaug', hh), ('WK', 'q', hh, qb), ('WK', 'qaug', hh, qb)], [kps(sb_)])
                                    pt = PT[rotate('pt', 4)]
                                    kpt = ('pt', id(pt))
                                    act(pt[:, 0:N], PS[sb_][:, 0:N], AF.Exp, [kps(sb_), ('c', 'fbtab')], [kpt], bias=fbtab[:, qb, j, h:h + 1])
                                    if j >= 4 * qb:
                                        dve_tt(pt[:, 0:128], pt[:, 0:128], tri_bf[:], ALU.mult, [kpt, ('c', 'tribf')], [kpt], eng='pool')
                                    return (n0, N, pt, kpt)

                                def pv(info, hh=hh, j=j, nj=nj, blk=blk):
                                    n0, N, pt, kpt = info
                                    mm(PS[blk['ob']][:, n0:512], Vaug[:, j, hh, :], pt[:, 0:N], j == 0, j == nj - 1, [('WK', 'v', j // 4), ('WK', 'vones'), kpt], [kps(blk['ob'])])
                                t = {'score': score, 'pv': pv}
                                if j == 0:
                                    t['first'] = lambda blk=blk: blk.__setitem__('ob', ps_acc())
                                if j == nj - 1:
                                    t['fin'] = lambda blk=blk, hh=hh, qb=qb: attn_finalize(blk['ob'], OTv[64 * hh:64 * hh + 64, qb * 512:(qb + 1) * 512], [('WK', 'ot', qb, hh)], on_dve=True)
                                    t['tb'] = qb
                                tiles.append(t)
                    run_tiles(tiles, fillF)
                    fillF.flush()
                    outproj_fill(fillF, [wo], lambda i, tb: OTv[:, tb * 512:(tb + 1) * 512], lambda i, tb: [('WK', 'ot', tb, 0), ('WK', 'ot', tb, 1)], [okey])
                    fillF.flush(0)
                fillF.flush()
                rec.fence('W')
            else:
                e_ = l // 2
                if e_ == 0:
                    rec.op('pool', lambda e: e.memset(oml[:], 1.0), [], [('c', 'oml')])
                else:
                    dma('sp', [(TMP[0][:], lbl_d[0].partition_broadcast(128))], [], [('tmp', 0)], 'c1')
                    dma('sp', [(TMP[1][:], lbl_d[1].partition_broadcast(128))], [], [('tmp', 1)], 'c4')
                    act(TMP[0][:], TMP[0][:], AF.Exp, [('tmp', 0)], [('tmp', 0)])
                    act(TMP[1][:], TMP[1][:], AF.Exp, [('tmp', 1)], [('tmp', 1)])
                    dve_tt(TMP[0][:], TMP[0][:], TMP[1][:], ALU.add, [('tmp', 0), ('tmp', 1)], [('tmp', 0)])
                    dve_recip(TMP[0][:], TMP[0][:], [('tmp', 0)], [('tmp', 0)])
                    dve_tt(TMP[0][:], TMP[1][:], TMP[0][:], ALU.mult, [('tmp', 0), ('tmp', 1)], [('tmp', 0)])
                    dve_ts(oml[:], TMP[0][:], -1.0, 1.0, ALU.mult, ALU.add, [('tmp', 0)], [('c', 'oml')])
                Qh = [wk_bf(0, 2048), wk_bf(1024, 2048)]
                Kh = [wk_bf(2048, 2048), wk_bf(3072, 2048)]
                OTv = wk_bf(4096, 2048)
                Vaug = wk_bf(6144, 4096).rearrange("p (j h d) -> p j h d", j=16, h=2)
                EB = [wk_bf(5760, 640), EB1[:]]
                rec.op('pool', lambda e, Vaug=Vaug: e.memset(Vaug[:, :, :, 64:128], 1.0), [], [('WK', 'vones')])
                for hh in range(2):
                    rec.op('pool', lambda e, t=Kh[hh]: e.memset(t[64:128, :], 0.0), [], [('WK', 'kaug', hh)])
                    rec.op('pool', lambda e, t=Qh[hh]: e.memset(t[64:128, :], 0.0), [], [('WK', 'qaug', hh)])
                nxt = wslot()
                def b_w(slot, p):
                    par = (p // 2) % 2
                    wv = wview(slot, 0, 8, 384)
                    wo = wview(slot, 3072 + 1024 * par, 1, 1024)
                    dma('pool', [(wv[:, :, 0:128], wsrc_cols(w_in_ab[e_], 2048 + p * 128, 128)),
                                 (wv[:, :, 128:256], wsrc_cols(w_in_ab[e_], 2560 + p * 128, 128)),
                                 (wv[:, :, 256:384], wsrc_cols(w_in_ab[e_], 3072 + p * 128, 128)),
                                 (wo, wsrc_rows(w_out_ab[e_], 512 + p * 128, 1))], [], [('W', (slot, 'v')), ('W', (slot, 'o', par))], ('W', slot))
                    return wv, wo, (slot, 'v'), (slot, 'o', par)
                rec.fence('W')
                wcur = b_w(nxt, 0)
                fillB = Fill()
                for p in range(4):
                    wv, wo, slot, okey = wcur
                    if p + 1 < 4:
                        nxt = wslot()
                        wcur = b_w(nxt, p + 1)
                    for tb in range(NTB):
                        sl = slice(tb * 512, (tb + 1) * 512)
                        pb = projT(wv, 0, tb, slot)
                        for hh in range(2):
                            act(Qh[hh][0:64, sl], PS[pb][hh * 64:(hh + 1) * 64, :], AF.Copy, [kps(pb)], [('WK', 'q', hh, tb)], scale=0.125)
                        pb = projT(wv, 128, tb, slot)
                        for hh in range(2):
                            dve_copy(Kh[hh][0:64, sl], PS[pb][hh * 64:(hh + 1) * 64, :], [kps(pb)], [('WK', 'k', hh, tb)])
                    for j4 in range(4):
                        pb = ps_mm()
                        for jj in range(4):
                            j = j4 * 4 + jj
                            for kc in range(NCH):
                                mm(PS[pb][:, jj * 128:(jj + 1) * 128], xn[:, kc, j * 128:(j + 1) * 128], wv[:, kc, 256:384], kc == 0, kc == NCH - 1,
                                   [('W', slot), ('xn', j4, kc)], [kps(pb)])
                        act(Vaug[:, j4 * 4:(j4 + 1) * 4, :, 0:64], PS[pb][:].rearrange("p (j h d) -> p j h d", j=4, h=2), AF.Copy, [kps(pb), ('WK', 'vones')], [('WK', 'v', j4)])
                    tiles = []
                    for hh in range(2):
                        h = 2 * p + hh
                        bh = BH[0]
                        kbh = ('WK', 'bh', 0)
                        eb = EB[hh]
                        keb = ('WK', 'eb', hh)
                        dma('sp', [(bh, relb_d[e_, h])], [], [kbh], ('bh', 0))
                        rec.op('pool', lambda e, bh=bh: e.memset(bh[0:64, 576:640], NEG), [kbh], [kbh])
                        rec.op('pool', lambda e, bh=bh: e.memset(bh[64:128, 0:64], NEG), [kbh], [kbh])
                        act(eb, bh, AF.Exp, [kbh], [keb])
                        for qb in range(NTB):
                            ds = [0] + [d for d in (-512, -384, -256, -128, 128, 256, 384) if 0 <= qb * 512 + d < T]
                            nd = len(ds)
                            blk = {}
                            for ii, d in enumerate(ds):
                                def score(hh=hh, qb=qb, d=d, eb=eb, keb=keb):
                                    kb = qb * 512 + d
                                    j = kb // 128
                                    lo = max(0, d)
                                    hi = 512 if d >= -128 else d + 640
                                    N = hi - lo
                                    sb_ = ps_mm()
                                    mm(PS[sb_][:, 0:N], Kh[hh][:, kb:kb + 128], Qh[hh][:, qb * 512 + lo:qb * 512 + hi], True, True,
                                       [('WK', 'k', hh, j // 4), ('WK', 'kaug', hh), ('WK', 'q', hh, qb), ('WK', 'qaug', hh)], [kps(sb_)])
                                    pt = PT[rotate('pt', 4)]
                                    kpt = ('pt', id(pt))
                                    act(pt[:, 0:N], PS[sb_][:, 0:N], AF.Exp, [kps(sb_)], [kpt])
                                    dve_tt(pt[:, 0:N], pt[:, 0:N], eb[:, lo - d:hi - d], ALU.mult, [kpt, keb], [kpt])
                                    return (j, lo, hi, N, pt, kpt)

                                def pv(info, hh=hh, ii=ii, nd=nd, blk=blk):
                                    j, lo, hi, N, pt, kpt = info
                                    mm(PS[blk['ob']][:, lo:hi], Vaug[:, j, hh, :], pt[:, 0:N], ii == 0, ii == nd - 1, [('WK', 'v', j // 4), ('WK', 'vones'), kpt], [kps(blk['ob'])])
                                t = {'score': score, 'pv': pv}
                                if ii == 0:
                                    t['first'] = lambda blk=blk: blk.__setitem__('ob', ps_acc())
                                if ii == nd - 1:
                                    t['fin'] = lambda blk=blk, hh=hh, qb=qb: attn_finalize(blk['ob'], OTv[64 * hh:64 * hh + 64, qb * 512:(qb + 1) * 512], [('WK', 'ot', qb, hh)])
                                    t['tb'] = qb
                                tiles.append(t)
                    run_tiles(tiles, fillB)
                    fillB.flush()
                    outproj_fill(fillB, [wo], lambda i, tb: OTv[:, tb * 512:(tb + 1) * 512], lambda i, tb: [('WK', 'ot', tb, 0), ('WK', 'ot', tb, 1)], [okey])
                    fillB.flush(0)
                fillB.flush()
                rec.fence('W')

                rec.fence('WK')

                def a_views(s_):
                    o = s_ * 3072
                    return dict(QA=WK[:, o:o + 512], GT=wk_bf(o + 512, 512),
                                KA=WK[:, o + 768:o + 1280].rearrange("p (j d) -> p j d", j=4),
                                LF=WK[:, o + 1280:o + 1792].rearrange("p (j d) -> p j d", j=4),
                                VA=wk_bf(o + 1792, 512).rearrange("p (j d) -> p j d", j=4),
                                OTA=wk_bf(o + 2048, 2048))

                def a_w(slot, h):
                    wv = wview(slot, 0, 8, 512)
                    wo = wview(slot, 4096, 1, 1024)
                    load_w(slot, [(wv[:, :, i * 128:(i + 1) * 128], wsrc_cols(w_in_ab[e_], i * 512 + h * 128, 128)) for i in range(4)]
                           + [(wo, wsrc_rows(w_out_ab[e_], h * 128, 1))])
                    return wv, wo

                def a_head_gen(h, s_, slot, wv):
                    V = a_views(s_)
                    QA, GT, KA, LF, VA, OTA = V['QA'], V['GT'], V['KA'], V['LF'], V['VA'], V['OTA']
                    W_ = lambda n, *a: ('WK', n, s_) + a
                    hs = slice(h * 128, (h + 1) * 128)
                    er, e0, ep, em, mp, mn = ER[s_], E0[s_], EP[s_], EM[s_], MP[s_], MN[s_]
                    qi, qh, kh, ktc, am = QI[s_], QH[s_], KH[s_], KTC[s_], AM[s_]
                    sf = SFS[s_]
                    sbs = [SB[2 * s_], SB[2 * s_ + 1]]
                    K = lambda n: ('a', n, s_)
                    st = {'first': True, 'sb': 0}
                    for tb in range(NTB):
                        sl = slice(tb * 512, (tb + 1) * 512)
                        pb = projT(wv, 0, tb, slot)
                        sg = TMP[s_]
                        ksg = ('tmp', s_)
                        act(sg[:], PS[pb][:], AF.Sigmoid, [kps(pb)], [ksg])
                        dve_stt(QA, PS[pb][:], float(128 ** -0.5), sg[:], ALU.mult, ALU.mult, [kps(pb), ksg], [W_('qa')])
                        yield
                        pb = projT(wv, 384, tb, slot)
                        act(sg[:], PS[pb][:], AF.Sigmoid, [kps(pb)], [ksg])
                        dve_tt(GT, PS[pb][:], sg[:], ALU.mult, [kps(pb), ksg], [W_('gt')])
                        yield
                        pv = ps_mm()
                        for jj in range(4):
                            j = tb * 4 + jj
                            for kc in range(NCH):
                                mm(PS[pv][:, jj * 128:(jj + 1) * 128], xn[:, kc, j * 128:(j + 1) * 128], wv[:, kc, 256:384], kc == 0, kc == NCH - 1,
                                   [('W', slot), ('xn', tb, kc)], [kps(pv)])
                        act(VA, PS[pv][:].rearrange("p (j d) -> p j d", j=4), AF.Copy, [kps(pv)], [W_('va')])
                        yield
                        pz = ps_mm()
                        for jj in range(4):
                            j = tb * 4 + jj
                            for kc in range(NCH):
                                mm(PS[pz][:, jj * 128:(jj + 1) * 128], xn[:, kc, j * 128:(j + 1) * 128], wv[:, kc, 128:256], kc == 0, kc == NCH - 1,
                                   [('W', slot), ('xn', tb, kc)], [kps(pz)])
                        act(sg[:], PS[pz][:], AF.Sigmoid, [kps(pz)], [ksg], scale=-1.0)
                        for jj in range(4):
                            dve_tt(KA[:, jj, :], sg[:, jj * 128:(jj + 1) * 128], oml[:, hs], ALU.mult, [ksg, ('c', 'oml')], [W_('ka')])
                        act(LF, KA, AF.Ln, [W_('ka')], [W_('lf')], bias=1.0, scale=-1.0)
                        yield
                        po = 4 + s_
                        for jj in range(4):
                            ts = slice(jj * 128, (jj + 1) * 128)
                            pbT = ps_mm()
                            mm(PS[pbT][:, 0:128], LF[:, jj, :], maskBD, True, True, [W_('lf'), CG], [kps(pbT)])
                            act(e0[:], PS[pbT][:, 0:128], AF.Exp, [kps(pbT)], [K('e0')])
                            dve_copy(mp[:, 0:NCK], PS[pbT][:, 0:128].rearrange("p (c t) -> p c t", c=NCK)[:, :, CH // 2 - 1], [kps(pbT), K('e0')], [K('mp')])
                            dve_ts(mn[:, 0:NCK], mp[:, 0:NCK], -1.0, None, ALU.mult, ALU.bypass, [K('mp')], [K('mn')])
                            for c in range(NCK):
                                cs = slice(c * CH, (c + 1) * CH)
                                act(ep[:, cs], PS[pbT][:, cs], AF.Exp, [kps(pbT), K('mn')], [K('ep')], bias=mn[:, c:c + 1])
                                act(em[:, cs], PS[pbT][:, cs], AF.Exp, [kps(pbT), K('mp')], [K('em')], bias=mp[:, c:c + 1], scale=-1.0)
                            yield
                            prv = ps_mm()
                            mm(PS[prv][:, 0:128], triRev, LF[:, jj, :], True, True, [W_('lf'), CG], [kps(prv)])
                            act(er[:], PS[prv][:, 0:128], AF.Exp, [kps(prv)], [K('er')])
                            for c in range(NCK):
                                dve_stt(ktc[:, c, :], KA[:, jj, :], cmask[:, c:c + 1], er[:], ALU.mult, ALU.mult, [W_('ka'), K('er'), CG], [K('ktc')])
                            yield
                            pkT = ps_mm()
                            rec.op('pe', lambda e, pkT=pkT, jj=jj, KA=KA: e.transpose(PS[pkT][:, 0:128], KA[:, jj, :], ident), [W_('ka'), CG], [kps(pkT)])
                            dve_tt(kh[:], PS[pkT][:, 0:128], em[:], ALU.mult, [kps(pkT), K('em')], [K('kh')])
                            dve_tt(qi[:], QA[:, ts], e0[:], ALU.mult, [W_('qa'), K('e0')], [K('qi')])
                            dve_tt(qh[:], QA[:, ts], ep[:], ALU.mult, [W_('qa'), K('ep')], [K('qh')])
                            yield
                            pu = 6 + s_
                            for c in range(NCK):
                                mm(PS[pu][:, c * 128:(c + 1) * 128], ktc[:, c, :], VA[:, jj, :], True, True, [K('ktc'), W_('va')], [kps(pu)])
                            pA = ps_mm()
                            mm(PS[pA][:, 0:128], kh[:], qh[:], True, True, [K('kh'), K('qh')], [kps(pA)])
                            dve_tt(am[:], PS[pA][:, 0:128], maskBD, ALU.mult, [kps(pA), CG], [K('am')])
                            mm(PS[po][:, ts], VA[:, jj, :], am[:], True, False, [W_('va'), K('am')], [kps(po)])
                            yield
                            for c in range(NCK):
                                cs = slice(c * CH, (c + 1) * CH)
                                if not st['first']:
                                    sbc = sbs[st['sb'] % 2]
                                    mm(PS[po][:, jj * 128 + c * CH:jj * 128 + (c + 1) * CH], sbc[:], qi[:, cs], False, c == NCK - 1, [('sbf', id(sbc)), K('qi')], [kps(po)])
                                if st['first']:
                                    dve_copy(sf[:], PS[pu][:, c * 128:(c + 1) * 128], [kps(pu)], [K('sf')])
                                else:
                                    dve_stt(sf[:], sf[:], e0[:, c * CH + CH - 1:c * CH + CH], PS[pu][:, c * 128:(c + 1) * 128], ALU.mult, ALU.add, [K('sf'), K('e0'), kps(pu)], [K('sf')])
                                st['sb'] += 1
                                sbn = sbs[st['sb'] % 2]
                                dve_copy(sbn[:], sf[:], [K('sf')], [('sbf', id(sbn))])
                                st['first'] = False
                                yield
                        q = SQ[s_]
                        act(q[:], PS[po][:], AF.Square, [kps(po)], [('sq', id(q))])
                        pn = ps_mm()
                        mm(PS[pn][:], ones_bf[:], q[:], True, True, [('sq', id(q)), ('c', 'ones')], [kps(pn)])
                        rs = RS[s_]
                        rstd_from_ps(pn, 512, rs, [], 1.0 / 128)
                        on = REC[s_]
                        dve_stt(on[:], PS[po][:], again[:, e_ * 4 + h:e_ * 4 + h + 1], rs[:], ALU.mult, ALU.mult, [kps(po), ('rs', id(rs)), CG], [('rec', id(on))])
                        dve_tt(OTA[:, sl], on[:], GT, ALU.mult, [('rec', id(on)), W_('gt')], [W_('ota', tb)])
                        yield

                for hp in range(2):
                    heads = (2 * hp, 2 * hp + 1)
                    ws = [a_w(s_, heads[s_]) for s_ in range(2)]
                    gens = [a_head_gen(heads[s_], s_, s_, ws[s_][0]) for s_ in range(2)]
                    alive = [True, True]
                    for _ in range(AOFF):
                        next(gens[0])
                    while any(alive):
                        for s_ in range(2):
                            if alive[s_]:
                                try:
                                    next(gens[s_])
                                except StopIteration:
                                    alive[s_] = False
                    OT2 = [a_views(0)['OTA'], a_views(1)['OTA']]
                    wos = [ws[0][1], ws[1][1]]
                    for tb in range(NTB):
                        for oc in range(NCH):
                            sl = slice(tb * 512, (tb + 1) * 512)
                            pb = ps_mm()
                            for s_ in range(2):
                                mm(PS[pb][:], wos[s_][:, 0, oc * 128:(oc + 1) * 128], OT2[s_][:, tb * 512:(tb + 1) * 512], s_ == 0, s_ == 1,
                                   [('W', s_), ('WK', 'ota', s_, tb)], [kps(pb)])
                            dve_tt(hT[:, oc, sl], hT[:, oc, sl], PS[pb][:], ALU.add, [('hT', oc, tb), kps(pb)], [('hT', oc, tb)])
                wstate['i'] = 1

            rec.fence('WK')
            QX = wk_bf(0, 4096).rearrange("p (c t) -> p c t", c=2)
            OX = wk_bf(2048, 4096).rearrange("p (c t) -> p c t", c=2)
            MT = WK[:, 4096:6144].rearrange("p (c t) -> p c t", c=8)
            MN_ = wk_bf(6144, 2048).rearrange("p (c t) -> p c t", c=8)
            KX = wk_bf(7168, 512).rearrange("p (c t) -> p c t", c=2)
            VX = wk_bf(7424, 512).rearrange("p (c t) -> p c t", c=2)
            dma('sp', [(MT[:, c, :], memT_d[b, c * 128:(c + 1) * 128, :]) for c in range(NCH)], [], [('WK', 'mt')], 'mt')
            pb = ps_acc()
            for c in range(NCH):
                q = SQ[rotate('sq', 2)]
                act(q[:, 0:256], MT[:, c, :], AF.Square, [('WK', 'mt')], [('sq', id(q))])
                mm(PS[pb][:, 0:256], ones_bf[:], q[:, 0:256], c == 0, c == NCH - 1, [('sq', id(q)), ('c', 'ones')], [kps(pb)])
            rs = RS[rotate('rs', 2)]
            rstd_from_ps(pb, 256, rs, [], 1.0 / D)
            for c in range(NCH):
                dve_stt(MN_[:, c, :], MT[:, c, :], gains[:, (8 + l) * 8 + c:(8 + l) * 8 + c + 1], rs[:, 0:256], ALU.mult, ALU.mult,
                        [('WK', 'mt'), ('rs', id(rs)), CG], [('WK', 'mn')])
            norm_to_xn(4 + l)
            nxt = wslot()
            def x_w(slot, h):
                wq = wview(slot, 0, 8, 256)
                wkv = wview(slot, 2048, 8, 512)
                wo = wview(slot, 6144, 2, 1024)
                load_w(slot, [(wq, wsrc_cols(w_xq[l], h * 256, 256)),
                              (wkv[:, :, 0:256], wsrc_cols(w_xkv[l], h * 256, 256)),
                              (wkv[:, :, 256:512], wsrc_cols(w_xkv[l], 1024 + h * 256, 256)),
                              (wo, wsrc_rows(w_xo[l], h * 256, 2))])
                return wq, wkv, wo
            wcur = x_w(nxt, 0)
            for h in range(4):
                slot = nxt
                wq, wkv, wo = wcur
                if h + 1 < 4:
                    nxt = wslot()
                    wcur = x_w(nxt, h + 1)
                for dc in range(2):
                    pb = ps_mm()
                    for kc in range(NCH):
                        mm(PS[pb][:, 0:256], wkv[:, kc, dc * 128:(dc + 1) * 128], MN_[:, kc, :], kc == 0, kc == NCH - 1, [('W', slot), ('WK', 'mn')], [kps(pb)])
                    act(KX[:, dc, :], PS[pb][:, 0:256], AF.Copy, [kps(pb)], [('WK', 'kx')], scale=1.0 / 16)
                for mt in range(2):
                    pb = ps_mm()
                    for kc in range(NCH):
                        mm(PS[pb][:, 0:256], MN_[:, kc, mt * 128:(mt + 1) * 128], wkv[:, kc, 256:512], kc == 0, kc == NCH - 1, [('W', slot), ('WK', 'mn')], [kps(pb)])
                    dve_copy(VX[:, mt, :], PS[pb][:, 0:256], [kps(pb)], [('WK', 'vx')])
                for tb in range(NTB):
                    sl = slice(tb * 512, (tb + 1) * 512)
                    for dc in range(2):
                        pb = projT(wq, dc * 128, tb, slot)
                        if dc == 0:
                            act(QX[:, dc, sl], PS[pb][:], AF.Copy, [kps(pb)], [('WK', 'qx', tb, dc)])
                        else:
                            dve_copy(QX[:, dc, sl], PS[pb][:], [kps(pb)], [('WK', 'qx', tb, dc)])
                for qb in range(NTB):
                    sl = slice(qb * 512, (qb + 1) * 512)
                    pts = []
                    for mt in range(2):
                        sb_ = ps_mm()
                        for dc in range(2):
                            mm(PS[sb_][:], KX[:, dc, mt * 128:(mt + 1) * 128], QX[:, dc, sl], dc == 0, dc == 1, [('WK', 'kx'), ('WK', 'qx', qb, dc)], [kps(sb_)])
                        pt = PT[rotate('pt', 4)]
                        act(pt[:], PS[sb_][:], AF.Exp, [kps(sb_)], [('pt', id(pt))])
                        pts.append(pt)
                    db = ps_acc()
                    for mt in range(2):
                        mm(PS[db][:], ones_bf[:], pts[mt][:], mt == 0, mt == 1, [('c', 'ones'), ('pt', id(pts[mt]))], [kps(db)])
                    rc = REC[rotate('rec', 2)]
                    act_recip(rc[:], PS[db][:], [kps(db)], [('rec', id(rc))])
                    for dvc in range(2):
                        ob = ps_acc()
                        for mt in range(2):
                            mm(PS[ob][:], VX[:, mt, dvc * 128:(dvc + 1) * 128], pts[mt][:], mt == 0, mt == 1, [('WK', 'vx'), ('pt', id(pts[mt]))], [kps(ob)])
                        dve_tt(OX[:, dvc, sl], PS[ob][:], rc[:], ALU.mult, [kps(ob), ('rec', id(rc))], [('WK', 'ox', qb, dvc)])
                outproj_acc(wo, 2, lambda kc, tb: OX[:, kc, tb * 512:(tb + 1) * 512], lambda kc, tb: [('WK', 'ox', tb, kc)], slot)

            rec.fence('WK')
            norm_to_xn(12 + l)
            nxt = wslot()
            def m_w(slot, g):
                wu = wview(slot, 0, 8, 512)
                wd = wview(slot, 4096, 4, 1024)
                load_w(slot, [(wu, wsrc_cols(w_up[l], g * 512, 512)), (wd, wsrc_rows(w_down[l], g * 512, 4))])
                return wu, wd
            wcur = m_w(nxt, 0)
            for g in range(8):
                slot = nxt
                wu, wd = wcur
                if g + 1 < 8:
                    nxt = wslot()
                    wcur = m_w(nxt, g + 1)
                hb = g % 2
                HFF = wk_bf(hb * 4096, 8192).rearrange("p (c t) -> p c t", c=4)
                for tb in range(NTB):
                    sl = slice(tb * 512, (tb + 1) * 512)
                    for fc in range(4):
                        pb = projT(wu, fc * 128, tb, slot)
                        tm = PT[rotate('pt', 4)]
                        ktm = ('pt', id(tm))
                        act(tm[:], PS[pb][:], AF.Relu, [kps(pb)], [ktm])
                        dve_tt(HFF[:, fc, sl], tm[:], tm[:], ALU.mult, [ktm], [('WK', 'hff', hb, tb, fc)], eng='pool')
                outproj_acc(wd, 4, lambda kc, tb, HFF=HFF: HFF[:, kc, tb * 512:(tb + 1) * 512], lambda kc, tb, hb=hb: [('WK', 'hff', hb, tb, kc)], slot)

        rec.fence('WK')
        OUTS = [WK[:, 0:4096].rearrange("p (c t) -> p c t", c=8), WK[:, 4096:8192].rearrange("p (c t) -> p c t", c=8)]
        out_v = outT_d[b].rearrange("(c p) t -> p c t", p=128)

        def store(tb, b=b, OUTS=OUTS, out_v=out_v):
            dma('sp', [(out_v[:, :, tb * 512:(tb + 1) * 512], OUTS[tb % 2][:])], [('WK', 'out', tb % 2, c) for c in range(NCH)],
                [('od', b, tb)], ('st', tb % 2))
        if final_norm:
            norm_h(16, lambda c, tb: OUTS[tb % 2][:, c, :], lambda c, tb: [('WK', 'out', tb % 2, c)], after_block=store)
        else:
            for tb in range(NTB):
                for c in range(NCH):
                    dve_copy(OUTS[tb % 2][:, c, :], hT[:, c, tb * 512:(tb + 1) * 512], [('hT', c, tb)], [('WK', 'out', tb % 2, c)])
                store(tb)
    rec.op('sp', None, [('od', b, tb) for b in range(nbatch) for tb in range(NTB)], [])

    rec.plan()
    sems = {}
    for e in Rec.ENGS:
        for ep_ in range(max(1, rec.nepoch[e])):
            sems[(e, ep_)] = es.enter_context(nc.semaphore(f"s_{e}_{ep_}"))
    dsems = {}
    for i, k in enumerate(rec.dma_cnt):
        dsems[k] = es.enter_context(nc.semaphore(f"d_{i}"))
    block = es.enter_context(nc.Block())

    def make_body(eng):
        ops = rec.eng_ops[eng]

        def body(e):
            for o in ops:
                for Dd in o.w_eng:
                    e.wait_ge(sems[(Dd.eng, Dd.epoch)], Dd.sigval)
                for (k, t) in o.w_dma:
                    e.wait_ge(dsems[k], t)
                if o.fn is None:
                    continue
                if o.dma:
                    for ins in o.fn(e):
                        ins.then_inc(dsems[o.dsemkey], 16)
                else:
                    ins = o.fn(e)
                    if o.signal:
                        ins.then_inc(sems[(eng, o.epoch)], 1)
        return body

    block.sync(make_body('sp'))
    block.tensor(make_body('pe'))
    block.scalar(make_body('act'))
    block.vector(make_body('dve'))
    block.gpsimd(make_body('pool'))
    es.close()
    return nc, rec


def _consts():
    c = np.zeros((128, 516), np.float32)
    i = np.arange(128)
    c[:, 0:128] = np.eye(128, dtype=np.float32)
    c[:, 128:256] = (i[:, None] <= i[None, :])
    same = (i[:, None] // CH) == (i[None, :] // CH)
    c[:, 256:384] = same & (i[:, None] <= i[None, :])
    c[:, 384:512] = same & (i[:, None] > i[None, :])
    c[:, 512:516] = (i[:, None] // CH) == np.arange(4)[None, :]
    return c


def _prep_shared(inp):
    f = lambda a: np.ascontiguousarray(np.asarray(a, dtype=np.float32))
    g = np.concatenate([f(inp['norm_mix']), f(inp['norm_xattn']), f(inp['norm_mem']), f(inp['norm_mlp']),
                        f(inp['norm_final'])[None, :]], axis=0)
    gains = np.ascontiguousarray(g.reshape(17, 8, 128).transpose(2, 0, 1).reshape(128, 136))
    again = np.ascontiguousarray(f(inp['a_out_gain']).reshape(2, 4, 128).transpose(2, 0, 1).reshape(128, 8))
    sidx = np.arange(128)[:, None]
    uidx = np.arange(640)[None, :]
    ridx = np.clip(uidx - sidx, -128, 128) + 128
    relb = np.ascontiguousarray(f(inp['b_rel_bias'])[:, :, ridx])
    selc = np.zeros((64, 16, 128), np.float32)
    for hh_ in range(16):
        selc[hh_, hh_, 64] = 1.0
        selc[32 + hh_, hh_, 96] = 1.0
    sh = dict(selc=selc.reshape(64, 2048), gains=gains, again=again, lbl=f(inp['a_lb_logits']), fbias=f(inp['c_fgate_bias']), relb=relb, consts=_consts())
    for k in ('w_in_ab', 'w_out_ab', 'w_in_c', 'w_out_c', 'w_xq', 'w_xkv', 'w_xo', 'w_up', 'w_down'):
        sh[k] = f(inp[k])
    return sh


_CACHE = {}


def kernel(**inp):
    ncores = 8
    x = np.asarray(inp['x'], dtype=np.float32)
    mem = np.asarray(inp['mem'], dtype=np.float32)
    sh = _prep_shared(inp)
    in_maps = []
    for c in range(ncores):
        m = dict(sh)
        m['xT'] = np.ascontiguousarray(x[2 * c:2 * c + 2].transpose(0, 2, 1))
        m['memT'] = np.ascontiguousarray(mem[2 * c:2 * c + 2].transpose(0, 2, 1))
        in_maps.append(m)
    if 'nc' not in _CACHE:
        _CACHE['nc'] = build()[0]
    res = run_bass_kernel_spmd(_CACHE['nc'], in_maps, core_ids=list(range(ncores)))
    out = np.empty((16, T, D), np.float32)
    for c in range(ncores):
        out[2 * c:2 * c + 2] = res.results[c]['outT'].transpose(0, 2, 1)
    return out
```
